# Optimizing a Trainium2 kernel written in Bass

```python
import math
import jax, jax.numpy as jnp
from jax import lax
import numpy as np

D_MODEL = 1024
BATCH = 8
SEQ = 2048
DEPTH = 2

MEM_LEN = 256
EPS = 1e-6

SB_HEADS = 8
SB_HD = 64
SB_W = SB_HEADS * SB_HD
SB_BLOCK = 128

DN_HEADS = 4
DN_HD = 128
DN_W = DN_HEADS * DN_HD
DN_CONV = 4
DN_CHUNK = 64

RET_HEADS = 4
RET_QK_HD = 64
RET_V_HD = 128
RET_QK_W = RET_HEADS * RET_QK_HD
RET_V_W = RET_HEADS * RET_V_HD
RET_CHUNK = 128
ROPE_BASE = 10000.0

MEM_HEADS = 4
MEM_HD = 64
MEM_W = MEM_HEADS * MEM_HD

N_BRANCH = 4

IN_SPLITS = (SB_W, SB_W, SB_W, SB_W,
             DN_W, DN_W, DN_W, DN_W, DN_HEADS, DN_HEADS,
             RET_QK_W, RET_QK_W, RET_V_W, RET_V_W,
             MEM_W,
             N_BRANCH * D_MODEL)
IN_COLS = sum(IN_SPLITS)

kernel_name = "hybrid_gated_parallel_mixers"


def rmsnorm(x, g):
    xf = x.astype(jnp.float32)
    y = xf * lax.rsqrt(jnp.mean(xf * xf, axis=-1, keepdims=True) + EPS) * g.astype(jnp.float32)
    return y.astype(x.dtype)


def l2norm(x):
    return x * lax.rsqrt(jnp.sum(x * x, axis=-1, keepdims=True) + EPS)


def split_heads(t, n_heads):
    b, s, _ = t.shape
    return t.reshape(b, s, n_heads, -1).transpose(0, 2, 1, 3)


def merge_heads(t):
    b, h, s, d = t.shape
    return t.transpose(0, 2, 1, 3).reshape(b, s, h * d)


def stick_breaking_attention(q, k, v):
    s_len = q.shape[2]
    scale = SB_HD ** -0.5
    qf, kf, vf = q.astype(jnp.float32), k.astype(jnp.float32), v.astype(jnp.float32)
    outs = []
    for i in range(s_len // SB_BLOCK):
        q0, q1 = i * SB_BLOCK, (i + 1) * SB_BLOCK
        z = jnp.einsum('bhqd,bhkd->bhqk', qf[:, :, q0:q1], kf[:, :, :q1]) * scale
        t_idx = q0 + jnp.arange(SB_BLOCK)[:, None]
        s_idx = jnp.arange(q1)[None, :]
        causal = s_idx < t_idx
        log_stay = jnp.where(causal, jax.nn.log_sigmoid(-z), 0.0)
        after = lax.cumsum(log_stay, axis=3, reverse=True) - log_stay
        w = jnp.where(causal, jnp.exp(jax.nn.log_sigmoid(z) + after), 0.0)
        outs.append(jnp.einsum('bhqk,bhkd->bhqd', w, vf[:, :, :q1]))
    return jnp.concatenate(outs, axis=2).astype(q.dtype)


def causal_depthwise_conv(x, w):
    c = x.shape[-1]
    return lax.conv_general_dilated(
        x, w[:, None, :].astype(x.dtype), window_strides=(1,), padding=[(DN_CONV - 1, 0)],
        dimension_numbers=('NWC', 'WIO', 'NWC'), feature_group_count=c)


def gated_delta_rule(q, k, v, g, beta):
    out_dtype = v.dtype
    q, k, v = q.astype(jnp.float32), k.astype(jnp.float32), v.astype(jnp.float32)
    b, h, s_len, dk = q.shape
    dv = v.shape[-1]
    c = DN_CHUNK
    n = s_len // c
    q = q * dk ** -0.5

    def chunk(t):
        return t.reshape(b, h, n, c, *t.shape[3:])

    q, k, v, g, beta = chunk(q), chunk(k), chunk(v), chunk(g), chunk(beta)
    gc = jnp.cumsum(g, axis=-1)
    tril = jnp.tril(jnp.ones((c, c), bool))
    strict = jnp.tril(jnp.ones((c, c), bool), -1)
    decay = jnp.exp(jnp.where(tril, gc[..., :, None] - gc[..., None, :], -jnp.inf))
    k_beta = k * beta[..., None]
    v_beta = v * beta[..., None]
    lower = jnp.where(strict, jnp.einsum('bhncd,bhnmd->bhncm', k_beta, k) * decay, 0.0)
    eye = jnp.eye(c, dtype=jnp.float32)
    t_inv = lax.linalg.triangular_solve(eye + lower, jnp.broadcast_to(eye, lower.shape),
                                        left_side=True, lower=True, unit_diagonal=True)
    u = jnp.einsum('bhncm,bhnme->bhnce', t_inv, v_beta)
    w = jnp.einsum('bhncm,bhnmd->bhncd', t_inv, k_beta * jnp.exp(gc)[..., None])
    a_intra = jnp.where(tril, jnp.einsum('bhncd,bhnmd->bhncm', q, k) * decay, 0.0)

    def step(state, xs):
        q_c, k_c, u_c, w_c, a_c, g_c = xs
        v_new = u_c - jnp.einsum('bhcd,bhde->bhce', w_c, state)
        o = jnp.einsum('bhcd,bhde->bhce', q_c * jnp.exp(g_c)[..., None], state) \
            + jnp.einsum('bhcm,bhme->bhce', a_c, v_new)
        g_last = g_c[..., -1]
        state = state * jnp.exp(g_last)[..., None, None] + jnp.einsum(
            'bhcd,bhce->bhde', k_c * jnp.exp(g_last[..., None] - g_c)[..., None], v_new)
        return state, o

    xs = tuple(jnp.moveaxis(t, 2, 0) for t in (q, k, u, w, a_intra, gc))
    _, o = lax.scan(step, jnp.zeros((b, h, dk, dv), jnp.float32), xs)
    o = jnp.moveaxis(o, 0, 2).reshape(b, h, s_len, dv)
    return o.astype(out_dtype)


def rope(x, positions):
    half = x.shape[-1] // 2
    inv = ROPE_BASE ** (-jnp.arange(half, dtype=jnp.float32) / half)
    ang = positions[:, None, :, None].astype(jnp.float32) * inv
    cos, sin = jnp.cos(ang), jnp.sin(ang)
    xf = x.astype(jnp.float32)
    x1, x2 = xf[..., :half], xf[..., half:]
    return jnp.concatenate([x1 * cos - x2 * sin, x1 * sin + x2 * cos], axis=-1).astype(x.dtype)


def retention_chunkwise(q, k, v):
    out_dtype = v.dtype
    q, k, v = q.astype(jnp.float32), k.astype(jnp.float32), v.astype(jnp.float32)
    b, h, s_len, dk = q.shape
    dv = v.shape[-1]
    c = RET_CHUNK
    n = s_len // c
    k = k * dk ** -0.5
    log_gamma = jnp.log1p(-(2.0 ** (-5.0 - jnp.arange(h, dtype=jnp.float32))))
    idx = jnp.arange(c, dtype=jnp.float32)
    rel = idx[:, None] - idx[None, :]
    intra_decay = jnp.where(rel >= 0, jnp.exp(jnp.maximum(rel, 0.0) * log_gamma[:, None, None]), 0.0)
    cross_decay = jnp.exp((idx + 1.0) * log_gamma[:, None])
    state_decay = jnp.exp((c - 1.0 - idx) * log_gamma[:, None])
    chunk_decay = jnp.exp(c * log_gamma)

    def chunk(t):
        return jnp.moveaxis(t.reshape(b, h, n, c, t.shape[-1]), 2, 0)

    qc, kc, vc = chunk(q), chunk(k), chunk(v)
    scores = jnp.einsum('nbhcd,nbhmd->nbhcm', qc, kc) * intra_decay[None, None]
    o_intra = jnp.einsum('nbhcm,nbhme->nbhce', scores, vc)

    def step(state, xs):
        q_c, k_c, v_c = xs
        o_cross = jnp.einsum('bhcd,bhde->bhce', q_c, state) * cross_decay[None, :, :, None]
        state = state * chunk_decay[None, :, None, None] + jnp.einsum(
            'bhcd,bhce->bhde', k_c * state_decay[None, :, :, None], v_c)
        return state, o_cross

    _, o_cross = lax.scan(step, jnp.zeros((b, h, dk, dv), jnp.float32), (qc, kc, vc))
    o = jnp.moveaxis(o_intra + o_cross, 0, 2).reshape(b, h, s_len, dv)
    return o.astype(out_dtype)


def head_groupnorm(o, g):
    of = o.astype(jnp.float32)
    mu = jnp.mean(of, axis=-1, keepdims=True)
    var = jnp.mean(jnp.square(of - mu), axis=-1, keepdims=True)
    y = (of - mu) * lax.rsqrt(var + EPS) * g[None, :, None, :].astype(jnp.float32)
    return y.astype(o.dtype)


def memory_cross_attention(q, mem, mem_g, w_kv):
    mem_n = rmsnorm(mem, mem_g)
    kv = jnp.einsum('bmd,de->bme', mem_n, w_kv)
    km, vm = jnp.split(kv, 2, axis=-1)
    qh, kh, vh = split_heads(q, MEM_HEADS), split_heads(km, MEM_HEADS), split_heads(vm, MEM_HEADS)
    scores = jnp.einsum('bhsd,bhmd->bhsm', qh.astype(jnp.float32), kh.astype(jnp.float32)) * MEM_HD ** -0.5
    p = jax.nn.softmax(scores, axis=-1).astype(vh.dtype)
    return merge_heads(jnp.einsum('bhsm,bhmd->bhsd', p, vh))


def setup_inputs(seed: int = 0) -> dict:
    key = jax.random.key(seed)
    ks = jax.random.split(key, 24)
    f32 = jnp.float32

    def nrm(k, shape, scale):
        return jax.random.normal(k, shape, f32) * scale

    x = nrm(ks[0], (BATCH, SEQ, D_MODEL), 1.0)
    mem = nrm(ks[1], (BATCH, MEM_LEN, D_MODEL), 1.0)
    offset = jax.random.randint(ks[2], (BATCH, 1), 0, 4096, dtype=jnp.int32)
    positions = (offset + jnp.arange(SEQ, dtype=jnp.int32)[None, :]).astype(jnp.int32)

    norm_g = 1.0 + nrm(ks[3], (DEPTH, D_MODEL), 0.02)
    mem_norm_g = 1.0 + nrm(ks[4], (DEPTH, D_MODEL), 0.02)
    w_in = nrm(ks[5], (DEPTH, D_MODEL, IN_COLS), D_MODEL ** -0.5)
    b_gate = nrm(ks[6], (DEPTH, N_BRANCH * D_MODEL), 0.1)
    dn_conv_w = nrm(ks[7], (DEPTH, DN_CONV, 3 * DN_W), DN_CONV ** -0.5)
    dn_a_log = jnp.log(jax.random.uniform(ks[8], (DEPTH, DN_HEADS), f32, 1.0, 16.0))
    dt = jnp.exp(jax.random.uniform(ks[9], (DEPTH, DN_HEADS), f32, math.log(1e-3), math.log(1e-1)))
    dn_dt_bias = dt + jnp.log(-jnp.expm1(-dt))
    dn_norm_g = 1.0 + nrm(ks[10], (DEPTH, DN_HD), 0.02)
    ret_norm_g = 1.0 + nrm(ks[11], (DEPTH, RET_V_W), 0.02)
    w_mem_kv = nrm(ks[12], (DEPTH, D_MODEL, 2 * MEM_W), D_MODEL ** -0.5)
    w_br_sb = nrm(ks[13], (DEPTH, SB_W, D_MODEL), SB_W ** -0.5)
    w_br_dn = nrm(ks[14], (DEPTH, DN_W, D_MODEL), DN_W ** -0.5)
    w_br_ret = nrm(ks[15], (DEPTH, RET_V_W, D_MODEL), RET_V_W ** -0.5)
    w_br_mem = nrm(ks[16], (DEPTH, MEM_W, D_MODEL), MEM_W ** -0.5)
    w_out = nrm(ks[17], (DEPTH, D_MODEL, D_MODEL), D_MODEL ** -0.5)
    final_norm_g = 1.0 + nrm(ks[18], (D_MODEL,), 0.02)
    return {"x": x, "mem": mem, "positions": positions, "norm_g": norm_g,
            "mem_norm_g": mem_norm_g, "w_in": w_in, "b_gate": b_gate, "dn_conv_w": dn_conv_w,
            "dn_a_log": dn_a_log, "dn_dt_bias": dn_dt_bias, "dn_norm_g": dn_norm_g,
            "ret_norm_g": ret_norm_g, "w_mem_kv": w_mem_kv, "w_br_sb": w_br_sb,
            "w_br_dn": w_br_dn, "w_br_ret": w_br_ret, "w_br_mem": w_br_mem,
            "w_out": w_out, "final_norm_g": final_norm_g}


def reference(x, mem, positions, norm_g, mem_norm_g, w_in, b_gate, dn_conv_w, dn_a_log,
              dn_dt_bias, dn_norm_g, ret_norm_g, w_mem_kv, w_br_sb, w_br_dn, w_br_ret,
              w_br_mem, w_out, final_norm_g):
    b, s_len, _ = x.shape
    split_idx = np.cumsum(IN_SPLITS)[:-1].tolist()
    for l in range(DEPTH):
        h = rmsnorm(x, norm_g[l])
        proj = jnp.einsum('bsd,de->bse', h, w_in[l])
        (sb_q, sb_k, sb_v, sb_z,
         dn_q, dn_k, dn_v, dn_z, dn_a, dn_b,
         rt_q, rt_k, rt_v, rt_z,
         mem_q, gate_logits) = jnp.split(proj, split_idx, axis=-1)

        o_sb = stick_breaking_attention(split_heads(sb_q, SB_HEADS), split_heads(sb_k, SB_HEADS),
                                        split_heads(sb_v, SB_HEADS))
        o_sb = merge_heads(o_sb) * jax.nn.silu(sb_z)

        qkv = jax.nn.silu(causal_depthwise_conv(jnp.concatenate([dn_q, dn_k, dn_v], axis=-1), dn_conv_w[l]))
        cq, ck, cv = jnp.split(qkv, 3, axis=-1)
        cq = l2norm(split_heads(cq, DN_HEADS).astype(jnp.float32))
        ck = l2norm(split_heads(ck, DN_HEADS).astype(jnp.float32))
        cv = split_heads(cv, DN_HEADS)
        g_log = -jnp.exp(dn_a_log[l].astype(jnp.float32)) * jax.nn.softplus(
            dn_a.astype(jnp.float32) + dn_dt_bias[l].astype(jnp.float32))
        beta = jax.nn.sigmoid(dn_b.astype(jnp.float32))
        o_dn = gated_delta_rule(cq, ck, cv, g_log.transpose(0, 2, 1), beta.transpose(0, 2, 1))
        o_dn = merge_heads(rmsnorm(o_dn, dn_norm_g[l])) * jax.nn.silu(dn_z)

        rq = rope(split_heads(rt_q, RET_HEADS), positions)
        rk = rope(split_heads(rt_k, RET_HEADS), positions)
        o_rt = retention_chunkwise(rq, rk, split_heads(rt_v, RET_HEADS))
        o_rt = head_groupnorm(o_rt, ret_norm_g[l].reshape(RET_HEADS, RET_V_HD))
        o_rt = merge_heads(o_rt) * jax.nn.silu(rt_z)

        o_mem = memory_cross_attention(mem_q, mem, mem_norm_g[l], w_mem_kv[l])

        gates = jax.nn.sigmoid(gate_logits.astype(jnp.float32) + b_gate[l].astype(jnp.float32))
        gates = gates.reshape(b, s_len, N_BRANCH, D_MODEL).astype(x.dtype)
        merged = (gates[:, :, 0] * jnp.einsum('bsw,wd->bsd', o_sb, w_br_sb[l])
                  + gates[:, :, 1] * jnp.einsum('bsw,wd->bsd', o_dn, w_br_dn[l])
                  + gates[:, :, 2] * jnp.einsum('bsw,wd->bsd', o_rt, w_br_ret[l])
                  + gates[:, :, 3] * jnp.einsum('bsw,wd->bsd', o_mem, w_br_mem[l]))
        x = x + jnp.einsum('bsd,de->bse', merged, w_out[l])
    return rmsnorm(x, final_norm_g)
```

```python
import math
from contextlib import ExitStack
import numpy as np
import concourse.bass as bass
import concourse.mybir as mybir
from concourse.bass_utils import run_bass_kernel_spmd

F32 = mybir.dt.float32
BF16 = mybir.dt.bfloat16
I32 = mybir.dt.int32
AF = mybir.ActivationFunctionType
ALU = mybir.AluOpType

D = 1024
S = 2048
DEPTH = 2
MEM_LEN = 256
EPS = 1e-6
IN_COLS = 9992
C_SBQ, C_SBK, C_SBV, C_SBZ = 0, 512, 1024, 1536
C_DNQ, C_DNK, C_DNV, C_DNZ, C_DNA, C_DNB = 2048, 2560, 3072, 3584, 4096, 4100
C_RTQ, C_RTK, C_RTV, C_RTZ = 4104, 4360, 4616, 5128
C_MQ = 5640
C_G = 5896
NT = 4
TC = 512


class _Uniq:
    def __init__(self):
        self.n = 0

    def __add__(self, name):
        self.n += 1
        return f"{name}_{self.n}"


class _Rec:
    def __getattr__(self, name):
        return lambda *a, **k: (name, a, k)


_REC = _Rec()


class Prog:
    LIMIT = 30000

    def __init__(self, nc, stack):
        self.nc = nc
        self.stack = stack
        self.names = ["pe", "act", "dve", "pool", "sp"]
        self.q = {n: [] for n in self.names}
        self.cnt = {n: 0 for n in self.names}
        self.sems = {n: [] for n in self.names}
        self.seen = {n: {} for n in self.names}
        self.bufs = {}
        self.dma_sem = {}
        self.dma_cnt = {}
        self.same_sync = True
        self.dma_i = 0
        self.nins = 0

    def _eng_sem(self, eng, g):
        ep = (g - 1) // self.LIMIT
        while len(self.sems[eng]) <= ep:
            s = self.stack.enter_context(self.nc.semaphore(f"s_{eng}_{len(self.sems[eng])}"))
            self.sems[eng].append(s)
        return self.sems[eng][ep], (g - 1) % self.LIMIT + 1

    def _tok_sem(self, tok):
        kind, g = tok
        if kind.startswith("dma:"):
            return self.dma_sem[kind], 16 * g
        return self._eng_sem(kind, g)

    def _need(self, eng, tok):
        kind, g = tok
        if kind == eng:
            if eng in ("pe", "sp") or not self.same_sync:
                return False
        return self.seen[eng].get(kind, 0) < g

    def _collect(self, eng, reads, writes):
        toks = []
        for k in reads:
            b = self.bufs.get(k)
            if b and b[0] is not None:
                toks.append(b[0])
        for k in writes:
            b = self.bufs.get(k)
            if b:
                if b[0] is not None:
                    toks.append(b[0])
                toks.extend(b[1].items())
        need = {}
        for t in toks:
            if self._need(eng, t):
                need[t[0]] = max(need.get(t[0], 0), t[1])
        return list(need.items())

    def _update(self, tok, reads, writes):
        for k in writes:
            self.bufs[k] = [tok, {}]
        for k in reads:
            b = self.bufs.setdefault(k, [None, {}])
            if k in writes:
                continue
            b[1][tok[0]] = max(b[1].get(tok[0], 0), tok[1])

    def op(self, eng, fn, reads=(), writes=(), inc=True):
        call = fn(_REC)
        fn = lambda e, c=call: getattr(e, c[0])(*c[1], **c[2])
        waits = self._collect(eng, reads, writes)
        for t in waits:
            self.seen[eng][t[0]] = t[1]
        ws = [self._tok_sem(t) for t in waits]
        for w in ws[1:]:
            self.q[eng].append(("wait", w[0], w[1]))
        if inc:
            self.cnt[eng] += 1
            tok = (eng, self.cnt[eng])
            sem, val = self._eng_sem(eng, self.cnt[eng])
            self.q[eng].append(("ins", fn, ws[0] if ws else None, (sem, 1)))
        else:
            tok = (eng, self.cnt[eng] + 1)
            self.q[eng].append(("ins", fn, ws[0] if ws else None, None))
        self._update(tok, reads, writes)
        self.nins += 1
        return tok

    NDS = 16

    def dma(self, out, in_, reads=(), writes=(), stream="d0", queue="sp"):
        j = self.dma_i % self.NDS
        self.dma_i += 1
        kind = f"dma:{j}"
        if kind not in self.dma_sem:
            self.dma_sem[kind] = self.stack.enter_context(self.nc.semaphore(f"sd_{j}"))
            self.dma_cnt[kind] = 0
        waits = self._collect(queue, reads, writes)
        if self.dma_cnt[kind] > 0 and self.seen[queue].get(kind, 0) < self.dma_cnt[kind]:
            waits = [w for w in waits if w[0] != kind] + [(kind, self.dma_cnt[kind])]
        for t in waits:
            self.seen[queue][t[0]] = t[1]
        for t in waits:
            s, v = self._tok_sem(t)
            self.q[queue].append(("wait", s, v))
        self.dma_cnt[kind] += 1
        tok = (kind, self.dma_cnt[kind])
        self.q[queue].append(("ins", lambda e, o=out, i=in_: e.dma_start(out=o, in_=i), None,
                              (self.dma_sem[kind], 16)))
        self._update(tok, reads, writes)
        self.nins += 1
        return tok

    def barrier(self):
        toks = [(n, self.cnt[n]) for n in self.names if self.cnt[n] > 0]
        toks += [(k, c) for k, c in self.dma_cnt.items() if c > 0]
        for eng in self.names:
            for t in toks:
                if t[0] == eng and eng in ("pe", "sp"):
                    continue
                if self.seen[eng].get(t[0], 0) < t[1]:
                    self.seen[eng][t[0]] = t[1]
                    s, v = self._tok_sem(t)
                    self.q[eng].append(("wait", s, v))
        self.bufs = {}

    def finish(self, final_keys):
        toks = []
        for k in final_keys:
            b = self.bufs.get(k)
            if b and b[0] is not None:
                toks.append(b[0])
        for t in toks:
            s, v = self._tok_sem(t)
            self.q["sp"].append(("wait", s, v))

    def emit(self):
        nc = self.nc
        with nc.Block() as block:
            def replay(e, name):
                for ent in self.q[name]:
                    if ent[0] == "wait":
                        e.wait_ge(ent[1], ent[2])
                    else:
                        _, fn, w, inc = ent
                        ins = fn(e)
                        if w is not None:
                            ins._wait_ge(w[0], w[1])
                        if inc is not None:
                            ins.then_inc(inc[0], inc[1])

            @block.sync
            def _(e):
                replay(e, "sp")

            @block.scalar
            def _(e):
                replay(e, "act")

            @block.vector
            def _(e):
                replay(e, "dve")

            @block.tensor
            def _(e):
                replay(e, "pe")

            @block.gpsimd
            def _(e):
                replay(e, "pool")


def _consts():
    c = {}
    i = np.arange(128)
    c["ident"] = np.eye(128, dtype=np.float32)
    c["ones"] = np.ones((128, 128), np.float32)
    c["trineg"] = -(i[:, None] >= i[None, :]).astype(np.float32)
    e0 = np.zeros((128, 128), np.float32)
    e0[0, :] = 1.0
    c["e0"] = e0
    c["masksb"] = (i[:, None] < i[None, :]).astype(np.float32)
    return c


def _rt_consts():
    gam = [1.0 - 2.0 ** (-5.0 - h) for h in range(4)]
    i = np.arange(128)
    dt = np.zeros((128, 4, 128), np.float64)
    sd = np.zeros((128, 4), np.float64)
    cd = np.zeros((128, 2, 128), np.float64)
    for h in range(4):
        rel = i[None, :] - i[:, None]
        dt[:, h, :] = np.where(rel >= 0, gam[h] ** np.maximum(rel, 0), 0.0)
        sd[:, h] = gam[h] ** (127 - i)
    for p in range(2):
        for hh in range(2):
            cd[hh * 64:(hh + 1) * 64, p, :] = (gam[2 * p + hh] ** (i + 1.0))[None, :]
    inv = 10000.0 ** (-(np.arange(32, dtype=np.float32)) / np.float32(32))
    rp = np.zeros((128, 4), np.float32)
    rp[:, 0] = np.tile(inv.astype(np.float32), 4)
    rp[:, 1] = np.where((i % 64) < 32, -1.0, 1.0)
    rp[:, 2] = np.float32(math.pi / 2)
    rp[:, 3] = EPS
    cn = np.eye(128) - 1.0 / 128.0
    return {"rt_dt": dt.astype(np.float32), "rt_sd": sd.astype(np.float32), "rt_cd": cd.astype(np.float32),
            "rt_rp": rp, "rt_cn": cn.astype(np.float32)}


def _dn_consts():
    i = np.arange(64)
    c = np.zeros((128, 4, 128), np.float32)
    c[:, 0, :] = np.eye(128)
    c[:, 1, :] = 1.0
    c[0:64, 2, 0:64] = (i[:, None] <= i[None, :])
    c[0:64, 3, 0:64] = (i[:, None] < i[None, :])
    return c


CONST_NAMES = ["ident", "ones", "trineg", "e0", "masksb"]


def build(n_layers=DEPTH, debug=False, branches=(0, 1, 2, 3)):
    _u = _Uniq()
    nc = bass.Bass("TRN2", target_bir_lowering=False)
    dr = {}

    def din(name, shape, dt=F32):
        dr[name] = nc.dram_tensor(name, list(shape), dt, kind="ExternalInput").ap()
        return dr[name]

    xT_d = din("xT", [D, S])
    memT_d = din("memT", [D, MEM_LEN])
    w_in_d = din("w_in", [DEPTH, D, IN_COLS])
    w_kv_d = din("w_mem_kv", [DEPTH, D, 512])
    w_br_d = {0: din("w_br_sb", [DEPTH, 512, D]), 1: din("w_br_dn", [DEPTH, 512, D]),
              2: din("w_br_ret", [DEPTH, 512, D]), 3: din("w_br_mem", [DEPTH, 256, D])}
    w_out_d = din("w_out", [DEPTH, D, D])
    normg_d = din("norm_g", [DEPTH, 128, 8])
    memg_d = din("mem_norm_g", [DEPTH, 128, 8])
    bgate_d = din("b_gate", [DEPTH, 128, 32])
    fing_d = din("final_norm_g", [128, 8])
    cst_d = {n: din("c_" + n, [128, 128]) for n in CONST_NAMES}
    dnc_d = din("c_dn", [128, 4, 128])
    dncw_d = din("dn_conv_w", [DEPTH, 128, 12, 4])
    dng_d = din("dn_norm_g", [128, DEPTH])
    dnal_d = din("dn_alog", [DEPTH, 128, 128])
    dndt_d = din("dn_dtb", [DEPTH, 128, 128])
    pos_d = din("pos", [128, S], I32)
    retg_d = din("ret_norm_g", [DEPTH, 128, 4])
    rtdt_d = din("c_rt_dt", [128, 4, 128])
    rtsd_d = din("c_rt_sd", [128, 4])
    rtcd_d = din("c_rt_cd", [128, 2, 128])
    rtrp_d = din("c_rt_rp", [128, 4])
    rtcn_d = din("c_rt_cn", [128, 128])
    outT_d = nc.dram_tensor("outT", [D, S], F32, kind="ExternalOutput").ap()
    dbg_d = {}
    if debug:
        dbg_d["hT"] = nc.dram_tensor("dbg_hT", [D, S], F32, kind="ExternalOutput").ap()
        dbg_d["omem"] = nc.dram_tensor("dbg_omem", [256, S], F32, kind="ExternalOutput").ap()
        for nm in ["osb", "odn", "ort"]:
            dbg_d[nm] = nc.dram_tensor("dbg_" + nm, [512, S], F32, kind="ExternalOutput").ap()

    with ExitStack() as st:
        P = Prog(nc, st)

        def sb(name, shape, dt):
            return st.enter_context(nc.sbuf_tensor(_u + "s_" + name, list(shape), dt))

        xT = sb("xT", [128, 8, S], F32)
        hT = sb("hT", [128, 8, S], BF16)
        mT = sb("mT", [128, 8, S], BF16)
        obT = sb("obT", [128, 4, S], BF16)
        wst = [sb(f"wst{i}", [128, 8, 128], F32) for i in range(2)]
        NWB = 5
        wbp = [sb(f"wb{i}", [128, 8, 128], BF16) for i in range(NWB)]
        cst = {n: sb("k_" + n, [128, 128], BF16) for n in CONST_NAMES}
        cstf = sb("cstf", [128, 128], F32)
        normg = sb("normg", [128, DEPTH, 8], F32)
        memg = sb("memg", [128, DEPTH, 8], F32)
        bgate = sb("bgate", [128, DEPTH, 32], F32)
        fing = sb("fing", [128, 8], F32)
        ps = [st.enter_context(nc.psum_tensor(f"ps{i}", [128, 512], F32)) for i in range(8)]
        PS = [("ps", i) for i in range(8)]

        wcount = [0]
        wbcount = [0]

        def load_w(src, n=128, rows=8, eng="pool"):
            i = wcount[0] % 2
            wcount[0] += 1
            j = wbcount[0] % NWB
            wbcount[0] += 1
            stg, wb = wst[i], wbp[j]
            P.dma(stg[:, 0:rows, 0:n], src.rearrange("(k p) n -> p k n", p=128),
                  writes=[("wst", i)], stream=f"w{i}")
            fn = lambda e, o=wb[:, 0:rows, 0:n], a=stg[:, 0:rows, 0:n]: e.tensor_copy(out=o, in_=a)
            P.op(eng, fn, reads=[("wst", i)], writes=[("wb", j)])
            return wb, ("wb", j)

        for n in CONST_NAMES:
            P.dma(cstf[:, :], cst_d[n][:, :], writes=["cstf"], stream="c")
            P.op("dve", lambda e, o=cst[n][:, :]: e.tensor_copy(out=o, in_=cstf[:, :]),
                 reads=["cstf"], writes=["k_" + n])
        for l in range(DEPTH):
            P.dma(normg[:, l, :], normg_d[l, :, :], writes=["normg"], stream="c")
            P.dma(memg[:, l, :], memg_d[l, :, :], writes=["memg"], stream="c")
            P.dma(bgate[:, l, :], bgate_d[l, :, :], writes=["bgate"], stream="c")
        P.dma(fing[:, :], fing_d[:, :], writes=["fing"], stream="c")
        for k in range(8):
            P.dma(xT[:, k, :], xT_d[k * 128:(k + 1) * 128, :], writes=[("xT", k)], stream="x")

        rtdt = sb("rtdt", [128, 4, 128], F32)
        rtsd = sb("rtsd", [128, 4], F32)
        rtcd = sb("rtcd", [128, 2, 128], F32)
        rtrp = sb("rtrp", [128, 4], F32)
        rtcn = sb("rtcn", [128, 128], BF16)
        retg = sb("retg", [128, DEPTH, 4], F32)
        P.dma(rtdt[:, :, :], rtdt_d[:, :, :], writes=["rtdt"], stream="c")
        P.dma(rtsd[:, :], rtsd_d[:, :], writes=["rtsd"], stream="c")
        P.dma(rtcd[:, :, :], rtcd_d[:, :, :], writes=["rtcd"], stream="c")
        P.dma(rtrp[:, :], rtrp_d[:, :], writes=["rtrp"], stream="c")
        P.dma(cstf[:, :], rtcn_d[:, :], writes=["cstf"], stream="c")
        P.op("dve", lambda e: e.tensor_copy(out=rtcn[:, :], in_=cstf[:, :]), reads=["cstf"], writes=["rtcn"])
        for l in range(DEPTH):
            P.dma(retg[:, l, :], retg_d[l, :, :], writes=["retg"], stream="c")
        dnc = sb("dnc", [128, 4, 128], F32)
        dncw = sb("dncw", [128, DEPTH, 12, 4], F32)
        dng = sb("dng", [128, DEPTH], F32)
        P.dma(dnc[:, :, :], dnc_d[:, :, :], writes=["dnc"], stream="c")
        P.dma(dng[:, :], dng_d[:, :], writes=["dng"], stream="c")
        for l in range(DEPTH):
            P.dma(dncw[:, l, :, :], dncw_d[l, :, :, :], writes=["dncw"], stream="c")
        def rms_stats(src_tile, src_keys, nk, ncols, c0, psum_i, sq_tile, sq_key, rstd_tile, rstd_key, extra):
            for k in range(nk):
                P.op("act", lambda e, k=k: e.activation(out=sq_tile[:, k % 2, 0:ncols], in_=src_tile[:, k, c0:c0 + ncols],
                                                        func=AF.Square),
                     reads=[src_keys[k]], writes=[(sq_key, k % 2)])
                P.op("pe", lambda e, k=k: e.matmul(ps[psum_i][:, 0:ncols], lhsT=cst["ones"][:, :],
                                                   rhs=sq_tile[:, k % 2, 0:ncols], start=(k == 0), stop=(k == nk - 1)),
                     reads=[(sq_key, k % 2), "k_ones"], writes=[PS[psum_i]])
            P.op("act", lambda e: e.activation(out=rstd_tile[:, 0:ncols], in_=ps[psum_i][:, 0:ncols], func=AF.Ln,
                                               bias=epsb[:, 0:1], scale=1.0),
                 reads=[PS[psum_i], "epsb"], writes=[rstd_key])
            P.op("act", lambda e: e.activation(out=rstd_tile[:, 0:ncols], in_=rstd_tile[:, 0:ncols], func=AF.Exp,
                                               scale=-0.5),
                 reads=[rstd_key], writes=[rstd_key])

        epsb = sb("epsb", [128, 1], F32)
        P.op("dve", lambda e: e.memset(epsb[:, :], float(D * EPS)), writes=["epsb"])
        sqt = sb("sqt", [128, 2, TC], BF16)
        rstd = sb("rstd", [128, TC], F32)
        g32 = sb("g32", [128, 8], F32)

        def norm_to(dst_fn, gsrc, layer_tag):
            P.op("dve", lambda e: e.tensor_scalar(out=g32[:, :], in0=gsrc, scalar1=float(math.sqrt(D)), scalar2=None,
                                                  op0=ALU.mult), reads=["normg", "fing"], writes=["g32"])
            for tc in range(NT):
                rms_stats(xT, [("xT", k) for k in range(8)], 8, TC, tc * TC, 0, sqt, "sqt", rstd, "rstd", None)
                for k in range(8):
                    ap, key = dst_fn(k, tc)
                    P.op("dve", lambda e, k=k, tc=tc, ap=ap: e.scalar_tensor_tensor(
                        out=ap, in0=xT[:, k, tc * TC:(tc + 1) * TC], scalar=g32[:, k:k + 1], in1=rstd[:, :],
                        op0=ALU.mult, op1=ALU.mult),
                        reads=[("xT", k), "g32", "rstd"], writes=[key])

        def proj_fm(l, col0, ncols, evac, wsrc=None):
            src = (w_in_d[l, :, col0:col0 + ncols] if wsrc is None else wsrc)
            wb, wkey = load_w(src, n=ncols)
            for tc in range(NT):
                pi = 1 + (tc % 2)
                for k in range(8):
                    P.op("pe", lambda e, k=k, tc=tc, pi=pi: e.matmul(
                        ps[pi][0:ncols, :], lhsT=wb[:, k, 0:ncols], rhs=hT[:, k, tc * TC:(tc + 1) * TC],
                        start=(k == 0), stop=(k == 7)),
                        reads=[wkey, ("hT", k)], writes=[PS[pi]], inc=(k == 7))
                evac(tc, ps[pi][0:ncols, :], PS[pi])

        def proj_tm(l, col0, ncols, evac, ntok=128):
            wb, wkey = load_w(w_in_d[l, :, col0:col0 + ncols], n=ncols)
            for tt in range(S // ntok):
                pi = 1 + (tt % 2)
                for k in range(8):
                    P.op("pe", lambda e: e.matmul(ps[pi][0:ntok, 0:ncols], lhsT=hT[:, k, tt * ntok:(tt + 1) * ntok],
                                                  rhs=wb[:, k, 0:ncols], start=(k == 0), stop=(k == 7)),
                         reads=[wkey, ("hT", k)], writes=[PS[pi]], inc=(k == 7))
                evac(tt, ps[pi][0:ntok, 0:ncols], PS[pi])

        oneb = sb("oneb", [128, 1], F32)
        P.op("dve", lambda e: e.memset(oneb[:, :], 1.0), writes=["oneb"])

        def sb_attention(l):
            for hp in range(4):
                with ExitStack() as ph:
                    def sbp(name, shape, dt):
                        return ph.enter_context(nc.sbuf_tensor(_u + "s_" + name, list(shape), dt))
                    qT = sbp("sbq", [128, S], BF16)
                    kT = sbp("sbk", [128, S], BF16)
                    zsT = sbp("sbz", [128, S], BF16)
                    vtm = sbp("sbv", [128, 16, 128], BF16)
                    ez = [sbp(f"ez{i}", [128, TC], F32) for i in range(2)]
                    spb = [sbp(f"spb{i}", [128, TC], BF16) for i in range(2)]
                    Gb = [sbp(f"Gb{i}", [128, TC], BF16) for i in range(2)]
                    wbf = [sbp(f"wbf{i}", [128, TC], BF16) for i in range(2)]
                    proj_fm(l, C_SBQ + hp * 128, 128, lambda tc, pap, pkey: P.op(
                        "act", lambda e: e.activation(out=qT[:, tc * TC:(tc + 1) * TC], in_=pap, func=AF.Copy, scale=0.125),
                        reads=[pkey], writes=["sbq"]))
                    proj_fm(l, C_SBK + hp * 128, 128, lambda tc, pap, pkey: P.op(
                        "dve", lambda e: e.tensor_copy(out=kT[:, tc * TC:(tc + 1) * TC], in_=pap),
                        reads=[pkey], writes=["sbk"]))
                    proj_fm(l, C_SBZ + hp * 128, 128, lambda tc, pap, pkey: P.op(
                        "act", lambda e: e.activation(out=zsT[:, tc * TC:(tc + 1) * TC], in_=pap, func=AF.Silu),
                        reads=[pkey], writes=["sbz"]))
                    proj_tm(l, C_SBV + hp * 128, 128, lambda tt, pap, pkey: P.op(
                        "dve", lambda e: e.tensor_copy(out=vtm[:, tt, :], in_=pap),
                        reads=[pkey], writes=[("sbv", tt)]))
                    it = 0
                    for hh in range(2):
                        r0 = hh * 64
                        rs = slice(r0, r0 + 64)
                        for qc in range(4):
                            q0 = qc * TC
                            po = 6 + (qc % 2)
                            kmax = qc * 4 + 3
                            prev = None
                            for kb in range(kmax, -1, -1):
                                j = kb - qc * 4
                                c0 = 128 * j if j >= 0 else 0
                                b = it % 2
                                it += 1
                                pz, pg, pa = b, 2 + b, 4 + b
                                cs = slice(c0, TC)
                                tsl = slice(q0 + c0, q0 + TC)
                                ksl = slice(kb * 128, (kb + 1) * 128)
                                P.op("pe", lambda e: e.matmul(ps[pz][:, cs], lhsT=kT[rs, ksl], rhs=qT[rs, tsl],
                                                              start=True, stop=True),
                                     reads=["sbk", "sbq"], writes=[PS[pz]])
                                P.op("act", lambda e: e.activation(out=ez[b][:, cs], in_=ps[pz][:, cs], func=AF.Exp),
                                     reads=[PS[pz]], writes=[("ez", b)])
                                P.op("act", lambda e: e.activation(out=spb[b][:, cs], in_=ez[b][:, cs], func=AF.Ln,
                                                                   bias=oneb[:, 0:1], scale=1.0),
                                     reads=[("ez", b), "oneb"], writes=[("spb", b)])
                                if j >= 0:
                                    P.op("dve", lambda e: e.tensor_tensor(out=spb[b][:, c0:c0 + 128], in0=spb[b][:, c0:c0 + 128],
                                                                          in1=cst["masksb"][:, :], op=ALU.mult),
                                         reads=[("spb", b), "k_masksb"], writes=[("spb", b)])
                                P.op("pe", lambda e: e.matmul(ps[pg][:, cs], lhsT=cst["trineg"][:, :], rhs=spb[b][:, cs],
                                                              start=True, stop=(prev is None)),
                                     reads=["k_trineg", ("spb", b)], writes=[PS[pg]], inc=(prev is None))
                                if prev is not None:
                                    pb, pc0 = prev
                                    pcs = slice(pc0, TC)
                                    P.op("pe", lambda e: e.matmul(ps[pg][:, pcs], lhsT=cst["e0"][:, :], rhs=Gb[pb][:, pcs],
                                                                  start=False, stop=True),
                                         reads=["k_e0", ("Gb", pb)], writes=[PS[pg]])
                                P.op("dve", lambda e: e.tensor_copy(out=Gb[b][:, cs], in_=ps[pg][:, cs]),
                                     reads=[PS[pg]], writes=[("Gb", b)])
                                P.op("pe", lambda e: e.matmul(ps[pa][:, cs], lhsT=kT[rs, ksl], rhs=qT[rs, tsl],
                                                              start=True, stop=False),
                                     reads=["sbk", "sbq"], writes=[PS[pa]], inc=False)
                                P.op("pe", lambda e: e.matmul(ps[pa][:, cs], lhsT=cst["ident"][:, :], rhs=Gb[b][:, cs],
                                                              start=False, stop=True),
                                     reads=["k_ident", ("Gb", b)], writes=[PS[pa]])
                                P.op("act", lambda e: e.activation(out=wbf[b][:, cs], in_=ps[pa][:, cs], func=AF.Exp),
                                     reads=[PS[pa]], writes=[("wbf", b)])
                                if j >= 0:
                                    P.op("dve", lambda e: e.tensor_tensor(out=wbf[b][:, c0:c0 + 128], in0=wbf[b][:, c0:c0 + 128],
                                                                          in1=cst["masksb"][:, :], op=ALU.mult),
                                         reads=[("wbf", b), "k_masksb"], writes=[("wbf", b)])
                                P.op("pe", lambda e: e.matmul(ps[po][rs, cs], lhsT=vtm[:, kb, rs], rhs=wbf[b][:, cs],
                                                              start=(kb == kmax), stop=(kb == 0)),
                                     reads=[("sbv", kb), ("wbf", b)], writes=[PS[po]])
                                prev = (b, c0)
                            P.op("dve", lambda e: e.tensor_tensor(out=obT[rs, hp, q0:q0 + TC], in0=ps[po][rs, :],
                                                                  in1=zsT[rs, q0:q0 + TC], op=ALU.mult),
                                 reads=[PS[po], "sbz"], writes=[("obT", hp)])
                    P.barrier()

        def retention(l):
            TWO_PI = 2.0 * math.pi
            C1 = 6.28125
            C2 = TWO_PI - C1
            with ExitStack() as ph0:
                def sb0(name, shape, dt):
                    return ph0.enter_context(nc.sbuf_tensor(_u + ("s_" + name), list(shape), dt))
                qrT = sb0("rqr", [128, 2, S], BF16)
                krT = sb0("rkr", [128, 2, S], BF16)
                with ExitStack() as ph1:
                    def sb1(name, shape, dt):
                        return ph1.enter_context(nc.sbuf_tensor(_u + ("s_" + name), list(shape), dt))
                    COS2 = sb1("rcos", [128, S], BF16)
                    SIN2 = sb1("rsin", [128, S], BF16)
                    pint = sb1("rpint", [128, TC], I32)
                    ta = sb1("rta", [128, TC], F32)
                    tk = sb1("rtk", [128, TC], F32)
                    tm = sb1("rtm", [128, TC], F32)
                    ki = pint
                    for tc in range(NT):
                        tsl = slice(tc * TC, (tc + 1) * TC)
                        P.dma(pint[:, :], pos_d[:, tsl], writes=["rpint"], stream="x")
                        P.op("dve", lambda e: e.tensor_copy(out=ta[:, :], in_=pint[:, :]), reads=["rpint"], writes=["rta"])
                        P.op("dve", lambda e: e.tensor_scalar(out=ta[:, :], in0=ta[:, :], scalar1=rtrp[:, 0:1], scalar2=None,
                                                              op0=ALU.mult), reads=["rta", "rtrp"], writes=["rta"])
                        P.op("dve", lambda e: e.tensor_scalar(out=ki[:, :], in0=ta[:, :], scalar1=float(1.0 / TWO_PI),
                                                              scalar2=None, op0=ALU.mult), reads=["rta"], writes=["rpint"])
                        P.op("dve", lambda e: e.tensor_copy(out=tk[:, :], in_=ki[:, :]), reads=["rpint"], writes=["rtk"])
                        P.op("dve", lambda e: e.scalar_tensor_tensor(out=ta[:, :], in0=tk[:, :], scalar=-C1, in1=ta[:, :],
                                                                     op0=ALU.mult, op1=ALU.add),
                             reads=["rtk", "rta"], writes=["rta"])
                        P.op("dve", lambda e: e.scalar_tensor_tensor(out=ta[:, :], in0=tk[:, :], scalar=-C2, in1=ta[:, :],
                                                                     op0=ALU.mult, op1=ALU.add),
                             reads=["rtk", "rta"], writes=["rta"])
                        P.op("dve", lambda e: e.tensor_single_scalar(out=tm[:, :], in_=ta[:, :], scalar=float(math.pi),
                                                                     op=ALU.is_gt), reads=["rta"], writes=["rtm"])
                        P.op("dve", lambda e: e.scalar_tensor_tensor(out=ta[:, :], in0=tm[:, :], scalar=-TWO_PI, in1=ta[:, :],
                                                                     op0=ALU.mult, op1=ALU.add),
                             reads=["rtm", "rta"], writes=["rta"])
                        P.op("dve", lambda e: e.tensor_single_scalar(out=tm[:, :], in_=ta[:, :], scalar=float(-math.pi),
                                                                     op=ALU.is_lt), reads=["rta"], writes=["rtm"])
                        P.op("dve", lambda e: e.scalar_tensor_tensor(out=ta[:, :], in0=tm[:, :], scalar=TWO_PI, in1=ta[:, :],
                                                                     op0=ALU.mult, op1=ALU.add),
                             reads=["rtm", "rta"], writes=["rta"])
                        P.op("act", lambda e: e.activation(out=tk[:, :], in_=ta[:, :], func=AF.Sin),
                             reads=["rta"], writes=["rtk"])
                        P.op("dve", lambda e: e.tensor_scalar(out=SIN2[:, tsl], in0=tk[:, :], scalar1=rtrp[:, 1:2], scalar2=None,
                                                              op0=ALU.mult), reads=["rtk", "rtrp"], writes=["rsin"])
                        P.op("dve", lambda e: e.tensor_single_scalar(out=tm[:, :], in_=ta[:, :], scalar=float(math.pi / 2),
                                                                     op=ALU.is_gt), reads=["rta"], writes=["rtm"])
                        P.op("dve", lambda e: e.scalar_tensor_tensor(out=ta[:, :], in0=tm[:, :], scalar=-TWO_PI, in1=ta[:, :],
                                                                     op0=ALU.mult, op1=ALU.add),
                             reads=["rtm", "rta"], writes=["rta"])
                        P.op("act", lambda e: e.activation(out=COS2[:, tsl], in_=ta[:, :], func=AF.Sin, bias=rtrp[:, 2:3],
                                                           scale=1.0), reads=["rta", "rtrp"], writes=["rcos"])
                    for which, (c_base, dstT, scl) in enumerate([(C_RTQ, qrT, 1.0), (C_RTK, krT, 0.125)]):
                        for hp in range(2):
                            wb, wkey = load_w(w_in_d[l, :, c_base + hp * 128:c_base + (hp + 1) * 128])
                            jsw = wbcount[0] % NWB
                            wbcount[0] += 1
                            wsw = wbp[jsw]
                            for hh in range(2):
                                o = hh * 64
                                P.op("pool", lambda e: e.tensor_copy(out=wsw[:, :, o:o + 32], in_=wb[:, :, o + 32:o + 64]),
                                     reads=[wkey], writes=[("wb", jsw)])
                                P.op("pool", lambda e: e.tensor_copy(out=wsw[:, :, o + 32:o + 64], in_=wb[:, :, o:o + 32]),
                                     reads=[wkey], writes=[("wb", jsw)])
                            for tc in range(NT):
                                tsl = slice(tc * TC, (tc + 1) * TC)
                                for k in range(8):
                                    P.op("pe", lambda e: e.matmul(ps[1][:, :], lhsT=wb[:, k, :], rhs=hT[:, k, tsl],
                                                                  start=(k == 0), stop=(k == 7)),
                                         reads=[wkey, ("hT", k)], writes=[PS[1]], inc=(k == 7))
                                for k in range(8):
                                    P.op("pe", lambda e: e.matmul(ps[2][:, :], lhsT=wsw[:, k, :], rhs=hT[:, k, tsl],
                                                                  start=(k == 0), stop=(k == 7)),
                                         reads=[("wb", jsw), ("hT", k)], writes=[PS[2]], inc=(k == 7))
                                P.op("dve", lambda e: e.scalar_tensor_tensor(out=ta[:, :], in0=ps[1][:, :], scalar=float(scl),
                                                                             in1=COS2[:, tsl], op0=ALU.mult, op1=ALU.mult),
                                     reads=[PS[1], "rcos"], writes=["rta"])
                                P.op("dve", lambda e: e.scalar_tensor_tensor(out=tk[:, :], in0=ps[2][:, :], scalar=float(scl),
                                                                             in1=SIN2[:, tsl], op0=ALU.mult, op1=ALU.mult),
                                     reads=[PS[2], "rsin"], writes=["rtk"])
                                P.op("dve", lambda e: e.tensor_tensor(out=dstT[:, hp, tsl], in0=ta[:, :], in1=tk[:, :], op=ALU.add),
                                     reads=["rta", "rtk"], writes=[("rq", which, hp)])
                    P.barrier()
                for hp in range(2):
                    with ExitStack() as ph2:
                        def sb2(name, shape, dt):
                            return ph2.enter_context(nc.sbuf_tensor(_u + ("s_" + name), list(shape), dt))
                        kdtm = sb2("rkd", [128, 16, 128], BF16)
                        vtm = sb2("rv", [128, 16, 128], BF16)
                        zsT = sb2("rz", [128, S], BF16)
                        qc = [sb2(f"rqc{i}", [128, 128], BF16) for i in range(2)]
                        scm = [sb2(f"rscm{i}", [128, 128], BF16) for i in range(2)]
                        Sf = sb2("rSf", [128, 128], F32)
                        Sb = sb2("rSb", [128, 128], BF16)
                        ob = sb2("rob", [128, TC], BF16)
                        sq = sb2("rsq", [128, TC], BF16)
                        rs_t = rstd
                        tt_t = sb2("rtt", [128, TC], F32)
                        for n in range(16):
                            nsl = slice(n * 128, (n + 1) * 128)
                            pk = ps[3][:, 0:64].bitcast(BF16)
                            P.op("pe", lambda e: e.transpose(pk, krT[:, hp, nsl], cst["ident"][:, :]),
                                 reads=[("rq", 1, hp), "k_ident"], writes=[PS[3]])
                            for hh in range(2):
                                cs = slice(hh * 64, (hh + 1) * 64)
                                h = 2 * hp + hh
                                P.op("dve", lambda e: e.tensor_scalar(out=kdtm[:, n, cs], in0=pk[:, cs], scalar1=rtsd[:, h:h + 1],
                                                                      scalar2=None, op0=ALU.mult),
                                     reads=[PS[3], "rtsd"], writes=[("rkd", n)])
                        for hh in range(2):
                            h = 2 * hp + hh
                            rs = slice(hh * 64, (hh + 1) * 64)
                            proj_tm(l, C_RTV + h * 128, 128, lambda tt, pap, pkey: P.op(
                                "dve", lambda e: e.tensor_copy(out=vtm[:, tt, :], in_=pap), reads=[pkey], writes=[("rv", tt)]))
                            proj_fm(l, C_RTZ + h * 128, 128, lambda tc, pap, pkey: P.op(
                                "act", lambda e: e.activation(out=zsT[:, tc * TC:(tc + 1) * TC], in_=pap, func=AF.Silu),
                                reads=[pkey], writes=["rz"]))
                            for n in range(16):
                                nsl = slice(n * 128, (n + 1) * 128)
                                b = n % 2
                                csl = slice((n % 4) * 128, (n % 4 + 1) * 128)
                                po = 5 + (n // 4) % 2
                                P.op("pe", lambda e: e.matmul(ps[4][:, 0:128], lhsT=krT[rs, hp, nsl], rhs=qrT[rs, hp, nsl],
                                                              start=True, stop=True),
                                     reads=[("rq", 1, hp), ("rq", 0, hp)], writes=[PS[4]])
                                P.op("dve", lambda e: e.tensor_tensor(out=scm[b][:, :], in0=ps[4][:, 0:128], in1=rtdt[:, h, :],
                                                                      op=ALU.mult),
                                     reads=[PS[4], "rtdt"], writes=[("rscm", b)])
                                if n > 0:
                                    P.op("dve", lambda e: e.tensor_tensor(out=qc[b][rs, :], in0=qrT[rs, hp, nsl], in1=rtcd[rs, hp, :],
                                                                          op=ALU.mult),
                                         reads=[("rq", 0, hp), "rtcd"], writes=[("rqc", b)])
                                P.op("pe", lambda e: e.matmul(ps[po][:, csl], lhsT=vtm[:, n, :], rhs=scm[b][:, :],
                                                              start=True, stop=(n == 0)),
                                     reads=[("rv", n), ("rscm", b)], writes=[PS[po]], inc=(n == 0))
                                if n > 0:
                                    P.op("pe", lambda e: e.matmul(ps[po][:, csl], lhsT=Sb[rs, :], rhs=qc[b][rs, :],
                                                                  start=False, stop=True),
                                         reads=["rSb", ("rqc", b)], writes=[PS[po]])
                                if n < 15:
                                    P.op("pe", lambda e: e.matmul(ps[7][rs, 0:128], lhsT=kdtm[:, n, rs], rhs=vtm[:, n, :],
                                                                  start=True, stop=True),
                                         reads=[("rkd", n), ("rv", n)], writes=[PS[7]])
                                    gch = float((1.0 - 2.0 ** (-5.0 - h)) ** 128)
                                    if n == 0:
                                        P.op("dve", lambda e: e.tensor_copy(out=Sf[rs, :], in_=ps[7][rs, 0:128]),
                                             reads=[PS[7]], writes=["rSf"])
                                    else:
                                        P.op("dve", lambda e: e.scalar_tensor_tensor(out=Sf[rs, :], in0=Sf[rs, :], scalar=gch,
                                                                                     in1=ps[7][rs, 0:128], op0=ALU.mult,
                                                                                     op1=ALU.add),
                                             reads=[PS[7], "rSf"], writes=["rSf"])
                                    P.op("act", lambda e: e.activation(out=Sb[rs, :], in_=Sf[rs, :], func=AF.Copy),
                                         reads=["rSf"], writes=["rSb"])
                                if n % 4 == 3:
                                    tc = n // 4
                                    tsl = slice(tc * TC, (tc + 1) * TC)
                                    P.op("act", lambda e: e.activation(out=ob[:, :], in_=ps[po][:, :], func=AF.Copy),
                                         reads=[PS[po]], writes=["rob"])
                                    P.op("pe", lambda e: e.matmul(ps[1][:, :], lhsT=rtcn[:, :], rhs=ob[:, :], start=True, stop=True),
                                         reads=["rtcn", "rob"], writes=[PS[1]])
                                    P.op("act", lambda e: e.activation(out=sq[:, :], in_=ps[1][:, :], func=AF.Square),
                                         reads=[PS[1]], writes=["rsq"])
                                    P.op("pe", lambda e: e.matmul(ps[2][:, :], lhsT=cst["ones"][:, :], rhs=sq[:, :],
                                                                  start=True, stop=True),
                                         reads=["k_ones", "rsq"], writes=[PS[2]])
                                    P.op("act", lambda e: e.activation(out=rs_t[:, :], in_=ps[2][:, :], func=AF.Ln,
                                                                       bias=rtrp[:, 3:4], scale=float(1.0 / 128.0)),
                                         reads=[PS[2], "rtrp"], writes=["rstd"])
                                    P.op("act", lambda e: e.activation(out=rs_t[:, :], in_=rs_t[:, :], func=AF.Exp, scale=-0.5),
                                         reads=["rstd"], writes=["rstd"])
                                    P.op("dve", lambda e: e.scalar_tensor_tensor(out=tt_t[:, :], in0=ps[1][:, :],
                                                                                 scalar=retg[:, l, h:h + 1], in1=rs_t[:, :],
                                                                                 op0=ALU.mult, op1=ALU.mult),
                                         reads=[PS[1], "retg", "rstd"], writes=["rtt"])
                                    P.op("dve", lambda e: e.tensor_tensor(out=obT[:, h, tsl], in0=tt_t[:, :], in1=zsT[:, tsl],
                                                                          op=ALU.mult),
                                         reads=["rtt", "rz"], writes=[("obT", h)])
                        P.barrier()

        def deltanet(l):
            identf, onesf = dnc[:, 0, :], dnc[:, 1, :]
            mincl, mstrict = dnc[0:64, 2, 0:64], dnc[0:64, 3, 0:64]
            H = slice(0, 64)
            with ExitStack() as ph0:
                def sb0(name, shape, dt):
                    return ph0.enter_context(nc.sbuf_tensor(_u + ("s_" + name), list(shape), dt))
                abraw = sb0("dab", [64, 32, 8], F32)
                rep = sb0("drep", [64, 128], F32)
                t1 = sb0("dt1", [64, 32, 4], F32)
                g_tm = sb0("dg", [64, 32, 4], F32)
                beta_tm = sb0("dbeta", [64, 32, 4], F32)
                gc_tm = sb0("dgc", [64, 32, 4], F32)
                egl = sb0("degl", [128, 32, 4], F32)
                bg_tm = sb0("dbg", [64, 32, 4], F32)
                ed_tm = sb0("ded", [64, 32, 4], F32)
                proj_tm(l, C_DNA, 8, lambda tt, pap, pkey: P.op(
                    "dve", lambda e: e.tensor_copy(out=abraw[:, tt, :], in_=pap), reads=[pkey], writes=["dab"]), ntok=64)
                fl = lambda t: t[:, :, :].rearrange("p a b -> p (a b)")
                P.dma(rep[:, :], dndt_d[l, 0:64, :], writes=["drep"], stream="c")
                P.op("dve", lambda e: e.tensor_tensor(out=t1[:, :, :], in0=abraw[:, :, 0:4],
                                                      in1=rep[:, :].rearrange("p (a b) -> p a b", b=4), op=ALU.add),
                     reads=["dab", "drep"], writes=["dt1"])
                P.op("act", lambda e: e.activation(out=t1[:, :, :], in_=t1[:, :, :], func=AF.Exp), reads=["dt1"], writes=["dt1"])
                P.op("act", lambda e: e.activation(out=t1[:, :, :], in_=t1[:, :, :], func=AF.Ln, bias=oneb[0:64, 0:1], scale=1.0),
                     reads=["dt1", "oneb"], writes=["dt1"])
                P.dma(rep[:, :], dnal_d[l, 0:64, :], writes=["drep"], stream="c")
                P.op("act", lambda e: e.activation(out=rep[:, :], in_=rep[:, :], func=AF.Exp), reads=["drep"], writes=["drep"])
                P.op("dve", lambda e: e.scalar_tensor_tensor(out=g_tm[:, :, :], in0=t1[:, :, :], scalar=-1.0,
                                                             in1=rep[:, :].rearrange("p (a b) -> p a b", b=4),
                                                             op0=ALU.mult, op1=ALU.mult),
                     reads=["dt1", "drep"], writes=["dg"])
                P.op("act", lambda e: e.activation(out=beta_tm[:, :, :], in_=abraw[:, :, 4:8], func=AF.Sigmoid),
                     reads=["dab"], writes=["dbeta"])
                P.op("pe", lambda e: e.matmul(ps[0][0:64, 0:128], lhsT=dnc[0:64, 2, 0:64], rhs=fl(g_tm), start=True, stop=True),
                     reads=["dnc", "dg"], writes=[PS[0]])
                P.op("dve", lambda e: e.tensor_copy(out=fl(gc_tm), in_=ps[0][0:64, 0:128]), reads=[PS[0]], writes=["dgc"])
                P.op("pe", lambda e: e.matmul(ps[0][:, 128:256], lhsT=dnc[0:64, 1, :], rhs=fl(g_tm), start=True, stop=True),
                     reads=["dnc", "dg"], writes=[PS[0]])
                P.op("dve", lambda e: e.tensor_tensor(out=fl(ed_tm), in0=ps[0][0:64, 128:256], in1=fl(gc_tm), op=ALU.subtract),
                     reads=[PS[0], "dgc"], writes=["ded"])
                P.op("act", lambda e: e.activation(out=fl(ed_tm), in_=fl(ed_tm), func=AF.Exp), reads=["ded"], writes=["ded"])
                P.op("act", lambda e: e.activation(out=fl(egl), in_=ps[0][:, 128:256], func=AF.Exp), reads=[PS[0]], writes=["degl"])
                P.op("act", lambda e: e.activation(out=fl(bg_tm), in_=fl(gc_tm), func=AF.Exp), reads=["dgc"], writes=["dbg"])
                P.op("dve", lambda e: e.tensor_tensor(out=fl(bg_tm), in0=fl(bg_tm), in1=fl(beta_tm), op=ALU.mult),
                     reads=["dbg", "dbeta"], writes=["dbg"])
                P.barrier()
                for h in range(4):
                    with ExitStack() as ph1:
                        def sb1(name, shape, dt):
                            return ph1.enter_context(nc.sbuf_tensor(_u + ("s_" + name), list(shape), dt))
                        qT = sb1("dq", [128, S], BF16)
                        kT = sb1("dk", [128, S], BF16)
                        vT = sb1("dv", [128, S], BF16)
                        with ExitStack() as ph2:
                            xpad = ph2.enter_context(nc.sbuf_tensor(_u + "s_dxpad", [128, S + 3], F32))
                            acc = ph2.enter_context(nc.sbuf_tensor(_u + "s_dacc", [128, S], F32))
                            P.op("dve", lambda e: e.memset(xpad[:, 0:3], 0.0), writes=["dxpad0"])
                            for xi, (c_base, dstT) in enumerate([(C_DNQ, qT), (C_DNK, kT), (C_DNV, vT)]):
                                proj_fm(l, c_base + h * 128, 128, lambda tc, pap, pkey: P.op(
                                    "act", lambda e: e.activation(out=xpad[:, 3 + tc * TC:3 + (tc + 1) * TC], in_=pap, func=AF.Copy),
                                    reads=[pkey], writes=[("dxpad", tc)]))
                                allx = [("dxpad", tc) for tc in range(NT)] + ["dxpad0"]
                                cw = dncw[:, l, xi * 4 + h, :]
                                P.op("dve", lambda e: e.tensor_scalar(out=acc[:, :], in0=xpad[:, 3:3 + S], scalar1=cw[:, 3:4],
                                                                      scalar2=None, op0=ALU.mult),
                                     reads=allx + ["dncw"], writes=["dacc"])
                                for j in range(3):
                                    P.op("dve", lambda e: e.scalar_tensor_tensor(out=acc[:, :], in0=xpad[:, j:j + S],
                                                                                 scalar=cw[:, j:j + 1], in1=acc[:, :],
                                                                                 op0=ALU.mult, op1=ALU.add),
                                         reads=allx + ["dncw", "dacc"], writes=["dacc"])
                                P.op("act", lambda e: e.activation(out=acc[:, :], in_=acc[:, :], func=AF.Silu),
                                     reads=["dacc"], writes=["dacc"])
                                if xi == 2:
                                    P.op("dve", lambda e: e.tensor_copy(out=vT[:, :], in_=acc[:, :]), reads=["dacc"], writes=["dvT"])
                                else:
                                    for tc in range(NT):
                                        tsl = slice(tc * TC, (tc + 1) * TC)
                                        P.op("act", lambda e: e.activation(out=sqt[:, 0, :], in_=acc[:, tsl], func=AF.Square),
                                             reads=["dacc"], writes=[("sqt", 0)])
                                        P.op("pe", lambda e: e.matmul(ps[0][:, :], lhsT=cst["ones"][:, :], rhs=sqt[:, 0, :],
                                                                      start=True, stop=True),
                                             reads=[("sqt", 0), "k_ones"], writes=[PS[0]])
                                        P.op("act", lambda e: e.activation(out=rstd[:, :], in_=ps[0][:, :], func=AF.Ln,
                                                                           bias=rtrp[:, 3:4], scale=1.0),
                                             reads=[PS[0], "rtrp"], writes=["rstd"])
                                        P.op("act", lambda e: e.activation(out=rstd[:, :], in_=rstd[:, :], func=AF.Exp, scale=-0.5),
                                             reads=["rstd"], writes=["rstd"])
                                        sc = float(128.0 ** -0.5) if xi == 0 else 1.0
                                        P.op("dve", lambda e: e.scalar_tensor_tensor(out=dstT[:, tsl], in0=acc[:, tsl], scalar=sc,
                                                                                     in1=rstd[:, :], op0=ALU.mult, op1=ALU.mult),
                                             reads=["dacc", "rstd"], writes=["dqT" if xi == 0 else "dkT"])
                            P.barrier()
                        zsT = sb1("dz", [128, S], BF16)
                        proj_fm(l, C_DNZ + h * 128, 128, lambda tc, pap, pkey: P.op(
                            "act", lambda e: e.activation(out=zsT[:, tc * TC:(tc + 1) * TC], in_=pap, func=AF.Silu),
                            reads=[pkey], writes=["dz"]))
                        f64 = lambda nm: sb1(nm, [64, 64], F32)
                        gsc, bsc, dmin, decm, bs, LT = [f64(n_) for n_ in ["dgsc", "dbsc", "ddmin", "ddecm", "dbs", "dLT"]]
                        egcb = sb1("degcb", [128, 64], F32)
                        qg = sb1("dqg", [128, 64], BF16)
                        aT = sb1("daT", [64, 64], BF16)
                        PP = sb1("dPP", [64, 128], BF16)
                        IpPT = sb1("dIpPT", [64, 64], BF16)
                        Tt = [sb1(f"dTt{i}", [64, 64], BF16) for i in range(2)]
                        kbg = sb1("dkbg", [64, 128], BF16)
                        kd = sb1("dkd", [64, 128], BF16)
                        vb = sb1("dvb", [64, 128], BF16)
                        wT = sb1("dwT", [128, 64], BF16)
                        u_sb = sb1("du", [64, 128], F32)
                        vnew = sb1("dvnew", [64, 128], BF16)
                        Sf = sb1("dSf", [128, 128], F32)
                        Sb = sb1("dSb", [128, 128], BF16)
                        nsq = sb1("dnsq", [128, TC], BF16)
                        ntmp = sb1("dntmp", [128, TC], F32)
                        identb = cst["ident"]
                        for n in range(32):
                            csl = slice(n * 64, (n + 1) * 64)
                            P.op("dve", lambda e: e.tensor_scalar(out=gsc[:, :], in0=mincl, scalar1=g_tm[:, n, h:h + 1], scalar2=None,
                                                                  op0=ALU.mult), reads=["dnc", "dg"], writes=["dgsc"])
                            P.op("dve", lambda e: e.tensor_scalar(out=bsc[:, :], in0=identf[0:64, 0:64], scalar1=beta_tm[:, n, h:h + 1],
                                                                  scalar2=None, op0=ALU.mult), reads=["dnc", "dbeta"], writes=["dbsc"])
                            P.op("pe", lambda e: e.matmul(ps[0][:, 0:64], lhsT=onesf[0:64, :], rhs=gsc[:, :], start=True, stop=True),
                                 reads=["dnc", "dgsc"], writes=[PS[0]])
                            P.op("pe", lambda e: e.matmul(ps[0][:, 64:128], lhsT=onesf[0:64, :], rhs=bsc[:, :], start=True, stop=True),
                                 reads=["dnc", "dbsc"], writes=[PS[0]])
                            P.op("act", lambda e: e.activation(out=egcb[:, :], in_=ps[0][:, 0:64], func=AF.Exp),
                                 reads=[PS[0]], writes=["degcb"])
                            P.op("dve", lambda e: e.tensor_tensor(out=qg[:, :], in0=qT[:, csl], in1=egcb[:, :], op=ALU.mult),
                                 reads=["dqT", "degcb"], writes=["dqg"])
                            P.op("dve", lambda e: e.tensor_scalar(out=dmin[:, :], in0=ps[0][H, 0:64], scalar1=gc_tm[:, n, h:h + 1],
                                                                  scalar2=0.0, op0=ALU.subtract, op1=ALU.min),
                                 reads=[PS[0], "dgc"], writes=["ddmin"])
                            P.op("act", lambda e: e.activation(out=dmin[:, :], in_=dmin[:, :], func=AF.Exp),
                                 reads=["ddmin"], writes=["ddmin"])
                            P.op("dve", lambda e: e.tensor_tensor(out=decm[:, :], in0=dmin[:, :], in1=mincl, op=ALU.mult),
                                 reads=["ddmin", "dnc"], writes=["ddecm"])
                            P.op("dve", lambda e: e.tensor_tensor(out=bs[:, :], in0=ps[0][H, 64:128], in1=mstrict, op=ALU.mult),
                                 reads=[PS[0], "dnc"], writes=["dbs"])
                            P.op("pe", lambda e: e.matmul(ps[1][H, 0:64], lhsT=kT[:, csl], rhs=kT[:, csl], start=True, stop=True),
                                 reads=["dkT"], writes=[PS[1]])
                            P.op("pe", lambda e: e.matmul(ps[1][H, 64:128], lhsT=kT[:, csl], rhs=qT[:, csl], start=True, stop=True),
                                 reads=["dkT", "dqT"], writes=[PS[1]])
                            P.op("dve", lambda e: e.tensor_tensor(out=aT[:, :], in0=ps[1][H, 64:128], in1=decm[:, :], op=ALU.mult),
                                 reads=[PS[1], "ddecm"], writes=["daT"])
                            P.op("dve", lambda e: e.tensor_tensor(out=LT[:, :], in0=ps[1][H, 0:64], in1=decm[:, :], op=ALU.mult),
                                 reads=[PS[1], "ddecm"], writes=["dLT"])
                            P.op("dve", lambda e: e.scalar_tensor_tensor(out=PP[:, 0:64], in0=LT[:, :], scalar=-1.0, in1=bs[:, :],
                                                                         op0=ALU.mult, op1=ALU.mult),
                                 reads=["dLT", "dbs"], writes=["dPP"])
                            ptv = ps[2][H, 0:32].bitcast(BF16)
                            P.op("pe", lambda e: e.transpose(ptv, PP[:, 0:64], identb[0:64, 0:64]),
                                 reads=["dPP", "k_ident"], writes=[PS[2]])
                            P.op("act", lambda e: e.activation(out=PP[:, 64:128], in_=ptv, func=AF.Copy),
                                 reads=[PS[2]], writes=["dPP"])
                            P.op("dve", lambda e: e.tensor_tensor(out=Tt[0][:, :], in0=PP[:, 0:64], in1=identb[0:64, 0:64], op=ALU.add),
                                 reads=["dPP", "k_ident"], writes=[("dTt", 0)])
                            cur = 0
                            for lev in range(1, 6):
                                P.op("pe", lambda e: e.matmul(ps[2][H, 0:64], lhsT=PP[:, 64:128], rhs=PP[:, 0:64], start=True, stop=True),
                                     reads=["dPP"], writes=[PS[2]])
                                P.op("pe", lambda e: e.matmul(ps[2][H, 64:128], lhsT=PP[:, 0:64], rhs=PP[:, 64:128], start=True, stop=True),
                                     reads=["dPP"], writes=[PS[2]])
                                P.op("act", lambda e: e.activation(out=PP[:, :], in_=ps[2][H, 0:128], func=AF.Copy),
                                     reads=[PS[2]], writes=["dPP"])
                                P.op("dve", lambda e: e.tensor_tensor(out=IpPT[:, :], in0=ps[2][H, 64:128], in1=identb[0:64, 0:64], op=ALU.add),
                                     reads=[PS[2], "k_ident"], writes=["dIpPT"])
                                P.op("pe", lambda e: e.matmul(ps[3][H, 0:64], lhsT=IpPT[:, :], rhs=Tt[cur][:, :], start=True, stop=True),
                                     reads=["dIpPT", ("dTt", cur)], writes=[PS[3]])
                                cur = 1 - cur
                                P.op("dve", lambda e: e.tensor_copy(out=Tt[cur][:, :], in_=ps[3][H, 0:64]),
                                     reads=[PS[3]], writes=[("dTt", cur)])
                            kv = ps[4][H, 0:64].bitcast(BF16)
                            vv = ps[4][H, 64:128].bitcast(BF16)
                            P.op("pe", lambda e: e.transpose(kv, kT[:, csl], identb[:, :]), reads=["dkT", "k_ident"], writes=[PS[4]])
                            P.op("pe", lambda e: e.transpose(vv, vT[:, csl], identb[:, :]), reads=["dvT", "k_ident"], writes=[PS[4]])
                            P.op("dve", lambda e: e.tensor_scalar(out=kbg[:, :], in0=kv, scalar1=bg_tm[:, n, h:h + 1], scalar2=None,
                                                                  op0=ALU.mult), reads=[PS[4], "dbg"], writes=["dkbg"])
                            P.op("dve", lambda e: e.tensor_scalar(out=kd[:, :], in0=kv, scalar1=ed_tm[:, n, h:h + 1], scalar2=None,
                                                                  op0=ALU.mult), reads=[PS[4], "ded"], writes=["dkd"])
                            P.op("dve", lambda e: e.tensor_scalar(out=vb[:, :], in0=vv, scalar1=beta_tm[:, n, h:h + 1], scalar2=None,
                                                                  op0=ALU.mult), reads=[PS[4], "dbeta"], writes=["dvb"])
                            P.op("pe", lambda e: e.matmul(ps[5][H, 0:128], lhsT=Tt[cur][:, :], rhs=vb[:, :], start=True, stop=True),
                                 reads=[("dTt", cur), "dvb"], writes=[PS[5]])
                            P.op("pe", lambda e: e.matmul(ps[5][:, 128:192], lhsT=kbg[:, :], rhs=Tt[cur][:, :], start=True, stop=True),
                                 reads=[("dTt", cur), "dkbg"], writes=[PS[5]])
                            P.op("act", lambda e: e.activation(out=u_sb[:, :], in_=ps[5][H, 0:128], func=AF.Copy),
                                 reads=[PS[5]], writes=["du"])
                            P.op("act", lambda e: e.activation(out=wT[:, :], in_=ps[5][:, 128:192], func=AF.Copy),
                                 reads=[PS[5]], writes=["dwT"])
                            if n == 0:
                                P.op("dve", lambda e: e.tensor_copy(out=vnew[:, :], in_=u_sb[:, :]), reads=["du"], writes=["dvnew"])
                            else:
                                P.op("pe", lambda e: e.matmul(ps[5][H, 256:384], lhsT=wT[:, :], rhs=Sb[:, :], start=True, stop=True),
                                     reads=["dwT", "dSb"], writes=[PS[5]])
                                P.op("dve", lambda e: e.tensor_tensor(out=vnew[:, :], in0=u_sb[:, :], in1=ps[5][H, 256:384], op=ALU.subtract),
                                     reads=["du", PS[5]], writes=["dvnew"])
                            osl = slice((n % 8) * 64, (n % 8 + 1) * 64)
                            if n > 0:
                                P.op("pe", lambda e: e.matmul(ps[6][:, osl], lhsT=Sb[:, :], rhs=qg[:, :], start=True, stop=False),
                                     reads=["dSb", "dqg"], writes=[PS[6]], inc=False)
                            P.op("pe", lambda e: e.matmul(ps[6][:, osl], lhsT=vnew[:, :], rhs=aT[:, :], start=(n == 0), stop=True),
                                 reads=["dvnew", "daT"], writes=[PS[6]])
                            if n < 31:
                                P.op("pe", lambda e: e.matmul(ps[7][:, 0:128], lhsT=kd[:, :], rhs=vnew[:, :], start=True, stop=True),
                                     reads=["dkd", "dvnew"], writes=[PS[7]])
                                if n == 0:
                                    P.op("dve", lambda e: e.tensor_copy(out=Sf[:, :], in_=ps[7][:, 0:128]), reads=[PS[7]], writes=["dSf"])
                                else:
                                    P.op("dve", lambda e: e.scalar_tensor_tensor(out=Sf[:, :], in0=Sf[:, :], scalar=egl[:, n, h:h + 1],
                                                                                 in1=ps[7][:, 0:128], op0=ALU.mult, op1=ALU.add),
                                         reads=[PS[7], "dSf", "degl"], writes=["dSf"])
                                P.op("act", lambda e: e.activation(out=Sb[:, :], in_=Sf[:, :], func=AF.Copy), reads=["dSf"], writes=["dSb"])
                            if n % 8 == 7:
                                tc = n // 8
                                tsl = slice(tc * TC, (tc + 1) * TC)
                                P.op("act", lambda e: e.activation(out=nsq[:, :], in_=ps[6][:, :], func=AF.Square),
                                     reads=[PS[6]], writes=["dnsq"])
                                P.op("pe", lambda e: e.matmul(ps[0][:, :], lhsT=cst["ones"][:, :], rhs=nsq[:, :], start=True, stop=True),
                                     reads=["k_ones", "dnsq"], writes=[PS[0]])
                                P.op("act", lambda e: e.activation(out=rstd[:, :], in_=ps[0][:, :], func=AF.Ln, bias=rtrp[:, 3:4],
                                                                   scale=float(1.0 / 128.0)), reads=[PS[0], "rtrp"], writes=["rstd"])
                                P.op("act", lambda e: e.activation(out=rstd[:, :], in_=rstd[:, :], func=AF.Exp, scale=-0.5),
                                     reads=["rstd"], writes=["rstd"])
                                P.op("dve", lambda e: e.scalar_tensor_tensor(out=ntmp[:, :], in0=ps[6][:, :], scalar=dng[:, l:l + 1],
                                                                             in1=rstd[:, :], op0=ALU.mult, op1=ALU.mult),
                                     reads=[PS[6], "dng", "rstd"], writes=["dntmp"])
                                P.op("dve", lambda e: e.tensor_tensor(out=obT[:, h, tsl], in0=ntmp[:, :], in1=zsT[:, tsl], op=ALU.mult),
                                     reads=["dntmp", "dz"], writes=[("obT", h)])
                        P.barrier()

        def dump(nm, tile, nchunks, keyname, nokey=False):
            if nokey:
                P.barrier()
            with nc.sbuf_tensor(_u + "s_dbgf_" + nm, [128, S], F32) as dbgf:
                for k in range(nchunks):
                    P.op("dve", lambda e: e.tensor_copy(out=dbgf[:, :], in_=tile[:, k, :]),
                         reads=[(keyname, k)], writes=["dbgf"])
                    P.dma(dbg_d[nm][k * 128:(k + 1) * 128, :], dbgf[:, :], reads=["dbgf"], writes=["dbg_" + nm], stream="o")
                P.barrier()

        for l in range(n_layers):
            norm_to(lambda k, tc: (hT[:, k, tc * TC:(tc + 1) * TC], ("hT", k)), normg[:, l, :], l)
            if debug and l == 0:
                with nc.sbuf_tensor(_u + "dbgf", [128, S], F32) as dbgf:
                    for k in range(8):
                        P.op("dve", lambda e, k=k: e.tensor_copy(out=dbgf[:, :], in_=hT[:, k, :]),
                             reads=[("hT", k)], writes=["dbgf"])
                        P.dma(dbg_d["hT"][k * 128:(k + 1) * 128, :], dbgf[:, :], reads=["dbgf"], writes=["dbg_hT"],
                              stream="o")
                    P.barrier()

            with ExitStack() as ph:
                def sbp(name, shape, dt):
                    return ph.enter_context(nc.sbuf_tensor(_u + "s_" + name, list(shape), dt))
                memT = sbp("memT", [128, 8, MEM_LEN], F32)
                memn = sbp("memn", [128, 8, MEM_LEN], BF16)
                kmT = sbp("kmT", [128, 2, MEM_LEN], BF16)
                vm = sbp("vm", [128, 2, 256], BF16)
                qmT = sbp("qmT", [128, 2, S], BF16)
                pT = [sbp(f"pT{i}", [128, TC], BF16) for i in range(2)]
                rden = sbp("rden", [128, TC], F32)
                for k in range(8):
                    P.dma(memT[:, k, :], memT_d[k * 128:(k + 1) * 128, :], writes=[("memT", k)], stream="x")
                P.op("dve", lambda e: e.tensor_scalar(out=g32[:, :], in0=memg[:, l, :], scalar1=float(math.sqrt(D)),
                                                      scalar2=None, op0=ALU.mult), reads=["memg"], writes=["g32"])
                rms_stats(memT, [("memT", k) for k in range(8)], 8, MEM_LEN, 0, 0, sqt, "sqt", rstd, "rstd", None)
                for k in range(8):
                    P.op("dve", lambda e, k=k: e.scalar_tensor_tensor(
                        out=memn[:, k, :], in0=memT[:, k, :], scalar=g32[:, k:k + 1], in1=rstd[:, 0:MEM_LEN],
                        op0=ALU.mult, op1=ALU.mult), reads=[("memT", k), "g32", "rstd"], writes=[("memn", k)])
                for ec in range(2):
                    wb, wkey = load_w(w_kv_d[l, :, ec * 128:(ec + 1) * 128])
                    for k in range(8):
                        P.op("pe", lambda e, k=k, wb=wb: e.matmul(ps[1][:, 0:MEM_LEN], lhsT=wb[:, k, :], rhs=memn[:, k, :],
                                                                  start=(k == 0), stop=(k == 7)),
                             reads=[wkey, ("memn", k)], writes=[PS[1]], inc=(k == 7))
                    P.op("dve", lambda e, ec=ec: e.tensor_copy(out=kmT[:, ec, :], in_=ps[1][:, 0:MEM_LEN]),
                         reads=[PS[1]], writes=[("kmT", ec)])
                for vc in range(2):
                    wb, wkey = load_w(w_kv_d[l, :, 256 + vc * 128:256 + (vc + 1) * 128])
                    for mt in range(2):
                        for k in range(8):
                            P.op("pe", lambda e, k=k, wb=wb, mt=mt: e.matmul(
                                ps[2][:, 0:128], lhsT=memn[:, k, mt * 128:(mt + 1) * 128], rhs=wb[:, k, :],
                                start=(k == 0), stop=(k == 7)),
                                reads=[wkey, ("memn", k)], writes=[PS[2]], inc=(k == 7))
                        P.op("dve", lambda e, mt=mt, vc=vc: e.tensor_copy(out=vm[:, mt, vc * 128:(vc + 1) * 128],
                                                                          in_=ps[2][:, 0:128]),
                             reads=[PS[2]], writes=[("vm", mt)])
                for ec in range(2):
                    def ev(tc, pap, pkey, ec=ec):
                        P.op("act", lambda e: e.activation(out=qmT[:, ec, tc * TC:(tc + 1) * TC], in_=pap,
                                                           func=AF.Copy, scale=0.125),
                             reads=[pkey], writes=[("qmT", ec)])
                    proj_fm(l, C_MQ + ec * 128, 128, ev)
                for h in range(4):
                    ec, r0 = h // 2, (h % 2) * 64
                    for tc in range(NT):
                        tsl = slice(tc * TC, (tc + 1) * TC)
                        for mb in range(2):
                            pi = 3 + mb
                            P.op("pe", lambda e, mb=mb, pi=pi: e.matmul(
                                ps[pi][:, :], lhsT=kmT[r0:r0 + 64, ec, mb * 128:(mb + 1) * 128],
                                rhs=qmT[r0:r0 + 64, ec, tsl], start=True, stop=True),
                                reads=[("kmT", ec), ("qmT", ec)], writes=[PS[pi]])
                            P.op("act", lambda e, mb=mb, pi=pi: e.activation(out=pT[mb][:, :], in_=ps[pi][:, :],
                                                                             func=AF.Exp),
                                 reads=[PS[pi]], writes=[("pT", mb)])
                        for mb in range(2):
                            P.op("pe", lambda e, mb=mb: e.matmul(
                                ps[5][r0:r0 + 64, :], lhsT=vm[:, mb, h * 64:(h + 1) * 64], rhs=pT[mb][:, :],
                                start=(mb == 0), stop=(mb == 1)),
                                reads=[("vm", mb), ("pT", mb)], writes=[PS[5]], inc=(mb == 1))
                        for mb in range(2):
                            P.op("pe", lambda e, mb=mb: e.matmul(
                                ps[6][r0:r0 + 64, :], lhsT=cst["ones"][:, 0:64], rhs=pT[mb][:, :],
                                start=(mb == 0), stop=(mb == 1)),
                                reads=["k_ones", ("pT", mb)], writes=[PS[6]], inc=(mb == 1))
                        P.op("dve", lambda e: e.reciprocal(out=rden[r0:r0 + 64, :], in_=ps[6][r0:r0 + 64, :]),
                             reads=[PS[6]], writes=["rden"])
                        P.op("dve", lambda e, tsl=tsl: e.tensor_tensor(out=obT[r0:r0 + 64, ec, tsl], in0=ps[5][r0:r0 + 64, :],
                                                                      in1=rden[r0:r0 + 64, :], op=ALU.mult),
                             reads=[PS[5], "rden"], writes=[("obT", ec)])
                P.barrier()
            if debug and l == 0:
                with nc.sbuf_tensor(_u + "dbgf2", [128, S], F32) as dbgf:
                    for k in range(2):
                        P.op("dve", lambda e, k=k: e.tensor_copy(out=dbgf[:, :], in_=obT[:, k, :]),
                             reads=[("obT", k)], writes=["dbgf"])
                        P.dma(dbg_d["omem"][k * 128:(k + 1) * 128, :], dbgf[:, :], reads=["dbgf"], writes=["dbg_omem"],
                              stream="o")
                    P.barrier()

            def merge(br, nwc, first):
                with ExitStack() as ph:
                    sg = [ph.enter_context(nc.sbuf_tensor(_u + f"sg{i}", [128, TC], F32)) for i in range(2)]
                    tmp = [ph.enter_context(nc.sbuf_tensor(_u + f"mtmp{i}", [128, TC], F32)) for i in range(2)]
                    it = 0
                    for dc in range(8):
                        wg, wgk = load_w(w_in_d[l, :, C_G + br * D + dc * 128:C_G + br * D + (dc + 1) * 128])
                        wr, wrk = load_w(w_br_d[br][l, :, dc * 128:(dc + 1) * 128], rows=nwc)
                        for tc in range(NT):
                            tsl = slice(tc * TC, (tc + 1) * TC)
                            b = it % 2
                            it += 1
                            pg, pp = 1 + b, 3 + b
                            for k in range(8):
                                P.op("pe", lambda e, k=k, pg=pg, tsl=tsl, wg=wg: e.matmul(
                                    ps[pg][:, :], lhsT=wg[:, k, :], rhs=hT[:, k, tsl], start=(k == 0), stop=(k == 7)),
                                    reads=[wgk, ("hT", k)], writes=[PS[pg]], inc=(k == 7))
                            for k in range(nwc):
                                P.op("pe", lambda e, k=k, pp=pp, tsl=tsl, wr=wr: e.matmul(
                                    ps[pp][:, :], lhsT=wr[:, k, :], rhs=obT[:, k, tsl], start=(k == 0), stop=(k == nwc - 1)),
                                    reads=[wrk, ("obT", k)], writes=[PS[pp]], inc=(k == nwc - 1))
                            P.op("act", lambda e, b=b, pg=pg, dc=dc: e.activation(
                                out=sg[b][:, :], in_=ps[pg][:, :], func=AF.Sigmoid,
                                bias=bgate[:, l, br * 8 + dc:br * 8 + dc + 1], scale=1.0),
                                reads=[PS[pg], "bgate"], writes=[("sg", b)])
                            if first:
                                P.op("dve", lambda e, b=b, pp=pp, dc=dc, tsl=tsl: e.tensor_tensor(
                                    out=mT[:, dc, tsl], in0=ps[pp][:, :], in1=sg[b][:, :], op=ALU.mult),
                                    reads=[PS[pp], ("sg", b)], writes=[("mT", dc, tc)])
                            else:
                                P.op("dve", lambda e, b=b, pp=pp: e.tensor_tensor(
                                    out=tmp[b][:, :], in0=ps[pp][:, :], in1=sg[b][:, :], op=ALU.mult),
                                    reads=[PS[pp], ("sg", b)], writes=[("mtmp", b)])
                                P.op("dve", lambda e, b=b, dc=dc, tsl=tsl: e.tensor_tensor(
                                    out=mT[:, dc, tsl], in0=mT[:, dc, tsl], in1=tmp[b][:, :], op=ALU.add),
                                    reads=[("mtmp", b), ("mT", dc, tc)], writes=[("mT", dc, tc)])
                    P.barrier()

            merge(3, 2, True)
            if 0 in branches:
                sb_attention(l)
                if debug and l == 0:
                    dump('osb', obT, 4, 'obT')
                merge(0, 4, False)
            if 1 in branches:
                deltanet(l)
                if debug and l == 0:
                    dump('odn', obT, 4, 'obT')
                merge(1, 4, False)
            if 2 in branches:
                retention(l)
                if debug and l == 0:
                    dump('ort', obT, 4, 'obT')
                merge(2, 4, False)

            for ec in range(8):
                wo, wok = load_w(w_out_d[l, :, ec * 128:(ec + 1) * 128])
                for tc in range(NT):
                    tsl = slice(tc * TC, (tc + 1) * TC)
                    pi = 1 + tc % 2
                    for dc in range(8):
                        P.op("pe", lambda e, dc=dc, pi=pi, tsl=tsl, wo=wo: e.matmul(
                            ps[pi][:, :], lhsT=wo[:, dc, :], rhs=mT[:, dc, tsl], start=(dc == 0), stop=(dc == 7)),
                            reads=[wok, ("mT", dc, tc)], writes=[PS[pi]], inc=(dc == 7))
                    P.op("dve", lambda e, ec=ec, pi=pi, tsl=tsl: e.tensor_tensor(
                        out=xT[:, ec, tsl], in0=xT[:, ec, tsl], in1=ps[pi][:, :], op=ALU.add),
                        reads=[PS[pi], ("xT", ec)], writes=[("xT", ec)])
            P.barrier()

        with ExitStack() as ph:
            ot = [ph.enter_context(nc.sbuf_tensor(_u + f"ot{i}", [128, TC], F32)) for i in range(2)]
            cnt = [0]

            def dst(k, tc):
                b = cnt[0] % 2
                cnt[0] += 1
                return ot[b][:, :], ("ot", b)
            P.op("dve", lambda e: e.tensor_scalar(out=g32[:, :], in0=fing[:, :], scalar1=float(math.sqrt(D)), scalar2=None,
                                                  op0=ALU.mult), reads=["fing"], writes=["g32"])
            for tc in range(NT):
                rms_stats(xT, [("xT", k) for k in range(8)], 8, TC, tc * TC, 0, sqt, "sqt", rstd, "rstd", None)
                for k in range(8):
                    ap, key = dst(k, tc)
                    P.op("dve", lambda e, k=k, tc=tc, ap=ap: e.scalar_tensor_tensor(
                        out=ap, in0=xT[:, k, tc * TC:(tc + 1) * TC], scalar=g32[:, k:k + 1], in1=rstd[:, :],
                        op0=ALU.mult, op1=ALU.mult),
                        reads=[("xT", k), "g32", "rstd"], writes=[key])
                    P.dma(outT_d[k * 128:(k + 1) * 128, tc * TC:(tc + 1) * TC], ap, reads=[key], writes=["outT"],
                          stream="o")
            P.finish(["outT", "dbg_hT", "dbg_omem", "dbg_osb", "dbg_odn", "dbg_ort"])
            P.barrier()
        P.emit()
        print("instructions recorded:", P.nins, {n: len(P.q[n]) for n in P.names})
    return nc


_NC_CACHE = {}


def _prep_inputs(inputs, b):
    f = np.float32
    m = {}
    m["xT"] = np.ascontiguousarray(inputs["x"][b].T.astype(f))
    m["memT"] = np.ascontiguousarray(inputs["mem"][b].T.astype(f))
    m["w_in"] = np.ascontiguousarray(inputs["w_in"], dtype=f)
    m["w_mem_kv"] = np.ascontiguousarray(inputs["w_mem_kv"], dtype=f)
    for n in ["w_br_sb", "w_br_dn", "w_br_ret", "w_br_mem", "w_out"]:
        m[n] = np.ascontiguousarray(inputs[n], dtype=f)
    m["norm_g"] = np.ascontiguousarray(inputs["norm_g"].reshape(DEPTH, 8, 128).transpose(0, 2, 1), dtype=f)
    m["mem_norm_g"] = np.ascontiguousarray(inputs["mem_norm_g"].reshape(DEPTH, 8, 128).transpose(0, 2, 1), dtype=f)
    m["b_gate"] = np.ascontiguousarray(inputs["b_gate"].reshape(DEPTH, 32, 128).transpose(0, 2, 1), dtype=f)
    m["final_norm_g"] = np.ascontiguousarray(inputs["final_norm_g"].reshape(8, 128).T, dtype=f)
    for n, v in _consts().items():
        m["c_" + n] = v
    for n, v in _rt_consts().items():
        m["c_" + n] = v
    m["c_dn"] = _dn_consts()
    m["dn_conv_w"] = np.ascontiguousarray(inputs["dn_conv_w"].reshape(DEPTH, 4, 12, 128).transpose(0, 3, 2, 1), dtype=f)
    m["dn_norm_g"] = np.ascontiguousarray(inputs["dn_norm_g"].T, dtype=f)
    m["dn_alog"] = np.ascontiguousarray(np.broadcast_to(np.tile(inputs["dn_a_log"], (1, 32))[:, None, :], (DEPTH, 128, 128)), dtype=f)
    m["dn_dtb"] = np.ascontiguousarray(np.broadcast_to(np.tile(inputs["dn_dt_bias"], (1, 32))[:, None, :], (DEPTH, 128, 128)), dtype=f)
    m["pos"] = np.ascontiguousarray(np.broadcast_to(inputs["positions"][b].astype(np.int32)[None, :], (128, S)))
    m["ret_norm_g"] = np.ascontiguousarray(inputs["ret_norm_g"].reshape(DEPTH, 4, 128).transpose(0, 2, 1), dtype=f)
    return m


def kernel(**inputs):
    inputs = {k: np.asarray(v) for k, v in inputs.items()}
    if "nc" not in _NC_CACHE:
        _NC_CACHE["nc"] = build()
    nc = _NC_CACHE["nc"]
    in_maps = [_prep_inputs(inputs, b) for b in range(8)]
    res = run_bass_kernel_spmd(nc, in_maps, core_ids=list(range(8)))
    out = np.stack([np.ascontiguousarray(res.results[b]["outT"].T) for b in range(8)], axis=0)
    return out.astype(np.float32)
```

```python
import math
from contextlib import ExitStack
import numpy as np
import concourse.bass as bass
import concourse.mybir as mybir
from concourse.bass_utils import run_bass_kernel_spmd

F32 = mybir.dt.float32
BF16 = mybir.dt.bfloat16
I32 = mybir.dt.int32
AF = mybir.ActivationFunctionType
ALU = mybir.AluOpType

D = 1024
S = 2048
DEPTH = 2
MEM_LEN = 256
EPS = 1e-6
IN_COLS = 9992
C_SBQ, C_SBK, C_SBV, C_SBZ = 0, 512, 1024, 1536
C_DNQ, C_DNK, C_DNV, C_DNZ, C_DNA, C_DNB = 2048, 2560, 3072, 3584, 4096, 4100
C_RTQ, C_RTK, C_RTV, C_RTZ = 4104, 4360, 4616, 5128
C_MQ = 5640
C_G = 5896
NT = 4
TC = 512


class _Uniq:
    def __init__(self):
        self.n = 0

    def __add__(self, name):
        self.n += 1
        return f"{name}_{self.n}"


class _Rec:
    def __getattr__(self, name):
        return lambda *a, **k: (name, a, k)


_REC = _Rec()


class Prog:
    LIMIT = 30000

    def __init__(self, nc, stack):
        self.nc = nc
        self.stack = stack
        self.names = ["pe", "act", "dve", "pool", "sp"]
        self.q = {n: [] for n in self.names}
        self.cnt = {n: 0 for n in self.names}
        self.sems = {n: [] for n in self.names}
        self.seen = {n: {} for n in self.names}
        self.bufs = {}
        self.dma_sem = {}
        self.dma_cnt = {}
        self.same_sync = True
        self.dma_i = 0
        self.nins = 0

    def _eng_sem(self, eng, g):
        ep = (g - 1) // self.LIMIT
        while len(self.sems[eng]) <= ep:
            s = self.stack.enter_context(self.nc.semaphore(f"s_{eng}_{len(self.sems[eng])}"))
            self.sems[eng].append(s)
        return self.sems[eng][ep], (g - 1) % self.LIMIT + 1

    def _tok_sem(self, tok):
        kind, g = tok
        if kind.startswith("dma:"):
            return self.dma_sem[kind], 16 * g
        return self._eng_sem(kind, g)

    def _need(self, eng, tok):
        kind, g = tok
        if kind == eng:
            if eng in ("pe", "sp") or not self.same_sync:
                return False
        return self.seen[eng].get(kind, 0) < g

    def _collect(self, eng, reads, writes):
        toks = []
        for k in reads:
            b = self.bufs.get(k)
            if b and b[0] is not None:
                toks.append(b[0])
        for k in writes:
            b = self.bufs.get(k)
            if b:
                if b[0] is not None:
                    toks.append(b[0])
                toks.extend(b[1].items())
        need = {}
        for t in toks:
            if self._need(eng, t):
                need[t[0]] = max(need.get(t[0], 0), t[1])
        return list(need.items())

    def _update(self, tok, reads, writes):
        for k in writes:
            self.bufs[k] = [tok, {}]
        for k in reads:
            b = self.bufs.setdefault(k, [None, {}])
            if k in writes:
                continue
            b[1][tok[0]] = max(b[1].get(tok[0], 0), tok[1])

    def op(self, eng, fn, reads=(), writes=(), inc=True):
        call = fn(_REC)
        fn = lambda e, c=call: getattr(e, c[0])(*c[1], **c[2])
        waits = self._collect(eng, reads, writes)
        for t in waits:
            self.seen[eng][t[0]] = t[1]
        ws = [self._tok_sem(t) for t in waits]
        for w in ws[1:]:
            self.q[eng].append(("wait", w[0], w[1]))
        if inc:
            self.cnt[eng] += 1
            tok = (eng, self.cnt[eng])
            sem, val = self._eng_sem(eng, self.cnt[eng])
            self.q[eng].append(("ins", fn, ws[0] if ws else None, (sem, 1)))
        else:
            tok = (eng, self.cnt[eng] + 1)
            self.q[eng].append(("ins", fn, ws[0] if ws else None, None))
        self._update(tok, reads, writes)
        self.nins += 1
        return tok

    NDS = 16

    def dma(self, out, in_, reads=(), writes=(), stream="d0", queue="sp"):
        j = self.dma_i % self.NDS
        self.dma_i += 1
        kind = f"dma:{j}"
        if kind not in self.dma_sem:
            self.dma_sem[kind] = self.stack.enter_context(self.nc.semaphore(f"sd_{j}"))
            self.dma_cnt[kind] = 0
        waits = self._collect(queue, reads, writes)
        if self.dma_cnt[kind] > 0 and self.seen[queue].get(kind, 0) < self.dma_cnt[kind]:
            waits = [w for w in waits if w[0] != kind] + [(kind, self.dma_cnt[kind])]
        for t in waits:
            self.seen[queue][t[0]] = t[1]
        for t in waits:
            s, v = self._tok_sem(t)
            self.q[queue].append(("wait", s, v))
        self.dma_cnt[kind] += 1
        tok = (kind, self.dma_cnt[kind])
        self.q[queue].append(("ins", lambda e, o=out, i=in_: e.dma_start(out=o, in_=i), None,
                              (self.dma_sem[kind], 16)))
        self._update(tok, reads, writes)
        self.nins += 1
        return tok

    def barrier(self):
        toks = [(n, self.cnt[n]) for n in self.names if self.cnt[n] > 0]
        toks += [(k, c) for k, c in self.dma_cnt.items() if c > 0]
        for eng in self.names:
            for t in toks:
                if t[0] == eng and eng in ("pe", "sp"):
                    continue
                if self.seen[eng].get(t[0], 0) < t[1]:
                    self.seen[eng][t[0]] = t[1]
                    s, v = self._tok_sem(t)
                    self.q[eng].append(("wait", s, v))
        self.bufs = {}

    def finish(self, final_keys):
        toks = []
        for k in final_keys:
            b = self.bufs.get(k)
            if b and b[0] is not None:
                toks.append(b[0])
        for t in toks:
            s, v = self._tok_sem(t)
            self.q["sp"].append(("wait", s, v))

    def simulate(self):
        sem = {}
        pos = {n: 0 for n in self.names}
        def ok(w):
            return w is None or sem.get(id(w[0]), 0) >= w[1]
        progress = True
        while progress:
            progress = False
            for n in self.names:
                q = self.q[n]
                while pos[n] < len(q):
                    ent = q[pos[n]]
                    if ent[0] == "wait":
                        if not ok((ent[1], ent[2])):
                            break
                    else:
                        if not ok(ent[2]):
                            break
                        if ent[3] is not None:
                            sem[id(ent[3][0])] = sem.get(id(ent[3][0]), 0) + ent[3][1]
                    pos[n] += 1
                    progress = True
        stuck = {n: (pos[n], len(self.q[n])) for n in self.names if pos[n] < len(self.q[n])}
        if stuck:
            msg = []
            for n, (p, ln) in stuck.items():
                ent = self.q[n][p]
                w = (ent[1], ent[2]) if ent[0] == "wait" else ent[2]
                msg.append(f"{n}@{p}/{ln} waits {w} have {sem.get(id(w[0]), 0)} kind={ent[0]} tag={ent[4] if len(ent) > 4 else None}")
            raise RuntimeError("DEADLOCK in recorded program: " + "; ".join(msg))

    def emit(self):
        self.simulate()
        nc = self.nc
        with nc.Block() as block:
            def replay(e, name):
                for ent in self.q[name]:
                    if ent[0] == "wait":
                        e.wait_ge(ent[1], ent[2])
                    else:
                        _, fn, w, inc = ent
                        ins = fn(e)
                        if w is not None:
                            ins._wait_ge(w[0], w[1])
                        if inc is not None:
                            ins.then_inc(inc[0], inc[1])

            @block.sync
            def _(e):
                replay(e, "sp")

            @block.scalar
            def _(e):
                replay(e, "act")

            @block.vector
            def _(e):
                replay(e, "dve")

            @block.tensor
            def _(e):
                replay(e, "pe")

            @block.gpsimd
            def _(e):
                replay(e, "pool")


def _consts():
    c = {}
    i = np.arange(128)
    c["ident"] = np.eye(128, dtype=np.float32)
    c["ones"] = np.ones((128, 128), np.float32)
    c["trineg"] = -(i[:, None] >= i[None, :]).astype(np.float32)
    e0 = np.zeros((128, 128), np.float32)
    e0[0, :] = 1.0
    c["e0"] = e0
    c["masksb"] = (i[:, None] < i[None, :]).astype(np.float32)
    return c


def _rt_consts():
    gam = [1.0 - 2.0 ** (-5.0 - h) for h in range(4)]
    i = np.arange(128)
    dt = np.zeros((128, 4, 128), np.float64)
    sd = np.zeros((128, 4), np.float64)
    cd = np.zeros((128, 2, 128), np.float64)
    for h in range(4):
        rel = i[None, :] - i[:, None]
        dt[:, h, :] = np.where(rel >= 0, gam[h] ** np.maximum(rel, 0), 0.0)
        sd[:, h] = gam[h] ** (127 - i)
    for p in range(2):
        for hh in range(2):
            cd[hh * 64:(hh + 1) * 64, p, :] = (gam[2 * p + hh] ** (i + 1.0))[None, :]
    inv = 10000.0 ** (-(np.arange(32, dtype=np.float32)) / np.float32(32))
    rp = np.zeros((128, 4), np.float32)
    rp[:, 0] = np.tile(inv.astype(np.float32), 4)
    rp[:, 1] = np.where((i % 64) < 32, -1.0, 1.0)
    rp[:, 2] = np.float32(math.pi / 2)
    rp[:, 3] = EPS
    cn = np.eye(128) - 1.0 / 128.0
    return {"rt_dt": dt.astype(np.float32), "rt_sd": sd.astype(np.float32), "rt_cd": cd.astype(np.float32),
            "rt_rp": rp, "rt_cn": cn.astype(np.float32)}


def _dn_consts():
    i = np.arange(64)
    c = np.zeros((128, 4, 128), np.float32)
    c[:, 0, :] = np.eye(128)
    c[:, 1, :] = 1.0
    c[0:64, 2, 0:64] = (i[:, None] <= i[None, :])
    c[0:64, 3, 0:64] = (i[:, None] < i[None, :])
    return c


CONST_NAMES = ["ident", "ones", "trineg", "e0", "masksb"]


def build(n_layers=DEPTH, debug=False, branches=(0, 1, 2, 3)):
    _u = _Uniq()
    nc = bass.Bass("TRN2", target_bir_lowering=False)
    dr = {}

    def din(name, shape, dt=F32):
        dr[name] = nc.dram_tensor(name, list(shape), dt, kind="ExternalInput").ap()
        return dr[name]

    xT_d = din("xT", [D, S])
    memT_d = din("memT", [D, MEM_LEN])
    w_in_d = din("w_in", [DEPTH, D, IN_COLS])
    w_kv_d = din("w_mem_kv", [DEPTH, D, 512])
    w_br_d = {0: din("w_br_sb", [DEPTH, 512, D]), 1: din("w_br_dn", [DEPTH, 512, D]),
              2: din("w_br_ret", [DEPTH, 512, D]), 3: din("w_br_mem", [DEPTH, 256, D])}
    w_out_d = din("w_out", [DEPTH, D, D])
    normg_d = din("norm_g", [DEPTH, 128, 8])
    memg_d = din("mem_norm_g", [DEPTH, 128, 8])
    bgate_d = din("b_gate", [DEPTH, 128, 32])
    fing_d = din("final_norm_g", [128, 8])
    cst_d = {n: din("c_" + n, [128, 128]) for n in CONST_NAMES}
    dnc_d = din("c_dn", [128, 4, 128])
    dncw_d = din("dn_conv_w", [DEPTH, 128, 12, 4])
    dng_d = din("dn_norm_g", [128, DEPTH])
    dnal_d = din("dn_alog", [DEPTH, 128, 128])
    dndt_d = din("dn_dtb", [DEPTH, 128, 128])
    pos_d = din("pos", [128, S], I32)
    retg_d = din("ret_norm_g", [DEPTH, 128, 4])
    rtdt_d = din("c_rt_dt", [128, 4, 128])
    rtsd_d = din("c_rt_sd", [128, 4])
    rtcd_d = din("c_rt_cd", [128, 2, 128])
    rtrp_d = din("c_rt_rp", [128, 4])
    rtcn_d = din("c_rt_cn", [128, 128])
    outT_d = nc.dram_tensor("outT", [D, S], F32, kind="ExternalOutput").ap()
    dbg_d = {}
    if debug:
        dbg_d["hT"] = nc.dram_tensor("dbg_hT", [D, S], F32, kind="ExternalOutput").ap()
        dbg_d["omem"] = nc.dram_tensor("dbg_omem", [256, S], F32, kind="ExternalOutput").ap()
        for nm in ["osb", "odn", "ort"]:
            dbg_d[nm] = nc.dram_tensor("dbg_" + nm, [512, S], F32, kind="ExternalOutput").ap()

    with ExitStack() as st:
        P = Prog(nc, st)

        def sb(name, shape, dt):
            return st.enter_context(nc.sbuf_tensor(_u + "s_" + name, list(shape), dt))

        xT = sb("xT", [128, 8, S], F32)
        hT = sb("hT", [128, 8, S], BF16)
        mT = sb("mT", [128, 8, S], BF16)
        obT = sb("obT", [128, 4, S], BF16)
        wst = [sb(f"wst{i}", [128, 8, 128], F32) for i in range(2)]
        NWB = 5
        wbp = [sb(f"wb{i}", [128, 8, 128], BF16) for i in range(NWB)]
        cst = {n: sb("k_" + n, [128, 128], BF16) for n in CONST_NAMES}
        cstf = sb("cstf", [128, 128], F32)
        normg = sb("normg", [128, DEPTH, 8], F32)
        memg = sb("memg", [128, DEPTH, 8], F32)
        bgate = sb("bgate", [128, DEPTH, 32], F32)
        fing = sb("fing", [128, 8], F32)
        ps = [st.enter_context(nc.psum_tensor(f"ps{i}", [128, 512], F32)) for i in range(8)]
        PS = [("ps", i) for i in range(8)]

        wcount = [0]
        wbcount = [0]

        def load_w(src, n=128, rows=8, eng="pool"):
            i = wcount[0] % 2
            wcount[0] += 1
            j = wbcount[0] % NWB
            wbcount[0] += 1
            stg, wb = wst[i], wbp[j]
            P.dma(stg[:, 0:rows, 0:n], src.rearrange("(k p) n -> p k n", p=128),
                  writes=[("wst", i)], stream=f"w{i}")
            fn = lambda e, o=wb[:, 0:rows, 0:n], a=stg[:, 0:rows, 0:n]: e.tensor_copy(out=o, in_=a)
            P.op(eng, fn, reads=[("wst", i)], writes=[("wb", j)])
            return wb, ("wb", j)

        for n in CONST_NAMES:
            P.dma(cstf[:, :], cst_d[n][:, :], writes=["cstf"], stream="c")
            P.op("dve", lambda e, o=cst[n][:, :]: e.tensor_copy(out=o, in_=cstf[:, :]),
                 reads=["cstf"], writes=["k_" + n])
        for l in range(DEPTH):
            P.dma(normg[:, l, :], normg_d[l, :, :], writes=["normg"], stream="c")
            P.dma(memg[:, l, :], memg_d[l, :, :], writes=["memg"], stream="c")
            P.dma(bgate[:, l, :], bgate_d[l, :, :], writes=["bgate"], stream="c")
        P.dma(fing[:, :], fing_d[:, :], writes=["fing"], stream="c")
        for k in range(8):
            P.dma(xT[:, k, :], xT_d[k * 128:(k + 1) * 128, :], writes=[("xT", k)], stream="x")

        rtdt = sb("rtdt", [128, 4, 128], F32)
        rtsd = sb("rtsd", [128, 4], F32)
        rtcd = sb("rtcd", [128, 2, 128], F32)
        rtrp = sb("rtrp", [128, 4], F32)
        rtcn = sb("rtcn", [128, 128], BF16)
        retg = sb("retg", [128, DEPTH, 4], F32)
        P.dma(rtdt[:, :, :], rtdt_d[:, :, :], writes=["rtdt"], stream="c")
        P.dma(rtsd[:, :], rtsd_d[:, :], writes=["rtsd"], stream="c")
        P.dma(rtcd[:, :, :], rtcd_d[:, :, :], writes=["rtcd"], stream="c")
        P.dma(rtrp[:, :], rtrp_d[:, :], writes=["rtrp"], stream="c")
        P.dma(cstf[:, :], rtcn_d[:, :], writes=["cstf"], stream="c")
        P.op("dve", lambda e: e.tensor_copy(out=rtcn[:, :], in_=cstf[:, :]), reads=["cstf"], writes=["rtcn"])
        for l in range(DEPTH):
            P.dma(retg[:, l, :], retg_d[l, :, :], writes=["retg"], stream="c")
        dnc = sb("dnc", [128, 4, 128], F32)
        dncw = sb("dncw", [128, DEPTH, 12, 4], F32)
        dng = sb("dng", [128, DEPTH], F32)
        P.dma(dnc[:, :, :], dnc_d[:, :, :], writes=["dnc"], stream="c")
        P.dma(dng[:, :], dng_d[:, :], writes=["dng"], stream="c")
        for l in range(DEPTH):
            P.dma(dncw[:, l, :, :], dncw_d[l, :, :, :], writes=["dncw"], stream="c")
        def rms_stats(src_tile, src_keys, nk, ncols, c0, psum_i, sq_tile, sq_key, rstd_tile, rstd_key, extra):
            for k in range(nk):
                P.op("act", lambda e, k=k: e.activation(out=sq_tile[:, k % 2, 0:ncols], in_=src_tile[:, k, c0:c0 + ncols],
                                                        func=AF.Square),
                     reads=[src_keys[k]], writes=[(sq_key, k % 2)])
                P.op("pe", lambda e, k=k: e.matmul(ps[psum_i][:, 0:ncols], lhsT=cst["ones"][:, :],
                                                   rhs=sq_tile[:, k % 2, 0:ncols], start=(k == 0), stop=(k == nk - 1)),
                     reads=[(sq_key, k % 2), "k_ones"], writes=[PS[psum_i]])
            P.op("act", lambda e: e.activation(out=rstd_tile[:, 0:ncols], in_=ps[psum_i][:, 0:ncols], func=AF.Ln,
                                               bias=epsb[:, 0:1], scale=1.0),
                 reads=[PS[psum_i], "epsb"], writes=[rstd_key])
            P.op("act", lambda e: e.activation(out=rstd_tile[:, 0:ncols], in_=rstd_tile[:, 0:ncols], func=AF.Exp,
                                               scale=-0.5),
                 reads=[rstd_key], writes=[rstd_key])

        epsb = sb("epsb", [128, 1], F32)
        P.op("dve", lambda e: e.memset(epsb[:, :], float(D * EPS)), writes=["epsb"])
        sqt = sb("sqt", [128, 2, TC], BF16)
        rstd = sb("rstd", [128, TC], F32)
        g32 = sb("g32", [128, 8], F32)

        def norm_to(dst_fn, gsrc, layer_tag):
            P.op("dve", lambda e: e.tensor_scalar(out=g32[:, :], in0=gsrc, scalar1=float(math.sqrt(D)), scalar2=None,
                                                  op0=ALU.mult), reads=["normg", "fing"], writes=["g32"])
            for tc in range(NT):
                rms_stats(xT, [("xT", k) for k in range(8)], 8, TC, tc * TC, 0, sqt, "sqt", rstd, "rstd", None)
                for k in range(8):
                    ap, key = dst_fn(k, tc)
                    P.op("dve", lambda e, k=k, tc=tc, ap=ap: e.scalar_tensor_tensor(
                        out=ap, in0=xT[:, k, tc * TC:(tc + 1) * TC], scalar=g32[:, k:k + 1], in1=rstd[:, :],
                        op0=ALU.mult, op1=ALU.mult),
                        reads=[("xT", k), "g32", "rstd"], writes=[key])

        def proj_fm(l, col0, ncols, evac, wsrc=None):
            src = (w_in_d[l, :, col0:col0 + ncols] if wsrc is None else wsrc)
            wb, wkey = load_w(src, n=ncols)
            for tc in range(NT):
                pi = 1 + (tc % 2)
                for k in range(8):
                    P.op("pe", lambda e, k=k, tc=tc, pi=pi: e.matmul(
                        ps[pi][0:ncols, :], lhsT=wb[:, k, 0:ncols], rhs=hT[:, k, tc * TC:(tc + 1) * TC],
                        start=(k == 0), stop=(k == 7)),
                        reads=[wkey, ("hT", k)], writes=[PS[pi]], inc=(k == 7))
                evac(tc, ps[pi][0:ncols, :], PS[pi])

        def proj_tm(l, col0, ncols, evac, ntok=128):
            wb, wkey = load_w(w_in_d[l, :, col0:col0 + ncols], n=ncols)
            for tt in range(S // ntok):
                pi = 1 + (tt % 2)
                for k in range(8):
                    P.op("pe", lambda e: e.matmul(ps[pi][0:ntok, 0:ncols], lhsT=hT[:, k, tt * ntok:(tt + 1) * ntok],
                                                  rhs=wb[:, k, 0:ncols], start=(k == 0), stop=(k == 7)),
                         reads=[wkey, ("hT", k)], writes=[PS[pi]], inc=(k == 7))
                evac(tt, ps[pi][0:ntok, 0:ncols], PS[pi])

        oneb = sb("oneb", [128, 1], F32)
        P.op("dve", lambda e: e.memset(oneb[:, :], 1.0), writes=["oneb"])

        def run_streams(gens):
            gens = list(gens)
            while gens:
                for g in list(gens):
                    try:
                        next(g)
                    except StopIteration:
                        gens.remove(g)

        def sb_attention(l):
            for hp in range(4):
                with ExitStack() as ph:
                    def sbp(name, shape, dt):
                        return ph.enter_context(nc.sbuf_tensor(_u + ("s_" + name), list(shape), dt))
                    qT = sbp("sbq", [128, S], BF16)
                    kT = sbp("sbk", [128, S], BF16)
                    vtm = sbp("sbv", [128, 16, 128], BF16)
                    proj_fm(l, C_SBQ + hp * 128, 128, lambda tc, pap, pkey: P.op(
                        "act", lambda e: e.activation(out=qT[:, tc * TC:(tc + 1) * TC], in_=pap, func=AF.Copy, scale=0.125),
                        reads=[pkey], writes=["sbq"]))
                    proj_fm(l, C_SBK + hp * 128, 128, lambda tc, pap, pkey: P.op(
                        "dve", lambda e: e.tensor_copy(out=kT[:, tc * TC:(tc + 1) * TC], in_=pap),
                        reads=[pkey], writes=["sbk"]))
                    proj_tm(l, C_SBV + hp * 128, 128, lambda tt, pap, pkey: P.op(
                        "dve", lambda e: e.tensor_copy(out=vtm[:, tt, :], in_=pap),
                        reads=[pkey], writes=[("sbv", tt)]))
                    with ExitStack() as ph2:
                        def sbw(name, shape, dt):
                            return ph2.enter_context(nc.sbuf_tensor(_u + ("s_" + name), list(shape), dt))

                        import os as _os

                        def stream(sid, hh, qcs):
                            ez = sbw(f"ez{sid}", [128, TC], F32)
                            spb = sbw(f"spb{sid}", [128, TC], BF16)
                            eg = sbw(f"eg{sid}", [128, TC], BF16)
                            Gb = sbw(f"Gb{sid}", [128, TC], BF16)
                            wv = eg
                            K_wv = ("eg", sid)
                            K_ez, K_spb, K_eg, K_Gb = ("ez", sid), ("spb", sid), ("eg", sid), ("Gb", sid)
                            rs = slice(hh * 64, hh * 64 + 64)
                            pzg, po = 2 * sid, 2 * sid + 1
                            pgg = (4 + 2 * sid) if _os.environ.get('SB_SEPG') else pzg
                            yield
                            for qc in qcs:
                                q0 = qc * TC
                                kmax = qc * 4 + 3
                                pc0 = None
                                P.op("pool", lambda e: e.memset(wv[:, 0:384], 0.0), writes=[K_wv])
                                for kb in range(kmax, -1, -1):
                                    if kmax - kb >= int(_os.environ.get("SB_MAXIT", "99")):
                                        continue
                                    j = kb - qc * 4
                                    c0 = 128 * j if j >= 0 else 0
                                    cs = slice(c0, TC)
                                    tsl = slice(q0 + c0, q0 + TC)
                                    ksl = slice(kb * 128, (kb + 1) * 128)
                                    P.op("pe", lambda e: e.matmul(ps[pzg][:, cs], lhsT=kT[rs, ksl], rhs=qT[rs, tsl],
                                                                  start=True, stop=True),
                                         reads=["sbk", "sbq"], writes=[PS[pzg]])
                                    yield
                                    P.op("act", lambda e: e.activation(out=ez[:, cs], in_=ps[pzg][:, cs], func=AF.Exp),
                                         reads=[PS[pzg]], writes=[K_ez])
                                    yield
                                    P.op("act", lambda e: e.activation(out=spb[:, cs], in_=ez[:, cs], func=AF.Ln,
                                                                       bias=oneb[:, 0:1], scale=1.0),
                                         reads=[K_ez, "oneb"], writes=[K_spb])
                                    yield
                                    if j >= 0:
                                        P.op("dve", lambda e: e.tensor_tensor(out=spb[:, c0:c0 + 128], in0=spb[:, c0:c0 + 128],
                                                                              in1=cst["masksb"][:, :], op=ALU.mult),
                                             reads=[K_spb, "k_masksb"], writes=[K_spb])
                                    P.op("pe", lambda e: e.matmul(ps[pgg][:, cs], lhsT=cst["trineg"][:, :], rhs=spb[:, cs],
                                                                  start=True, stop=(pc0 is None)),
                                         reads=["k_trineg", K_spb], writes=[PS[pgg]], inc=(pc0 is None))
                                    if pc0 is not None:
                                        pcs = slice(pc0, TC)
                                        P.op("pe", lambda e: e.matmul(ps[pgg][:, pcs], lhsT=cst["e0"][:, :], rhs=Gb[:, pcs],
                                                                      start=False, stop=True),
                                             reads=["k_e0", K_Gb], writes=[PS[pgg]])
                                    yield
                                    P.op("act", lambda e: e.activation(out=eg[:, cs], in_=ps[pgg][:, cs], func=AF.Exp),
                                         reads=[PS[pgg]], writes=[K_eg])
                                    if kb > 0:
                                        P.op("dve", lambda e: e.tensor_copy(out=Gb[:, cs], in_=ps[pgg][:, cs]),
                                             reads=[PS[pgg], K_eg], writes=[K_Gb])
                                    yield
                                    P.op("dve", lambda e: e.tensor_tensor(out=wv[:, cs], in0=ez[:, cs], in1=eg[:, cs], op=ALU.mult),
                                         reads=[K_ez, K_eg], writes=[K_wv])
                                    if j >= 0:
                                        P.op("dve", lambda e: e.tensor_tensor(out=wv[:, c0:c0 + 128], in0=wv[:, c0:c0 + 128],
                                                                              in1=cst["masksb"][:, :], op=ALU.mult),
                                             reads=[K_wv, "k_masksb"], writes=[K_wv])
                                    yield
                                    P.op("pe", lambda e: e.matmul(ps[po][rs, :], lhsT=vtm[:, kb, rs], rhs=wv[:, :],
                                                                  start=(kb == kmax), stop=(kb == 0)),
                                         reads=[("sbv", kb), K_wv], writes=[PS[po]])
                                    pc0 = c0
                                    yield
                                P.op("dve", lambda e: e.tensor_copy(out=obT[rs, hp, q0:q0 + TC], in_=ps[po][rs, :]),
                                     reads=[PS[po]], writes=[("obT", hp, hh, qc)])
                                yield

                        _mode = _os.environ.get("SB_MODE", "4")
                        if _mode == "1":
                            run_streams([stream(0, 0, [3, 0, 2, 1])])
                            run_streams([stream(1, 1, [3, 0, 2, 1])])
                        elif _mode == "2":
                            run_streams([stream(0, 0, [3, 0, 2, 1]), stream(1, 1, [3, 0, 2, 1])])
                        else:
                            run_streams([stream(0, 0, [3, 0]), stream(1, 1, [3, 0]), stream(2, 0, [2, 1]), stream(3, 1, [2, 1])])
                        P.barrier()
                    with ExitStack() as ph3:
                        zs = [ph3.enter_context(nc.sbuf_tensor(_u + f"s_sbzs{i}", [128, TC], BF16)) for i in range(2)]

                        def evz(tc, pap, pkey):
                            b = tc % 2
                            P.op("act", lambda e: e.activation(out=zs[b][:, :], in_=pap, func=AF.Silu),
                                 reads=[pkey], writes=[("sbzs", b)])
                            tsl = slice(tc * TC, (tc + 1) * TC)
                            P.op("dve", lambda e: e.tensor_tensor(out=obT[:, hp, tsl], in0=obT[:, hp, tsl], in1=zs[b][:, :], op=ALU.mult),
                                 reads=[("sbzs", b)], writes=[("obT", hp)])
                        proj_fm(l, C_SBZ + hp * 128, 128, evz)
                        P.barrier()

        def retention(l):
            TWO_PI = 2.0 * math.pi
            C1 = 6.28125
            C2 = TWO_PI - C1
            with ExitStack() as ph0:
                def sb0(name, shape, dt):
                    return ph0.enter_context(nc.sbuf_tensor(_u + ("s_" + name), list(shape), dt))
                qrT = sb0("rqr", [128, 2, S], BF16)
                krT = sb0("rkr", [128, 2, S], BF16)
                with ExitStack() as ph1:
                    def sb1(name, shape, dt):
                        return ph1.enter_context(nc.sbuf_tensor(_u + ("s_" + name), list(shape), dt))
                    COS2 = sb1("rcos", [128, S], BF16)
                    SIN2 = sb1("rsin", [128, S], BF16)
                    pint = sb1("rpint", [128, TC], I32)
                    ta = sb1("rta", [128, TC], F32)
                    tk = sb1("rtk", [128, TC], F32)
                    tm_full = sb1("rtm", [128, TC], F32)
                    tm = tm_full[:, 0:256]
                    ki = tm_full[:, 256:512].bitcast(I32)
                    ta_f, tk_f, pint_f = ta, tk, pint
                    for tc in range(8):
                        tsl = slice(tc * 256, (tc + 1) * 256)
                        ta, tk, pint = ta_f[:, 0:256], tk_f[:, 0:256], pint_f[:, 0:256]
                        P.dma(pint, pos_d[:, tsl], writes=["rpint"], stream="x")
                        P.op("dve", lambda e: e.tensor_copy(out=ta, in_=pint), reads=["rpint"], writes=["rta"])
                        P.op("dve", lambda e: e.tensor_scalar(out=ta, in0=ta, scalar1=rtrp[:, 0:1], scalar2=None,
                                                              op0=ALU.mult), reads=["rta", "rtrp"], writes=["rta"])
                        P.op("dve", lambda e: e.tensor_scalar(out=ki, in0=ta, scalar1=float(1.0 / TWO_PI),
                                                              scalar2=None, op0=ALU.mult), reads=["rta"], writes=["rki"])
                        P.op("dve", lambda e: e.tensor_copy(out=tk, in_=ki), reads=["rki"], writes=["rtk"])
                        P.op("dve", lambda e: e.scalar_tensor_tensor(out=ta, in0=tk, scalar=-C1, in1=ta,
                                                                     op0=ALU.mult, op1=ALU.add),
                             reads=["rtk", "rta"], writes=["rta"])
                        P.op("dve", lambda e: e.scalar_tensor_tensor(out=ta, in0=tk, scalar=-C2, in1=ta,
                                                                     op0=ALU.mult, op1=ALU.add),
                             reads=["rtk", "rta"], writes=["rta"])
                        P.op("dve", lambda e: e.tensor_single_scalar(out=tm, in_=ta, scalar=float(math.pi),
                                                                     op=ALU.is_gt), reads=["rta"], writes=["rtm"])
                        P.op("dve", lambda e: e.scalar_tensor_tensor(out=ta, in0=tm, scalar=-TWO_PI, in1=ta,
                                                                     op0=ALU.mult, op1=ALU.add),
                             reads=["rtm", "rta"], writes=["rta"])
                        P.op("dve", lambda e: e.tensor_single_scalar(out=tm, in_=ta, scalar=float(-math.pi),
                                                                     op=ALU.is_lt), reads=["rta"], writes=["rtm"])
                        P.op("dve", lambda e: e.scalar_tensor_tensor(out=ta, in0=tm, scalar=TWO_PI, in1=ta,
                                                                     op0=ALU.mult, op1=ALU.add),
                             reads=["rtm", "rta"], writes=["rta"])
                        P.op("act", lambda e: e.activation(out=tk, in_=ta, func=AF.Sin),
                             reads=["rta"], writes=["rtk"])
                        P.op("dve", lambda e: e.tensor_scalar(out=SIN2[:, tsl], in0=tk, scalar1=rtrp[:, 1:2], scalar2=None,
                                                              op0=ALU.mult), reads=["rtk", "rtrp"], writes=["rsin"])
                        P.op("dve", lambda e: e.tensor_single_scalar(out=tm, in_=ta, scalar=float(math.pi / 2),
                                                                     op=ALU.is_gt), reads=["rta"], writes=["rtm"])
                        P.op("dve", lambda e: e.scalar_tensor_tensor(out=ta, in0=tm, scalar=-TWO_PI, in1=ta,
                                                                     op0=ALU.mult, op1=ALU.add),
                             reads=["rtm", "rta"], writes=["rta"])
                        P.op("act", lambda e: e.activation(out=COS2[:, tsl], in_=ta, func=AF.Sin, bias=rtrp[:, 2:3],
                                                           scale=1.0), reads=["rta", "rtrp"], writes=["rcos"])
                    ta, tk, pint = ta_f, tk_f, pint_f
                    for which, (c_base, dstT, scl) in enumerate([(C_RTQ, qrT, 1.0), (C_RTK, krT, 0.125)]):
                        for hp in range(2):
                            wb, wkey = load_w(w_in_d[l, :, c_base + hp * 128:c_base + (hp + 1) * 128])
                            jsw = wbcount[0] % NWB
                            wbcount[0] += 1
                            wsw = wbp[jsw]
                            for hh in range(2):
                                o = hh * 64
                                P.op("pool", lambda e: e.tensor_copy(out=wsw[:, :, o:o + 32], in_=wb[:, :, o + 32:o + 64]),
                                     reads=[wkey], writes=[("wb", jsw)])
                                P.op("pool", lambda e: e.tensor_copy(out=wsw[:, :, o + 32:o + 64], in_=wb[:, :, o:o + 32]),
                                     reads=[wkey], writes=[("wb", jsw)])
                            for tc in range(NT):
                                tsl = slice(tc * TC, (tc + 1) * TC)
                                for k in range(8):
                                    P.op("pe", lambda e: e.matmul(ps[1][:, :], lhsT=wb[:, k, :], rhs=hT[:, k, tsl],
                                                                  start=(k == 0), stop=(k == 7)),
                                         reads=[wkey, ("hT", k)], writes=[PS[1]], inc=(k == 7))
                                for k in range(8):
                                    P.op("pe", lambda e: e.matmul(ps[2][:, :], lhsT=wsw[:, k, :], rhs=hT[:, k, tsl],
                                                                  start=(k == 0), stop=(k == 7)),
                                         reads=[("wb", jsw), ("hT", k)], writes=[PS[2]], inc=(k == 7))
                                P.op("dve", lambda e: e.scalar_tensor_tensor(out=ta[:, :], in0=ps[1][:, :], scalar=float(scl),
                                                                             in1=COS2[:, tsl], op0=ALU.mult, op1=ALU.mult),
                                     reads=[PS[1], "rcos"], writes=["rta"])
                                P.op("dve", lambda e: e.scalar_tensor_tensor(out=tk[:, :], in0=ps[2][:, :], scalar=float(scl),
                                                                             in1=SIN2[:, tsl], op0=ALU.mult, op1=ALU.mult),
                                     reads=[PS[2], "rsin"], writes=["rtk"])
                                P.op("dve", lambda e: e.tensor_tensor(out=dstT[:, hp, tsl], in0=ta[:, :], in1=tk[:, :], op=ALU.add),
                                     reads=["rta", "rtk"], writes=[("rq", which, hp)])
                    P.barrier()
                for hp in range(2):
                    with ExitStack() as ph2:
                        def sb2(name, shape, dt):
                            return ph2.enter_context(nc.sbuf_tensor(_u + ("s_" + name), list(shape), dt))
                        kdtm = sb2("rkd", [128, 16, 128], BF16)
                        vtm = sb2("rv", [128, 16, 128], BF16)
                        zsT = sb2("rz", [128, S], BF16)
                        qc = [sb2(f"rqc{i}", [128, 128], BF16) for i in range(2)]
                        scm = [sb2(f"rscm{i}", [128, 128], BF16) for i in range(2)]
                        Sf = sb2("rSf", [128, 128], F32)
                        Sb = sb2("rSb", [128, 128], BF16)
                        ob = sb2("rob", [128, TC], BF16)
                        sq = sb2("rsq", [128, TC], BF16)
                        rs_t = rstd
                        tt_t = sb2("rtt", [128, TC], F32)
                        for n in range(16):
                            nsl = slice(n * 128, (n + 1) * 128)
                            pk = ps[3][:, 0:64].bitcast(BF16)
                            P.op("pe", lambda e: e.transpose(pk, krT[:, hp, nsl], cst["ident"][:, :]),
                                 reads=[("rq", 1, hp), "k_ident"], writes=[PS[3]])
                            for hh in range(2):
                                cs = slice(hh * 64, (hh + 1) * 64)
                                h = 2 * hp + hh
                                P.op("dve", lambda e: e.tensor_scalar(out=kdtm[:, n, cs], in0=pk[:, cs], scalar1=rtsd[:, h:h + 1],
                                                                      scalar2=None, op0=ALU.mult),
                                     reads=[PS[3], "rtsd"], writes=[("rkd", n)])
                        for hh in range(2):
                            h = 2 * hp + hh
                            rs = slice(hh * 64, (hh + 1) * 64)
                            proj_tm(l, C_RTV + h * 128, 128, lambda tt, pap, pkey: P.op(
                                "dve", lambda e: e.tensor_copy(out=vtm[:, tt, :], in_=pap), reads=[pkey], writes=[("rv", tt)]))
                            proj_fm(l, C_RTZ + h * 128, 128, lambda tc, pap, pkey: P.op(
                                "act", lambda e: e.activation(out=zsT[:, tc * TC:(tc + 1) * TC], in_=pap, func=AF.Silu),
                                reads=[pkey], writes=["rz"]))
                            for n in range(16):
                                nsl = slice(n * 128, (n + 1) * 128)
                                b = n % 2
                                csl = slice((n % 4) * 128, (n % 4 + 1) * 128)
                                po = 5 + (n // 4) % 2
                                P.op("pe", lambda e: e.matmul(ps[4][:, 0:128], lhsT=krT[rs, hp, nsl], rhs=qrT[rs, hp, nsl],
                                                              start=True, stop=True),
                                     reads=[("rq", 1, hp), ("rq", 0, hp)], writes=[PS[4]])
                                P.op("dve", lambda e: e.tensor_tensor(out=scm[b][:, :], in0=ps[4][:, 0:128], in1=rtdt[:, h, :],
                                                                      op=ALU.mult),
                                     reads=[PS[4], "rtdt"], writes=[("rscm", b)])
                                if n > 0:
                                    P.op("dve", lambda e: e.tensor_tensor(out=qc[b][rs, :], in0=qrT[rs, hp, nsl], in1=rtcd[rs, hp, :],
                                                                          op=ALU.mult),
                                         reads=[("rq", 0, hp), "rtcd"], writes=[("rqc", b)])
                                P.op("pe", lambda e: e.matmul(ps[po][:, csl], lhsT=vtm[:, n, :], rhs=scm[b][:, :],
                                                              start=True, stop=(n == 0)),
                                     reads=[("rv", n), ("rscm", b)], writes=[PS[po]], inc=(n == 0))
                                if n > 0:
                                    P.op("pe", lambda e: e.matmul(ps[po][:, csl], lhsT=Sb[rs, :], rhs=qc[b][rs, :],
                                                                  start=False, stop=True),
                                         reads=["rSb", ("rqc", b)], writes=[PS[po]])
                                if n < 15:
                                    P.op("pe", lambda e: e.matmul(ps[7][rs, 0:128], lhsT=kdtm[:, n, rs], rhs=vtm[:, n, :],
                                                                  start=True, stop=True),
                                         reads=[("rkd", n), ("rv", n)], writes=[PS[7]])
                                    gch = float((1.0 - 2.0 ** (-5.0 - h)) ** 128)
                                    if n == 0:
                                        P.op("dve", lambda e: e.tensor_copy(out=Sf[rs, :], in_=ps[7][rs, 0:128]),
                                             reads=[PS[7]], writes=["rSf"])
                                    else:
                                        P.op("dve", lambda e: e.scalar_tensor_tensor(out=Sf[rs, :], in0=Sf[rs, :], scalar=gch,
                                                                                     in1=ps[7][rs, 0:128], op0=ALU.mult,
                                                                                     op1=ALU.add),
                                             reads=[PS[7], "rSf"], writes=["rSf"])
                                    P.op("act", lambda e: e.activation(out=Sb[rs, :], in_=Sf[rs, :], func=AF.Copy),
                                         reads=["rSf"], writes=["rSb"])
                                if n % 4 == 3:
                                    tc = n // 4
                                    tsl = slice(tc * TC, (tc + 1) * TC)
                                    P.op("act", lambda e: e.activation(out=ob[:, :], in_=ps[po][:, :], func=AF.Copy),
                                         reads=[PS[po]], writes=["rob"])
                                    P.op("pe", lambda e: e.matmul(ps[1][:, :], lhsT=rtcn[:, :], rhs=ob[:, :], start=True, stop=True),
                                         reads=["rtcn", "rob"], writes=[PS[1]])
                                    P.op("act", lambda e: e.activation(out=sq[:, :], in_=ps[1][:, :], func=AF.Square),
                                         reads=[PS[1]], writes=["rsq"])
                                    P.op("pe", lambda e: e.matmul(ps[2][:, :], lhsT=cst["ones"][:, :], rhs=sq[:, :],
                                                                  start=True, stop=True),
                                         reads=["k_ones", "rsq"], writes=[PS[2]])
                                    P.op("act", lambda e: e.activation(out=rs_t[:, :], in_=ps[2][:, :], func=AF.Ln,
                                                                       bias=rtrp[:, 3:4], scale=float(1.0 / 128.0)),
                                         reads=[PS[2], "rtrp"], writes=["rstd"])
                                    P.op("act", lambda e: e.activation(out=rs_t[:, :], in_=rs_t[:, :], func=AF.Exp, scale=-0.5),
                                         reads=["rstd"], writes=["rstd"])
                                    P.op("dve", lambda e: e.scalar_tensor_tensor(out=tt_t[:, :], in0=ps[1][:, :],
                                                                                 scalar=retg[:, l, h:h + 1], in1=rs_t[:, :],
                                                                                 op0=ALU.mult, op1=ALU.mult),
                                         reads=[PS[1], "retg", "rstd"], writes=["rtt"])
                                    P.op("dve", lambda e: e.tensor_tensor(out=obT[:, h, tsl], in0=tt_t[:, :], in1=zsT[:, tsl],
                                                                          op=ALU.mult),
                                         reads=["rtt", "rz"], writes=[("obT", h)])
                        P.barrier()

        def deltanet(l):
            identf, onesf = dnc[:, 0, :], dnc[:, 1, :]
            mincl, mstrict = dnc[0:64, 2, 0:64], dnc[0:64, 3, 0:64]
            H = slice(0, 64)
            with ExitStack() as ph0:
                def sb0(name, shape, dt):
                    return ph0.enter_context(nc.sbuf_tensor(_u + ("s_" + name), list(shape), dt))
                abraw = sb0("dab", [64, 32, 8], F32)
                rep = sb0("drep", [64, 128], F32)
                t1 = sb0("dt1", [64, 32, 4], F32)
                g_tm = sb0("dg", [64, 32, 4], F32)
                beta_tm = sb0("dbeta", [64, 32, 4], F32)
                gc_tm = sb0("dgc", [64, 32, 4], F32)
                egl = sb0("degl", [128, 32, 4], F32)
                bg_tm = sb0("dbg", [64, 32, 4], F32)
                ed_tm = sb0("ded", [64, 32, 4], F32)
                proj_tm(l, C_DNA, 8, lambda tt, pap, pkey: P.op(
                    "dve", lambda e: e.tensor_copy(out=abraw[:, tt, :], in_=pap), reads=[pkey], writes=["dab"]), ntok=64)
                fl = lambda t: t[:, :, :].rearrange("p a b -> p (a b)")
                P.dma(rep[:, :], dndt_d[l, 0:64, :], writes=["drep"], stream="c")
                P.op("dve", lambda e: e.tensor_tensor(out=t1[:, :, :], in0=abraw[:, :, 0:4],
                                                      in1=rep[:, :].rearrange("p (a b) -> p a b", b=4), op=ALU.add),
                     reads=["dab", "drep"], writes=["dt1"])
                P.op("act", lambda e: e.activation(out=t1[:, :, :], in_=t1[:, :, :], func=AF.Exp), reads=["dt1"], writes=["dt1"])
                P.op("act", lambda e: e.activation(out=t1[:, :, :], in_=t1[:, :, :], func=AF.Ln, bias=oneb[0:64, 0:1], scale=1.0),
                     reads=["dt1", "oneb"], writes=["dt1"])
                P.dma(rep[:, :], dnal_d[l, 0:64, :], writes=["drep"], stream="c")
                P.op("act", lambda e: e.activation(out=rep[:, :], in_=rep[:, :], func=AF.Exp), reads=["drep"], writes=["drep"])
                P.op("dve", lambda e: e.scalar_tensor_tensor(out=g_tm[:, :, :], in0=t1[:, :, :], scalar=-1.0,
                                                             in1=rep[:, :].rearrange("p (a b) -> p a b", b=4),
                                                             op0=ALU.mult, op1=ALU.mult),
                     reads=["dt1", "drep"], writes=["dg"])
                P.op("act", lambda e: e.activation(out=beta_tm[:, :, :], in_=abraw[:, :, 4:8], func=AF.Sigmoid),
                     reads=["dab"], writes=["dbeta"])
                P.op("pe", lambda e: e.matmul(ps[0][0:64, 0:128], lhsT=dnc[0:64, 2, 0:64], rhs=fl(g_tm), start=True, stop=True),
                     reads=["dnc", "dg"], writes=[PS[0]])
                P.op("dve", lambda e: e.tensor_copy(out=fl(gc_tm), in_=ps[0][0:64, 0:128]), reads=[PS[0]], writes=["dgc"])
                P.op("pe", lambda e: e.matmul(ps[0][:, 128:256], lhsT=dnc[0:64, 1, :], rhs=fl(g_tm), start=True, stop=True),
                     reads=["dnc", "dg"], writes=[PS[0]])
                P.op("dve", lambda e: e.tensor_tensor(out=fl(ed_tm), in0=ps[0][0:64, 128:256], in1=fl(gc_tm), op=ALU.subtract),
                     reads=[PS[0], "dgc"], writes=["ded"])
                P.op("act", lambda e: e.activation(out=fl(ed_tm), in_=fl(ed_tm), func=AF.Exp), reads=["ded"], writes=["ded"])
                P.op("act", lambda e: e.activation(out=fl(egl), in_=ps[0][:, 128:256], func=AF.Exp), reads=[PS[0]], writes=["degl"])
                P.op("act", lambda e: e.activation(out=fl(bg_tm), in_=fl(gc_tm), func=AF.Exp), reads=["dgc"], writes=["dbg"])
                P.op("dve", lambda e: e.tensor_tensor(out=fl(bg_tm), in0=fl(bg_tm), in1=fl(beta_tm), op=ALU.mult),
                     reads=["dbg", "dbeta"], writes=["dbg"])
                P.barrier()
                for h in range(4):
                    with ExitStack() as ph1:
                        def sb1(name, shape, dt):
                            return ph1.enter_context(nc.sbuf_tensor(_u + ("s_" + name), list(shape), dt))
                        qT = sb1("dq", [128, S], BF16)
                        kT = sb1("dk", [128, S], BF16)
                        vT = sb1("dv", [128, S], BF16)
                        with ExitStack() as ph2:
                            xpad = ph2.enter_context(nc.sbuf_tensor(_u + "s_dxpad", [128, S + 3], F32))
                            acc = ph2.enter_context(nc.sbuf_tensor(_u + "s_dacc", [128, S], F32))
                            P.op("dve", lambda e: e.memset(xpad[:, 0:3], 0.0), writes=["dxpad0"])
                            for xi, (c_base, dstT) in enumerate([(C_DNQ, qT), (C_DNK, kT), (C_DNV, vT)]):
                                proj_fm(l, c_base + h * 128, 128, lambda tc, pap, pkey: P.op(
                                    "act", lambda e: e.activation(out=xpad[:, 3 + tc * TC:3 + (tc + 1) * TC], in_=pap, func=AF.Copy),
                                    reads=[pkey], writes=[("dxpad", tc)]))
                                allx = [("dxpad", tc) for tc in range(NT)] + ["dxpad0"]
                                cw = dncw[:, l, xi * 4 + h, :]
                                P.op("dve", lambda e: e.tensor_scalar(out=acc[:, :], in0=xpad[:, 3:3 + S], scalar1=cw[:, 3:4],
                                                                      scalar2=None, op0=ALU.mult),
                                     reads=allx + ["dncw"], writes=["dacc"])
                                for j in range(3):
                                    P.op("dve", lambda e: e.scalar_tensor_tensor(out=acc[:, :], in0=xpad[:, j:j + S],
                                                                                 scalar=cw[:, j:j + 1], in1=acc[:, :],
                                                                                 op0=ALU.mult, op1=ALU.add),
                                         reads=allx + ["dncw", "dacc"], writes=["dacc"])
                                P.op("act", lambda e: e.activation(out=acc[:, :], in_=acc[:, :], func=AF.Silu),
                                     reads=["dacc"], writes=["dacc"])
                                if xi == 2:
                                    P.op("dve", lambda e: e.tensor_copy(out=vT[:, :], in_=acc[:, :]), reads=["dacc"], writes=["dvT"])
                                else:
                                    for tc in range(NT):
                                        tsl = slice(tc * TC, (tc + 1) * TC)
                                        P.op("act", lambda e: e.activation(out=sqt[:, 0, :], in_=acc[:, tsl], func=AF.Square),
                                             reads=["dacc"], writes=[("sqt", 0)])
                                        P.op("pe", lambda e: e.matmul(ps[0][:, :], lhsT=cst["ones"][:, :], rhs=sqt[:, 0, :],
                                                                      start=True, stop=True),
                                             reads=[("sqt", 0), "k_ones"], writes=[PS[0]])
                                        P.op("act", lambda e: e.activation(out=rstd[:, :], in_=ps[0][:, :], func=AF.Ln,
                                                                           bias=rtrp[:, 3:4], scale=1.0),
                                             reads=[PS[0], "rtrp"], writes=["rstd"])
                                        P.op("act", lambda e: e.activation(out=rstd[:, :], in_=rstd[:, :], func=AF.Exp, scale=-0.5),
                                             reads=["rstd"], writes=["rstd"])
                                        sc = float(128.0 ** -0.5) if xi == 0 else 1.0
                                        P.op("dve", lambda e: e.scalar_tensor_tensor(out=dstT[:, tsl], in0=acc[:, tsl], scalar=sc,
                                                                                     in1=rstd[:, :], op0=ALU.mult, op1=ALU.mult),
                                             reads=["dacc", "rstd"], writes=["dqT" if xi == 0 else "dkT"])
                            P.barrier()
                        zsT = sb1("dz", [128, S], BF16)
                        proj_fm(l, C_DNZ + h * 128, 128, lambda tc, pap, pkey: P.op(
                            "act", lambda e: e.activation(out=zsT[:, tc * TC:(tc + 1) * TC], in_=pap, func=AF.Silu),
                            reads=[pkey], writes=["dz"]))
                        f64 = lambda nm: sb1(nm, [64, 64], F32)
                        gsc, bsc, dmin, decm, bs, LT = [f64(n_) for n_ in ["dgsc", "dbsc", "ddmin", "ddecm", "dbs", "dLT"]]
                        egcb = sb1("degcb", [128, 64], F32)
                        qg = sb1("dqg", [128, 64], BF16)
                        aT = sb1("daT", [64, 64], BF16)
                        PP = sb1("dPP", [64, 128], BF16)
                        IpPT = sb1("dIpPT", [64, 64], BF16)
                        Tt = [sb1(f"dTt{i}", [64, 64], BF16) for i in range(2)]
                        kbg = sb1("dkbg", [64, 128], BF16)
                        kd = sb1("dkd", [64, 128], BF16)
                        vb = sb1("dvb", [64, 128], BF16)
                        wT = sb1("dwT", [128, 64], BF16)
                        u_sb = sb1("du", [64, 128], F32)
                        vnew = sb1("dvnew", [64, 128], BF16)
                        Sf = sb1("dSf", [128, 128], F32)
                        Sb = sb1("dSb", [128, 128], BF16)
                        nsq = sb1("dnsq", [128, TC], BF16)
                        ntmp = sb1("dntmp", [128, TC], F32)
                        identb = cst["ident"]
                        for n in range(32):
                            csl = slice(n * 64, (n + 1) * 64)
                            P.op("dve", lambda e: e.tensor_scalar(out=gsc[:, :], in0=mincl, scalar1=g_tm[:, n, h:h + 1], scalar2=None,
                                                                  op0=ALU.mult), reads=["dnc", "dg"], writes=["dgsc"])
                            P.op("dve", lambda e: e.tensor_scalar(out=bsc[:, :], in0=identf[0:64, 0:64], scalar1=beta_tm[:, n, h:h + 1],
                                                                  scalar2=None, op0=ALU.mult), reads=["dnc", "dbeta"], writes=["dbsc"])
                            P.op("pe", lambda e: e.matmul(ps[0][:, 0:64], lhsT=onesf[0:64, :], rhs=gsc[:, :], start=True, stop=True),
                                 reads=["dnc", "dgsc"], writes=[PS[0]])
                            P.op("pe", lambda e: e.matmul(ps[0][:, 64:128], lhsT=onesf[0:64, :], rhs=bsc[:, :], start=True, stop=True),
                                 reads=["dnc", "dbsc"], writes=[PS[0]])
                            P.op("act", lambda e: e.activation(out=egcb[:, :], in_=ps[0][:, 0:64], func=AF.Exp),
                                 reads=[PS[0]], writes=["degcb"])
                            P.op("dve", lambda e: e.tensor_tensor(out=qg[:, :], in0=qT[:, csl], in1=egcb[:, :], op=ALU.mult),
                                 reads=["dqT", "degcb"], writes=["dqg"])
                            P.op("dve", lambda e: e.tensor_scalar(out=dmin[:, :], in0=ps[0][H, 0:64], scalar1=gc_tm[:, n, h:h + 1],
                                                                  scalar2=0.0, op0=ALU.subtract, op1=ALU.min),
                                 reads=[PS[0], "dgc"], writes=["ddmin"])
                            P.op("act", lambda e: e.activation(out=dmin[:, :], in_=dmin[:, :], func=AF.Exp),
                                 reads=["ddmin"], writes=["ddmin"])
                            P.op("dve", lambda e: e.tensor_tensor(out=decm[:, :], in0=dmin[:, :], in1=mincl, op=ALU.mult),
                                 reads=["ddmin", "dnc"], writes=["ddecm"])
                            P.op("dve", lambda e: e.tensor_tensor(out=bs[:, :], in0=ps[0][H, 64:128], in1=mstrict, op=ALU.mult),
                                 reads=[PS[0], "dnc"], writes=["dbs"])
                            P.op("pe", lambda e: e.matmul(ps[1][H, 0:64], lhsT=kT[:, csl], rhs=kT[:, csl], start=True, stop=True),
                                 reads=["dkT"], writes=[PS[1]])
                            P.op("pe", lambda e: e.matmul(ps[1][H, 64:128], lhsT=kT[:, csl], rhs=qT[:, csl], start=True, stop=True),
                                 reads=["dkT", "dqT"], writes=[PS[1]])
                            P.op("dve", lambda e: e.tensor_tensor(out=aT[:, :], in0=ps[1][H, 64:128], in1=decm[:, :], op=ALU.mult),
                                 reads=[PS[1], "ddecm"], writes=["daT"])
                            P.op("dve", lambda e: e.tensor_tensor(out=LT[:, :], in0=ps[1][H, 0:64], in1=decm[:, :], op=ALU.mult),
                                 reads=[PS[1], "ddecm"], writes=["dLT"])
                            P.op("dve", lambda e: e.scalar_tensor_tensor(out=PP[:, 0:64], in0=LT[:, :], scalar=-1.0, in1=bs[:, :],
                                                                         op0=ALU.mult, op1=ALU.mult),
                                 reads=["dLT", "dbs"], writes=["dPP"])
                            ptv = ps[2][H, 0:32].bitcast(BF16)
                            P.op("pe", lambda e: e.transpose(ptv, PP[:, 0:64], identb[0:64, 0:64]),
                                 reads=["dPP", "k_ident"], writes=[PS[2]])
                            P.op("act", lambda e: e.activation(out=PP[:, 64:128], in_=ptv, func=AF.Copy),
                                 reads=[PS[2]], writes=["dPP"])
                            P.op("dve", lambda e: e.tensor_tensor(out=Tt[0][:, :], in0=PP[:, 0:64], in1=identb[0:64, 0:64], op=ALU.add),
                                 reads=["dPP", "k_ident"], writes=[("dTt", 0)])
                            cur = 0
                            for lev in range(1, 6):
                                P.op("pe", lambda e: e.matmul(ps[2][H, 0:64], lhsT=PP[:, 64:128], rhs=PP[:, 0:64], start=True, stop=True),
                                     reads=["dPP"], writes=[PS[2]])
                                P.op("pe", lambda e: e.matmul(ps[2][H, 64:128], lhsT=PP[:, 0:64], rhs=PP[:, 64:128], start=True, stop=True),
                                     reads=["dPP"], writes=[PS[2]])
                                P.op("act", lambda e: e.activation(out=PP[:, :], in_=ps[2][H, 0:128], func=AF.Copy),
                                     reads=[PS[2]], writes=["dPP"])
                                P.op("dve", lambda e: e.tensor_tensor(out=IpPT[:, :], in0=ps[2][H, 64:128], in1=identb[0:64, 0:64], op=ALU.add),
                                     reads=[PS[2], "k_ident"], writes=["dIpPT"])
                                P.op("pe", lambda e: e.matmul(ps[3][H, 0:64], lhsT=IpPT[:, :], rhs=Tt[cur][:, :], start=True, stop=True),
                                     reads=["dIpPT", ("dTt", cur)], writes=[PS[3]])
                                cur = 1 - cur
                                P.op("dve", lambda e: e.tensor_copy(out=Tt[cur][:, :], in_=ps[3][H, 0:64]),
                                     reads=[PS[3]], writes=[("dTt", cur)])
                            kv = ps[4][H, 0:64].bitcast(BF16)
                            vv = ps[4][H, 64:128].bitcast(BF16)
                            P.op("pe", lambda e: e.transpose(kv, kT[:, csl], identb[:, :]), reads=["dkT", "k_ident"], writes=[PS[4]])
                            P.op("pe", lambda e: e.transpose(vv, vT[:, csl], identb[:, :]), reads=["dvT", "k_ident"], writes=[PS[4]])
                            P.op("dve", lambda e: e.tensor_scalar(out=kbg[:, :], in0=kv, scalar1=bg_tm[:, n, h:h + 1], scalar2=None,
                                                                  op0=ALU.mult), reads=[PS[4], "dbg"], writes=["dkbg"])
                            P.op("dve", lambda e: e.tensor_scalar(out=kd[:, :], in0=kv, scalar1=ed_tm[:, n, h:h + 1], scalar2=None,
                                                                  op0=ALU.mult), reads=[PS[4], "ded"], writes=["dkd"])
                            P.op("dve", lambda e: e.tensor_scalar(out=vb[:, :], in0=vv, scalar1=beta_tm[:, n, h:h + 1], scalar2=None,
                                                                  op0=ALU.mult), reads=[PS[4], "dbeta"], writes=["dvb"])
                            P.op("pe", lambda e: e.matmul(ps[5][H, 0:128], lhsT=Tt[cur][:, :], rhs=vb[:, :], start=True, stop=True),
                                 reads=[("dTt", cur), "dvb"], writes=[PS[5]])
                            P.op("pe", lambda e: e.matmul(ps[5][:, 128:192], lhsT=kbg[:, :], rhs=Tt[cur][:, :], start=True, stop=True),
                                 reads=[("dTt", cur), "dkbg"], writes=[PS[5]])
                            P.op("act", lambda e: e.activation(out=u_sb[:, :], in_=ps[5][H, 0:128], func=AF.Copy),
                                 reads=[PS[5]], writes=["du"])
                            P.op("act", lambda e: e.activation(out=wT[:, :], in_=ps[5][:, 128:192], func=AF.Copy),
                                 reads=[PS[5]], writes=["dwT"])
                            if n == 0:
                                P.op("dve", lambda e: e.tensor_copy(out=vnew[:, :], in_=u_sb[:, :]), reads=["du"], writes=["dvnew"])
                            else:
                                P.op("pe", lambda e: e.matmul(ps[5][H, 256:384], lhsT=wT[:, :], rhs=Sb[:, :], start=True, stop=True),
                                     reads=["dwT", "dSb"], writes=[PS[5]])
                                P.op("dve", lambda e: e.tensor_tensor(out=vnew[:, :], in0=u_sb[:, :], in1=ps[5][H, 256:384], op=ALU.subtract),
                                     reads=["du", PS[5]], writes=["dvnew"])
                            osl = slice((n % 8) * 64, (n % 8 + 1) * 64)
                            if n > 0:
                                P.op("pe", lambda e: e.matmul(ps[6][:, osl], lhsT=Sb[:, :], rhs=qg[:, :], start=True, stop=False),
                                     reads=["dSb", "dqg"], writes=[PS[6]], inc=False)
                            P.op("pe", lambda e: e.matmul(ps[6][:, osl], lhsT=vnew[:, :], rhs=aT[:, :], start=(n == 0), stop=True),
                                 reads=["dvnew", "daT"], writes=[PS[6]])
                            if n < 31:
                                P.op("pe", lambda e: e.matmul(ps[7][:, 0:128], lhsT=kd[:, :], rhs=vnew[:, :], start=True, stop=True),
                                     reads=["dkd", "dvnew"], writes=[PS[7]])
                                if n == 0:
                                    P.op("dve", lambda e: e.tensor_copy(out=Sf[:, :], in_=ps[7][:, 0:128]), reads=[PS[7]], writes=["dSf"])
                                else:
                                    P.op("dve", lambda e: e.scalar_tensor_tensor(out=Sf[:, :], in0=Sf[:, :], scalar=egl[:, n, h:h + 1],
                                                                                 in1=ps[7][:, 0:128], op0=ALU.mult, op1=ALU.add),
                                         reads=[PS[7], "dSf", "degl"], writes=["dSf"])
                                P.op("act", lambda e: e.activation(out=Sb[:, :], in_=Sf[:, :], func=AF.Copy), reads=["dSf"], writes=["dSb"])
                            if n % 8 == 7:
                                tc = n // 8
                                tsl = slice(tc * TC, (tc + 1) * TC)
                                P.op("act", lambda e: e.activation(out=nsq[:, :], in_=ps[6][:, :], func=AF.Square),
                                     reads=[PS[6]], writes=["dnsq"])
                                P.op("pe", lambda e: e.matmul(ps[0][:, :], lhsT=cst["ones"][:, :], rhs=nsq[:, :], start=True, stop=True),
                                     reads=["k_ones", "dnsq"], writes=[PS[0]])
                                P.op("act", lambda e: e.activation(out=rstd[:, :], in_=ps[0][:, :], func=AF.Ln, bias=rtrp[:, 3:4],
                                                                   scale=float(1.0 / 128.0)), reads=[PS[0], "rtrp"], writes=["rstd"])
                                P.op("act", lambda e: e.activation(out=rstd[:, :], in_=rstd[:, :], func=AF.Exp, scale=-0.5),
                                     reads=["rstd"], writes=["rstd"])
                                P.op("dve", lambda e: e.scalar_tensor_tensor(out=ntmp[:, :], in0=ps[6][:, :], scalar=dng[:, l:l + 1],
                                                                             in1=rstd[:, :], op0=ALU.mult, op1=ALU.mult),
                                     reads=[PS[6], "dng", "rstd"], writes=["dntmp"])
                                P.op("dve", lambda e: e.tensor_tensor(out=obT[:, h, tsl], in0=ntmp[:, :], in1=zsT[:, tsl], op=ALU.mult),
                                     reads=["dntmp", "dz"], writes=[("obT", h)])
                        P.barrier()

        def dump(nm, tile, nchunks, keyname, nokey=False):
            if nokey:
                P.barrier()
            with nc.sbuf_tensor(_u + "s_dbgf_" + nm, [128, S], F32) as dbgf:
                for k in range(nchunks):
                    P.op("dve", lambda e: e.tensor_copy(out=dbgf[:, :], in_=tile[:, k, :]),
                         reads=[(keyname, k)], writes=["dbgf"])
                    P.dma(dbg_d[nm][k * 128:(k + 1) * 128, :], dbgf[:, :], reads=["dbgf"], writes=["dbg_" + nm], stream="o")
                P.barrier()

        for l in range(n_layers):
            norm_to(lambda k, tc: (hT[:, k, tc * TC:(tc + 1) * TC], ("hT", k)), normg[:, l, :], l)
            if debug and l == 0:
                with nc.sbuf_tensor(_u + "dbgf", [128, S], F32) as dbgf:
                    for k in range(8):
                        P.op("dve", lambda e, k=k: e.tensor_copy(out=dbgf[:, :], in_=hT[:, k, :]),
                             reads=[("hT", k)], writes=["dbgf"])
                        P.dma(dbg_d["hT"][k * 128:(k + 1) * 128, :], dbgf[:, :], reads=["dbgf"], writes=["dbg_hT"],
                              stream="o")
                    P.barrier()

            with ExitStack() as ph:
                def sbp(name, shape, dt):
                    return ph.enter_context(nc.sbuf_tensor(_u + "s_" + name, list(shape), dt))
                memT = sbp("memT", [128, 8, MEM_LEN], F32)
                memn = sbp("memn", [128, 8, MEM_LEN], BF16)
                kmT = sbp("kmT", [128, 2, MEM_LEN], BF16)
                vm = sbp("vm", [128, 2, 256], BF16)
                qmT = sbp("qmT", [128, 2, S], BF16)
                pT = [sbp(f"pT{i}", [128, TC], BF16) for i in range(2)]
                rden = sbp("rden", [128, TC], F32)
                for k in range(8):
                    P.dma(memT[:, k, :], memT_d[k * 128:(k + 1) * 128, :], writes=[("memT", k)], stream="x")
                P.op("dve", lambda e: e.tensor_scalar(out=g32[:, :], in0=memg[:, l, :], scalar1=float(math.sqrt(D)),
                                                      scalar2=None, op0=ALU.mult), reads=["memg"], writes=["g32"])
                rms_stats(memT, [("memT", k) for k in range(8)], 8, MEM_LEN, 0, 0, sqt, "sqt", rstd, "rstd", None)
                for k in range(8):
                    P.op("dve", lambda e, k=k: e.scalar_tensor_tensor(
                        out=memn[:, k, :], in0=memT[:, k, :], scalar=g32[:, k:k + 1], in1=rstd[:, 0:MEM_LEN],
                        op0=ALU.mult, op1=ALU.mult), reads=[("memT", k), "g32", "rstd"], writes=[("memn", k)])
                for ec in range(2):
                    wb, wkey = load_w(w_kv_d[l, :, ec * 128:(ec + 1) * 128])
                    for k in range(8):
                        P.op("pe", lambda e, k=k, wb=wb: e.matmul(ps[1][:, 0:MEM_LEN], lhsT=wb[:, k, :], rhs=memn[:, k, :],
                                                                  start=(k == 0), stop=(k == 7)),
                             reads=[wkey, ("memn", k)], writes=[PS[1]], inc=(k == 7))
                    P.op("dve", lambda e, ec=ec: e.tensor_copy(out=kmT[:, ec, :], in_=ps[1][:, 0:MEM_LEN]),
                         reads=[PS[1]], writes=[("kmT", ec)])
                for vc in range(2):
                    wb, wkey = load_w(w_kv_d[l, :, 256 + vc * 128:256 + (vc + 1) * 128])
                    for mt in range(2):
                        for k in range(8):
                            P.op("pe", lambda e, k=k, wb=wb, mt=mt: e.matmul(
                                ps[2][:, 0:128], lhsT=memn[:, k, mt * 128:(mt + 1) * 128], rhs=wb[:, k, :],
                                start=(k == 0), stop=(k == 7)),
                                reads=[wkey, ("memn", k)], writes=[PS[2]], inc=(k == 7))
                        P.op("dve", lambda e, mt=mt, vc=vc: e.tensor_copy(out=vm[:, mt, vc * 128:(vc + 1) * 128],
                                                                          in_=ps[2][:, 0:128]),
                             reads=[PS[2]], writes=[("vm", mt)])
                for ec in range(2):
                    def ev(tc, pap, pkey, ec=ec):
                        P.op("act", lambda e: e.activation(out=qmT[:, ec, tc * TC:(tc + 1) * TC], in_=pap,
                                                           func=AF.Copy, scale=0.125),
                             reads=[pkey], writes=[("qmT", ec)])
                    proj_fm(l, C_MQ + ec * 128, 128, ev)
                for h in range(4):
                    ec, r0 = h // 2, (h % 2) * 64
                    for tc in range(NT):
                        tsl = slice(tc * TC, (tc + 1) * TC)
                        for mb in range(2):
                            pi = 3 + mb
                            P.op("pe", lambda e, mb=mb, pi=pi: e.matmul(
                                ps[pi][:, :], lhsT=kmT[r0:r0 + 64, ec, mb * 128:(mb + 1) * 128],
                                rhs=qmT[r0:r0 + 64, ec, tsl], start=True, stop=True),
                                reads=[("kmT", ec), ("qmT", ec)], writes=[PS[pi]])
                            P.op("act", lambda e, mb=mb, pi=pi: e.activation(out=pT[mb][:, :], in_=ps[pi][:, :],
                                                                             func=AF.Exp),
                                 reads=[PS[pi]], writes=[("pT", mb)])
                        for mb in range(2):
                            P.op("pe", lambda e, mb=mb: e.matmul(
                                ps[5][r0:r0 + 64, :], lhsT=vm[:, mb, h * 64:(h + 1) * 64], rhs=pT[mb][:, :],
                                start=(mb == 0), stop=(mb == 1)),
                                reads=[("vm", mb), ("pT", mb)], writes=[PS[5]], inc=(mb == 1))
                        for mb in range(2):
                            P.op("pe", lambda e, mb=mb: e.matmul(
                                ps[6][r0:r0 + 64, :], lhsT=cst["ones"][:, 0:64], rhs=pT[mb][:, :],
                                start=(mb == 0), stop=(mb == 1)),
                                reads=["k_ones", ("pT", mb)], writes=[PS[6]], inc=(mb == 1))
                        P.op("dve", lambda e: e.reciprocal(out=rden[r0:r0 + 64, :], in_=ps[6][r0:r0 + 64, :]),
                             reads=[PS[6]], writes=["rden"])
                        P.op("dve", lambda e, tsl=tsl: e.tensor_tensor(out=obT[r0:r0 + 64, ec, tsl], in0=ps[5][r0:r0 + 64, :],
                                                                      in1=rden[r0:r0 + 64, :], op=ALU.mult),
                             reads=[PS[5], "rden"], writes=[("obT", ec)])
                P.barrier()
            if debug and l == 0:
                with nc.sbuf_tensor(_u + "dbgf2", [128, S], F32) as dbgf:
                    for k in range(2):
                        P.op("dve", lambda e, k=k: e.tensor_copy(out=dbgf[:, :], in_=obT[:, k, :]),
                             reads=[("obT", k)], writes=["dbgf"])
                        P.dma(dbg_d["omem"][k * 128:(k + 1) * 128, :], dbgf[:, :], reads=["dbgf"], writes=["dbg_omem"],
                              stream="o")
                    P.barrier()

            def merge(br, nwc, first):
                with ExitStack() as ph:
                    sg = [ph.enter_context(nc.sbuf_tensor(_u + f"sg{i}", [128, TC], F32)) for i in range(2)]
                    tmp = [ph.enter_context(nc.sbuf_tensor(_u + f"mtmp{i}", [128, TC], F32)) for i in range(2)]
                    it = 0
                    for dc in range(8):
                        wg, wgk = load_w(w_in_d[l, :, C_G + br * D + dc * 128:C_G + br * D + (dc + 1) * 128])
                        wr, wrk = load_w(w_br_d[br][l, :, dc * 128:(dc + 1) * 128], rows=nwc)
                        for tc in range(NT):
                            tsl = slice(tc * TC, (tc + 1) * TC)
                            b = it % 2
                            it += 1
                            pg, pp = 1 + b, 3 + b
                            for k in range(8):
                                P.op("pe", lambda e, k=k, pg=pg, tsl=tsl, wg=wg: e.matmul(
                                    ps[pg][:, :], lhsT=wg[:, k, :], rhs=hT[:, k, tsl], start=(k == 0), stop=(k == 7)),
                                    reads=[wgk, ("hT", k)], writes=[PS[pg]], inc=(k == 7))
                            for k in range(nwc):
                                P.op("pe", lambda e, k=k, pp=pp, tsl=tsl, wr=wr: e.matmul(
                                    ps[pp][:, :], lhsT=wr[:, k, :], rhs=obT[:, k, tsl], start=(k == 0), stop=(k == nwc - 1)),
                                    reads=[wrk, ("obT", k)], writes=[PS[pp]], inc=(k == nwc - 1))
                            P.op("act", lambda e, b=b, pg=pg, dc=dc: e.activation(
                                out=sg[b][:, :], in_=ps[pg][:, :], func=AF.Sigmoid,
                                bias=bgate[:, l, br * 8 + dc:br * 8 + dc + 1], scale=1.0),
                                reads=[PS[pg], "bgate"], writes=[("sg", b)])
                            if first:
                                P.op("dve", lambda e, b=b, pp=pp, dc=dc, tsl=tsl: e.tensor_tensor(
                                    out=mT[:, dc, tsl], in0=ps[pp][:, :], in1=sg[b][:, :], op=ALU.mult),
                                    reads=[PS[pp], ("sg", b)], writes=[("mT", dc, tc)])
                            else:
                                P.op("dve", lambda e, b=b, pp=pp: e.tensor_tensor(
                                    out=tmp[b][:, :], in0=ps[pp][:, :], in1=sg[b][:, :], op=ALU.mult),
                                    reads=[PS[pp], ("sg", b)], writes=[("mtmp", b)])
                                P.op("dve", lambda e, b=b, dc=dc, tsl=tsl: e.tensor_tensor(
                                    out=mT[:, dc, tsl], in0=mT[:, dc, tsl], in1=tmp[b][:, :], op=ALU.add),
                                    reads=[("mtmp", b), ("mT", dc, tc)], writes=[("mT", dc, tc)])
                    P.barrier()

            merge(3, 2, True)
            if 0 in branches:
                sb_attention(l)
                if debug and l == 0:
                    dump('osb', obT, 4, 'obT')
                merge(0, 4, False)
            if 1 in branches:
                deltanet(l)
                if debug and l == 0:
                    dump('odn', obT, 4, 'obT')
                merge(1, 4, False)
            if 2 in branches:
                retention(l)
                if debug and l == 0:
                    dump('ort', obT, 4, 'obT')
                merge(2, 4, False)

            for ec in range(8):
                wo, wok = load_w(w_out_d[l, :, ec * 128:(ec + 1) * 128])
                for tc in range(NT):
                    tsl = slice(tc * TC, (tc + 1) * TC)
                    pi = 1 + tc % 2
                    for dc in range(8):
                        P.op("pe", lambda e, dc=dc, pi=pi, tsl=tsl, wo=wo: e.matmul(
                            ps[pi][:, :], lhsT=wo[:, dc, :], rhs=mT[:, dc, tsl], start=(dc == 0), stop=(dc == 7)),
                            reads=[wok, ("mT", dc, tc)], writes=[PS[pi]], inc=(dc == 7))
                    P.op("dve", lambda e, ec=ec, pi=pi, tsl=tsl: e.tensor_tensor(
                        out=xT[:, ec, tsl], in0=xT[:, ec, tsl], in1=ps[pi][:, :], op=ALU.add),
                        reads=[PS[pi], ("xT", ec)], writes=[("xT", ec)])
            P.barrier()

        with ExitStack() as ph:
            ot = [ph.enter_context(nc.sbuf_tensor(_u + f"ot{i}", [128, TC], F32)) for i in range(2)]
            cnt = [0]

            def dst(k, tc):
                b = cnt[0] % 2
                cnt[0] += 1
                return ot[b][:, :], ("ot", b)
            P.op("dve", lambda e: e.tensor_scalar(out=g32[:, :], in0=fing[:, :], scalar1=float(math.sqrt(D)), scalar2=None,
                                                  op0=ALU.mult), reads=["fing"], writes=["g32"])
            for tc in range(NT):
                rms_stats(xT, [("xT", k) for k in range(8)], 8, TC, tc * TC, 0, sqt, "sqt", rstd, "rstd", None)
                for k in range(8):
                    ap, key = dst(k, tc)
                    P.op("dve", lambda e, k=k, tc=tc, ap=ap: e.scalar_tensor_tensor(
                        out=ap, in0=xT[:, k, tc * TC:(tc + 1) * TC], scalar=g32[:, k:k + 1], in1=rstd[:, :],
                        op0=ALU.mult, op1=ALU.mult),
                        reads=[("xT", k), "g32", "rstd"], writes=[key])
                    P.dma(outT_d[k * 128:(k + 1) * 128, tc * TC:(tc + 1) * TC], ap, reads=[key], writes=["outT"],
                          stream="o")
            P.finish(["outT", "dbg_hT", "dbg_omem", "dbg_osb", "dbg_odn", "dbg_ort"])
            P.barrier()
        P.emit()
        print("instructions recorded:", P.nins, {n: len(P.q[n]) for n in P.names})
    return nc


_NC_CACHE = {}


def _prep_inputs(inputs, b):
    f = np.float32
    m = {}
    m["xT"] = np.ascontiguousarray(inputs["x"][b].T.astype(f))
    m["memT"] = np.ascontiguousarray(inputs["mem"][b].T.astype(f))
    m["w_in"] = np.ascontiguousarray(inputs["w_in"], dtype=f)
    m["w_mem_kv"] = np.ascontiguousarray(inputs["w_mem_kv"], dtype=f)
    for n in ["w_br_sb", "w_br_dn", "w_br_ret", "w_br_mem", "w_out"]:
        m[n] = np.ascontiguousarray(inputs[n], dtype=f)
    m["norm_g"] = np.ascontiguousarray(inputs["norm_g"].reshape(DEPTH, 8, 128).transpose(0, 2, 1), dtype=f)
    m["mem_norm_g"] = np.ascontiguousarray(inputs["mem_norm_g"].reshape(DEPTH, 8, 128).transpose(0, 2, 1), dtype=f)
    m["b_gate"] = np.ascontiguousarray(inputs["b_gate"].reshape(DEPTH, 32, 128).transpose(0, 2, 1), dtype=f)
    m["final_norm_g"] = np.ascontiguousarray(inputs["final_norm_g"].reshape(8, 128).T, dtype=f)
    for n, v in _consts().items():
        m["c_" + n] = v
    for n, v in _rt_consts().items():
        m["c_" + n] = v
    m["c_dn"] = _dn_consts()
    m["dn_conv_w"] = np.ascontiguousarray(inputs["dn_conv_w"].reshape(DEPTH, 4, 12, 128).transpose(0, 3, 2, 1), dtype=f)
    m["dn_norm_g"] = np.ascontiguousarray(inputs["dn_norm_g"].T, dtype=f)
    m["dn_alog"] = np.ascontiguousarray(np.broadcast_to(np.tile(inputs["dn_a_log"], (1, 32))[:, None, :], (DEPTH, 128, 128)), dtype=f)
    m["dn_dtb"] = np.ascontiguousarray(np.broadcast_to(np.tile(inputs["dn_dt_bias"], (1, 32))[:, None, :], (DEPTH, 128, 128)), dtype=f)
    m["pos"] = np.ascontiguousarray(np.broadcast_to(inputs["positions"][b].astype(np.int32)[None, :], (128, S)))
    m["ret_norm_g"] = np.ascontiguousarray(inputs["ret_norm_g"].reshape(DEPTH, 4, 128).transpose(0, 2, 1), dtype=f)
    return m


def kernel(**inputs):
    inputs = {k: np.asarray(v) for k, v in inputs.items()}
    if "nc" not in _NC_CACHE:
        _NC_CACHE["nc"] = build()
    nc = _NC_CACHE["nc"]
    in_maps = [_prep_inputs(inputs, b) for b in range(8)]
    res = run_bass_kernel_spmd(nc, in_maps, core_ids=list(range(8)))
    out = np.stack([np.ascontiguousarray(res.results[b]["outT"].T) for b in range(8)], axis=0)
    return out.astype(np.float32)
```

```python
import math
from contextlib import ExitStack
import numpy as np
import concourse.bass as bass
import concourse.mybir as mybir
from concourse.bass_utils import run_bass_kernel_spmd

F32 = mybir.dt.float32
BF16 = mybir.dt.bfloat16
I32 = mybir.dt.int32
AF = mybir.ActivationFunctionType
ALU = mybir.AluOpType

D = 1024
S = 2048
DEPTH = 2
MEM_LEN = 256
EPS = 1e-6
IN_COLS = 9992
C_SBQ, C_SBK, C_SBV, C_SBZ = 0, 512, 1024, 1536
C_DNQ, C_DNK, C_DNV, C_DNZ, C_DNA, C_DNB = 2048, 2560, 3072, 3584, 4096, 4100
C_RTQ, C_RTK, C_RTV, C_RTZ = 4104, 4360, 4616, 5128
C_MQ = 5640
C_G = 5896
NT = 4
TC = 512


class _Uniq:
    def __init__(self):
        self.n = 0

    def __add__(self, name):
        self.n += 1
        return f"{name}_{self.n}"


class _Rec:
    def __getattr__(self, name):
        return lambda *a, **k: (name, a, k)


_REC = _Rec()


class Prog:
    LIMIT = 30000

    def __init__(self, nc, stack):
        self.nc = nc
        self.stack = stack
        self.names = ["pe", "act", "dve", "pool", "sp"]
        self.q = {n: [] for n in self.names}
        self.cnt = {n: 0 for n in self.names}
        self.sems = {n: [] for n in self.names}
        self.seen = {n: {} for n in self.names}
        self.bufs = {}
        self.dma_sem = {}
        self.dma_cnt = {}
        self.same_sync = True
        self.dma_i = 0
        self.nins = 0

    def _eng_sem(self, eng, g):
        ep = (g - 1) // self.LIMIT
        while len(self.sems[eng]) <= ep:
            s = self.stack.enter_context(self.nc.semaphore(f"s_{eng}_{len(self.sems[eng])}"))
            self.sems[eng].append(s)
        return self.sems[eng][ep], (g - 1) % self.LIMIT + 1

    def _tok_sem(self, tok):
        kind, g = tok
        if kind.startswith("dma:"):
            return self.dma_sem[kind], 16 * g
        return self._eng_sem(kind, g)

    def _need(self, eng, tok):
        kind, g = tok
        if kind == eng:
            if eng in ("pe", "sp") or not self.same_sync:
                return False
        return self.seen[eng].get(kind, 0) < g

    def _collect(self, eng, reads, writes):
        toks = []
        for k in reads:
            b = self.bufs.get(k)
            if b and b[0] is not None:
                toks.append(b[0])
            if b and isinstance(k, tuple) and k[0] == "ps" and eng in ("act", "dve"):
                other = "dve" if eng == "act" else "act"
                if other in b[1]:
                    toks.append((other, b[1][other]))
        for k in writes:
            b = self.bufs.get(k)
            if b:
                if b[0] is not None:
                    toks.append(b[0])
                toks.extend(b[1].items())
        need = {}
        for t in toks:
            if self._need(eng, t):
                need[t[0]] = max(need.get(t[0], 0), t[1])
        return list(need.items())

    def _update(self, tok, reads, writes):
        for k in writes:
            self.bufs[k] = [tok, {}]
        for k in reads:
            b = self.bufs.setdefault(k, [None, {}])
            if k in writes:
                continue
            b[1][tok[0]] = max(b[1].get(tok[0], 0), tok[1])

    def op(self, eng, fn, reads=(), writes=(), inc=True):
        call = fn(_REC)
        fn = lambda e, c=call: getattr(e, c[0])(*c[1], **c[2])
        waits = self._collect(eng, reads, writes)
        for t in waits:
            self.seen[eng][t[0]] = t[1]
        ws = [self._tok_sem(t) for t in waits]
        for w in ws[1:]:
            self.q[eng].append(("wait", w[0], w[1]))
        if inc:
            self.cnt[eng] += 1
            tok = (eng, self.cnt[eng])
            sem, val = self._eng_sem(eng, self.cnt[eng])
            self.q[eng].append(("ins", fn, ws[0] if ws else None, (sem, 1)))
        else:
            tok = (eng, self.cnt[eng] + 1)
            self.q[eng].append(("ins", fn, ws[0] if ws else None, None))
        self._update(tok, reads, writes)
        self.nins += 1
        return tok

    NDS = 16

    def dma(self, out, in_, reads=(), writes=(), stream="d0", queue="sp"):
        j = self.dma_i % self.NDS
        self.dma_i += 1
        kind = f"dma:{j}"
        if kind not in self.dma_sem:
            self.dma_sem[kind] = self.stack.enter_context(self.nc.semaphore(f"sd_{j}"))
            self.dma_cnt[kind] = 0
        waits = self._collect(queue, reads, writes)
        if self.dma_cnt[kind] > 0 and self.seen[queue].get(kind, 0) < self.dma_cnt[kind]:
            waits = [w for w in waits if w[0] != kind] + [(kind, self.dma_cnt[kind])]
        for t in waits:
            self.seen[queue][t[0]] = t[1]
        for t in waits:
            s, v = self._tok_sem(t)
            self.q[queue].append(("wait", s, v))
        self.dma_cnt[kind] += 1
        tok = (kind, self.dma_cnt[kind])
        self.q[queue].append(("ins", lambda e, o=out, i=in_: e.dma_start(out=o, in_=i), None,
                              (self.dma_sem[kind], 16)))
        self._update(tok, reads, writes)
        self.nins += 1
        return tok

    def barrier(self):
        toks = [(n, self.cnt[n]) for n in self.names if self.cnt[n] > 0]
        toks += [(k, c) for k, c in self.dma_cnt.items() if c > 0]
        for eng in self.names:
            for t in toks:
                if t[0] == eng and eng in ("pe", "sp"):
                    continue
                if self.seen[eng].get(t[0], 0) < t[1]:
                    self.seen[eng][t[0]] = t[1]
                    s, v = self._tok_sem(t)
                    self.q[eng].append(("wait", s, v))
        self.bufs = {}

    def finish(self, final_keys):
        toks = []
        for k in final_keys:
            b = self.bufs.get(k)
            if b and b[0] is not None:
                toks.append(b[0])
        for t in toks:
            s, v = self._tok_sem(t)
            self.q["sp"].append(("wait", s, v))

    def simulate(self):
        sem = {}
        pos = {n: 0 for n in self.names}
        def ok(w):
            return w is None or sem.get(id(w[0]), 0) >= w[1]
        progress = True
        while progress:
            progress = False
            for n in self.names:
                q = self.q[n]
                while pos[n] < len(q):
                    ent = q[pos[n]]
                    if ent[0] == "wait":
                        if not ok((ent[1], ent[2])):
                            break
                    else:
                        if not ok(ent[2]):
                            break
                        if ent[3] is not None:
                            sem[id(ent[3][0])] = sem.get(id(ent[3][0]), 0) + ent[3][1]
                    pos[n] += 1
                    progress = True
        stuck = {n: (pos[n], len(self.q[n])) for n in self.names if pos[n] < len(self.q[n])}
        if stuck:
            msg = []
            for n, (p, ln) in stuck.items():
                ent = self.q[n][p]
                w = (ent[1], ent[2]) if ent[0] == "wait" else ent[2]
                msg.append(f"{n}@{p}/{ln} waits {w} have {sem.get(id(w[0]), 0)} kind={ent[0]} tag={ent[4] if len(ent) > 4 else None}")
            raise RuntimeError("DEADLOCK in recorded program: " + "; ".join(msg))

    def emit(self):
        self.simulate()
        nc = self.nc
        with nc.Block() as block:
            def replay(e, name):
                for ent in self.q[name]:
                    if ent[0] == "wait":
                        e.wait_ge(ent[1], ent[2])
                    else:
                        _, fn, w, inc = ent
                        ins = fn(e)
                        if w is not None:
                            ins._wait_ge(w[0], w[1])
                        if inc is not None:
                            ins.then_inc(inc[0], inc[1])

            @block.sync
            def _(e):
                replay(e, "sp")

            @block.scalar
            def _(e):
                replay(e, "act")

            @block.vector
            def _(e):
                replay(e, "dve")

            @block.tensor
            def _(e):
                replay(e, "pe")

            @block.gpsimd
            def _(e):
                replay(e, "pool")


def _consts():
    c = {}
    i = np.arange(128)
    c["ident"] = np.eye(128, dtype=np.float32)
    c["ones"] = np.ones((128, 128), np.float32)
    c["trineg"] = -(i[:, None] >= i[None, :]).astype(np.float32)
    e0 = np.zeros((128, 128), np.float32)
    e0[0, :] = 1.0
    c["e0"] = e0
    c["masksb"] = (i[:, None] < i[None, :]).astype(np.float32)
    return c


def _rt_consts():
    gam = [1.0 - 2.0 ** (-5.0 - h) for h in range(4)]
    i = np.arange(128)
    dt = np.zeros((128, 4, 128), np.float64)
    sd = np.zeros((128, 4), np.float64)
    cd = np.zeros((128, 2, 128), np.float64)
    for h in range(4):
        rel = i[None, :] - i[:, None]
        dt[:, h, :] = np.where(rel >= 0, gam[h] ** np.maximum(rel, 0), 0.0)
        sd[:, h] = gam[h] ** (127 - i)
    for p in range(2):
        for hh in range(2):
            cd[hh * 64:(hh + 1) * 64, p, :] = (gam[2 * p + hh] ** (i + 1.0))[None, :]
    inv = 10000.0 ** (-(np.arange(32, dtype=np.float32)) / np.float32(32))
    rp = np.zeros((128, 4), np.float32)
    rp[:, 0] = np.tile(inv.astype(np.float32), 4)
    rp[:, 1] = np.where((i % 64) < 32, -1.0, 1.0)
    rp[:, 2] = np.float32(math.pi / 2)
    rp[:, 3] = EPS
    cn = np.eye(128) - 1.0 / 128.0
    return {"rt_dt": dt.astype(np.float32), "rt_sd": sd.astype(np.float32), "rt_cd": cd.astype(np.float32),
            "rt_rp": rp, "rt_cn": cn.astype(np.float32)}


def _dn_consts():
    i = np.arange(64)
    c = np.zeros((128, 4, 128), np.float32)
    c[:, 0, :] = np.eye(128)
    c[:, 1, :] = 1.0
    c[0:64, 2, 0:64] = (i[:, None] <= i[None, :])
    c[0:64, 3, 0:64] = (i[:, None] < i[None, :])
    return c


CONST_NAMES = ["ident", "ones", "trineg", "e0", "masksb"]


def build(n_layers=DEPTH, debug=False, branches=(0, 1, 2, 3)):
    _u = _Uniq()
    nc = bass.Bass("TRN2", target_bir_lowering=False)
    dr = {}

    def din(name, shape, dt=F32):
        dr[name] = nc.dram_tensor(name, list(shape), dt, kind="ExternalInput").ap()
        return dr[name]

    xT_d = din("xT", [D, S])
    memT_d = din("memT", [D, MEM_LEN])
    w_in_d = din("w_in", [DEPTH, D, IN_COLS])
    w_kv_d = din("w_mem_kv", [DEPTH, D, 512])
    w_br_d = {0: din("w_br_sb", [DEPTH, 512, D]), 1: din("w_br_dn", [DEPTH, 512, D]),
              2: din("w_br_ret", [DEPTH, 512, D]), 3: din("w_br_mem", [DEPTH, 256, D])}
    w_out_d = din("w_out", [DEPTH, D, D])
    normg_d = din("norm_g", [DEPTH, 128, 8])
    memg_d = din("mem_norm_g", [DEPTH, 128, 8])
    bgate_d = din("b_gate", [DEPTH, 128, 32])
    fing_d = din("final_norm_g", [128, 8])
    cst_d = {n: din("c_" + n, [128, 128]) for n in CONST_NAMES}
    dnc_d = din("c_dn", [128, 4, 128])
    dncw_d = din("dn_conv_w", [DEPTH, 128, 12, 4])
    dng_d = din("dn_norm_g", [128, DEPTH])
    dnal_d = din("dn_alog", [DEPTH, 128, 128])
    dndt_d = din("dn_dtb", [DEPTH, 128, 128])
    pos_d = din("pos", [128, S], I32)
    retg_d = din("ret_norm_g", [DEPTH, 128, 4])
    rtdt_d = din("c_rt_dt", [128, 4, 128])
    rtsd_d = din("c_rt_sd", [128, 4])
    rtcd_d = din("c_rt_cd", [128, 2, 128])
    rtrp_d = din("c_rt_rp", [128, 4])
    rtcn_d = din("c_rt_cn", [128, 128])
    outT_d = nc.dram_tensor("outT", [D, S], F32, kind="ExternalOutput").ap()
    dbg_d = {}
    if debug:
        dbg_d["hT"] = nc.dram_tensor("dbg_hT", [D, S], F32, kind="ExternalOutput").ap()
        dbg_d["omem"] = nc.dram_tensor("dbg_omem", [256, S], F32, kind="ExternalOutput").ap()
        for nm in ["osb", "odn", "ort"]:
            dbg_d[nm] = nc.dram_tensor("dbg_" + nm, [512, S], F32, kind="ExternalOutput").ap()

    with ExitStack() as st:
        P = Prog(nc, st)

        def sb(name, shape, dt):
            return st.enter_context(nc.sbuf_tensor(_u + "s_" + name, list(shape), dt))

        xT = sb("xT", [128, 8, S], F32)
        hT = sb("hT", [128, 8, S], BF16)
        mT = sb("mT", [128, 8, S], BF16)
        obT = sb("obT", [128, 4, S], BF16)
        wst = [sb(f"wst{i}", [128, 8, 128], F32) for i in range(2)]
        NWB = 5
        wbp = [sb(f"wb{i}", [128, 8, 128], BF16) for i in range(NWB)]
        cst = {n: sb("k_" + n, [128, 128], BF16) for n in CONST_NAMES}
        cstf = sb("cstf", [128, 128], F32)
        normg = sb("normg", [128, DEPTH, 8], F32)
        memg = sb("memg", [128, DEPTH, 8], F32)
        bgate = sb("bgate", [128, DEPTH, 32], F32)
        fing = sb("fing", [128, 8], F32)
        ps = [st.enter_context(nc.psum_tensor(f"ps{i}", [128, 512], F32)) for i in range(8)]
        PS = [("ps", i) for i in range(8)]

        wcount = [0]
        wbcount = [0]

        def load_w(src, n=128, rows=8, eng="pool"):
            i = wcount[0] % 2
            wcount[0] += 1
            j = wbcount[0] % NWB
            wbcount[0] += 1
            stg, wb = wst[i], wbp[j]
            P.dma(stg[:, 0:rows, 0:n], src.rearrange("(k p) n -> p k n", p=128),
                  writes=[("wst", i)], stream=f"w{i}")
            fn = lambda e, o=wb[:, 0:rows, 0:n], a=stg[:, 0:rows, 0:n]: e.tensor_copy(out=o, in_=a)
            P.op(eng, fn, reads=[("wst", i)], writes=[("wb", j)])
            return wb, ("wb", j)

        for n in CONST_NAMES:
            P.dma(cstf[:, :], cst_d[n][:, :], writes=["cstf"], stream="c")
            P.op("dve", lambda e, o=cst[n][:, :]: e.tensor_copy(out=o, in_=cstf[:, :]),
                 reads=["cstf"], writes=["k_" + n])
        for l in range(DEPTH):
            P.dma(normg[:, l, :], normg_d[l, :, :], writes=["normg"], stream="c")
            P.dma(memg[:, l, :], memg_d[l, :, :], writes=["memg"], stream="c")
            P.dma(bgate[:, l, :], bgate_d[l, :, :], writes=["bgate"], stream="c")
        P.dma(fing[:, :], fing_d[:, :], writes=["fing"], stream="c")
        for k in range(8):
            P.dma(xT[:, k, :], xT_d[k * 128:(k + 1) * 128, :], writes=[("xT", k)], stream="x")

        rtdt = sb("rtdt", [128, 4, 128], F32)
        rtsd = sb("rtsd", [128, 4], F32)
        rtcd = sb("rtcd", [128, 2, 128], F32)
        rtrp = sb("rtrp", [128, 4], F32)
        rtcn = sb("rtcn", [128, 128], BF16)
        retg = sb("retg", [128, DEPTH, 4], F32)
        P.dma(rtdt[:, :, :], rtdt_d[:, :, :], writes=["rtdt"], stream="c")
        P.dma(rtsd[:, :], rtsd_d[:, :], writes=["rtsd"], stream="c")
        P.dma(rtcd[:, :, :], rtcd_d[:, :, :], writes=["rtcd"], stream="c")
        P.dma(rtrp[:, :], rtrp_d[:, :], writes=["rtrp"], stream="c")
        P.dma(cstf[:, :], rtcn_d[:, :], writes=["cstf"], stream="c")
        P.op("dve", lambda e: e.tensor_copy(out=rtcn[:, :], in_=cstf[:, :]), reads=["cstf"], writes=["rtcn"])
        for l in range(DEPTH):
            P.dma(retg[:, l, :], retg_d[l, :, :], writes=["retg"], stream="c")
        dnc = sb("dnc", [128, 4, 128], F32)
        dncw = sb("dncw", [128, DEPTH, 12, 4], F32)
        dng = sb("dng", [128, DEPTH], F32)
        P.dma(dnc[:, :, :], dnc_d[:, :, :], writes=["dnc"], stream="c")
        P.dma(dng[:, :], dng_d[:, :], writes=["dng"], stream="c")
        for l in range(DEPTH):
            P.dma(dncw[:, l, :, :], dncw_d[l, :, :, :], writes=["dncw"], stream="c")
        def rms_stats(src_tile, src_keys, nk, ncols, c0, psum_i, sq_tile, sq_key, rstd_tile, rstd_key, extra):
            for k in range(nk):
                P.op("act", lambda e, k=k: e.activation(out=sq_tile[:, k % 2, 0:ncols], in_=src_tile[:, k, c0:c0 + ncols],
                                                        func=AF.Square),
                     reads=[src_keys[k]], writes=[(sq_key, k % 2)])
                P.op("pe", lambda e, k=k: e.matmul(ps[psum_i][:, 0:ncols], lhsT=cst["ones"][:, :],
                                                   rhs=sq_tile[:, k % 2, 0:ncols], start=(k == 0), stop=(k == nk - 1)),
                     reads=[(sq_key, k % 2), "k_ones"], writes=[PS[psum_i]])
            P.op("act", lambda e: e.activation(out=rstd_tile[:, 0:ncols], in_=ps[psum_i][:, 0:ncols], func=AF.Ln,
                                               bias=epsb[:, 0:1], scale=1.0),
                 reads=[PS[psum_i], "epsb"], writes=[rstd_key])
            P.op("act", lambda e: e.activation(out=rstd_tile[:, 0:ncols], in_=rstd_tile[:, 0:ncols], func=AF.Exp,
                                               scale=-0.5),
                 reads=[rstd_key], writes=[rstd_key])

        epsb = sb("epsb", [128, 1], F32)
        P.op("dve", lambda e: e.memset(epsb[:, :], float(D * EPS)), writes=["epsb"])
        sqt = sb("sqt", [128, 2, TC], BF16)
        rstd = sb("rstd", [128, TC], F32)
        g32 = sb("g32", [128, 8], F32)

        def norm_to(dst_fn, gsrc, layer_tag):
            P.op("dve", lambda e: e.tensor_scalar(out=g32[:, :], in0=gsrc, scalar1=float(math.sqrt(D)), scalar2=None,
                                                  op0=ALU.mult), reads=["normg", "fing"], writes=["g32"])
            for tc in range(NT):
                rms_stats(xT, [("xT", k) for k in range(8)], 8, TC, tc * TC, 0, sqt, "sqt", rstd, "rstd", None)
                for k in range(8):
                    ap, key = dst_fn(k, tc)
                    P.op("dve", lambda e, k=k, tc=tc, ap=ap: e.scalar_tensor_tensor(
                        out=ap, in0=xT[:, k, tc * TC:(tc + 1) * TC], scalar=g32[:, k:k + 1], in1=rstd[:, :],
                        op0=ALU.mult, op1=ALU.mult),
                        reads=[("xT", k), "g32", "rstd"], writes=[key])

        def proj_fm(l, col0, ncols, evac, wsrc=None):
            src = (w_in_d[l, :, col0:col0 + ncols] if wsrc is None else wsrc)
            wb, wkey = load_w(src, n=ncols)
            for tc in range(NT):
                pi = 1 + (tc % 2)
                for k in range(8):
                    P.op("pe", lambda e, k=k, tc=tc, pi=pi: e.matmul(
                        ps[pi][0:ncols, :], lhsT=wb[:, k, 0:ncols], rhs=hT[:, k, tc * TC:(tc + 1) * TC],
                        start=(k == 0), stop=(k == 7)),
                        reads=[wkey, ("hT", k)], writes=[PS[pi]], inc=(k == 7))
                evac(tc, ps[pi][0:ncols, :], PS[pi])

        def proj_tm(l, col0, ncols, evac, ntok=128):
            wb, wkey = load_w(w_in_d[l, :, col0:col0 + ncols], n=ncols)
            for tt in range(S // ntok):
                pi = 1 + (tt % 2)
                for k in range(8):
                    P.op("pe", lambda e: e.matmul(ps[pi][0:ntok, 0:ncols], lhsT=hT[:, k, tt * ntok:(tt + 1) * ntok],
                                                  rhs=wb[:, k, 0:ncols], start=(k == 0), stop=(k == 7)),
                         reads=[wkey, ("hT", k)], writes=[PS[pi]], inc=(k == 7))
                evac(tt, ps[pi][0:ntok, 0:ncols], PS[pi])

        oneb = sb("oneb", [128, 1], F32)
        P.op("dve", lambda e: e.memset(oneb[:, :], 1.0), writes=["oneb"])

        def run_streams(gens):
            gens = list(gens)
            while gens:
                for g in list(gens):
                    try:
                        next(g)
                    except StopIteration:
                        gens.remove(g)

        def sb_attention(l):
            for hp in range(4):
                with ExitStack() as ph:
                    def sbp(name, shape, dt):
                        return ph.enter_context(nc.sbuf_tensor(_u + ("s_" + name), list(shape), dt))
                    qT = sbp("sbq", [128, S], BF16)
                    kT = sbp("sbk", [128, S], BF16)
                    vtm = sbp("sbv", [128, 16, 128], BF16)
                    proj_fm(l, C_SBQ + hp * 128, 128, lambda tc, pap, pkey: P.op(
                        "act", lambda e: e.activation(out=qT[:, tc * TC:(tc + 1) * TC], in_=pap, func=AF.Copy, scale=0.125),
                        reads=[pkey], writes=["sbq"]))
                    proj_fm(l, C_SBK + hp * 128, 128, lambda tc, pap, pkey: P.op(
                        "dve", lambda e: e.tensor_copy(out=kT[:, tc * TC:(tc + 1) * TC], in_=pap),
                        reads=[pkey], writes=["sbk"]))
                    proj_tm(l, C_SBV + hp * 128, 128, lambda tt, pap, pkey: P.op(
                        "dve", lambda e: e.tensor_copy(out=vtm[:, tt, :], in_=pap),
                        reads=[pkey], writes=[("sbv", tt)]))
                    with ExitStack() as ph2:
                        def sbw(name, shape, dt):
                            return ph2.enter_context(nc.sbuf_tensor(_u + ("s_" + name), list(shape), dt))

                        import os as _os

                        def stream(sid, hh, qcs):
                            ez = sbw(f"ez{sid}", [128, TC], F32)
                            spb = sbw(f"spb{sid}", [128, TC], BF16)
                            eg = sbw(f"eg{sid}", [128, TC], BF16)
                            Gb = sbw(f"Gb{sid}", [128, TC], BF16)
                            wv = eg
                            K_wv = ("eg", sid)
                            K_ez, K_spb, K_eg, K_Gb = ("ez", sid), ("spb", sid), ("eg", sid), ("Gb", sid)
                            rs = slice(hh * 64, hh * 64 + 64)
                            pzg, po = 2 * sid, 2 * sid + 1
                            pgg = (4 + 2 * sid) if _os.environ.get('SB_SEPG') else pzg
                            yield
                            for qc in qcs:
                                q0 = qc * TC
                                kmax = qc * 4 + 3
                                pc0 = None
                                P.op("pool", lambda e: e.memset(wv[:, 0:384], 0.0), writes=[K_wv])
                                for kb in range(kmax, -1, -1):
                                    if kmax - kb >= int(_os.environ.get("SB_MAXIT", "99")):
                                        continue
                                    j = kb - qc * 4
                                    c0 = 128 * j if j >= 0 else 0
                                    cs = slice(c0, TC)
                                    tsl = slice(q0 + c0, q0 + TC)
                                    ksl = slice(kb * 128, (kb + 1) * 128)
                                    P.op("pe", lambda e: e.matmul(ps[pzg][:, cs], lhsT=kT[rs, ksl], rhs=qT[rs, tsl],
                                                                  start=True, stop=True),
                                         reads=["sbk", "sbq"], writes=[PS[pzg]])
                                    yield
                                    P.op("act", lambda e: e.activation(out=ez[:, cs], in_=ps[pzg][:, cs], func=AF.Exp),
                                         reads=[PS[pzg]], writes=[K_ez])
                                    yield
                                    P.op("act", lambda e: e.activation(out=spb[:, cs], in_=ez[:, cs], func=AF.Ln,
                                                                       bias=oneb[:, 0:1], scale=1.0),
                                         reads=[K_ez, "oneb"], writes=[K_spb])
                                    yield
                                    if j >= 0:
                                        P.op("dve", lambda e: e.tensor_tensor(out=spb[:, c0:c0 + 128], in0=spb[:, c0:c0 + 128],
                                                                              in1=cst["masksb"][:, :], op=ALU.mult),
                                             reads=[K_spb, "k_masksb"], writes=[K_spb])
                                    P.op("pe", lambda e: e.matmul(ps[pgg][:, cs], lhsT=cst["trineg"][:, :], rhs=spb[:, cs],
                                                                  start=True, stop=(pc0 is None)),
                                         reads=["k_trineg", K_spb], writes=[PS[pgg]], inc=(pc0 is None))
                                    if pc0 is not None:
                                        pcs = slice(pc0, TC)
                                        P.op("pe", lambda e: e.matmul(ps[pgg][:, pcs], lhsT=cst["e0"][:, :], rhs=Gb[:, pcs],
                                                                      start=False, stop=True),
                                             reads=["k_e0", K_Gb], writes=[PS[pgg]])
                                    yield
                                    P.op("act", lambda e: e.activation(out=eg[:, cs], in_=ps[pgg][:, cs], func=AF.Exp),
                                         reads=[PS[pgg]], writes=[K_eg])
                                    if kb > 0:
                                        P.op("dve", lambda e: e.tensor_copy(out=Gb[:, cs], in_=ps[pgg][:, cs]),
                                             reads=[PS[pgg], K_eg], writes=[K_Gb])
                                    yield
                                    P.op("dve", lambda e: e.tensor_tensor(out=wv[:, cs], in0=ez[:, cs], in1=eg[:, cs], op=ALU.mult),
                                         reads=[K_ez, K_eg], writes=[K_wv])
                                    if j >= 0:
                                        P.op("dve", lambda e: e.tensor_tensor(out=wv[:, c0:c0 + 128], in0=wv[:, c0:c0 + 128],
                                                                              in1=cst["masksb"][:, :], op=ALU.mult),
                                             reads=[K_wv, "k_masksb"], writes=[K_wv])
                                    yield
                                    P.op("pe", lambda e: e.matmul(ps[po][rs, :], lhsT=vtm[:, kb, rs], rhs=wv[:, :],
                                                                  start=(kb == kmax), stop=(kb == 0)),
                                         reads=[("sbv", kb), K_wv], writes=[PS[po]])
                                    pc0 = c0
                                    yield
                                P.op("dve", lambda e: e.tensor_copy(out=obT[rs, hp, q0:q0 + TC], in_=ps[po][rs, :]),
                                     reads=[PS[po]], writes=[("obT", hp, hh, qc)])
                                yield

                        _mode = _os.environ.get("SB_MODE", "4")
                        if _mode == "1":
                            run_streams([stream(0, 0, [3, 0, 2, 1])])
                            run_streams([stream(1, 1, [3, 0, 2, 1])])
                        elif _mode == "2":
                            run_streams([stream(0, 0, [3, 0, 2, 1]), stream(1, 1, [3, 0, 2, 1])])
                        else:
                            run_streams([stream(0, 0, [3, 0]), stream(1, 1, [3, 0]), stream(2, 0, [2, 1]), stream(3, 1, [2, 1])])
                        P.barrier()
                    with ExitStack() as ph3:
                        zs = [ph3.enter_context(nc.sbuf_tensor(_u + f"s_sbzs{i}", [128, TC], BF16)) for i in range(2)]

                        def evz(tc, pap, pkey):
                            b = tc % 2
                            P.op("act", lambda e: e.activation(out=zs[b][:, :], in_=pap, func=AF.Silu),
                                 reads=[pkey], writes=[("sbzs", b)])
                            tsl = slice(tc * TC, (tc + 1) * TC)
                            P.op("dve", lambda e: e.tensor_tensor(out=obT[:, hp, tsl], in0=obT[:, hp, tsl], in1=zs[b][:, :], op=ALU.mult),
                                 reads=[("sbzs", b)], writes=[("obT", hp)])
                        proj_fm(l, C_SBZ + hp * 128, 128, evz)
                        P.barrier()

        def retention(l):
            TWO_PI = 2.0 * math.pi
            C1 = 6.28125
            C2 = TWO_PI - C1
            with ExitStack() as ph0:
                def sb0(name, shape, dt):
                    return ph0.enter_context(nc.sbuf_tensor(_u + ("s_" + name), list(shape), dt))
                qrT = sb0("rqr", [128, 2, S], BF16)
                krT = sb0("rkr", [128, 2, S], BF16)
                with ExitStack() as ph1:
                    def sb1(name, shape, dt):
                        return ph1.enter_context(nc.sbuf_tensor(_u + ("s_" + name), list(shape), dt))
                    COS2 = sb1("rcos", [128, S], BF16)
                    SIN2 = sb1("rsin", [128, S], BF16)
                    pint = sb1("rpint", [128, TC], I32)
                    ta = sb1("rta", [128, TC], F32)
                    tk = sb1("rtk", [128, TC], F32)
                    tm_full = sb1("rtm", [128, TC], F32)
                    tm = tm_full[:, 0:256]
                    ki = tm_full[:, 256:512].bitcast(I32)
                    ta_f, tk_f, pint_f = ta, tk, pint
                    for tc in range(8):
                        tsl = slice(tc * 256, (tc + 1) * 256)
                        ta, tk, pint = ta_f[:, 0:256], tk_f[:, 0:256], pint_f[:, 0:256]
                        P.dma(pint, pos_d[:, tsl], writes=["rpint"], stream="x")
                        P.op("dve", lambda e: e.tensor_copy(out=ta, in_=pint), reads=["rpint"], writes=["rta"])
                        P.op("dve", lambda e: e.tensor_scalar(out=ta, in0=ta, scalar1=rtrp[:, 0:1], scalar2=None,
                                                              op0=ALU.mult), reads=["rta", "rtrp"], writes=["rta"])
                        P.op("dve", lambda e: e.tensor_scalar(out=ki, in0=ta, scalar1=float(1.0 / TWO_PI),
                                                              scalar2=None, op0=ALU.mult), reads=["rta"], writes=["rki"])
                        P.op("dve", lambda e: e.tensor_copy(out=tk, in_=ki), reads=["rki"], writes=["rtk"])
                        P.op("dve", lambda e: e.scalar_tensor_tensor(out=ta, in0=tk, scalar=-C1, in1=ta,
                                                                     op0=ALU.mult, op1=ALU.add),
                             reads=["rtk", "rta"], writes=["rta"])
                        P.op("dve", lambda e: e.scalar_tensor_tensor(out=ta, in0=tk, scalar=-C2, in1=ta,
                                                                     op0=ALU.mult, op1=ALU.add),
                             reads=["rtk", "rta"], writes=["rta"])
                        P.op("dve", lambda e: e.tensor_single_scalar(out=tm, in_=ta, scalar=float(math.pi),
                                                                     op=ALU.is_gt), reads=["rta"], writes=["rtm"])
                        P.op("dve", lambda e: e.scalar_tensor_tensor(out=ta, in0=tm, scalar=-TWO_PI, in1=ta,
                                                                     op0=ALU.mult, op1=ALU.add),
                             reads=["rtm", "rta"], writes=["rta"])
                        P.op("dve", lambda e: e.tensor_single_scalar(out=tm, in_=ta, scalar=float(-math.pi),
                                                                     op=ALU.is_lt), reads=["rta"], writes=["rtm"])
                        P.op("dve", lambda e: e.scalar_tensor_tensor(out=ta, in0=tm, scalar=TWO_PI, in1=ta,
                                                                     op0=ALU.mult, op1=ALU.add),
                             reads=["rtm", "rta"], writes=["rta"])
                        P.op("act", lambda e: e.activation(out=tk, in_=ta, func=AF.Sin),
                             reads=["rta"], writes=["rtk"])
                        P.op("dve", lambda e: e.tensor_scalar(out=SIN2[:, tsl], in0=tk, scalar1=rtrp[:, 1:2], scalar2=None,
                                                              op0=ALU.mult), reads=["rtk", "rtrp"], writes=["rsin"])
                        P.op("dve", lambda e: e.tensor_single_scalar(out=tm, in_=ta, scalar=float(math.pi / 2),
                                                                     op=ALU.is_gt), reads=["rta"], writes=["rtm"])
                        P.op("dve", lambda e: e.scalar_tensor_tensor(out=ta, in0=tm, scalar=-TWO_PI, in1=ta,
                                                                     op0=ALU.mult, op1=ALU.add),
                             reads=["rtm", "rta"], writes=["rta"])
                        P.op("act", lambda e: e.activation(out=COS2[:, tsl], in_=ta, func=AF.Sin, bias=rtrp[:, 2:3],
                                                           scale=1.0), reads=["rta", "rtrp"], writes=["rcos"])
                    ta, tk, pint = ta_f, tk_f, pint_f
                    for which, (c_base, dstT, scl) in enumerate([(C_RTQ, qrT, 1.0), (C_RTK, krT, 0.125)]):
                        for hp in range(2):
                            wb, wkey = load_w(w_in_d[l, :, c_base + hp * 128:c_base + (hp + 1) * 128])
                            jsw = wbcount[0] % NWB
                            wbcount[0] += 1
                            wsw = wbp[jsw]
                            for hh in range(2):
                                o = hh * 64
                                P.op("pool", lambda e: e.tensor_copy(out=wsw[:, :, o:o + 32], in_=wb[:, :, o + 32:o + 64]),
                                     reads=[wkey], writes=[("wb", jsw)])
                                P.op("pool", lambda e: e.tensor_copy(out=wsw[:, :, o + 32:o + 64], in_=wb[:, :, o:o + 32]),
                                     reads=[wkey], writes=[("wb", jsw)])
                            for tc in range(NT):
                                tsl = slice(tc * TC, (tc + 1) * TC)
                                for k in range(8):
                                    P.op("pe", lambda e: e.matmul(ps[1][:, :], lhsT=wb[:, k, :], rhs=hT[:, k, tsl],
                                                                  start=(k == 0), stop=(k == 7)),
                                         reads=[wkey, ("hT", k)], writes=[PS[1]], inc=(k == 7))
                                for k in range(8):
                                    P.op("pe", lambda e: e.matmul(ps[2][:, :], lhsT=wsw[:, k, :], rhs=hT[:, k, tsl],
                                                                  start=(k == 0), stop=(k == 7)),
                                         reads=[("wb", jsw), ("hT", k)], writes=[PS[2]], inc=(k == 7))
                                P.op("dve", lambda e: e.scalar_tensor_tensor(out=ta[:, :], in0=ps[1][:, :], scalar=float(scl),
                                                                             in1=COS2[:, tsl], op0=ALU.mult, op1=ALU.mult),
                                     reads=[PS[1], "rcos"], writes=["rta"])
                                P.op("dve", lambda e: e.scalar_tensor_tensor(out=tk[:, :], in0=ps[2][:, :], scalar=float(scl),
                                                                             in1=SIN2[:, tsl], op0=ALU.mult, op1=ALU.mult),
                                     reads=[PS[2], "rsin"], writes=["rtk"])
                                P.op("dve", lambda e: e.tensor_tensor(out=dstT[:, hp, tsl], in0=ta[:, :], in1=tk[:, :], op=ALU.add),
                                     reads=["rta", "rtk"], writes=[("rq", which, hp)])
                    P.barrier()
                for hp in range(2):
                    with ExitStack() as ph2:
                        def sb2(name, shape, dt):
                            return ph2.enter_context(nc.sbuf_tensor(_u + ("s_" + name), list(shape), dt))
                        kdtm = sb2("rkd", [128, 16, 128], BF16)
                        vtm = sb2("rv", [128, 16, 128], BF16)
                        zsT = sb2("rz", [128, S], BF16)
                        qc = [sb2(f"rqc{i}", [128, 128], BF16) for i in range(2)]
                        scm = [sb2(f"rscm{i}", [128, 128], BF16) for i in range(2)]
                        Sf = sb2("rSf", [128, 128], F32)
                        Sb = sb2("rSb", [128, 128], BF16)
                        ob = sb2("rob", [128, TC], BF16)
                        sq = sb2("rsq", [128, TC], BF16)
                        rs_t = rstd
                        tt_t = sb2("rtt", [128, TC], F32)
                        for n in range(16):
                            nsl = slice(n * 128, (n + 1) * 128)
                            pk = ps[3][:, 0:64].bitcast(BF16)
                            P.op("pe", lambda e: e.transpose(pk, krT[:, hp, nsl], cst["ident"][:, :]),
                                 reads=[("rq", 1, hp), "k_ident"], writes=[PS[3]])
                            for hh in range(2):
                                cs = slice(hh * 64, (hh + 1) * 64)
                                h = 2 * hp + hh
                                P.op("dve", lambda e: e.tensor_scalar(out=kdtm[:, n, cs], in0=pk[:, cs], scalar1=rtsd[:, h:h + 1],
                                                                      scalar2=None, op0=ALU.mult),
                                     reads=[PS[3], "rtsd"], writes=[("rkd", n)])
                        for hh in range(2):
                            h = 2 * hp + hh
                            rs = slice(hh * 64, (hh + 1) * 64)
                            proj_tm(l, C_RTV + h * 128, 128, lambda tt, pap, pkey: P.op(
                                "dve", lambda e: e.tensor_copy(out=vtm[:, tt, :], in_=pap), reads=[pkey], writes=[("rv", tt)]))
                            proj_fm(l, C_RTZ + h * 128, 128, lambda tc, pap, pkey: P.op(
                                "act", lambda e: e.activation(out=zsT[:, tc * TC:(tc + 1) * TC], in_=pap, func=AF.Silu),
                                reads=[pkey], writes=["rz"]))
                            for n in range(16):
                                nsl = slice(n * 128, (n + 1) * 128)
                                b = n % 2
                                csl = slice((n % 4) * 128, (n % 4 + 1) * 128)
                                po = 5 + (n // 4) % 2
                                P.op("pe", lambda e: e.matmul(ps[4][:, 0:128], lhsT=krT[rs, hp, nsl], rhs=qrT[rs, hp, nsl],
                                                              start=True, stop=True),
                                     reads=[("rq", 1, hp), ("rq", 0, hp)], writes=[PS[4]])
                                P.op("dve", lambda e: e.tensor_tensor(out=scm[b][:, :], in0=ps[4][:, 0:128], in1=rtdt[:, h, :],
                                                                      op=ALU.mult),
                                     reads=[PS[4], "rtdt"], writes=[("rscm", b)])
                                if n > 0:
                                    P.op("dve", lambda e: e.tensor_tensor(out=qc[b][rs, :], in0=qrT[rs, hp, nsl], in1=rtcd[rs, hp, :],
                                                                          op=ALU.mult),
                                         reads=[("rq", 0, hp), "rtcd"], writes=[("rqc", b)])
                                P.op("pe", lambda e: e.matmul(ps[po][:, csl], lhsT=vtm[:, n, :], rhs=scm[b][:, :],
                                                              start=True, stop=(n == 0)),
                                     reads=[("rv", n), ("rscm", b)], writes=[PS[po]], inc=(n == 0))
                                if n > 0:
                                    P.op("pe", lambda e: e.matmul(ps[po][:, csl], lhsT=Sb[rs, :], rhs=qc[b][rs, :],
                                                                  start=False, stop=True),
                                         reads=["rSb", ("rqc", b)], writes=[PS[po]])
                                if n < 15:
                                    P.op("pe", lambda e: e.matmul(ps[7][rs, 0:128], lhsT=kdtm[:, n, rs], rhs=vtm[:, n, :],
                                                                  start=True, stop=True),
                                         reads=[("rkd", n), ("rv", n)], writes=[PS[7]])
                                    gch = float((1.0 - 2.0 ** (-5.0 - h)) ** 128)
                                    if n == 0:
                                        P.op("dve", lambda e: e.tensor_copy(out=Sf[rs, :], in_=ps[7][rs, 0:128]),
                                             reads=[PS[7]], writes=["rSf"])
                                    else:
                                        P.op("dve", lambda e: e.scalar_tensor_tensor(out=Sf[rs, :], in0=Sf[rs, :], scalar=gch,
                                                                                     in1=ps[7][rs, 0:128], op0=ALU.mult,
                                                                                     op1=ALU.add),
                                             reads=[PS[7], "rSf"], writes=["rSf"])
                                    P.op("act", lambda e: e.activation(out=Sb[rs, :], in_=Sf[rs, :], func=AF.Copy),
                                         reads=["rSf"], writes=["rSb"])
                                if n % 4 == 3:
                                    tc = n // 4
                                    tsl = slice(tc * TC, (tc + 1) * TC)
                                    P.op("act", lambda e: e.activation(out=ob[:, :], in_=ps[po][:, :], func=AF.Copy),
                                         reads=[PS[po]], writes=["rob"])
                                    P.op("pe", lambda e: e.matmul(ps[1][:, :], lhsT=rtcn[:, :], rhs=ob[:, :], start=True, stop=True),
                                         reads=["rtcn", "rob"], writes=[PS[1]])
                                    P.op("act", lambda e: e.activation(out=sq[:, :], in_=ps[1][:, :], func=AF.Square),
                                         reads=[PS[1]], writes=["rsq"])
                                    P.op("pe", lambda e: e.matmul(ps[2][:, :], lhsT=cst["ones"][:, :], rhs=sq[:, :],
                                                                  start=True, stop=True),
                                         reads=["k_ones", "rsq"], writes=[PS[2]])
                                    P.op("act", lambda e: e.activation(out=rs_t[:, :], in_=ps[2][:, :], func=AF.Ln,
                                                                       bias=rtrp[:, 3:4], scale=float(1.0 / 128.0)),
                                         reads=[PS[2], "rtrp"], writes=["rstd"])
                                    P.op("act", lambda e: e.activation(out=rs_t[:, :], in_=rs_t[:, :], func=AF.Exp, scale=-0.5),
                                         reads=["rstd"], writes=["rstd"])
                                    P.op("dve", lambda e: e.scalar_tensor_tensor(out=tt_t[:, :], in0=ps[1][:, :],
                                                                                 scalar=retg[:, l, h:h + 1], in1=rs_t[:, :],
                                                                                 op0=ALU.mult, op1=ALU.mult),
                                         reads=[PS[1], "retg", "rstd"], writes=["rtt"])
                                    P.op("dve", lambda e: e.tensor_tensor(out=obT[:, h, tsl], in0=tt_t[:, :], in1=zsT[:, tsl],
                                                                          op=ALU.mult),
                                         reads=["rtt", "rz"], writes=[("obT", h)])
                        P.barrier()

        def deltanet(l):
            identf, onesf = dnc[:, 0, :], dnc[:, 1, :]
            mincl, mstrict = dnc[0:64, 2, 0:64], dnc[0:64, 3, 0:64]
            H = slice(0, 64)
            with ExitStack() as ph0:
                def sb0(name, shape, dt):
                    return ph0.enter_context(nc.sbuf_tensor(_u + ("s_" + name), list(shape), dt))
                abraw = sb0("dab", [64, 32, 8], F32)
                rep = sb0("drep", [64, 128], F32)
                t1 = sb0("dt1", [64, 32, 4], F32)
                g_tm = sb0("dg", [64, 32, 4], F32)
                beta_tm = sb0("dbeta", [64, 32, 4], F32)
                gc_tm = sb0("dgc", [64, 32, 4], F32)
                egl = sb0("degl", [128, 32, 4], F32)
                bg_tm = sb0("dbg", [64, 32, 4], F32)
                ed_tm = sb0("ded", [64, 32, 4], F32)
                proj_tm(l, C_DNA, 8, lambda tt, pap, pkey: P.op(
                    "dve", lambda e: e.tensor_copy(out=abraw[:, tt, :], in_=pap), reads=[pkey], writes=["dab"]), ntok=64)
                fl = lambda t: t[:, :, :].rearrange("p a b -> p (a b)")
                P.dma(rep[:, :], dndt_d[l, 0:64, :], writes=["drep"], stream="c")
                P.op("dve", lambda e: e.tensor_tensor(out=t1[:, :, :], in0=abraw[:, :, 0:4],
                                                      in1=rep[:, :].rearrange("p (a b) -> p a b", b=4), op=ALU.add),
                     reads=["dab", "drep"], writes=["dt1"])
                P.op("act", lambda e: e.activation(out=t1[:, :, :], in_=t1[:, :, :], func=AF.Exp), reads=["dt1"], writes=["dt1"])
                P.op("act", lambda e: e.activation(out=t1[:, :, :], in_=t1[:, :, :], func=AF.Ln, bias=oneb[0:64, 0:1], scale=1.0),
                     reads=["dt1", "oneb"], writes=["dt1"])
                P.dma(rep[:, :], dnal_d[l, 0:64, :], writes=["drep"], stream="c")
                P.op("act", lambda e: e.activation(out=rep[:, :], in_=rep[:, :], func=AF.Exp), reads=["drep"], writes=["drep"])
                P.op("dve", lambda e: e.scalar_tensor_tensor(out=g_tm[:, :, :], in0=t1[:, :, :], scalar=-1.0,
                                                             in1=rep[:, :].rearrange("p (a b) -> p a b", b=4),
                                                             op0=ALU.mult, op1=ALU.mult),
                     reads=["dt1", "drep"], writes=["dg"])
                P.op("act", lambda e: e.activation(out=beta_tm[:, :, :], in_=abraw[:, :, 4:8], func=AF.Sigmoid),
                     reads=["dab"], writes=["dbeta"])
                P.op("pe", lambda e: e.matmul(ps[0][0:64, 0:128], lhsT=dnc[0:64, 2, 0:64], rhs=fl(g_tm), start=True, stop=True),
                     reads=["dnc", "dg"], writes=[PS[0]])
                P.op("dve", lambda e: e.tensor_copy(out=fl(gc_tm), in_=ps[0][0:64, 0:128]), reads=[PS[0]], writes=["dgc"])
                P.op("pe", lambda e: e.matmul(ps[0][:, 128:256], lhsT=dnc[0:64, 1, :], rhs=fl(g_tm), start=True, stop=True),
                     reads=["dnc", "dg"], writes=[PS[0]])
                P.op("dve", lambda e: e.tensor_tensor(out=fl(ed_tm), in0=ps[0][0:64, 128:256], in1=fl(gc_tm), op=ALU.subtract),
                     reads=[PS[0], "dgc"], writes=["ded"])
                P.op("act", lambda e: e.activation(out=fl(ed_tm), in_=fl(ed_tm), func=AF.Exp), reads=["ded"], writes=["ded"])
                P.op("act", lambda e: e.activation(out=fl(egl), in_=ps[0][:, 128:256], func=AF.Exp), reads=[PS[0]], writes=["degl"])
                P.op("act", lambda e: e.activation(out=fl(bg_tm), in_=fl(gc_tm), func=AF.Exp), reads=["dgc"], writes=["dbg"])
                P.op("dve", lambda e: e.tensor_tensor(out=fl(bg_tm), in0=fl(bg_tm), in1=fl(beta_tm), op=ALU.mult),
                     reads=["dbg", "dbeta"], writes=["dbg"])
                P.barrier()
                for h in range(4):
                    with ExitStack() as ph1:
                        def sb1(name, shape, dt):
                            return ph1.enter_context(nc.sbuf_tensor(_u + ("s_" + name), list(shape), dt))
                        qT, kT, vT, zsT = mT[:, 0, :], mT[:, 1, :], mT[:, 2, :], mT[:, 3, :]
                        xpad = mT[:, 4:7, :].rearrange("p a b -> p (a b)").bitcast(F32)[:, 0:S + 3]
                        with ExitStack() as ph2:
                            acc = ph2.enter_context(nc.sbuf_tensor(_u + "s_dacc", [128, S], F32))
                            P.op("dve", lambda e: e.memset(xpad[:, 0:3], 0.0), writes=["dxpad0"])
                            for xi, (c_base, dstT) in enumerate([(C_DNQ, qT), (C_DNK, kT), (C_DNV, vT)]):
                                proj_fm(l, c_base + h * 128, 128, lambda tc, pap, pkey: P.op(
                                    "act", lambda e: e.activation(out=xpad[:, 3 + tc * TC:3 + (tc + 1) * TC], in_=pap, func=AF.Copy),
                                    reads=[pkey], writes=[("dxpad", tc)]))
                                allx = [("dxpad", tc) for tc in range(NT)] + ["dxpad0"]
                                cw = dncw[:, l, xi * 4 + h, :]
                                P.op("dve", lambda e: e.tensor_scalar(out=acc[:, :], in0=xpad[:, 3:3 + S], scalar1=cw[:, 3:4],
                                                                      scalar2=None, op0=ALU.mult),
                                     reads=allx + ["dncw"], writes=["dacc"])
                                for j in range(3):
                                    P.op("dve", lambda e: e.scalar_tensor_tensor(out=acc[:, :], in0=xpad[:, j:j + S],
                                                                                 scalar=cw[:, j:j + 1], in1=acc[:, :],
                                                                                 op0=ALU.mult, op1=ALU.add),
                                         reads=allx + ["dncw", "dacc"], writes=["dacc"])
                                P.op("act", lambda e: e.activation(out=acc[:, :], in_=acc[:, :], func=AF.Silu),
                                     reads=["dacc"], writes=["dacc"])
                                if xi == 2:
                                    P.op("dve", lambda e: e.tensor_copy(out=vT, in_=acc[:, :]), reads=["dacc"], writes=["dvT"])
                                else:
                                    for tc in range(NT):
                                        tsl = slice(tc * TC, (tc + 1) * TC)
                                        P.op("act", lambda e: e.activation(out=sqt[:, 0, :], in_=acc[:, tsl], func=AF.Square),
                                             reads=["dacc"], writes=[("sqt", 0)])
                                        P.op("pe", lambda e: e.matmul(ps[0][:, :], lhsT=cst["ones"][:, :], rhs=sqt[:, 0, :],
                                                                      start=True, stop=True),
                                             reads=[("sqt", 0), "k_ones"], writes=[PS[0]])
                                        P.op("act", lambda e: e.activation(out=rstd[:, :], in_=ps[0][:, :], func=AF.Ln,
                                                                           bias=rtrp[:, 3:4], scale=1.0),
                                             reads=[PS[0], "rtrp"], writes=["rstd"])
                                        P.op("act", lambda e: e.activation(out=rstd[:, :], in_=rstd[:, :], func=AF.Exp, scale=-0.5),
                                             reads=["rstd"], writes=["rstd"])
                                        sc = float(128.0 ** -0.5) if xi == 0 else 1.0
                                        P.op("dve", lambda e: e.scalar_tensor_tensor(out=dstT[:, tsl], in0=acc[:, tsl], scalar=sc,
                                                                                     in1=rstd[:, :], op0=ALU.mult, op1=ALU.mult),
                                             reads=["dacc", "rstd"], writes=["dqT" if xi == 0 else "dkT"])
                            P.barrier()
                        proj_fm(l, C_DNZ + h * 128, 128, lambda tc, pap, pkey: P.op(
                            "act", lambda e: e.activation(out=zsT[:, tc * TC:(tc + 1) * TC], in_=pap, func=AF.Silu),
                            reads=[pkey], writes=["dz"]))
                        NB = 8
                        B3 = [64, NB, 64]
                        gb = sb1("dgb", [64, NB, 64], F32)
                        dd = sb1("ddd", [64, NB, 64], F32)
                        LT = sb1("dLT", [64, NB, 64], F32)
                        egcb = sb1("degcb", [128, NB * 64], F32)
                        bs = sb1("dbs", [64, NB, 64], BF16)
                        Pm = sb1("dPm", [64, NB, 64], BF16)
                        PTm = sb1("dPTm", [64, NB, 64], BF16)
                        Tt = [sb1(f"dTt{i}", [64, NB, 64], BF16) for i in range(2)]
                        kbg = mT[0:64, 7, 0:1024].rearrange("p (a b) -> p a b", b=128)
                        vb = mT[0:64, 7, 1024:2048].rearrange("p (a b) -> p a b", b=128)
                        aT = sb1("daT", [64, NB, 64], BF16)
                        aTt = sb1("daTt", [64, NB, 64], BF16)
                        qg = sb1("dqg", [128, NB * 64], BF16)
                        u_sb = sb1("du", [64, NB, 128], BF16)
                        wT = sb1("dwT", [128, NB, 64], BF16)
                        kd = sb1("dkd", [64, NB, 128], BF16)
                        vnew = [sb1(f"dvnew{i}", [64, 128], BF16) for i in range(2)]
                        Sf = sb1("dSf", [128, 128], F32)
                        Sb = sb1("dSb", [128, 128], BF16)
                        nsq = sb1("dnsq", [128, TC], BF16)
                        ntmp = sb1("dntmp", [128, TC], F32)
                        identb = cst["ident"]
                        id64 = identb[0:64, 0:64]
                        fl3 = lambda t: t[:, :, :].rearrange("p a b -> p (a b)")
                        p3 = lambda i, w=64: ps[i][H, 0:NB * w].rearrange("p (a b) -> p a b", b=w)
                        bcast = lambda ap2: ap2.unsqueeze(1).broadcast_to(B3)

                        def stage_a(bt):
                            n0 = bt * NB
                            bsl = slice(n0 * 64, (n0 + NB) * 64)
                            nsl = slice(n0, n0 + NB)
                            sc_g = g_tm[:, nsl, h:h + 1].broadcast_to(B3)
                            sc_b = beta_tm[:, nsl, h:h + 1].broadcast_to(B3)
                            sc_gc = gc_tm[:, nsl, h:h + 1].broadcast_to(B3)
                            P.op("dve", lambda e: e.tensor_tensor(out=gb[:, :, :], in0=bcast(mincl), in1=sc_g, op=ALU.mult),
                                 reads=["dnc", "dg"], writes=["dgb"])
                            P.op("pe", lambda e: e.matmul(ps[0][:, :], lhsT=onesf[0:64, :], rhs=fl3(gb), start=True, stop=True),
                                 reads=["dnc", "dgb"], writes=[PS[0]])
                            yield
                            P.op("dve", lambda e: e.tensor_tensor(out=gb[:, :, :], in0=bcast(identf[0:64, 0:64]), in1=sc_b, op=ALU.mult),
                                 reads=["dnc", "dbeta"], writes=["dgb"])
                            P.op("pe", lambda e: e.matmul(ps[1][H, :], lhsT=onesf[0:64, 0:64], rhs=fl3(gb), start=True, stop=True),
                                 reads=["dnc", "dgb"], writes=[PS[1]])
                            yield
                            P.op("act", lambda e: e.activation(out=egcb[:, :], in_=ps[0][:, :], func=AF.Exp),
                                 reads=[PS[0]], writes=["degcb"])
                            P.op("dve", lambda e: e.tensor_tensor(out=dd[:, :, :], in0=p3(0), in1=sc_gc, op=ALU.subtract),
                                 reads=[PS[0], "dgc", "degcb"], writes=["ddd"])
                            yield
                            P.op("act", lambda e: e.activation(out=fl3(dd), in_=fl3(dd), func=AF.Exp), reads=["ddd"], writes=["ddd"])
                            P.op("dve", lambda e: e.tensor_tensor(out=bs[:, :, :], in0=p3(1), in1=bcast(mstrict), op=ALU.mult),
                                 reads=[PS[1], "dnc"], writes=["dbs"])
                            yield
                            P.op("dve", lambda e: e.scalar_tensor_tensor(out=dd[:, :, :], in0=dd[:, :, :], scalar=1.0, in1=bcast(mincl),
                                                                         op0=ALU.min, op1=ALU.mult),
                                 reads=["ddd", "dnc"], writes=["ddd"])
                            for i in range(NB):
                                csl = slice((n0 + i) * 64, (n0 + i + 1) * 64)
                                P.op("pe", lambda e: e.matmul(ps[2][H, i * 64:(i + 1) * 64], lhsT=kT[:, csl], rhs=kT[:, csl], start=True, stop=True),
                                     reads=["dkT"], writes=[PS[2]], inc=(i == NB - 1))
                            for i in range(NB):
                                csl = slice((n0 + i) * 64, (n0 + i + 1) * 64)
                                P.op("pe", lambda e: e.matmul(ps[3][H, i * 64:(i + 1) * 64], lhsT=kT[:, csl], rhs=qT[:, csl], start=True, stop=True),
                                     reads=["dkT", "dqT"], writes=[PS[3]], inc=(i == NB - 1))
                            yield
                            P.op("dve", lambda e: e.tensor_tensor(out=LT[:, :, :], in0=p3(2), in1=dd[:, :, :], op=ALU.mult),
                                 reads=[PS[2], "ddd"], writes=["dLT"])
                            P.op("dve", lambda e: e.tensor_tensor(out=aTt[:, :, :], in0=p3(3), in1=dd[:, :, :], op=ALU.mult),
                                 reads=[PS[3], "ddd"], writes=["daTt"])
                            yield
                            P.op("dve", lambda e: e.scalar_tensor_tensor(out=Pm[:, :, :], in0=LT[:, :, :], scalar=-1.0, in1=bs[:, :, :],
                                                                         op0=ALU.mult, op1=ALU.mult),
                                 reads=["dLT", "dbs"], writes=["dPm"])
                            yield
                            ptv = ps[4][H, 0:NB * 32].bitcast(BF16)
                            for i in range(NB):
                                P.op("pe", lambda e: e.transpose(ptv[:, i * 64:(i + 1) * 64], Pm[:, i, :], id64),
                                     reads=["dPm", "k_ident"], writes=[PS[4]], inc=(i == NB - 1))
                            P.op("dve", lambda e: e.tensor_tensor(out=Tt[0][:, :, :], in0=Pm[:, :, :], in1=bcast(id64), op=ALU.add),
                                 reads=["dPm", "k_ident"], writes=[("dTt", 0)])
                            yield
                            P.op("act", lambda e: e.activation(out=fl3(PTm), in_=ptv, func=AF.Copy), reads=[PS[4]], writes=["dPTm"])
                            yield
                            cur = 0
                            for lev in range(1, 6):
                                if lev < 5:
                                    for i in range(NB):
                                        P.op("pe", lambda e: e.matmul(ps[2][H, i * 64:(i + 1) * 64], lhsT=PTm[:, i, :], rhs=Pm[:, i, :],
                                                                      start=True, stop=True),
                                             reads=["dPm", "dPTm"], writes=[PS[2]], inc=(i == NB - 1))
                                for i in range(NB):
                                    P.op("pe", lambda e: e.matmul(ps[3][H, i * 64:(i + 1) * 64], lhsT=Pm[:, i, :], rhs=PTm[:, i, :],
                                                                  start=True, stop=True),
                                         reads=["dPm", "dPTm"], writes=[PS[3]], inc=(i == NB - 1))
                                yield
                                if lev < 5:
                                    P.op("act", lambda e: e.activation(out=fl3(Pm), in_=ps[2][H, :], func=AF.Copy), reads=[PS[2]], writes=["dPm"])
                                P.op("dve", lambda e: e.tensor_copy(out=fl3(PTm), in_=ps[3][H, :]), reads=[PS[3]], writes=["dPTm"])
                                yield
                                for i in range(NB):
                                    P.op("pe", lambda e: e.matmul(ps[4][H, i * 64:(i + 1) * 64], lhsT=PTm[:, i, :], rhs=Tt[cur][:, i, :],
                                                                  start=True, stop=False),
                                         reads=["dPTm", ("dTt", cur)], writes=[PS[4]], inc=False)
                                    P.op("pe", lambda e: e.matmul(ps[4][H, i * 64:(i + 1) * 64], lhsT=id64, rhs=Tt[cur][:, i, :],
                                                                  start=False, stop=True),
                                         reads=["k_ident", ("dTt", cur)], writes=[PS[4]], inc=(i == NB - 1))
                                yield
                                cur = 1 - cur
                                P.op("dve", lambda e: e.tensor_copy(out=fl3(Tt[cur]), in_=ps[4][H, :]), reads=[PS[4]], writes=[("dTt", cur)])
                                yield
                            kv = ps[0][H, :].bitcast(BF16)
                            vv = ps[1][H, :].bitcast(BF16)
                            for i in range(NB):
                                csl = slice((n0 + i) * 64, (n0 + i + 1) * 64)
                                P.op("pe", lambda e: e.transpose(kv[:, i * 128:(i + 1) * 128], kT[:, csl], identb[:, :]),
                                     reads=["dkT", "k_ident"], writes=[PS[0]], inc=(i == NB - 1))
                            for i in range(NB):
                                csl = slice((n0 + i) * 64, (n0 + i + 1) * 64)
                                P.op("pe", lambda e: e.transpose(vv[:, i * 128:(i + 1) * 128], vT[:, csl], identb[:, :]),
                                     reads=["dvT", "k_ident"], writes=[PS[1]], inc=(i == NB - 1))
                            yield
                            B3w = [64, NB, 128]
                            kv3 = kv.rearrange("p (a b) -> p a b", b=128)
                            vv3 = vv.rearrange("p (a b) -> p a b", b=128)
                            P.op("dve", lambda e: e.tensor_tensor(out=kbg[:, :, :], in0=kv3, in1=bg_tm[:, nsl, h:h + 1].broadcast_to(B3w), op=ALU.mult),
                                 reads=[PS[0], "dbg"], writes=["dkbg"])
                            P.op("dve", lambda e: e.tensor_tensor(out=vb[:, :, :], in0=vv3, in1=beta_tm[:, nsl, h:h + 1].broadcast_to(B3w), op=ALU.mult),
                                 reads=[PS[1], "dbeta"], writes=["dvb"])
                            yield
                            for i in range(NB):
                                pb = 2 + i // 4
                                P.op("pe", lambda e: e.matmul(ps[pb][H, (i % 4) * 128:(i % 4 + 1) * 128], lhsT=Tt[cur][:, i, :], rhs=vb[:, i, :],
                                                              start=True, stop=True),
                                     reads=[("dTt", cur), "dvb"], writes=[PS[pb]], inc=(i % 4 == 3))
                            for i in range(NB):
                                P.op("pe", lambda e: e.matmul(ps[4][:, i * 64:(i + 1) * 64], lhsT=kbg[:, i, :], rhs=Tt[cur][:, i, :],
                                                              start=True, stop=True),
                                     reads=[("dTt", cur), "dkbg"], writes=[PS[4]], inc=(i == NB - 1))
                            yield "OUT"
                            P.op("dve", lambda e: e.tensor_tensor(out=kd[:, :, :], in0=kv3, in1=ed_tm[:, nsl, h:h + 1].broadcast_to(B3w), op=ALU.mult),
                                 reads=[PS[0], "ded"], writes=["dkd"])
                            P.op("dve", lambda e: e.tensor_tensor(out=qg[:, :], in0=qT[:, bsl], in1=egcb[:, :], op=ALU.mult),
                                 reads=["dqT", "degcb"], writes=["dqg"])
                            P.op("pool", lambda e: e.tensor_copy(out=aT[:, :, :], in_=aTt[:, :, :]), reads=["daTt"], writes=["daT"])
                            for hb in range(2):
                                P.op("act", lambda e: e.activation(out=u_sb[:, hb * 4:(hb + 1) * 4, :].rearrange("p a b -> p (a b)"),
                                                                   in_=ps[2 + hb][H, :], func=AF.Copy),
                                     reads=[PS[2 + hb]], writes=["du"])
                            P.op("dve", lambda e: e.tensor_copy(out=wT[:, :, :].rearrange("p a b -> p (a b)"), in_=ps[4][:, :]),
                                 reads=[PS[4]], writes=["dwT"])
                            yield

                        def stage_b(bt):
                            n0 = bt * NB
                            bsl = slice(n0 * 64, (n0 + NB) * 64)
                            for i in range(NB):
                                n = n0 + i
                                vn = vnew[n % 2]
                                kvn = ("dvnew", n % 2)
                                if n == 0:
                                    P.op("dve", lambda e: e.tensor_copy(out=vn[:, :], in_=u_sb[:, i, :]), reads=["du"], writes=[kvn])
                                else:
                                    P.op("pe", lambda e: e.matmul(ps[5][H, 0:128], lhsT=wT[:, i, :], rhs=Sb[:, :], start=True, stop=True),
                                         reads=["dwT", "dSb"], writes=[PS[5]])
                                    yield
                                    P.op("dve", lambda e: e.tensor_tensor(out=vn[:, :], in0=u_sb[:, i, :], in1=ps[5][H, 0:128], op=ALU.subtract),
                                         reads=["du", PS[5]], writes=[kvn])
                                yield
                                osl = slice(i * 64, (i + 1) * 64)
                                if n < 31:
                                    P.op("pe", lambda e: e.matmul(ps[6][:, 0:128], lhsT=kd[:, i, :], rhs=vn[:, :], start=True, stop=True),
                                         reads=["dkd", kvn], writes=[PS[6]])
                                if n > 0:
                                    P.op("pe", lambda e: e.matmul(ps[7][:, osl], lhsT=Sb[:, :], rhs=qg[:, osl], start=True, stop=False),
                                         reads=["dSb", "dqg"], writes=[PS[7]], inc=False)
                                P.op("pe", lambda e: e.matmul(ps[7][:, osl], lhsT=vn[:, :], rhs=aT[:, i, :], start=(n == 0), stop=True),
                                     reads=[kvn, "daT"], writes=[PS[7]])
                                yield
                                if n < 31:
                                    if n == 0:
                                        P.op("dve", lambda e: e.tensor_copy(out=Sf[:, :], in_=ps[6][:, 0:128]), reads=[PS[6]], writes=["dSf"])
                                    else:
                                        P.op("dve", lambda e: e.scalar_tensor_tensor(out=Sf[:, :], in0=Sf[:, :], scalar=egl[:, n, h:h + 1],
                                                                                     in1=ps[6][:, 0:128], op0=ALU.mult, op1=ALU.add),
                                             reads=[PS[6], "dSf", "degl"], writes=["dSf"])
                                    yield
                                    P.op("act", lambda e: e.activation(out=Sb[:, :], in_=Sf[:, :], func=AF.Copy), reads=["dSf"], writes=["dSb"])
                                    yield
                            P.op("act", lambda e: e.activation(out=nsq[:, :], in_=ps[7][:, :], func=AF.Square),
                                 reads=[PS[7]], writes=["dnsq"])
                            yield
                            P.op("pe", lambda e: e.matmul(ps[5][:, :], lhsT=cst["ones"][:, :], rhs=nsq[:, :], start=True, stop=True),
                                 reads=["k_ones", "dnsq"], writes=[PS[5]])
                            yield
                            P.op("act", lambda e: e.activation(out=rstd[:, :], in_=ps[5][:, :], func=AF.Ln, bias=rtrp[:, 3:4],
                                                               scale=float(1.0 / 128.0)), reads=[PS[5], "rtrp"], writes=["rstd"])
                            P.op("act", lambda e: e.activation(out=rstd[:, :], in_=rstd[:, :], func=AF.Exp, scale=-0.5),
                                 reads=["rstd"], writes=["rstd"])
                            yield
                            P.op("dve", lambda e: e.scalar_tensor_tensor(out=ntmp[:, :], in0=ps[7][:, :], scalar=dng[:, l:l + 1],
                                                                         in1=rstd[:, :], op0=ALU.mult, op1=ALU.mult),
                                 reads=[PS[7], "dng", "rstd"], writes=["dntmp"])
                            P.op("dve", lambda e: e.tensor_tensor(out=obT[:, h, bsl], in0=ntmp[:, :], in1=zsT[:, bsl], op=ALU.mult),
                                 reads=["dntmp", "dz"], writes=[("obT", h)])
                            yield

                        import os as _os2
                        _stop = int(_os2.environ.get("DN_STOP", "-1"))
                        if _stop >= 0:
                            g_ = stage_a(0)
                            for _ in range(_stop):
                                next(g_)
                        else:
                            def drive(b_gen, a_gen):
                                a_wait, a_done, b_done = False, a_gen is None, b_gen is None
                                while not (b_done and (a_done or a_wait)):
                                    if not b_done:
                                        try:
                                            next(b_gen)
                                        except StopIteration:
                                            b_done = True
                                    if not a_done and not a_wait:
                                        try:
                                            if next(a_gen) == "OUT":
                                                a_wait = True
                                        except StopIteration:
                                            a_done = True
                                if not a_done:
                                    for _ in a_gen:
                                        pass

                            drive(None, stage_a(0))
                            for bt in range(4):
                                drive(stage_b(bt), stage_a(bt + 1) if bt + 1 < 4 else None)
                        P.barrier()

        def dump(nm, tile, nchunks, keyname, nokey=False):
            if nokey:
                P.barrier()
            with nc.sbuf_tensor(_u + "s_dbgf_" + nm, [128, S], F32) as dbgf:
                for k in range(nchunks):
                    P.op("dve", lambda e: e.tensor_copy(out=dbgf[:, :], in_=tile[:, k, :]),
                         reads=[(keyname, k)], writes=["dbgf"])
                    P.dma(dbg_d[nm][k * 128:(k + 1) * 128, :], dbgf[:, :], reads=["dbgf"], writes=["dbg_" + nm], stream="o")
                P.barrier()

        for l in range(n_layers):
            norm_to(lambda k, tc: (hT[:, k, tc * TC:(tc + 1) * TC], ("hT", k)), normg[:, l, :], l)
            if debug and l == 0:
                with nc.sbuf_tensor(_u + "dbgf", [128, S], F32) as dbgf:
                    for k in range(8):
                        P.op("dve", lambda e, k=k: e.tensor_copy(out=dbgf[:, :], in_=hT[:, k, :]),
                             reads=[("hT", k)], writes=["dbgf"])
                        P.dma(dbg_d["hT"][k * 128:(k + 1) * 128, :], dbgf[:, :], reads=["dbgf"], writes=["dbg_hT"],
                              stream="o")
                    P.barrier()

            def mem_attention(l):
                ph = ExitStack()
                def sbp(name, shape, dt):
                    return ph.enter_context(nc.sbuf_tensor(_u + "s_" + name, list(shape), dt))
                memT = sbp("memT", [128, 8, MEM_LEN], F32)
                memn = sbp("memn", [128, 8, MEM_LEN], BF16)
                kmT = sbp("kmT", [128, 2, MEM_LEN], BF16)
                vm = sbp("vm", [128, 2, 256], BF16)
                qmT = sbp("qmT", [128, 2, S], BF16)
                pT = [sbp(f"pT{i}", [128, TC], BF16) for i in range(2)]
                rden = sbp("rden", [128, TC], F32)
                for k in range(8):
                    P.dma(memT[:, k, :], memT_d[k * 128:(k + 1) * 128, :], writes=[("memT", k)], stream="x")
                P.op("dve", lambda e: e.tensor_scalar(out=g32[:, :], in0=memg[:, l, :], scalar1=float(math.sqrt(D)),
                                                      scalar2=None, op0=ALU.mult), reads=["memg"], writes=["g32"])
                rms_stats(memT, [("memT", k) for k in range(8)], 8, MEM_LEN, 0, 0, sqt, "sqt", rstd, "rstd", None)
                for k in range(8):
                    P.op("dve", lambda e, k=k: e.scalar_tensor_tensor(
                        out=memn[:, k, :], in0=memT[:, k, :], scalar=g32[:, k:k + 1], in1=rstd[:, 0:MEM_LEN],
                        op0=ALU.mult, op1=ALU.mult), reads=[("memT", k), "g32", "rstd"], writes=[("memn", k)])
                for ec in range(2):
                    wb, wkey = load_w(w_kv_d[l, :, ec * 128:(ec + 1) * 128])
                    for k in range(8):
                        P.op("pe", lambda e, k=k, wb=wb: e.matmul(ps[1][:, 0:MEM_LEN], lhsT=wb[:, k, :], rhs=memn[:, k, :],
                                                                  start=(k == 0), stop=(k == 7)),
                             reads=[wkey, ("memn", k)], writes=[PS[1]], inc=(k == 7))
                    P.op("dve", lambda e, ec=ec: e.tensor_copy(out=kmT[:, ec, :], in_=ps[1][:, 0:MEM_LEN]),
                         reads=[PS[1]], writes=[("kmT", ec)])
                for vc in range(2):
                    wb, wkey = load_w(w_kv_d[l, :, 256 + vc * 128:256 + (vc + 1) * 128])
                    for mt in range(2):
                        for k in range(8):
                            P.op("pe", lambda e, k=k, wb=wb, mt=mt: e.matmul(
                                ps[2][:, 0:128], lhsT=memn[:, k, mt * 128:(mt + 1) * 128], rhs=wb[:, k, :],
                                start=(k == 0), stop=(k == 7)),
                                reads=[wkey, ("memn", k)], writes=[PS[2]], inc=(k == 7))
                        P.op("dve", lambda e, mt=mt, vc=vc: e.tensor_copy(out=vm[:, mt, vc * 128:(vc + 1) * 128],
                                                                          in_=ps[2][:, 0:128]),
                             reads=[PS[2]], writes=[("vm", mt)])
                for ec in range(2):
                    def ev(tc, pap, pkey, ec=ec):
                        P.op("act", lambda e: e.activation(out=qmT[:, ec, tc * TC:(tc + 1) * TC], in_=pap,
                                                           func=AF.Copy, scale=0.125),
                             reads=[pkey], writes=[("qmT", ec)])
                    proj_fm(l, C_MQ + ec * 128, 128, ev)
                for h in range(4):
                    ec, r0 = h // 2, (h % 2) * 64
                    for tc in range(NT):
                        tsl = slice(tc * TC, (tc + 1) * TC)
                        for mb in range(2):
                            pi = 3 + mb
                            P.op("pe", lambda e, mb=mb, pi=pi: e.matmul(
                                ps[pi][:, :], lhsT=kmT[r0:r0 + 64, ec, mb * 128:(mb + 1) * 128],
                                rhs=qmT[r0:r0 + 64, ec, tsl], start=True, stop=True),
                                reads=[("kmT", ec), ("qmT", ec)], writes=[PS[pi]])
                            P.op("act", lambda e, mb=mb, pi=pi: e.activation(out=pT[mb][:, :], in_=ps[pi][:, :],
                                                                             func=AF.Exp),
                                 reads=[PS[pi]], writes=[("pT", mb)])
                        for mb in range(2):
                            P.op("pe", lambda e, mb=mb: e.matmul(
                                ps[5][r0:r0 + 64, :], lhsT=vm[:, mb, h * 64:(h + 1) * 64], rhs=pT[mb][:, :],
                                start=(mb == 0), stop=(mb == 1)),
                                reads=[("vm", mb), ("pT", mb)], writes=[PS[5]], inc=(mb == 1))
                        for mb in range(2):
                            P.op("pe", lambda e, mb=mb: e.matmul(
                                ps[6][r0:r0 + 64, :], lhsT=cst["ones"][:, 0:64], rhs=pT[mb][:, :],
                                start=(mb == 0), stop=(mb == 1)),
                                reads=["k_ones", ("pT", mb)], writes=[PS[6]], inc=(mb == 1))
                        P.op("dve", lambda e: e.reciprocal(out=rden[r0:r0 + 64, :], in_=ps[6][r0:r0 + 64, :]),
                             reads=[PS[6]], writes=["rden"])
                        P.op("dve", lambda e, tsl=tsl: e.tensor_tensor(out=obT[r0:r0 + 64, ec, tsl], in0=ps[5][r0:r0 + 64, :],
                                                                      in1=rden[r0:r0 + 64, :], op=ALU.mult),
                             reads=[PS[5], "rden"], writes=[("obT", ec)])
                P.barrier()
                ph.close()

            def merge(br, nwc, first):
                with ExitStack() as ph:
                    sg = [ph.enter_context(nc.sbuf_tensor(_u + f"sg{i}", [128, TC], F32)) for i in range(2)]
                    tmp = [ph.enter_context(nc.sbuf_tensor(_u + f"mtmp{i}", [128, TC], F32)) for i in range(2)]
                    it = 0
                    for dc in range(8):
                        wg, wgk = load_w(w_in_d[l, :, C_G + br * D + dc * 128:C_G + br * D + (dc + 1) * 128])
                        wr, wrk = load_w(w_br_d[br][l, :, dc * 128:(dc + 1) * 128], rows=nwc)
                        for tc in range(NT):
                            tsl = slice(tc * TC, (tc + 1) * TC)
                            b = it % 2
                            it += 1
                            pg, pp = 1 + b, 3 + b
                            for k in range(8):
                                P.op("pe", lambda e, k=k, pg=pg, tsl=tsl, wg=wg: e.matmul(
                                    ps[pg][:, :], lhsT=wg[:, k, :], rhs=hT[:, k, tsl], start=(k == 0), stop=(k == 7)),
                                    reads=[wgk, ("hT", k)], writes=[PS[pg]], inc=(k == 7))
                            for k in range(nwc):
                                P.op("pe", lambda e, k=k, pp=pp, tsl=tsl, wr=wr: e.matmul(
                                    ps[pp][:, :], lhsT=wr[:, k, :], rhs=obT[:, k, tsl], start=(k == 0), stop=(k == nwc - 1)),
                                    reads=[wrk, ("obT", k)], writes=[PS[pp]], inc=(k == nwc - 1))
                            P.op("act", lambda e, b=b, pg=pg, dc=dc: e.activation(
                                out=sg[b][:, :], in_=ps[pg][:, :], func=AF.Sigmoid,
                                bias=bgate[:, l, br * 8 + dc:br * 8 + dc + 1], scale=1.0),
                                reads=[PS[pg], "bgate"], writes=[("sg", b)])
                            if first:
                                P.op("dve", lambda e, b=b, pp=pp, dc=dc, tsl=tsl: e.tensor_tensor(
                                    out=mT[:, dc, tsl], in0=ps[pp][:, :], in1=sg[b][:, :], op=ALU.mult),
                                    reads=[PS[pp], ("sg", b)], writes=[("mT", dc, tc)])
                            else:
                                P.op("dve", lambda e, b=b, pp=pp: e.tensor_tensor(
                                    out=tmp[b][:, :], in0=ps[pp][:, :], in1=sg[b][:, :], op=ALU.mult),
                                    reads=[PS[pp], ("sg", b)], writes=[("mtmp", b)])
                                P.op("dve", lambda e, b=b, dc=dc, tsl=tsl: e.tensor_tensor(
                                    out=mT[:, dc, tsl], in0=mT[:, dc, tsl], in1=tmp[b][:, :], op=ALU.add),
                                    reads=[("mtmp", b), ("mT", dc, tc)], writes=[("mT", dc, tc)])
                    P.barrier()

            if 1 in branches:
                deltanet(l)
                if debug and l == 0:
                    dump('odn', obT, 4, 'obT')
                merge(1, 4, True)
            mem_attention(l)
            if debug and l == 0:
                dump('omem', obT, 2, 'obT')
            merge(3, 2, 1 not in branches)
            if 0 in branches:
                sb_attention(l)
                if debug and l == 0:
                    dump('osb', obT, 4, 'obT')
                merge(0, 4, False)
            if 2 in branches:
                retention(l)
                if debug and l == 0:
                    dump('ort', obT, 4, 'obT')
                merge(2, 4, False)

            for ec in range(8):
                wo, wok = load_w(w_out_d[l, :, ec * 128:(ec + 1) * 128])
                for tc in range(NT):
                    tsl = slice(tc * TC, (tc + 1) * TC)
                    pi = 1 + tc % 2
                    for dc in range(8):
                        P.op("pe", lambda e, dc=dc, pi=pi, tsl=tsl, wo=wo: e.matmul(
                            ps[pi][:, :], lhsT=wo[:, dc, :], rhs=mT[:, dc, tsl], start=(dc == 0), stop=(dc == 7)),
                            reads=[wok, ("mT", dc, tc)], writes=[PS[pi]], inc=(dc == 7))
                    P.op("dve", lambda e, ec=ec, pi=pi, tsl=tsl: e.tensor_tensor(
                        out=xT[:, ec, tsl], in0=xT[:, ec, tsl], in1=ps[pi][:, :], op=ALU.add),
                        reads=[PS[pi], ("xT", ec)], writes=[("xT", ec)])
            P.barrier()

        with ExitStack() as ph:
            ot = [ph.enter_context(nc.sbuf_tensor(_u + f"ot{i}", [128, TC], F32)) for i in range(2)]
            cnt = [0]

            def dst(k, tc):
                b = cnt[0] % 2
                cnt[0] += 1
                return ot[b][:, :], ("ot", b)
            P.op("dve", lambda e: e.tensor_scalar(out=g32[:, :], in0=fing[:, :], scalar1=float(math.sqrt(D)), scalar2=None,
                                                  op0=ALU.mult), reads=["fing"], writes=["g32"])
            for tc in range(NT):
                rms_stats(xT, [("xT", k) for k in range(8)], 8, TC, tc * TC, 0, sqt, "sqt", rstd, "rstd", None)
                for k in range(8):
                    ap, key = dst(k, tc)
                    P.op("dve", lambda e, k=k, tc=tc, ap=ap: e.scalar_tensor_tensor(
                        out=ap, in0=xT[:, k, tc * TC:(tc + 1) * TC], scalar=g32[:, k:k + 1], in1=rstd[:, :],
                        op0=ALU.mult, op1=ALU.mult),
                        reads=[("xT", k), "g32", "rstd"], writes=[key])
                    P.dma(outT_d[k * 128:(k + 1) * 128, tc * TC:(tc + 1) * TC], ap, reads=[key], writes=["outT"],
                          stream="o")
            P.finish(["outT", "dbg_hT", "dbg_omem", "dbg_osb", "dbg_odn", "dbg_ort"])
            P.barrier()
        P.emit()
        print("instructions recorded:", P.nins, {n: len(P.q[n]) for n in P.names})
    return nc


_NC_CACHE = {}


def _prep_inputs(inputs, b):
    f = np.float32
    m = {}
    m["xT"] = np.ascontiguousarray(inputs["x"][b].T.astype(f))
    m["memT"] = np.ascontiguousarray(inputs["mem"][b].T.astype(f))
    m["w_in"] = np.ascontiguousarray(inputs["w_in"], dtype=f)
    m["w_mem_kv"] = np.ascontiguousarray(inputs["w_mem_kv"], dtype=f)
    for n in ["w_br_sb", "w_br_dn", "w_br_ret", "w_br_mem", "w_out"]:
        m[n] = np.ascontiguousarray(inputs[n], dtype=f)
    m["norm_g"] = np.ascontiguousarray(inputs["norm_g"].reshape(DEPTH, 8, 128).transpose(0, 2, 1), dtype=f)
    m["mem_norm_g"] = np.ascontiguousarray(inputs["mem_norm_g"].reshape(DEPTH, 8, 128).transpose(0, 2, 1), dtype=f)
    m["b_gate"] = np.ascontiguousarray(inputs["b_gate"].reshape(DEPTH, 32, 128).transpose(0, 2, 1), dtype=f)
    m["final_norm_g"] = np.ascontiguousarray(inputs["final_norm_g"].reshape(8, 128).T, dtype=f)
    for n, v in _consts().items():
        m["c_" + n] = v
    for n, v in _rt_consts().items():
        m["c_" + n] = v
    m["c_dn"] = _dn_consts()
    m["dn_conv_w"] = np.ascontiguousarray(inputs["dn_conv_w"].reshape(DEPTH, 4, 12, 128).transpose(0, 3, 2, 1), dtype=f)
    m["dn_norm_g"] = np.ascontiguousarray(inputs["dn_norm_g"].T, dtype=f)
    m["dn_alog"] = np.ascontiguousarray(np.broadcast_to(np.tile(inputs["dn_a_log"], (1, 32))[:, None, :], (DEPTH, 128, 128)), dtype=f)
    m["dn_dtb"] = np.ascontiguousarray(np.broadcast_to(np.tile(inputs["dn_dt_bias"], (1, 32))[:, None, :], (DEPTH, 128, 128)), dtype=f)
    m["pos"] = np.ascontiguousarray(np.broadcast_to(inputs["positions"][b].astype(np.int32)[None, :], (128, S)))
    m["ret_norm_g"] = np.ascontiguousarray(inputs["ret_norm_g"].reshape(DEPTH, 4, 128).transpose(0, 2, 1), dtype=f)
    return m


def kernel(**inputs):
    inputs = {k: np.asarray(v) for k, v in inputs.items()}
    if "nc" not in _NC_CACHE:
        _NC_CACHE["nc"] = build()
    nc = _NC_CACHE["nc"]
    in_maps = [_prep_inputs(inputs, b) for b in range(8)]
    res = run_bass_kernel_spmd(nc, in_maps, core_ids=list(range(8)))
    out = np.stack([np.ascontiguousarray(res.results[b]["outT"].T) for b in range(8)], axis=0)
    return out.astype(np.float32)
```

```python
import math
from contextlib import ExitStack
import numpy as np
import concourse.bass as bass
import concourse.mybir as mybir
from concourse.bass_utils import run_bass_kernel_spmd

F32 = mybir.dt.float32
BF16 = mybir.dt.bfloat16
I32 = mybir.dt.int32
AF = mybir.ActivationFunctionType
ALU = mybir.AluOpType

D = 1024
S = 2048
DEPTH = 2
MEM_LEN = 256
EPS = 1e-6
IN_COLS = 9992
C_SBQ, C_SBK, C_SBV, C_SBZ = 0, 512, 1024, 1536
C_DNQ, C_DNK, C_DNV, C_DNZ, C_DNA, C_DNB = 2048, 2560, 3072, 3584, 4096, 4100
C_RTQ, C_RTK, C_RTV, C_RTZ = 4104, 4360, 4616, 5128
C_MQ = 5640
C_G = 5896
NT = 4
TC = 512


class _Uniq:
    def __init__(self):
        self.n = 0

    def __add__(self, name):
        self.n += 1
        return f"{name}_{self.n}"


class _Rec:
    def __getattr__(self, name):
        return lambda *a, **k: (name, a, k)


_REC = _Rec()


class Prog:
    LIMIT = 30000

    def __init__(self, nc, stack):
        self.nc = nc
        self.stack = stack
        self.names = ["pe", "act", "dve", "pool", "sp"]
        self.q = {n: [] for n in self.names}
        self.cnt = {n: 0 for n in self.names}
        self.sems = {n: [] for n in self.names}
        self.seen = {n: {} for n in self.names}
        self.bufs = {}
        self.dma_sem = {}
        self.dma_cnt = {}
        self.same_sync = True
        self.dma_i = 0
        self.nins = 0

    def _eng_sem(self, eng, g):
        ep = (g - 1) // self.LIMIT
        while len(self.sems[eng]) <= ep:
            s = self.stack.enter_context(self.nc.semaphore(f"s_{eng}_{len(self.sems[eng])}"))
            self.sems[eng].append(s)
        return self.sems[eng][ep], (g - 1) % self.LIMIT + 1

    def _tok_sem(self, tok):
        kind, g = tok
        if kind.startswith("dma:"):
            return self.dma_sem[kind], 16 * g
        return self._eng_sem(kind, g)

    def _need(self, eng, tok):
        kind, g = tok
        if kind == eng:
            if eng in ("pe", "sp") or not self.same_sync:
                return False
        return self.seen[eng].get(kind, 0) < g

    def _collect(self, eng, reads, writes):
        toks = []
        for k in reads:
            b = self.bufs.get(k)
            if b and b[0] is not None:
                toks.append(b[0])
            if b and isinstance(k, tuple) and k[0] == "ps" and eng in ("act", "dve"):
                other = "dve" if eng == "act" else "act"
                if other in b[1]:
                    toks.append((other, b[1][other]))
        for k in writes:
            b = self.bufs.get(k)
            if b:
                if b[0] is not None:
                    toks.append(b[0])
                toks.extend(b[1].items())
        need = {}
        for t in toks:
            if self._need(eng, t):
                need[t[0]] = max(need.get(t[0], 0), t[1])
        return list(need.items())

    def _update(self, tok, reads, writes):
        for k in writes:
            self.bufs[k] = [tok, {}]
        for k in reads:
            b = self.bufs.setdefault(k, [None, {}])
            if k in writes:
                continue
            b[1][tok[0]] = max(b[1].get(tok[0], 0), tok[1])

    def op(self, eng, fn, reads=(), writes=(), inc=True):
        call = fn(_REC)
        fn = lambda e, c=call: getattr(e, c[0])(*c[1], **c[2])
        waits = self._collect(eng, reads, writes)
        for t in waits:
            self.seen[eng][t[0]] = t[1]
        ws = [self._tok_sem(t) for t in waits]
        for w in ws[1:]:
            self.q[eng].append(("wait", w[0], w[1]))
        if inc:
            self.cnt[eng] += 1
            tok = (eng, self.cnt[eng])
            sem, val = self._eng_sem(eng, self.cnt[eng])
            self.q[eng].append(("ins", fn, ws[0] if ws else None, (sem, 1)))
        else:
            tok = (eng, self.cnt[eng] + 1)
            self.q[eng].append(("ins", fn, ws[0] if ws else None, None))
        self._update(tok, reads, writes)
        self.nins += 1
        return tok

    NDS = 16

    def dma(self, out, in_, reads=(), writes=(), stream="d0", queue="sp"):
        j = self.dma_i % self.NDS
        self.dma_i += 1
        kind = f"dma:{j}"
        if kind not in self.dma_sem:
            self.dma_sem[kind] = self.stack.enter_context(self.nc.semaphore(f"sd_{j}"))
            self.dma_cnt[kind] = 0
        waits = self._collect(queue, reads, writes)
        if self.dma_cnt[kind] > 0 and self.seen[queue].get(kind, 0) < self.dma_cnt[kind]:
            waits = [w for w in waits if w[0] != kind] + [(kind, self.dma_cnt[kind])]
        for t in waits:
            self.seen[queue][t[0]] = t[1]
        for t in waits:
            s, v = self._tok_sem(t)
            self.q[queue].append(("wait", s, v))
        self.dma_cnt[kind] += 1
        tok = (kind, self.dma_cnt[kind])
        self.q[queue].append(("ins", lambda e, o=out, i=in_: e.dma_start(out=o, in_=i), None,
                              (self.dma_sem[kind], 16)))
        self._update(tok, reads, writes)
        self.nins += 1
        return tok

    def barrier(self):
        toks = [(n, self.cnt[n]) for n in self.names if self.cnt[n] > 0]
        toks += [(k, c) for k, c in self.dma_cnt.items() if c > 0]
        for eng in self.names:
            for t in toks:
                if t[0] == eng and eng in ("pe", "sp"):
                    continue
                if self.seen[eng].get(t[0], 0) < t[1]:
                    self.seen[eng][t[0]] = t[1]
                    s, v = self._tok_sem(t)
                    self.q[eng].append(("wait", s, v))
        self.bufs = {}

    def finish(self, final_keys):
        toks = []
        for k in final_keys:
            b = self.bufs.get(k)
            if b and b[0] is not None:
                toks.append(b[0])
        for t in toks:
            s, v = self._tok_sem(t)
            self.q["sp"].append(("wait", s, v))

    def simulate(self):
        sem = {}
        pos = {n: 0 for n in self.names}
        def ok(w):
            return w is None or sem.get(id(w[0]), 0) >= w[1]
        progress = True
        while progress:
            progress = False
            for n in self.names:
                q = self.q[n]
                while pos[n] < len(q):
                    ent = q[pos[n]]
                    if ent[0] == "wait":
                        if not ok((ent[1], ent[2])):
                            break
                    else:
                        if not ok(ent[2]):
                            break
                        if ent[3] is not None:
                            sem[id(ent[3][0])] = sem.get(id(ent[3][0]), 0) + ent[3][1]
                    pos[n] += 1
                    progress = True
        stuck = {n: (pos[n], len(self.q[n])) for n in self.names if pos[n] < len(self.q[n])}
        if stuck:
            msg = []
            for n, (p, ln) in stuck.items():
                ent = self.q[n][p]
                w = (ent[1], ent[2]) if ent[0] == "wait" else ent[2]
                msg.append(f"{n}@{p}/{ln} waits {w} have {sem.get(id(w[0]), 0)} kind={ent[0]} tag={ent[4] if len(ent) > 4 else None}")
            raise RuntimeError("DEADLOCK in recorded program: " + "; ".join(msg))

    def emit(self):
        self.simulate()
        nc = self.nc
        with nc.Block() as block:
            def replay(e, name):
                for ent in self.q[name]:
                    if ent[0] == "wait":
                        e.wait_ge(ent[1], ent[2])
                    else:
                        _, fn, w, inc = ent
                        ins = fn(e)
                        if w is not None:
                            ins._wait_ge(w[0], w[1])
                        if inc is not None:
                            ins.then_inc(inc[0], inc[1])

            @block.sync
            def _(e):
                replay(e, "sp")

            @block.scalar
            def _(e):
                replay(e, "act")

            @block.vector
            def _(e):
                replay(e, "dve")

            @block.tensor
            def _(e):
                replay(e, "pe")

            @block.gpsimd
            def _(e):
                replay(e, "pool")


def _consts():
    c = {}
    i = np.arange(128)
    c["ident"] = np.eye(128, dtype=np.float32)
    c["ones"] = np.ones((128, 128), np.float32)
    c["trineg"] = -(i[:, None] >= i[None, :]).astype(np.float32)
    e0 = np.zeros((128, 128), np.float32)
    e0[0, :] = 1.0
    c["e0"] = e0
    c["masksb"] = (i[:, None] < i[None, :]).astype(np.float32)
    return c


def _rt_consts():
    gam = [1.0 - 2.0 ** (-5.0 - h) for h in range(4)]
    i = np.arange(128)
    dt = np.zeros((128, 4, 128), np.float64)
    sd = np.zeros((128, 4), np.float64)
    cd = np.zeros((128, 2, 128), np.float64)
    for h in range(4):
        rel = i[None, :] - i[:, None]
        dt[:, h, :] = np.where(rel >= 0, gam[h] ** np.maximum(rel, 0), 0.0)
        sd[:, h] = gam[h] ** (127 - i)
    for p in range(2):
        for hh in range(2):
            cd[hh * 64:(hh + 1) * 64, p, :] = (gam[2 * p + hh] ** (i + 1.0))[None, :]
    inv = 10000.0 ** (-(np.arange(32, dtype=np.float32)) / np.float32(32))
    rp = np.zeros((128, 4), np.float32)
    rp[:, 0] = np.tile(inv.astype(np.float32), 4)
    rp[:, 1] = np.where((i % 64) < 32, -1.0, 1.0)
    rp[:, 2] = np.float32(math.pi / 2)
    rp[:, 3] = EPS
    cn = np.eye(128) - 1.0 / 128.0
    return {"rt_dt": dt.astype(np.float32), "rt_sd": sd.astype(np.float32), "rt_cd": cd.astype(np.float32),
            "rt_rp": rp, "rt_cn": cn.astype(np.float32)}


def _dn_consts():
    i = np.arange(64)
    c = np.zeros((128, 4, 128), np.float32)
    c[:, 0, :] = np.eye(128)
    c[:, 1, :] = 1.0
    c[0:64, 2, 0:64] = (i[:, None] <= i[None, :])
    c[0:64, 3, 0:64] = (i[:, None] < i[None, :])
    return c


CONST_NAMES = ["ident", "ones", "trineg", "e0", "masksb"]


def build(n_layers=DEPTH, debug=False, branches=(0, 1, 2, 3)):
    _u = _Uniq()
    nc = bass.Bass("TRN2", target_bir_lowering=False)
    dr = {}

    def din(name, shape, dt=F32):
        dr[name] = nc.dram_tensor(name, list(shape), dt, kind="ExternalInput").ap()
        return dr[name]

    xT_d = din("xT", [D, S])
    memT_d = din("memT", [D, MEM_LEN])
    w_in_d = din("w_in", [DEPTH, D, IN_COLS])
    w_kv_d = din("w_mem_kv", [DEPTH, D, 512])
    w_br_d = {0: din("w_br_sb", [DEPTH, 512, D]), 1: din("w_br_dn", [DEPTH, 512, D]),
              2: din("w_br_ret", [DEPTH, 512, D]), 3: din("w_br_mem", [DEPTH, 256, D])}
    w_out_d = din("w_out", [DEPTH, D, D])
    normg_d = din("norm_g", [DEPTH, 128, 8])
    memg_d = din("mem_norm_g", [DEPTH, 128, 8])
    bgate_d = din("b_gate", [DEPTH, 128, 32])
    fing_d = din("final_norm_g", [128, 8])
    cst_d = {n: din("c_" + n, [128, 128]) for n in CONST_NAMES}
    dnc_d = din("c_dn", [128, 4, 128])
    dncw_d = din("dn_conv_w", [DEPTH, 128, 12, 4])
    dng_d = din("dn_norm_g", [128, DEPTH])
    dnal_d = din("dn_alog", [DEPTH, 128, 128])
    dndt_d = din("dn_dtb", [DEPTH, 128, 128])
    pos_d = din("pos", [128, S], I32)
    retg_d = din("ret_norm_g", [DEPTH, 128, 4])
    rtdt_d = din("c_rt_dt", [128, 4, 128])
    rtsd_d = din("c_rt_sd", [128, 4])
    rtcd_d = din("c_rt_cd", [128, 2, 128])
    rtrp_d = din("c_rt_rp", [128, 4])
    rtcn_d = din("c_rt_cn", [128, 128])
    outT_d = nc.dram_tensor("outT", [D, S], F32, kind="ExternalOutput").ap()
    dbg_d = {}
    if debug:
        dbg_d["hT"] = nc.dram_tensor("dbg_hT", [D, S], F32, kind="ExternalOutput").ap()
        dbg_d["omem"] = nc.dram_tensor("dbg_omem", [256, S], F32, kind="ExternalOutput").ap()
        for nm in ["osb", "odn", "ort"]:
            dbg_d[nm] = nc.dram_tensor("dbg_" + nm, [512, S], F32, kind="ExternalOutput").ap()

    with ExitStack() as st:
        P = Prog(nc, st)

        def sb(name, shape, dt):
            return st.enter_context(nc.sbuf_tensor(_u + "s_" + name, list(shape), dt))

        xT = sb("xT", [128, 8, S], F32)
        hT = sb("hT", [128, 8, S], BF16)
        mT = sb("mT", [128, 8, S], BF16)
        obT = sb("obT", [128, 4, S], BF16)
        wst = [sb(f"wst{i}", [128, 8, 128], F32) for i in range(2)]
        NWB = 5
        wbp = [sb(f"wb{i}", [128, 8, 128], BF16) for i in range(NWB)]
        cst = {n: sb("k_" + n, [128, 128], BF16) for n in CONST_NAMES}
        cstf = sb("cstf", [128, 128], F32)
        normg = sb("normg", [128, DEPTH, 8], F32)
        memg = sb("memg", [128, DEPTH, 8], F32)
        bgate = sb("bgate", [128, DEPTH, 32], F32)
        fing = sb("fing", [128, 8], F32)
        ps = [st.enter_context(nc.psum_tensor(f"ps{i}", [128, 512], F32)) for i in range(8)]
        PS = [("ps", i) for i in range(8)]

        wcount = [0]
        wbcount = [0]

        def load_w(src, n=128, rows=8, eng="pool"):
            i = wcount[0] % 2
            wcount[0] += 1
            j = wbcount[0] % NWB
            wbcount[0] += 1
            stg, wb = wst[i], wbp[j]
            P.dma(stg[:, 0:rows, 0:n], src.rearrange("(k p) n -> p k n", p=128),
                  writes=[("wst", i)], stream=f"w{i}")
            fn = lambda e, o=wb[:, 0:rows, 0:n], a=stg[:, 0:rows, 0:n]: e.tensor_copy(out=o, in_=a)
            P.op(eng, fn, reads=[("wst", i)], writes=[("wb", j)])
            return wb, ("wb", j)

        for n in CONST_NAMES:
            P.dma(cstf[:, :], cst_d[n][:, :], writes=["cstf"], stream="c")
            P.op("dve", lambda e, o=cst[n][:, :]: e.tensor_copy(out=o, in_=cstf[:, :]),
                 reads=["cstf"], writes=["k_" + n])
        for l in range(DEPTH):
            P.dma(normg[:, l, :], normg_d[l, :, :], writes=["normg"], stream="c")
            P.dma(memg[:, l, :], memg_d[l, :, :], writes=["memg"], stream="c")
            P.dma(bgate[:, l, :], bgate_d[l, :, :], writes=["bgate"], stream="c")
        P.dma(fing[:, :], fing_d[:, :], writes=["fing"], stream="c")
        for k in range(8):
            P.dma(xT[:, k, :], xT_d[k * 128:(k + 1) * 128, :], writes=[("xT", k)], stream="x")

        rtdt = sb("rtdt", [128, 4, 128], F32)
        rtsd = sb("rtsd", [128, 4], F32)
        rtcd = sb("rtcd", [128, 2, 128], F32)
        rtrp = sb("rtrp", [128, 4], F32)
        rtcn = sb("rtcn", [128, 128], BF16)
        retg = sb("retg", [128, DEPTH, 4], F32)
        P.dma(rtdt[:, :, :], rtdt_d[:, :, :], writes=["rtdt"], stream="c")
        P.dma(rtsd[:, :], rtsd_d[:, :], writes=["rtsd"], stream="c")
        P.dma(rtcd[:, :, :], rtcd_d[:, :, :], writes=["rtcd"], stream="c")
        P.dma(rtrp[:, :], rtrp_d[:, :], writes=["rtrp"], stream="c")
        P.dma(cstf[:, :], rtcn_d[:, :], writes=["cstf"], stream="c")
        P.op("dve", lambda e: e.tensor_copy(out=rtcn[:, :], in_=cstf[:, :]), reads=["cstf"], writes=["rtcn"])
        for l in range(DEPTH):
            P.dma(retg[:, l, :], retg_d[l, :, :], writes=["retg"], stream="c")
        dnc = sb("dnc", [128, 4, 128], F32)
        dncw = sb("dncw", [128, DEPTH, 12, 4], F32)
        dng = sb("dng", [128, DEPTH], F32)
        P.dma(dnc[:, :, :], dnc_d[:, :, :], writes=["dnc"], stream="c")
        P.dma(dng[:, :], dng_d[:, :], writes=["dng"], stream="c")
        for l in range(DEPTH):
            P.dma(dncw[:, l, :, :], dncw_d[l, :, :, :], writes=["dncw"], stream="c")
        def rms_stats(src_tile, src_keys, nk, ncols, c0, psum_i, sq_tile, sq_key, rstd_tile, rstd_key, extra):
            for k in range(nk):
                P.op("act", lambda e, k=k: e.activation(out=sq_tile[:, k % 2, 0:ncols], in_=src_tile[:, k, c0:c0 + ncols],
                                                        func=AF.Square),
                     reads=[src_keys[k]], writes=[(sq_key, k % 2)])
                P.op("pe", lambda e, k=k: e.matmul(ps[psum_i][:, 0:ncols], lhsT=cst["ones"][:, :],
                                                   rhs=sq_tile[:, k % 2, 0:ncols], start=(k == 0), stop=(k == nk - 1)),
                     reads=[(sq_key, k % 2), "k_ones"], writes=[PS[psum_i]])
            P.op("act", lambda e: e.activation(out=rstd_tile[:, 0:ncols], in_=ps[psum_i][:, 0:ncols], func=AF.Ln,
                                               bias=epsb[:, 0:1], scale=1.0),
                 reads=[PS[psum_i], "epsb"], writes=[rstd_key])
            P.op("act", lambda e: e.activation(out=rstd_tile[:, 0:ncols], in_=rstd_tile[:, 0:ncols], func=AF.Exp,
                                               scale=-0.5),
                 reads=[rstd_key], writes=[rstd_key])

        epsb = sb("epsb", [128, 1], F32)
        P.op("dve", lambda e: e.memset(epsb[:, :], float(D * EPS)), writes=["epsb"])
        sqt = sb("sqt", [128, 2, TC], BF16)
        rstd = sb("rstd", [128, TC], F32)
        g32 = sb("g32", [128, 8], F32)

        def norm_to(dst_fn, gsrc, layer_tag):
            P.op("dve", lambda e: e.tensor_scalar(out=g32[:, :], in0=gsrc, scalar1=float(math.sqrt(D)), scalar2=None,
                                                  op0=ALU.mult), reads=["normg", "fing"], writes=["g32"])
            for tc in range(NT):
                rms_stats(xT, [("xT", k) for k in range(8)], 8, TC, tc * TC, 0, sqt, "sqt", rstd, "rstd", None)
                for k in range(8):
                    ap, key = dst_fn(k, tc)
                    P.op("dve", lambda e, k=k, tc=tc, ap=ap: e.scalar_tensor_tensor(
                        out=ap, in0=xT[:, k, tc * TC:(tc + 1) * TC], scalar=g32[:, k:k + 1], in1=rstd[:, :],
                        op0=ALU.mult, op1=ALU.mult),
                        reads=[("xT", k), "g32", "rstd"], writes=[key])

        def proj_fm(l, col0, ncols, evac, wsrc=None):
            src = (w_in_d[l, :, col0:col0 + ncols] if wsrc is None else wsrc)
            wb, wkey = load_w(src, n=ncols)
            for tc in range(NT):
                pi = 1 + (tc % 2)
                for k in range(8):
                    P.op("pe", lambda e, k=k, tc=tc, pi=pi: e.matmul(
                        ps[pi][0:ncols, :], lhsT=wb[:, k, 0:ncols], rhs=hT[:, k, tc * TC:(tc + 1) * TC],
                        start=(k == 0), stop=(k == 7)),
                        reads=[wkey, ("hT", k)], writes=[PS[pi]], inc=(k == 7))
                evac(tc, ps[pi][0:ncols, :], PS[pi])

        def proj_tm(l, col0, ncols, evac, ntok=128):
            wb, wkey = load_w(w_in_d[l, :, col0:col0 + ncols], n=ncols)
            for tt in range(S // ntok):
                pi = 1 + (tt % 2)
                for k in range(8):
                    P.op("pe", lambda e: e.matmul(ps[pi][0:ntok, 0:ncols], lhsT=hT[:, k, tt * ntok:(tt + 1) * ntok],
                                                  rhs=wb[:, k, 0:ncols], start=(k == 0), stop=(k == 7)),
                         reads=[wkey, ("hT", k)], writes=[PS[pi]], inc=(k == 7))
                evac(tt, ps[pi][0:ntok, 0:ncols], PS[pi])

        oneb = sb("oneb", [128, 1], F32)
        P.op("dve", lambda e: e.memset(oneb[:, :], 1.0), writes=["oneb"])

        def run_streams(gens, stagger=None):
            gens = list(gens)
            if stagger:
                for g, k in zip(gens, stagger):
                    for _ in range(k):
                        next(g)
            while gens:
                for g in list(gens):
                    try:
                        next(g)
                    except StopIteration:
                        gens.remove(g)

        def sb_attention(l):
            for hp in range(4):
                with ExitStack() as ph:
                    def sbp(name, shape, dt):
                        return ph.enter_context(nc.sbuf_tensor(_u + ("s_" + name), list(shape), dt))
                    qT = sbp("sbq", [128, S], BF16)
                    kT = sbp("sbk", [128, S], BF16)
                    vtm = sbp("sbv", [128, 16, 128], BF16)
                    proj_fm(l, C_SBQ + hp * 128, 128, lambda tc, pap, pkey: P.op(
                        "act", lambda e: e.activation(out=qT[:, tc * TC:(tc + 1) * TC], in_=pap, func=AF.Copy, scale=0.125),
                        reads=[pkey], writes=["sbq"]))
                    proj_fm(l, C_SBK + hp * 128, 128, lambda tc, pap, pkey: P.op(
                        "dve", lambda e: e.tensor_copy(out=kT[:, tc * TC:(tc + 1) * TC], in_=pap),
                        reads=[pkey], writes=["sbk"]))
                    proj_tm(l, C_SBV + hp * 128, 128, lambda tt, pap, pkey: P.op(
                        "dve", lambda e: e.tensor_copy(out=vtm[:, tt, :], in_=pap),
                        reads=[pkey], writes=[("sbv", tt)]))
                    with ExitStack() as ph2:
                        def sbw(name, shape, dt):
                            return ph2.enter_context(nc.sbuf_tensor(_u + ("s_" + name), list(shape), dt))

                        import os as _os

                        def stream(sid, hh, qcs):
                            ez = sbw(f"ez{sid}", [128, TC], F32)
                            spb = sbw(f"spb{sid}", [128, TC], BF16)
                            eg = sbw(f"eg{sid}", [128, TC], BF16)
                            Gb = sbw(f"Gb{sid}", [128, TC], BF16)
                            wv = eg
                            K_wv = ("eg", sid)
                            K_ez, K_spb, K_eg, K_Gb = ("ez", sid), ("spb", sid), ("eg", sid), ("Gb", sid)
                            rs = slice(hh * 64, hh * 64 + 64)
                            pzg, po = 2 * sid, 2 * sid + 1
                            pgg = (4 + 2 * sid) if _os.environ.get('SB_SEPG') else pzg
                            yield
                            for qc in qcs:
                                q0 = qc * TC
                                kmax = qc * 4 + 3
                                pc0 = None
                                P.op("pool", lambda e: e.memset(wv[:, 0:384], 0.0), writes=[K_wv])
                                for kb in range(kmax, -1, -1):
                                    if kmax - kb >= int(_os.environ.get("SB_MAXIT", "99")):
                                        continue
                                    j = kb - qc * 4
                                    c0 = 128 * j if j >= 0 else 0
                                    cs = slice(c0, TC)
                                    tsl = slice(q0 + c0, q0 + TC)
                                    ksl = slice(kb * 128, (kb + 1) * 128)
                                    P.op("pe", lambda e: e.matmul(ps[pzg][:, cs], lhsT=kT[rs, ksl], rhs=qT[rs, tsl],
                                                                  start=True, stop=True),
                                         reads=["sbk", "sbq"], writes=[PS[pzg]])
                                    yield
                                    P.op("act", lambda e: e.activation(out=ez[:, cs], in_=ps[pzg][:, cs], func=AF.Exp),
                                         reads=[PS[pzg]], writes=[K_ez])
                                    yield
                                    P.op("act", lambda e: e.activation(out=spb[:, cs], in_=ez[:, cs], func=AF.Ln,
                                                                       bias=oneb[:, 0:1], scale=1.0),
                                         reads=[K_ez, "oneb"], writes=[K_spb])
                                    yield
                                    if j >= 0:
                                        P.op("dve", lambda e: e.tensor_tensor(out=spb[:, c0:c0 + 128], in0=spb[:, c0:c0 + 128],
                                                                              in1=cst["masksb"][:, :], op=ALU.mult),
                                             reads=[K_spb, "k_masksb"], writes=[K_spb])
                                    P.op("pe", lambda e: e.matmul(ps[pgg][:, cs], lhsT=cst["trineg"][:, :], rhs=spb[:, cs],
                                                                  start=True, stop=(pc0 is None)),
                                         reads=["k_trineg", K_spb], writes=[PS[pgg]], inc=(pc0 is None))
                                    if pc0 is not None:
                                        pcs = slice(pc0, TC)
                                        P.op("pe", lambda e: e.matmul(ps[pgg][:, pcs], lhsT=cst["e0"][:, :], rhs=Gb[:, pcs],
                                                                      start=False, stop=True),
                                             reads=["k_e0", K_Gb], writes=[PS[pgg]])
                                    yield
                                    P.op("act", lambda e: e.activation(out=eg[:, cs], in_=ps[pgg][:, cs], func=AF.Exp),
                                         reads=[PS[pgg]], writes=[K_eg])
                                    if kb > 0:
                                        P.op("dve", lambda e: e.tensor_copy(out=Gb[:, cs], in_=ps[pgg][:, cs]),
                                             reads=[PS[pgg], K_eg], writes=[K_Gb])
                                    yield
                                    P.op("dve", lambda e: e.tensor_tensor(out=wv[:, cs], in0=ez[:, cs], in1=eg[:, cs], op=ALU.mult),
                                         reads=[K_ez, K_eg], writes=[K_wv])
                                    if j >= 0:
                                        P.op("dve", lambda e: e.tensor_tensor(out=wv[:, c0:c0 + 128], in0=wv[:, c0:c0 + 128],
                                                                              in1=cst["masksb"][:, :], op=ALU.mult),
                                             reads=[K_wv, "k_masksb"], writes=[K_wv])
                                    yield
                                    P.op("pe", lambda e: e.matmul(ps[po][rs, :], lhsT=vtm[:, kb, rs], rhs=wv[:, :],
                                                                  start=(kb == kmax), stop=(kb == 0)),
                                         reads=[("sbv", kb), K_wv], writes=[PS[po]])
                                    pc0 = c0
                                    yield
                                P.op("dve", lambda e: e.tensor_copy(out=obT[rs, hp, q0:q0 + TC], in_=ps[po][rs, :]),
                                     reads=[PS[po]], writes=[("obT", hp, hh, qc)])
                                yield

                        _mode = _os.environ.get("SB_MODE", "4")
                        if _mode == "1":
                            run_streams([stream(0, 0, [3, 0, 2, 1])])
                            run_streams([stream(1, 1, [3, 0, 2, 1])])
                        elif _mode == "2":
                            run_streams([stream(0, 0, [3, 0, 2, 1]), stream(1, 1, [3, 0, 2, 1])])
                        else:
                            run_streams([stream(0, 0, [3, 0]), stream(1, 1, [3, 0]), stream(2, 0, [2, 1]), stream(3, 1, [2, 1])],
                                        stagger=[int(x) for x in _os.environ.get('SB_STAG', '0,2,4,6').split(',')])
                        P.barrier()
                    with ExitStack() as ph3:
                        zs = [ph3.enter_context(nc.sbuf_tensor(_u + f"s_sbzs{i}", [128, TC], BF16)) for i in range(2)]

                        def evz(tc, pap, pkey):
                            b = tc % 2
                            P.op("act", lambda e: e.activation(out=zs[b][:, :], in_=pap, func=AF.Silu),
                                 reads=[pkey], writes=[("sbzs", b)])
                            tsl = slice(tc * TC, (tc + 1) * TC)
                            P.op("dve", lambda e: e.tensor_tensor(out=obT[:, hp, tsl], in0=obT[:, hp, tsl], in1=zs[b][:, :], op=ALU.mult),
                                 reads=[("sbzs", b)], writes=[("obT", hp)])
                        proj_fm(l, C_SBZ + hp * 128, 128, evz)
                        P.barrier()

        def retention(l):
            TWO_PI = 2.0 * math.pi
            C1 = 6.28125
            C2 = TWO_PI - C1
            with ExitStack() as ph0:
                def sb0(name, shape, dt):
                    return ph0.enter_context(nc.sbuf_tensor(_u + ("s_" + name), list(shape), dt))
                qrT = sb0("rqr", [128, 2, S], BF16)
                krT = sb0("rkr", [128, 2, S], BF16)
                with ExitStack() as ph1:
                    def sb1(name, shape, dt):
                        return ph1.enter_context(nc.sbuf_tensor(_u + ("s_" + name), list(shape), dt))
                    COS2 = sb1("rcos", [128, S], BF16)
                    SIN2 = sb1("rsin", [128, S], BF16)
                    pint = sb1("rpint", [128, TC], I32)
                    ta = sb1("rta", [128, TC], F32)
                    tk = sb1("rtk", [128, TC], F32)
                    tm_full = sb1("rtm", [128, TC], F32)
                    tm = tm_full[:, 0:256]
                    ki = tm_full[:, 256:512].bitcast(I32)
                    ta_f, tk_f, pint_f = ta, tk, pint
                    for tc in range(8):
                        tsl = slice(tc * 256, (tc + 1) * 256)
                        ta, tk, pint = ta_f[:, 0:256], tk_f[:, 0:256], pint_f[:, 0:256]
                        P.dma(pint, pos_d[:, tsl], writes=["rpint"], stream="x")
                        P.op("dve", lambda e: e.tensor_copy(out=ta, in_=pint), reads=["rpint"], writes=["rta"])
                        P.op("dve", lambda e: e.tensor_scalar(out=ta, in0=ta, scalar1=rtrp[:, 0:1], scalar2=None,
                                                              op0=ALU.mult), reads=["rta", "rtrp"], writes=["rta"])
                        P.op("dve", lambda e: e.tensor_scalar(out=ki, in0=ta, scalar1=float(1.0 / TWO_PI),
                                                              scalar2=None, op0=ALU.mult), reads=["rta"], writes=["rki"])
                        P.op("dve", lambda e: e.tensor_copy(out=tk, in_=ki), reads=["rki"], writes=["rtk"])
                        P.op("dve", lambda e: e.scalar_tensor_tensor(out=ta, in0=tk, scalar=-C1, in1=ta,
                                                                     op0=ALU.mult, op1=ALU.add),
                             reads=["rtk", "rta"], writes=["rta"])
                        P.op("dve", lambda e: e.scalar_tensor_tensor(out=ta, in0=tk, scalar=-C2, in1=ta,
                                                                     op0=ALU.mult, op1=ALU.add),
                             reads=["rtk", "rta"], writes=["rta"])
                        P.op("dve", lambda e: e.tensor_single_scalar(out=tm, in_=ta, scalar=float(math.pi),
                                                                     op=ALU.is_gt), reads=["rta"], writes=["rtm"])
                        P.op("dve", lambda e: e.scalar_tensor_tensor(out=ta, in0=tm, scalar=-TWO_PI, in1=ta,
                                                                     op0=ALU.mult, op1=ALU.add),
                             reads=["rtm", "rta"], writes=["rta"])
                        P.op("dve", lambda e: e.tensor_single_scalar(out=tm, in_=ta, scalar=float(-math.pi),
                                                                     op=ALU.is_lt), reads=["rta"], writes=["rtm"])
                        P.op("dve", lambda e: e.scalar_tensor_tensor(out=ta, in0=tm, scalar=TWO_PI, in1=ta,
                                                                     op0=ALU.mult, op1=ALU.add),
                             reads=["rtm", "rta"], writes=["rta"])
                        P.op("act", lambda e: e.activation(out=tk, in_=ta, func=AF.Sin),
                             reads=["rta"], writes=["rtk"])
                        P.op("dve", lambda e: e.tensor_scalar(out=SIN2[:, tsl], in0=tk, scalar1=rtrp[:, 1:2], scalar2=None,
                                                              op0=ALU.mult), reads=["rtk", "rtrp"], writes=["rsin"])
                        P.op("dve", lambda e: e.tensor_single_scalar(out=tm, in_=ta, scalar=float(math.pi / 2),
                                                                     op=ALU.is_gt), reads=["rta"], writes=["rtm"])
                        P.op("dve", lambda e: e.scalar_tensor_tensor(out=ta, in0=tm, scalar=-TWO_PI, in1=ta,
                                                                     op0=ALU.mult, op1=ALU.add),
                             reads=["rtm", "rta"], writes=["rta"])
                        P.op("act", lambda e: e.activation(out=COS2[:, tsl], in_=ta, func=AF.Sin, bias=rtrp[:, 2:3],
                                                           scale=1.0), reads=["rta", "rtrp"], writes=["rcos"])
                    ta, tk, pint = ta_f, tk_f, pint_f
                    for which, (c_base, dstT, scl) in enumerate([(C_RTQ, qrT, 1.0), (C_RTK, krT, 0.125)]):
                        for hp in range(2):
                            wb, wkey = load_w(w_in_d[l, :, c_base + hp * 128:c_base + (hp + 1) * 128])
                            jsw = wbcount[0] % NWB
                            wbcount[0] += 1
                            wsw = wbp[jsw]
                            for hh in range(2):
                                o = hh * 64
                                P.op("pool", lambda e: e.tensor_copy(out=wsw[:, :, o:o + 32], in_=wb[:, :, o + 32:o + 64]),
                                     reads=[wkey], writes=[("wb", jsw)])
                                P.op("pool", lambda e: e.tensor_copy(out=wsw[:, :, o + 32:o + 64], in_=wb[:, :, o:o + 32]),
                                     reads=[wkey], writes=[("wb", jsw)])
                            for tc in range(NT):
                                tsl = slice(tc * TC, (tc + 1) * TC)
                                for k in range(8):
                                    P.op("pe", lambda e: e.matmul(ps[1][:, :], lhsT=wb[:, k, :], rhs=hT[:, k, tsl],
                                                                  start=(k == 0), stop=(k == 7)),
                                         reads=[wkey, ("hT", k)], writes=[PS[1]], inc=(k == 7))
                                for k in range(8):
                                    P.op("pe", lambda e: e.matmul(ps[2][:, :], lhsT=wsw[:, k, :], rhs=hT[:, k, tsl],
                                                                  start=(k == 0), stop=(k == 7)),
                                         reads=[("wb", jsw), ("hT", k)], writes=[PS[2]], inc=(k == 7))
                                P.op("dve", lambda e: e.scalar_tensor_tensor(out=ta[:, :], in0=ps[1][:, :], scalar=float(scl),
                                                                             in1=COS2[:, tsl], op0=ALU.mult, op1=ALU.mult),
                                     reads=[PS[1], "rcos"], writes=["rta"])
                                P.op("dve", lambda e: e.scalar_tensor_tensor(out=tk[:, :], in0=ps[2][:, :], scalar=float(scl),
                                                                             in1=SIN2[:, tsl], op0=ALU.mult, op1=ALU.mult),
                                     reads=[PS[2], "rsin"], writes=["rtk"])
                                P.op("dve", lambda e: e.tensor_tensor(out=dstT[:, hp, tsl], in0=ta[:, :], in1=tk[:, :], op=ALU.add),
                                     reads=["rta", "rtk"], writes=[("rq", which, hp)])
                    P.barrier()
                for hp in range(2):
                    with ExitStack() as ph2:
                        def sb2(name, shape, dt):
                            return ph2.enter_context(nc.sbuf_tensor(_u + ("s_" + name), list(shape), dt))
                        kdtm = sb2("rkd", [128, 16, 128], BF16)
                        vtm = sb2("rv", [128, 16, 128], BF16)
                        zsT = sb2("rz", [128, S], BF16)
                        qc = [sb2(f"rqc{i}", [128, 128], BF16) for i in range(2)]
                        scm = [sb2(f"rscm{i}", [128, 128], BF16) for i in range(2)]
                        Sf = sb2("rSf", [128, 128], F32)
                        Sb = sb2("rSb", [128, 128], BF16)
                        ob = sb2("rob", [128, TC], BF16)
                        sq = sb2("rsq", [128, TC], BF16)
                        rs_t = rstd
                        tt_t = sb2("rtt", [128, TC], F32)
                        for n in range(16):
                            nsl = slice(n * 128, (n + 1) * 128)
                            pk = ps[3][:, 0:64].bitcast(BF16)
                            P.op("pe", lambda e: e.transpose(pk, krT[:, hp, nsl], cst["ident"][:, :]),
                                 reads=[("rq", 1, hp), "k_ident"], writes=[PS[3]])
                            for hh in range(2):
                                cs = slice(hh * 64, (hh + 1) * 64)
                                h = 2 * hp + hh
                                P.op("dve", lambda e: e.tensor_scalar(out=kdtm[:, n, cs], in0=pk[:, cs], scalar1=rtsd[:, h:h + 1],
                                                                      scalar2=None, op0=ALU.mult),
                                     reads=[PS[3], "rtsd"], writes=[("rkd", n)])
                        for hh in range(2):
                            h = 2 * hp + hh
                            rs = slice(hh * 64, (hh + 1) * 64)
                            proj_tm(l, C_RTV + h * 128, 128, lambda tt, pap, pkey: P.op(
                                "dve", lambda e: e.tensor_copy(out=vtm[:, tt, :], in_=pap), reads=[pkey], writes=[("rv", tt)]))
                            proj_fm(l, C_RTZ + h * 128, 128, lambda tc, pap, pkey: P.op(
                                "act", lambda e: e.activation(out=zsT[:, tc * TC:(tc + 1) * TC], in_=pap, func=AF.Silu),
                                reads=[pkey], writes=["rz"]))
                            for n in range(16):
                                nsl = slice(n * 128, (n + 1) * 128)
                                b = n % 2
                                csl = slice((n % 4) * 128, (n % 4 + 1) * 128)
                                po = 5 + (n // 4) % 2
                                P.op("pe", lambda e: e.matmul(ps[4][:, 0:128], lhsT=krT[rs, hp, nsl], rhs=qrT[rs, hp, nsl],
                                                              start=True, stop=True),
                                     reads=[("rq", 1, hp), ("rq", 0, hp)], writes=[PS[4]])
                                P.op("dve", lambda e: e.tensor_tensor(out=scm[b][:, :], in0=ps[4][:, 0:128], in1=rtdt[:, h, :],
                                                                      op=ALU.mult),
                                     reads=[PS[4], "rtdt"], writes=[("rscm", b)])
                                if n > 0:
                                    P.op("dve", lambda e: e.tensor_tensor(out=qc[b][rs, :], in0=qrT[rs, hp, nsl], in1=rtcd[rs, hp, :],
                                                                          op=ALU.mult),
                                         reads=[("rq", 0, hp), "rtcd"], writes=[("rqc", b)])
                                P.op("pe", lambda e: e.matmul(ps[po][:, csl], lhsT=vtm[:, n, :], rhs=scm[b][:, :],
                                                              start=True, stop=(n == 0)),
                                     reads=[("rv", n), ("rscm", b)], writes=[PS[po]], inc=(n == 0))
                                if n > 0:
                                    P.op("pe", lambda e: e.matmul(ps[po][:, csl], lhsT=Sb[rs, :], rhs=qc[b][rs, :],
                                                                  start=False, stop=True),
                                         reads=["rSb", ("rqc", b)], writes=[PS[po]])
                                if n < 15:
                                    P.op("pe", lambda e: e.matmul(ps[7][rs, 0:128], lhsT=kdtm[:, n, rs], rhs=vtm[:, n, :],
                                                                  start=True, stop=True),
                                         reads=[("rkd", n), ("rv", n)], writes=[PS[7]])
                                    gch = float((1.0 - 2.0 ** (-5.0 - h)) ** 128)
                                    if n == 0:
                                        P.op("dve", lambda e: e.tensor_copy(out=Sf[rs, :], in_=ps[7][rs, 0:128]),
                                             reads=[PS[7]], writes=["rSf"])
                                    else:
                                        P.op("dve", lambda e: e.scalar_tensor_tensor(out=Sf[rs, :], in0=Sf[rs, :], scalar=gch,
                                                                                     in1=ps[7][rs, 0:128], op0=ALU.mult,
                                                                                     op1=ALU.add),
                                             reads=[PS[7], "rSf"], writes=["rSf"])
                                    P.op("act", lambda e: e.activation(out=Sb[rs, :], in_=Sf[rs, :], func=AF.Copy),
                                         reads=["rSf"], writes=["rSb"])
                                if n % 4 == 3:
                                    tc = n // 4
                                    tsl = slice(tc * TC, (tc + 1) * TC)
                                    P.op("act", lambda e: e.activation(out=ob[:, :], in_=ps[po][:, :], func=AF.Copy),
                                         reads=[PS[po]], writes=["rob"])
                                    P.op("pe", lambda e: e.matmul(ps[1][:, :], lhsT=rtcn[:, :], rhs=ob[:, :], start=True, stop=True),
                                         reads=["rtcn", "rob"], writes=[PS[1]])
                                    P.op("act", lambda e: e.activation(out=sq[:, :], in_=ps[1][:, :], func=AF.Square),
                                         reads=[PS[1]], writes=["rsq"])
                                    P.op("pe", lambda e: e.matmul(ps[2][:, :], lhsT=cst["ones"][:, :], rhs=sq[:, :],
                                                                  start=True, stop=True),
                                         reads=["k_ones", "rsq"], writes=[PS[2]])
                                    P.op("act", lambda e: e.activation(out=rs_t[:, :], in_=ps[2][:, :], func=AF.Ln,
                                                                       bias=rtrp[:, 3:4], scale=float(1.0 / 128.0)),
                                         reads=[PS[2], "rtrp"], writes=["rstd"])
                                    P.op("act", lambda e: e.activation(out=rs_t[:, :], in_=rs_t[:, :], func=AF.Exp, scale=-0.5),
                                         reads=["rstd"], writes=["rstd"])
                                    P.op("dve", lambda e: e.scalar_tensor_tensor(out=tt_t[:, :], in0=ps[1][:, :],
                                                                                 scalar=retg[:, l, h:h + 1], in1=rs_t[:, :],
                                                                                 op0=ALU.mult, op1=ALU.mult),
                                         reads=[PS[1], "retg", "rstd"], writes=["rtt"])
                                    P.op("dve", lambda e: e.tensor_tensor(out=obT[:, h, tsl], in0=tt_t[:, :], in1=zsT[:, tsl],
                                                                          op=ALU.mult),
                                         reads=["rtt", "rz"], writes=[("obT", h)])
                        P.barrier()

        def deltanet(l):
            identf, onesf = dnc[:, 0, :], dnc[:, 1, :]
            mincl, mstrict = dnc[0:64, 2, 0:64], dnc[0:64, 3, 0:64]
            H = slice(0, 64)
            with ExitStack() as ph0:
                def sb0(name, shape, dt):
                    return ph0.enter_context(nc.sbuf_tensor(_u + ("s_" + name), list(shape), dt))
                abraw = sb0("dab", [64, 32, 8], F32)
                rep = sb0("drep", [64, 128], F32)
                t1 = sb0("dt1", [64, 32, 4], F32)
                g_tm = sb0("dg", [64, 32, 4], F32)
                beta_tm = sb0("dbeta", [64, 32, 4], F32)
                gc_tm = sb0("dgc", [64, 32, 4], F32)
                egl = sb0("degl", [128, 32, 4], F32)
                bg_tm = sb0("dbg", [64, 32, 4], F32)
                ed_tm = sb0("ded", [64, 32, 4], F32)
                proj_tm(l, C_DNA, 8, lambda tt, pap, pkey: P.op(
                    "dve", lambda e: e.tensor_copy(out=abraw[:, tt, :], in_=pap), reads=[pkey], writes=["dab"]), ntok=64)
                fl = lambda t: t[:, :, :].rearrange("p a b -> p (a b)")
                P.dma(rep[:, :], dndt_d[l, 0:64, :], writes=["drep"], stream="c")
                P.op("dve", lambda e: e.tensor_tensor(out=t1[:, :, :], in0=abraw[:, :, 0:4],
                                                      in1=rep[:, :].rearrange("p (a b) -> p a b", b=4), op=ALU.add),
                     reads=["dab", "drep"], writes=["dt1"])
                P.op("act", lambda e: e.activation(out=t1[:, :, :], in_=t1[:, :, :], func=AF.Exp), reads=["dt1"], writes=["dt1"])
                P.op("act", lambda e: e.activation(out=t1[:, :, :], in_=t1[:, :, :], func=AF.Ln, bias=oneb[0:64, 0:1], scale=1.0),
                     reads=["dt1", "oneb"], writes=["dt1"])
                P.dma(rep[:, :], dnal_d[l, 0:64, :], writes=["drep"], stream="c")
                P.op("act", lambda e: e.activation(out=rep[:, :], in_=rep[:, :], func=AF.Exp), reads=["drep"], writes=["drep"])
                P.op("dve", lambda e: e.scalar_tensor_tensor(out=g_tm[:, :, :], in0=t1[:, :, :], scalar=-1.0,
                                                             in1=rep[:, :].rearrange("p (a b) -> p a b", b=4),
                                                             op0=ALU.mult, op1=ALU.mult),
                     reads=["dt1", "drep"], writes=["dg"])
                P.op("act", lambda e: e.activation(out=beta_tm[:, :, :], in_=abraw[:, :, 4:8], func=AF.Sigmoid),
                     reads=["dab"], writes=["dbeta"])
                P.op("pe", lambda e: e.matmul(ps[0][0:64, 0:128], lhsT=dnc[0:64, 2, 0:64], rhs=fl(g_tm), start=True, stop=True),
                     reads=["dnc", "dg"], writes=[PS[0]])
                P.op("dve", lambda e: e.tensor_copy(out=fl(gc_tm), in_=ps[0][0:64, 0:128]), reads=[PS[0]], writes=["dgc"])
                P.op("pe", lambda e: e.matmul(ps[0][:, 128:256], lhsT=dnc[0:64, 1, :], rhs=fl(g_tm), start=True, stop=True),
                     reads=["dnc", "dg"], writes=[PS[0]])
                P.op("dve", lambda e: e.tensor_tensor(out=fl(ed_tm), in0=ps[0][0:64, 128:256], in1=fl(gc_tm), op=ALU.subtract),
                     reads=[PS[0], "dgc"], writes=["ded"])
                P.op("act", lambda e: e.activation(out=fl(ed_tm), in_=fl(ed_tm), func=AF.Exp), reads=["ded"], writes=["ded"])
                P.op("act", lambda e: e.activation(out=fl(egl), in_=ps[0][:, 128:256], func=AF.Exp), reads=[PS[0]], writes=["degl"])
                P.op("act", lambda e: e.activation(out=fl(bg_tm), in_=fl(gc_tm), func=AF.Exp), reads=["dgc"], writes=["dbg"])
                P.op("dve", lambda e: e.tensor_tensor(out=fl(bg_tm), in0=fl(bg_tm), in1=fl(beta_tm), op=ALU.mult),
                     reads=["dbg", "dbeta"], writes=["dbg"])
                P.barrier()
                for h in range(4):
                    with ExitStack() as ph1:
                        def sb1(name, shape, dt):
                            return ph1.enter_context(nc.sbuf_tensor(_u + ("s_" + name), list(shape), dt))
                        qT, kT, vT, zsT = mT[:, 0, :], mT[:, 1, :], mT[:, 2, :], mT[:, 3, :]
                        xpad = mT[:, 4:7, :].rearrange("p a b -> p (a b)").bitcast(F32)[:, 0:S + 3]
                        with ExitStack() as ph2:
                            acc = ph2.enter_context(nc.sbuf_tensor(_u + "s_dacc", [128, S], F32))
                            P.op("dve", lambda e: e.memset(xpad[:, 0:3], 0.0), writes=["dxpad0"])
                            for xi, (c_base, dstT) in enumerate([(C_DNQ, qT), (C_DNK, kT), (C_DNV, vT)]):
                                proj_fm(l, c_base + h * 128, 128, lambda tc, pap, pkey: P.op(
                                    "act", lambda e: e.activation(out=xpad[:, 3 + tc * TC:3 + (tc + 1) * TC], in_=pap, func=AF.Copy),
                                    reads=[pkey], writes=[("dxpad", tc)]))
                                allx = [("dxpad", tc) for tc in range(NT)] + ["dxpad0"]
                                cw = dncw[:, l, xi * 4 + h, :]
                                P.op("dve", lambda e: e.tensor_scalar(out=acc[:, :], in0=xpad[:, 3:3 + S], scalar1=cw[:, 3:4],
                                                                      scalar2=None, op0=ALU.mult),
                                     reads=allx + ["dncw"], writes=["dacc"])
                                for j in range(3):
                                    P.op("dve", lambda e: e.scalar_tensor_tensor(out=acc[:, :], in0=xpad[:, j:j + S],
                                                                                 scalar=cw[:, j:j + 1], in1=acc[:, :],
                                                                                 op0=ALU.mult, op1=ALU.add),
                                         reads=allx + ["dncw", "dacc"], writes=["dacc"])
                                P.op("act", lambda e: e.activation(out=acc[:, :], in_=acc[:, :], func=AF.Silu),
                                     reads=["dacc"], writes=["dacc"])
                                if xi == 2:
                                    P.op("dve", lambda e: e.tensor_copy(out=vT, in_=acc[:, :]), reads=["dacc"], writes=["dvT"])
                                else:
                                    for tc in range(NT):
                                        tsl = slice(tc * TC, (tc + 1) * TC)
                                        P.op("act", lambda e: e.activation(out=sqt[:, 0, :], in_=acc[:, tsl], func=AF.Square),
                                             reads=["dacc"], writes=[("sqt", 0)])
                                        P.op("pe", lambda e: e.matmul(ps[0][:, :], lhsT=cst["ones"][:, :], rhs=sqt[:, 0, :],
                                                                      start=True, stop=True),
                                             reads=[("sqt", 0), "k_ones"], writes=[PS[0]])
                                        P.op("act", lambda e: e.activation(out=rstd[:, :], in_=ps[0][:, :], func=AF.Ln,
                                                                           bias=rtrp[:, 3:4], scale=1.0),
                                             reads=[PS[0], "rtrp"], writes=["rstd"])
                                        P.op("act", lambda e: e.activation(out=rstd[:, :], in_=rstd[:, :], func=AF.Exp, scale=-0.5),
                                             reads=["rstd"], writes=["rstd"])
                                        sc = float(128.0 ** -0.5) if xi == 0 else 1.0
                                        P.op("dve", lambda e: e.scalar_tensor_tensor(out=dstT[:, tsl], in0=acc[:, tsl], scalar=sc,
                                                                                     in1=rstd[:, :], op0=ALU.mult, op1=ALU.mult),
                                             reads=["dacc", "rstd"], writes=["dqT" if xi == 0 else "dkT"])
                            P.barrier()
                        proj_fm(l, C_DNZ + h * 128, 128, lambda tc, pap, pkey: P.op(
                            "act", lambda e: e.activation(out=zsT[:, tc * TC:(tc + 1) * TC], in_=pap, func=AF.Silu),
                            reads=[pkey], writes=["dz"]))
                        NB = 8
                        B3 = [64, NB, 64]
                        gb = sb1("dgb", [64, NB, 64], F32)
                        dd = sb1("ddd", [64, NB, 64], F32)
                        LT = sb1("dLT", [64, NB, 64], F32)
                        egcb = sb1("degcb", [128, NB * 64], F32)
                        bs = sb1("dbs", [64, NB, 64], BF16)
                        Pm = sb1("dPm", [64, NB, 64], BF16)
                        PTm = sb1("dPTm", [64, NB, 64], BF16)
                        Tt = [sb1(f"dTt{i}", [64, NB, 64], BF16) for i in range(2)]
                        kbg = mT[0:64, 7, 0:1024].rearrange("p (a b) -> p a b", b=128)
                        vb = mT[0:64, 7, 1024:2048].rearrange("p (a b) -> p a b", b=128)
                        aT = sb1("daT", [64, NB, 64], BF16)
                        aTt = sb1("daTt", [64, NB, 64], BF16)
                        qg = sb1("dqg", [128, NB * 64], BF16)
                        u_sb = sb1("du", [64, NB, 128], BF16)
                        wT = sb1("dwT", [128, NB, 64], BF16)
                        kd = sb1("dkd", [64, NB, 128], BF16)
                        vnew = [sb1(f"dvnew{i}", [64, 128], BF16) for i in range(2)]
                        Sf = sb1("dSf", [128, 128], F32)
                        Sb = sb1("dSb", [128, 128], BF16)
                        nsq = sb1("dnsq", [128, TC], BF16)
                        ntmp = sb1("dntmp", [128, TC], F32)
                        identb = cst["ident"]
                        id64 = identb[0:64, 0:64]
                        fl3 = lambda t: t[:, :, :].rearrange("p a b -> p (a b)")
                        p3 = lambda i, w=64: ps[i][H, 0:NB * w].rearrange("p (a b) -> p a b", b=w)
                        bcast = lambda ap2: ap2.unsqueeze(1).broadcast_to(B3)

                        def stage_a(bt):
                            n0 = bt * NB
                            bsl = slice(n0 * 64, (n0 + NB) * 64)
                            nsl = slice(n0, n0 + NB)
                            sc_g = g_tm[:, nsl, h:h + 1].broadcast_to(B3)
                            sc_b = beta_tm[:, nsl, h:h + 1].broadcast_to(B3)
                            sc_gc = gc_tm[:, nsl, h:h + 1].broadcast_to(B3)
                            P.op("dve", lambda e: e.tensor_tensor(out=gb[:, :, :], in0=bcast(mincl), in1=sc_g, op=ALU.mult),
                                 reads=["dnc", "dg"], writes=["dgb"])
                            P.op("pe", lambda e: e.matmul(ps[0][:, :], lhsT=onesf[0:64, :], rhs=fl3(gb), start=True, stop=True),
                                 reads=["dnc", "dgb"], writes=[PS[0]])
                            yield
                            P.op("dve", lambda e: e.tensor_tensor(out=gb[:, :, :], in0=bcast(identf[0:64, 0:64]), in1=sc_b, op=ALU.mult),
                                 reads=["dnc", "dbeta"], writes=["dgb"])
                            P.op("pe", lambda e: e.matmul(ps[1][H, :], lhsT=onesf[0:64, 0:64], rhs=fl3(gb), start=True, stop=True),
                                 reads=["dnc", "dgb"], writes=[PS[1]])
                            yield
                            P.op("act", lambda e: e.activation(out=egcb[:, :], in_=ps[0][:, :], func=AF.Exp),
                                 reads=[PS[0]], writes=["degcb"])
                            P.op("dve", lambda e: e.tensor_tensor(out=dd[:, :, :], in0=p3(0), in1=sc_gc, op=ALU.subtract),
                                 reads=[PS[0], "dgc", "degcb"], writes=["ddd"])
                            yield
                            P.op("act", lambda e: e.activation(out=fl3(dd), in_=fl3(dd), func=AF.Exp), reads=["ddd"], writes=["ddd"])
                            P.op("dve", lambda e: e.tensor_tensor(out=bs[:, :, :], in0=p3(1), in1=bcast(mstrict), op=ALU.mult),
                                 reads=[PS[1], "dnc"], writes=["dbs"])
                            yield
                            P.op("dve", lambda e: e.scalar_tensor_tensor(out=dd[:, :, :], in0=dd[:, :, :], scalar=1.0, in1=bcast(mincl),
                                                                         op0=ALU.min, op1=ALU.mult),
                                 reads=["ddd", "dnc"], writes=["ddd"])
                            for i in range(NB):
                                csl = slice((n0 + i) * 64, (n0 + i + 1) * 64)
                                P.op("pe", lambda e: e.matmul(ps[2][H, i * 64:(i + 1) * 64], lhsT=kT[:, csl], rhs=kT[:, csl], start=True, stop=True),
                                     reads=["dkT"], writes=[PS[2]], inc=(i == NB - 1))
                            for i in range(NB):
                                csl = slice((n0 + i) * 64, (n0 + i + 1) * 64)
                                P.op("pe", lambda e: e.matmul(ps[3][H, i * 64:(i + 1) * 64], lhsT=kT[:, csl], rhs=qT[:, csl], start=True, stop=True),
                                     reads=["dkT", "dqT"], writes=[PS[3]], inc=(i == NB - 1))
                            yield
                            P.op("dve", lambda e: e.tensor_tensor(out=LT[:, :, :], in0=p3(2), in1=dd[:, :, :], op=ALU.mult),
                                 reads=[PS[2], "ddd"], writes=["dLT"])
                            P.op("dve", lambda e: e.tensor_tensor(out=aTt[:, :, :], in0=p3(3), in1=dd[:, :, :], op=ALU.mult),
                                 reads=[PS[3], "ddd"], writes=["daTt"])
                            yield
                            P.op("dve", lambda e: e.scalar_tensor_tensor(out=Pm[:, :, :], in0=LT[:, :, :], scalar=-1.0, in1=bs[:, :, :],
                                                                         op0=ALU.mult, op1=ALU.mult),
                                 reads=["dLT", "dbs"], writes=["dPm"])
                            yield
                            ptv = ps[4][H, 0:NB * 32].bitcast(BF16)
                            for i in range(NB):
                                P.op("pe", lambda e: e.transpose(ptv[:, i * 64:(i + 1) * 64], Pm[:, i, :], id64),
                                     reads=["dPm", "k_ident"], writes=[PS[4]], inc=(i == NB - 1))
                            P.op("dve", lambda e: e.tensor_tensor(out=Tt[0][:, :, :], in0=Pm[:, :, :], in1=bcast(id64), op=ALU.add),
                                 reads=["dPm", "k_ident"], writes=[("dTt", 0)])
                            yield
                            P.op("act", lambda e: e.activation(out=fl3(PTm), in_=ptv, func=AF.Copy), reads=[PS[4]], writes=["dPTm"])
                            yield
                            cur = 0
                            for lev in range(1, 6):
                                if lev < 5:
                                    for i in range(NB):
                                        P.op("pe", lambda e: e.matmul(ps[2][H, i * 64:(i + 1) * 64], lhsT=PTm[:, i, :], rhs=Pm[:, i, :],
                                                                      start=True, stop=True),
                                             reads=["dPm", "dPTm"], writes=[PS[2]], inc=(i == NB - 1))
                                for i in range(NB):
                                    P.op("pe", lambda e: e.matmul(ps[3][H, i * 64:(i + 1) * 64], lhsT=Pm[:, i, :], rhs=PTm[:, i, :],
                                                                  start=True, stop=True),
                                         reads=["dPm", "dPTm"], writes=[PS[3]], inc=(i == NB - 1))
                                yield
                                if lev < 5:
                                    P.op("act", lambda e: e.activation(out=fl3(Pm), in_=ps[2][H, :], func=AF.Copy), reads=[PS[2]], writes=["dPm"])
                                P.op("dve", lambda e: e.tensor_copy(out=fl3(PTm), in_=ps[3][H, :]), reads=[PS[3]], writes=["dPTm"])
                                yield
                                for i in range(NB):
                                    P.op("pe", lambda e: e.matmul(ps[4][H, i * 64:(i + 1) * 64], lhsT=PTm[:, i, :], rhs=Tt[cur][:, i, :],
                                                                  start=True, stop=False),
                                         reads=["dPTm", ("dTt", cur)], writes=[PS[4]], inc=False)
                                    P.op("pe", lambda e: e.matmul(ps[4][H, i * 64:(i + 1) * 64], lhsT=id64, rhs=Tt[cur][:, i, :],
                                                                  start=False, stop=True),
                                         reads=["k_ident", ("dTt", cur)], writes=[PS[4]], inc=(i == NB - 1))
                                yield
                                cur = 1 - cur
                                P.op("dve", lambda e: e.tensor_copy(out=fl3(Tt[cur]), in_=ps[4][H, :]), reads=[PS[4]], writes=[("dTt", cur)])
                                yield
                            kv = ps[0][H, :].bitcast(BF16)
                            vv = ps[1][H, :].bitcast(BF16)
                            for i in range(NB):
                                csl = slice((n0 + i) * 64, (n0 + i + 1) * 64)
                                P.op("pe", lambda e: e.transpose(kv[:, i * 128:(i + 1) * 128], kT[:, csl], identb[:, :]),
                                     reads=["dkT", "k_ident"], writes=[PS[0]], inc=(i == NB - 1))
                            for i in range(NB):
                                csl = slice((n0 + i) * 64, (n0 + i + 1) * 64)
                                P.op("pe", lambda e: e.transpose(vv[:, i * 128:(i + 1) * 128], vT[:, csl], identb[:, :]),
                                     reads=["dvT", "k_ident"], writes=[PS[1]], inc=(i == NB - 1))
                            yield
                            B3w = [64, NB, 128]
                            kv3 = kv.rearrange("p (a b) -> p a b", b=128)
                            vv3 = vv.rearrange("p (a b) -> p a b", b=128)
                            P.op("dve", lambda e: e.tensor_tensor(out=kbg[:, :, :], in0=kv3, in1=bg_tm[:, nsl, h:h + 1].broadcast_to(B3w), op=ALU.mult),
                                 reads=[PS[0], "dbg"], writes=["dkbg"])
                            P.op("dve", lambda e: e.tensor_tensor(out=vb[:, :, :], in0=vv3, in1=beta_tm[:, nsl, h:h + 1].broadcast_to(B3w), op=ALU.mult),
                                 reads=[PS[1], "dbeta"], writes=["dvb"])
                            yield
                            for i in range(NB):
                                pb = 2 + i // 4
                                P.op("pe", lambda e: e.matmul(ps[pb][H, (i % 4) * 128:(i % 4 + 1) * 128], lhsT=Tt[cur][:, i, :], rhs=vb[:, i, :],
                                                              start=True, stop=True),
                                     reads=[("dTt", cur), "dvb"], writes=[PS[pb]], inc=(i % 4 == 3))
                            for i in range(NB):
                                P.op("pe", lambda e: e.matmul(ps[4][:, i * 64:(i + 1) * 64], lhsT=kbg[:, i, :], rhs=Tt[cur][:, i, :],
                                                              start=True, stop=True),
                                     reads=[("dTt", cur), "dkbg"], writes=[PS[4]], inc=(i == NB - 1))
                            yield "OUT"
                            P.op("dve", lambda e: e.tensor_tensor(out=kd[:, :, :], in0=kv3, in1=ed_tm[:, nsl, h:h + 1].broadcast_to(B3w), op=ALU.mult),
                                 reads=[PS[0], "ded"], writes=["dkd"])
                            P.op("dve", lambda e: e.tensor_tensor(out=qg[:, :], in0=qT[:, bsl], in1=egcb[:, :], op=ALU.mult),
                                 reads=["dqT", "degcb"], writes=["dqg"])
                            P.op("pool", lambda e: e.tensor_copy(out=aT[:, :, :], in_=aTt[:, :, :]), reads=["daTt"], writes=["daT"])
                            for hb in range(2):
                                P.op("act", lambda e: e.activation(out=u_sb[:, hb * 4:(hb + 1) * 4, :].rearrange("p a b -> p (a b)"),
                                                                   in_=ps[2 + hb][H, :], func=AF.Copy),
                                     reads=[PS[2 + hb]], writes=["du"])
                            P.op("dve", lambda e: e.tensor_copy(out=wT[:, :, :].rearrange("p a b -> p (a b)"), in_=ps[4][:, :]),
                                 reads=[PS[4]], writes=["dwT"])
                            yield

                        def stage_b(bt):
                            n0 = bt * NB
                            bsl = slice(n0 * 64, (n0 + NB) * 64)
                            for i in range(NB):
                                n = n0 + i
                                vn = vnew[n % 2]
                                kvn = ("dvnew", n % 2)
                                if n == 0:
                                    P.op("dve", lambda e: e.tensor_copy(out=vn[:, :], in_=u_sb[:, i, :]), reads=["du"], writes=[kvn])
                                else:
                                    P.op("pe", lambda e: e.matmul(ps[5][H, 0:128], lhsT=wT[:, i, :], rhs=Sb[:, :], start=True, stop=True),
                                         reads=["dwT", "dSb"], writes=[PS[5]])
                                    yield
                                    P.op("dve", lambda e: e.tensor_tensor(out=vn[:, :], in0=u_sb[:, i, :], in1=ps[5][H, 0:128], op=ALU.subtract),
                                         reads=["du", PS[5]], writes=[kvn])
                                yield
                                osl = slice(i * 64, (i + 1) * 64)
                                if n < 31:
                                    P.op("pe", lambda e: e.matmul(ps[6][:, 0:128], lhsT=kd[:, i, :], rhs=vn[:, :], start=True, stop=True),
                                         reads=["dkd", kvn], writes=[PS[6]])
                                if n > 0:
                                    P.op("pe", lambda e: e.matmul(ps[7][:, osl], lhsT=Sb[:, :], rhs=qg[:, osl], start=True, stop=False),
                                         reads=["dSb", "dqg"], writes=[PS[7]], inc=False)
                                P.op("pe", lambda e: e.matmul(ps[7][:, osl], lhsT=vn[:, :], rhs=aT[:, i, :], start=(n == 0), stop=True),
                                     reads=[kvn, "daT"], writes=[PS[7]])
                                yield
                                if n < 31:
                                    if n == 0:
                                        P.op("dve", lambda e: e.tensor_copy(out=Sf[:, :], in_=ps[6][:, 0:128]), reads=[PS[6]], writes=["dSf"])
                                    else:
                                        P.op("dve", lambda e: e.scalar_tensor_tensor(out=Sf[:, :], in0=Sf[:, :], scalar=egl[:, n, h:h + 1],
                                                                                     in1=ps[6][:, 0:128], op0=ALU.mult, op1=ALU.add),
                                             reads=[PS[6], "dSf", "degl"], writes=["dSf"])
                                    yield
                                    P.op("act", lambda e: e.activation(out=Sb[:, :], in_=Sf[:, :], func=AF.Copy), reads=["dSf"], writes=["dSb"])
                                    yield
                            P.op("act", lambda e: e.activation(out=nsq[:, :], in_=ps[7][:, :], func=AF.Square),
                                 reads=[PS[7]], writes=["dnsq"])
                            yield
                            P.op("pe", lambda e: e.matmul(ps[5][:, :], lhsT=cst["ones"][:, :], rhs=nsq[:, :], start=True, stop=True),
                                 reads=["k_ones", "dnsq"], writes=[PS[5]])
                            yield
                            P.op("act", lambda e: e.activation(out=rstd[:, :], in_=ps[5][:, :], func=AF.Ln, bias=rtrp[:, 3:4],
                                                               scale=float(1.0 / 128.0)), reads=[PS[5], "rtrp"], writes=["rstd"])
                            P.op("act", lambda e: e.activation(out=rstd[:, :], in_=rstd[:, :], func=AF.Exp, scale=-0.5),
                                 reads=["rstd"], writes=["rstd"])
                            yield
                            P.op("dve", lambda e: e.scalar_tensor_tensor(out=ntmp[:, :], in0=ps[7][:, :], scalar=dng[:, l:l + 1],
                                                                         in1=rstd[:, :], op0=ALU.mult, op1=ALU.mult),
                                 reads=[PS[7], "dng", "rstd"], writes=["dntmp"])
                            P.op("dve", lambda e: e.tensor_tensor(out=obT[:, h, bsl], in0=ntmp[:, :], in1=zsT[:, bsl], op=ALU.mult),
                                 reads=["dntmp", "dz"], writes=[("obT", h)])
                            yield

                        import os as _os2
                        _stop = int(_os2.environ.get("DN_STOP", "-1"))
                        if _stop >= 0:
                            g_ = stage_a(0)
                            for _ in range(_stop):
                                next(g_)
                        else:
                            def drive(b_gen, a_gen):
                                a_wait, a_done, b_done = False, a_gen is None, b_gen is None
                                while not (b_done and (a_done or a_wait)):
                                    if not b_done:
                                        try:
                                            next(b_gen)
                                        except StopIteration:
                                            b_done = True
                                    if not a_done and not a_wait:
                                        try:
                                            if next(a_gen) == "OUT":
                                                a_wait = True
                                        except StopIteration:
                                            a_done = True
                                if not a_done:
                                    for _ in a_gen:
                                        pass

                            drive(None, stage_a(0))
                            for bt in range(4):
                                drive(stage_b(bt), stage_a(bt + 1) if bt + 1 < 4 else None)
                        P.barrier()

        def dump(nm, tile, nchunks, keyname, nokey=False):
            if nokey:
                P.barrier()
            with nc.sbuf_tensor(_u + "s_dbgf_" + nm, [128, S], F32) as dbgf:
                for k in range(nchunks):
                    P.op("dve", lambda e: e.tensor_copy(out=dbgf[:, :], in_=tile[:, k, :]),
                         reads=[(keyname, k)], writes=["dbgf"])
                    P.dma(dbg_d[nm][k * 128:(k + 1) * 128, :], dbgf[:, :], reads=["dbgf"], writes=["dbg_" + nm], stream="o")
                P.barrier()

        for l in range(n_layers):
            norm_to(lambda k, tc: (hT[:, k, tc * TC:(tc + 1) * TC], ("hT", k)), normg[:, l, :], l)
            if debug and l == 0:
                with nc.sbuf_tensor(_u + "dbgf", [128, S], F32) as dbgf:
                    for k in range(8):
                        P.op("dve", lambda e, k=k: e.tensor_copy(out=dbgf[:, :], in_=hT[:, k, :]),
                             reads=[("hT", k)], writes=["dbgf"])
                        P.dma(dbg_d["hT"][k * 128:(k + 1) * 128, :], dbgf[:, :], reads=["dbgf"], writes=["dbg_hT"],
                              stream="o")
                    P.barrier()

            def mem_attention(l):
                ph = ExitStack()
                def sbp(name, shape, dt):
                    return ph.enter_context(nc.sbuf_tensor(_u + "s_" + name, list(shape), dt))
                memT = sbp("memT", [128, 8, MEM_LEN], F32)
                memn = sbp("memn", [128, 8, MEM_LEN], BF16)
                kmT = sbp("kmT", [128, 2, MEM_LEN], BF16)
                vm = sbp("vm", [128, 2, 256], BF16)
                qmT = sbp("qmT", [128, 2, S], BF16)
                pT = [sbp(f"pT{i}", [128, TC], BF16) for i in range(2)]
                rden = sbp("rden", [128, TC], F32)
                for k in range(8):
                    P.dma(memT[:, k, :], memT_d[k * 128:(k + 1) * 128, :], writes=[("memT", k)], stream="x")
                P.op("dve", lambda e: e.tensor_scalar(out=g32[:, :], in0=memg[:, l, :], scalar1=float(math.sqrt(D)),
                                                      scalar2=None, op0=ALU.mult), reads=["memg"], writes=["g32"])
                rms_stats(memT, [("memT", k) for k in range(8)], 8, MEM_LEN, 0, 0, sqt, "sqt", rstd, "rstd", None)
                for k in range(8):
                    P.op("dve", lambda e, k=k: e.scalar_tensor_tensor(
                        out=memn[:, k, :], in0=memT[:, k, :], scalar=g32[:, k:k + 1], in1=rstd[:, 0:MEM_LEN],
                        op0=ALU.mult, op1=ALU.mult), reads=[("memT", k), "g32", "rstd"], writes=[("memn", k)])
                for ec in range(2):
                    wb, wkey = load_w(w_kv_d[l, :, ec * 128:(ec + 1) * 128])
                    for k in range(8):
                        P.op("pe", lambda e, k=k, wb=wb: e.matmul(ps[1][:, 0:MEM_LEN], lhsT=wb[:, k, :], rhs=memn[:, k, :],
                                                                  start=(k == 0), stop=(k == 7)),
                             reads=[wkey, ("memn", k)], writes=[PS[1]], inc=(k == 7))
                    P.op("dve", lambda e, ec=ec: e.tensor_copy(out=kmT[:, ec, :], in_=ps[1][:, 0:MEM_LEN]),
                         reads=[PS[1]], writes=[("kmT", ec)])
                for vc in range(2):
                    wb, wkey = load_w(w_kv_d[l, :, 256 + vc * 128:256 + (vc + 1) * 128])
                    for mt in range(2):
                        for k in range(8):
                            P.op("pe", lambda e, k=k, wb=wb, mt=mt: e.matmul(
                                ps[2][:, 0:128], lhsT=memn[:, k, mt * 128:(mt + 1) * 128], rhs=wb[:, k, :],
                                start=(k == 0), stop=(k == 7)),
                                reads=[wkey, ("memn", k)], writes=[PS[2]], inc=(k == 7))
                        P.op("dve", lambda e, mt=mt, vc=vc: e.tensor_copy(out=vm[:, mt, vc * 128:(vc + 1) * 128],
                                                                          in_=ps[2][:, 0:128]),
                             reads=[PS[2]], writes=[("vm", mt)])
                for ec in range(2):
                    def ev(tc, pap, pkey, ec=ec):
                        P.op("act", lambda e: e.activation(out=qmT[:, ec, tc * TC:(tc + 1) * TC], in_=pap,
                                                           func=AF.Copy, scale=0.125),
                             reads=[pkey], writes=[("qmT", ec)])
                    proj_fm(l, C_MQ + ec * 128, 128, ev)
                for h in range(4):
                    ec, r0 = h // 2, (h % 2) * 64
                    for tc in range(NT):
                        tsl = slice(tc * TC, (tc + 1) * TC)
                        for mb in range(2):
                            pi = 3 + mb
                            P.op("pe", lambda e, mb=mb, pi=pi: e.matmul(
                                ps[pi][:, :], lhsT=kmT[r0:r0 + 64, ec, mb * 128:(mb + 1) * 128],
                                rhs=qmT[r0:r0 + 64, ec, tsl], start=True, stop=True),
                                reads=[("kmT", ec), ("qmT", ec)], writes=[PS[pi]])
                            P.op("act", lambda e, mb=mb, pi=pi: e.activation(out=pT[mb][:, :], in_=ps[pi][:, :],
                                                                             func=AF.Exp),
                                 reads=[PS[pi]], writes=[("pT", mb)])
                        for mb in range(2):
                            P.op("pe", lambda e, mb=mb: e.matmul(
                                ps[5][r0:r0 + 64, :], lhsT=vm[:, mb, h * 64:(h + 1) * 64], rhs=pT[mb][:, :],
                                start=(mb == 0), stop=(mb == 1)),
                                reads=[("vm", mb), ("pT", mb)], writes=[PS[5]], inc=(mb == 1))
                        for mb in range(2):
                            P.op("pe", lambda e, mb=mb: e.matmul(
                                ps[6][r0:r0 + 64, :], lhsT=cst["ones"][:, 0:64], rhs=pT[mb][:, :],
                                start=(mb == 0), stop=(mb == 1)),
                                reads=["k_ones", ("pT", mb)], writes=[PS[6]], inc=(mb == 1))
                        P.op("dve", lambda e: e.reciprocal(out=rden[r0:r0 + 64, :], in_=ps[6][r0:r0 + 64, :]),
                             reads=[PS[6]], writes=["rden"])
                        P.op("dve", lambda e, tsl=tsl: e.tensor_tensor(out=obT[r0:r0 + 64, ec, tsl], in0=ps[5][r0:r0 + 64, :],
                                                                      in1=rden[r0:r0 + 64, :], op=ALU.mult),
                             reads=[PS[5], "rden"], writes=[("obT", ec)])
                P.barrier()
                ph.close()

            def merge(br, nwc, first):
                with ExitStack() as ph:
                    sg = [ph.enter_context(nc.sbuf_tensor(_u + f"sg{i}", [128, TC], F32)) for i in range(2)]
                    tmp = [ph.enter_context(nc.sbuf_tensor(_u + f"mtmp{i}", [128, TC], F32)) for i in range(2)]
                    it = 0
                    for dc in range(8):
                        wg, wgk = load_w(w_in_d[l, :, C_G + br * D + dc * 128:C_G + br * D + (dc + 1) * 128])
                        wr, wrk = load_w(w_br_d[br][l, :, dc * 128:(dc + 1) * 128], rows=nwc)
                        for tc in range(NT):
                            tsl = slice(tc * TC, (tc + 1) * TC)
                            b = it % 2
                            it += 1
                            pg, pp = 1 + b, 3 + b
                            for k in range(8):
                                P.op("pe", lambda e, k=k, pg=pg, tsl=tsl, wg=wg: e.matmul(
                                    ps[pg][:, :], lhsT=wg[:, k, :], rhs=hT[:, k, tsl], start=(k == 0), stop=(k == 7)),
                                    reads=[wgk, ("hT", k)], writes=[PS[pg]], inc=(k == 7))
                            for k in range(nwc):
                                P.op("pe", lambda e, k=k, pp=pp, tsl=tsl, wr=wr: e.matmul(
                                    ps[pp][:, :], lhsT=wr[:, k, :], rhs=obT[:, k, tsl], start=(k == 0), stop=(k == nwc - 1)),
                                    reads=[wrk, ("obT", k)], writes=[PS[pp]], inc=(k == nwc - 1))
                            P.op("act", lambda e, b=b, pg=pg, dc=dc: e.activation(
                                out=sg[b][:, :], in_=ps[pg][:, :], func=AF.Sigmoid,
                                bias=bgate[:, l, br * 8 + dc:br * 8 + dc + 1], scale=1.0),
                                reads=[PS[pg], "bgate"], writes=[("sg", b)])
                            if first:
                                P.op("dve", lambda e, b=b, pp=pp, dc=dc, tsl=tsl: e.tensor_tensor(
                                    out=mT[:, dc, tsl], in0=ps[pp][:, :], in1=sg[b][:, :], op=ALU.mult),
                                    reads=[PS[pp], ("sg", b)], writes=[("mT", dc, tc)])
                            else:
                                P.op("dve", lambda e, b=b, pp=pp: e.tensor_tensor(
                                    out=tmp[b][:, :], in0=ps[pp][:, :], in1=sg[b][:, :], op=ALU.mult),
                                    reads=[PS[pp], ("sg", b)], writes=[("mtmp", b)])
                                P.op("dve", lambda e, b=b, dc=dc, tsl=tsl: e.tensor_tensor(
                                    out=mT[:, dc, tsl], in0=mT[:, dc, tsl], in1=tmp[b][:, :], op=ALU.add),
                                    reads=[("mtmp", b), ("mT", dc, tc)], writes=[("mT", dc, tc)])
                    P.barrier()

            if 1 in branches:
                deltanet(l)
                if debug and l == 0:
                    dump('odn', obT, 4, 'obT')
                merge(1, 4, True)
            mem_attention(l)
            if debug and l == 0:
                dump('omem', obT, 2, 'obT')
            merge(3, 2, 1 not in branches)
            if 0 in branches:
                sb_attention(l)
                if debug and l == 0:
                    dump('osb', obT, 4, 'obT')
                merge(0, 4, False)
            if 2 in branches:
                retention(l)
                if debug and l == 0:
                    dump('ort', obT, 4, 'obT')
                merge(2, 4, False)

            for ec in range(8):
                wo, wok = load_w(w_out_d[l, :, ec * 128:(ec + 1) * 128])
                for tc in range(NT):
                    tsl = slice(tc * TC, (tc + 1) * TC)
                    pi = 1 + tc % 2
                    for dc in range(8):
                        P.op("pe", lambda e, dc=dc, pi=pi, tsl=tsl, wo=wo: e.matmul(
                            ps[pi][:, :], lhsT=wo[:, dc, :], rhs=mT[:, dc, tsl], start=(dc == 0), stop=(dc == 7)),
                            reads=[wok, ("mT", dc, tc)], writes=[PS[pi]], inc=(dc == 7))
                    P.op("dve", lambda e, ec=ec, pi=pi, tsl=tsl: e.tensor_tensor(
                        out=xT[:, ec, tsl], in0=xT[:, ec, tsl], in1=ps[pi][:, :], op=ALU.add),
                        reads=[PS[pi], ("xT", ec)], writes=[("xT", ec)])
            P.barrier()

        with ExitStack() as ph:
            ot = [ph.enter_context(nc.sbuf_tensor(_u + f"ot{i}", [128, TC], F32)) for i in range(2)]
            cnt = [0]

            def dst(k, tc):
                b = cnt[0] % 2
                cnt[0] += 1
                return ot[b][:, :], ("ot", b)
            P.op("dve", lambda e: e.tensor_scalar(out=g32[:, :], in0=fing[:, :], scalar1=float(math.sqrt(D)), scalar2=None,
                                                  op0=ALU.mult), reads=["fing"], writes=["g32"])
            for tc in range(NT):
                rms_stats(xT, [("xT", k) for k in range(8)], 8, TC, tc * TC, 0, sqt, "sqt", rstd, "rstd", None)
                for k in range(8):
                    ap, key = dst(k, tc)
                    P.op("dve", lambda e, k=k, tc=tc, ap=ap: e.scalar_tensor_tensor(
                        out=ap, in0=xT[:, k, tc * TC:(tc + 1) * TC], scalar=g32[:, k:k + 1], in1=rstd[:, :],
                        op0=ALU.mult, op1=ALU.mult),
                        reads=[("xT", k), "g32", "rstd"], writes=[key])
                    P.dma(outT_d[k * 128:(k + 1) * 128, tc * TC:(tc + 1) * TC], ap, reads=[key], writes=["outT"],
                          stream="o")
            P.finish(["outT", "dbg_hT", "dbg_omem", "dbg_osb", "dbg_odn", "dbg_ort"])
            P.barrier()
        P.emit()
        print("instructions recorded:", P.nins, {n: len(P.q[n]) for n in P.names})
    return nc


_NC_CACHE = {}


def _prep_inputs(inputs, b):
    f = np.float32
    m = {}
    m["xT"] = np.ascontiguousarray(inputs["x"][b].T.astype(f))
    m["memT"] = np.ascontiguousarray(inputs["mem"][b].T.astype(f))
    m["w_in"] = np.ascontiguousarray(inputs["w_in"], dtype=f)
    m["w_mem_kv"] = np.ascontiguousarray(inputs["w_mem_kv"], dtype=f)
    for n in ["w_br_sb", "w_br_dn", "w_br_ret", "w_br_mem", "w_out"]:
        m[n] = np.ascontiguousarray(inputs[n], dtype=f)
    m["norm_g"] = np.ascontiguousarray(inputs["norm_g"].reshape(DEPTH, 8, 128).transpose(0, 2, 1), dtype=f)
    m["mem_norm_g"] = np.ascontiguousarray(inputs["mem_norm_g"].reshape(DEPTH, 8, 128).transpose(0, 2, 1), dtype=f)
    m["b_gate"] = np.ascontiguousarray(inputs["b_gate"].reshape(DEPTH, 32, 128).transpose(0, 2, 1), dtype=f)
    m["final_norm_g"] = np.ascontiguousarray(inputs["final_norm_g"].reshape(8, 128).T, dtype=f)
    for n, v in _consts().items():
        m["c_" + n] = v
    for n, v in _rt_consts().items():
        m["c_" + n] = v
    m["c_dn"] = _dn_consts()
    m["dn_conv_w"] = np.ascontiguousarray(inputs["dn_conv_w"].reshape(DEPTH, 4, 12, 128).transpose(0, 3, 2, 1), dtype=f)
    m["dn_norm_g"] = np.ascontiguousarray(inputs["dn_norm_g"].T, dtype=f)
    m["dn_alog"] = np.ascontiguousarray(np.broadcast_to(np.tile(inputs["dn_a_log"], (1, 32))[:, None, :], (DEPTH, 128, 128)), dtype=f)
    m["dn_dtb"] = np.ascontiguousarray(np.broadcast_to(np.tile(inputs["dn_dt_bias"], (1, 32))[:, None, :], (DEPTH, 128, 128)), dtype=f)
    m["pos"] = np.ascontiguousarray(np.broadcast_to(inputs["positions"][b].astype(np.int32)[None, :], (128, S)))
    m["ret_norm_g"] = np.ascontiguousarray(inputs["ret_norm_g"].reshape(DEPTH, 4, 128).transpose(0, 2, 1), dtype=f)
    return m


def kernel(**inputs):
    inputs = {k: np.asarray(v) for k, v in inputs.items()}
    if "nc" not in _NC_CACHE:
        _NC_CACHE["nc"] = build()
    nc = _NC_CACHE["nc"]
    in_maps = [_prep_inputs(inputs, b) for b in range(8)]
    res = run_bass_kernel_spmd(nc, in_maps, core_ids=list(range(8)))
    out = np.stack([np.ascontiguousarray(res.results[b]["outT"].T) for b in range(8)], axis=0)
    return out.astype(np.float32)
```

```python
import math
from contextlib import ExitStack
import numpy as np
import concourse.bass as bass
import concourse.mybir as mybir
from concourse.bass_utils import run_bass_kernel_spmd

F32 = mybir.dt.float32
BF16 = mybir.dt.bfloat16
I32 = mybir.dt.int32
AF = mybir.ActivationFunctionType
ALU = mybir.AluOpType

D = 1024
S = 2048
DEPTH = 2
MEM_LEN = 256
EPS = 1e-6
IN_COLS = 9992
C_SBQ, C_SBK, C_SBV, C_SBZ = 0, 512, 1024, 1536
C_DNQ, C_DNK, C_DNV, C_DNZ, C_DNA, C_DNB = 2048, 2560, 3072, 3584, 4096, 4100
C_RTQ, C_RTK, C_RTV, C_RTZ = 4104, 4360, 4616, 5128
C_MQ = 5640
C_G = 5896
NT = 4
TC = 512


class _Uniq:
    def __init__(self):
        self.n = 0

    def __add__(self, name):
        self.n += 1
        return f"{name}_{self.n}"


class _Rec:
    def __getattr__(self, name):
        return lambda *a, **k: (name, a, k)


_REC = _Rec()


class Prog:
    LIMIT = 30000

    def __init__(self, nc, stack):
        self.nc = nc
        self.stack = stack
        self.names = ["pe", "act", "dve", "pool", "sp"]
        self.q = {n: [] for n in self.names}
        self.cnt = {n: 0 for n in self.names}
        self.sems = {n: [] for n in self.names}
        self.seen = {n: {} for n in self.names}
        self.bufs = {}
        self.dma_sem = {}
        self.dma_cnt = {}
        self.same_sync = True
        self.dma_i = 0
        self.nins = 0

    def _eng_sem(self, eng, g):
        ep = (g - 1) // self.LIMIT
        while len(self.sems[eng]) <= ep:
            s = self.stack.enter_context(self.nc.semaphore(f"s_{eng}_{len(self.sems[eng])}"))
            self.sems[eng].append(s)
        return self.sems[eng][ep], (g - 1) % self.LIMIT + 1

    def _tok_sem(self, tok):
        kind, g = tok
        if kind.startswith("dma:"):
            return self.dma_sem[kind], 16 * g
        return self._eng_sem(kind, g)

    def _need(self, eng, tok):
        kind, g = tok
        if kind == eng:
            if eng in ("pe", "sp") or not self.same_sync:
                return False
        return self.seen[eng].get(kind, 0) < g

    def _collect(self, eng, reads, writes):
        toks = []
        for k in reads:
            b = self.bufs.get(k)
            if b and b[0] is not None:
                toks.append(b[0])
            if b and isinstance(k, tuple) and k[0] == "ps" and eng in ("act", "dve"):
                other = "dve" if eng == "act" else "act"
                if other in b[1]:
                    toks.append((other, b[1][other]))
        for k in writes:
            b = self.bufs.get(k)
            if b:
                if b[0] is not None:
                    toks.append(b[0])
                toks.extend(b[1].items())
        need = {}
        for t in toks:
            if self._need(eng, t):
                need[t[0]] = max(need.get(t[0], 0), t[1])
        return list(need.items())

    def _update(self, tok, reads, writes):
        for k in writes:
            self.bufs[k] = [tok, {}]
        for k in reads:
            b = self.bufs.setdefault(k, [None, {}])
            if k in writes:
                continue
            b[1][tok[0]] = max(b[1].get(tok[0], 0), tok[1])

    def op(self, eng, fn, reads=(), writes=(), inc=True):
        call = fn(_REC)
        fn = lambda e, c=call: getattr(e, c[0])(*c[1], **c[2])
        waits = self._collect(eng, reads, writes)
        for t in waits:
            self.seen[eng][t[0]] = t[1]
        ws = [self._tok_sem(t) for t in waits]
        for w in ws[1:]:
            self.q[eng].append(("wait", w[0], w[1]))
        if inc:
            self.cnt[eng] += 1
            tok = (eng, self.cnt[eng])
            sem, val = self._eng_sem(eng, self.cnt[eng])
            self.q[eng].append(("ins", fn, ws[0] if ws else None, (sem, 1)))
        else:
            tok = (eng, self.cnt[eng] + 1)
            self.q[eng].append(("ins", fn, ws[0] if ws else None, None))
        self._update(tok, reads, writes)
        self.nins += 1
        return tok

    NDS = 16

    def dma(self, out, in_, reads=(), writes=(), stream="d0", queue="act"):
        j = self.dma_i % self.NDS
        self.dma_i += 1
        kind = f"dma:{j}"
        if kind not in self.dma_sem:
            self.dma_sem[kind] = self.stack.enter_context(self.nc.semaphore(f"sd_{j}"))
            self.dma_cnt[kind] = 0
        waits = self._collect(queue, reads, writes)
        if self.dma_cnt[kind] > 0 and self.seen[queue].get(kind, 0) < self.dma_cnt[kind]:
            waits = [w for w in waits if w[0] != kind] + [(kind, self.dma_cnt[kind])]
        for t in waits:
            self.seen[queue][t[0]] = t[1]
        for t in waits:
            s, v = self._tok_sem(t)
            self.q[queue].append(("wait", s, v))
        self.dma_cnt[kind] += 1
        tok = (kind, self.dma_cnt[kind])
        self.q[queue].append(("ins", lambda e, o=out, i=in_: e.dma_start(out=o, in_=i), None,
                              (self.dma_sem[kind], 16)))
        self._update(tok, reads, writes)
        self.nins += 1
        return tok

    PERSIST = {"cstf", "normg", "memg", "bgate", "fing", "xT", "hT", "mT", "obT", "wst", "wb", "epsb", "g32", "sqt",
               "rstd", "oneb", "rtdt", "rtsd", "rtcd", "rtrp", "rtcn", "retg", "dnc", "dncw", "dng", "ps", "outT"}

    def _persistent(self, k):
        h = k[0] if isinstance(k, tuple) else k
        return h in self.PERSIST or (isinstance(h, str) and (h.startswith("k_") or h.startswith("dbg_")))

    def barrier(self, full=False):
        if full:
            toks = [(n, self.cnt[n]) for n in self.names if self.cnt[n] > 0]
            toks += [(k, c) for k, c in self.dma_cnt.items() if c > 0]
            engines = self.names
        else:
            need = {}
            for k, b in self.bufs.items():
                if self._persistent(k):
                    continue
                if b[0] is not None:
                    need[b[0][0]] = max(need.get(b[0][0], 0), b[0][1])
                for kind, c in b[1].items():
                    need[kind] = max(need.get(kind, 0), c)
            toks = list(need.items())
            engines = [n for n in self.names if n != "sp"]
        for eng in engines:
            for t in toks:
                if t[0] == eng and eng in ("pe", "sp"):
                    continue
                if self.seen[eng].get(t[0], 0) < t[1]:
                    self.seen[eng][t[0]] = t[1]
                    s, v = self._tok_sem(t)
                    self.q[eng].append(("wait", s, v))
        if full:
            self.bufs = {}
        else:
            self.bufs = {k: b for k, b in self.bufs.items() if self._persistent(k)}

    def finish(self, final_keys):
        toks = []
        for k in final_keys:
            b = self.bufs.get(k)
            if b and b[0] is not None:
                toks.append(b[0])
        for t in toks:
            s, v = self._tok_sem(t)
            self.q["act"].append(("wait", s, v))

    def simulate(self):
        sem = {}
        pos = {n: 0 for n in self.names}
        def ok(w):
            return w is None or sem.get(id(w[0]), 0) >= w[1]
        progress = True
        while progress:
            progress = False
            for n in self.names:
                q = self.q[n]
                while pos[n] < len(q):
                    ent = q[pos[n]]
                    if ent[0] == "wait":
                        if not ok((ent[1], ent[2])):
                            break
                    else:
                        if not ok(ent[2]):
                            break
                        if ent[3] is not None:
                            sem[id(ent[3][0])] = sem.get(id(ent[3][0]), 0) + ent[3][1]
                    pos[n] += 1
                    progress = True
        stuck = {n: (pos[n], len(self.q[n])) for n in self.names if pos[n] < len(self.q[n])}
        if stuck:
            msg = []
            for n, (p, ln) in stuck.items():
                ent = self.q[n][p]
                w = (ent[1], ent[2]) if ent[0] == "wait" else ent[2]
                msg.append(f"{n}@{p}/{ln} waits {w} have {sem.get(id(w[0]), 0)} kind={ent[0]} tag={ent[4] if len(ent) > 4 else None}")
            raise RuntimeError("DEADLOCK in recorded program: " + "; ".join(msg))

    def emit(self):
        self.simulate()
        nc = self.nc
        with nc.Block() as block:
            def replay(e, name):
                for ent in self.q[name]:
                    if ent[0] == "wait":
                        e.wait_ge(ent[1], ent[2])
                    else:
                        _, fn, w, inc = ent
                        ins = fn(e)
                        if w is not None:
                            ins._wait_ge(w[0], w[1])
                        if inc is not None:
                            ins.then_inc(inc[0], inc[1])

            @block.sync
            def _(e):
                replay(e, "sp")

            @block.scalar
            def _(e):
                replay(e, "act")

            @block.vector
            def _(e):
                replay(e, "dve")

            @block.tensor
            def _(e):
                replay(e, "pe")

            @block.gpsimd
            def _(e):
                replay(e, "pool")


def _consts():
    c = {}
    i = np.arange(128)
    c["ident"] = np.eye(128, dtype=np.float32)
    c["ones"] = np.ones((128, 128), np.float32)
    c["trineg"] = -(i[:, None] >= i[None, :]).astype(np.float32)
    e0 = np.zeros((128, 128), np.float32)
    e0[0, :] = 1.0
    c["e0"] = e0
    c["masksb"] = (i[:, None] < i[None, :]).astype(np.float32)
    return c


def _rt_consts():
    gam = [1.0 - 2.0 ** (-5.0 - h) for h in range(4)]
    i = np.arange(128)
    dt = np.zeros((128, 4, 128), np.float64)
    sd = np.zeros((128, 4), np.float64)
    cd = np.zeros((128, 2, 128), np.float64)
    for h in range(4):
        rel = i[None, :] - i[:, None]
        dt[:, h, :] = np.where(rel >= 0, gam[h] ** np.maximum(rel, 0), 0.0)
        sd[:, h] = gam[h] ** (127 - i)
    for p in range(2):
        for hh in range(2):
            cd[hh * 64:(hh + 1) * 64, p, :] = (gam[2 * p + hh] ** (i + 1.0))[None, :]
    inv = 10000.0 ** (-(np.arange(32, dtype=np.float32)) / np.float32(32))
    rp = np.zeros((128, 4), np.float32)
    rp[:, 0] = np.tile(inv.astype(np.float32), 4)
    rp[:, 1] = np.where((i % 64) < 32, -1.0, 1.0)
    rp[:, 2] = np.float32(math.pi / 2)
    rp[:, 3] = EPS
    cn = np.eye(128) - 1.0 / 128.0
    return {"rt_dt": dt.astype(np.float32), "rt_sd": sd.astype(np.float32), "rt_cd": cd.astype(np.float32),
            "rt_rp": rp, "rt_cn": cn.astype(np.float32)}


def _dn_consts():
    i = np.arange(64)
    c = np.zeros((128, 4, 128), np.float32)
    c[:, 0, :] = np.eye(128)
    c[:, 1, :] = 1.0
    c[0:64, 2, 0:64] = (i[:, None] <= i[None, :])
    c[0:64, 3, 0:64] = (i[:, None] < i[None, :])
    return c


CONST_NAMES = ["ident", "ones", "trineg", "e0", "masksb"]


def build(n_layers=DEPTH, debug=False, branches=(0, 1, 2, 3)):
    _u = _Uniq()
    nc = bass.Bass("TRN2", target_bir_lowering=False)
    dr = {}

    def din(name, shape, dt=F32):
        dr[name] = nc.dram_tensor(name, list(shape), dt, kind="ExternalInput").ap()
        return dr[name]

    xT_d = din("xT", [D, S])
    memT_d = din("memT", [D, MEM_LEN])
    w_in_d = din("w_in", [DEPTH, D, IN_COLS])
    w_kv_d = din("w_mem_kv", [DEPTH, D, 512])
    w_br_d = {0: din("w_br_sb", [DEPTH, 512, D]), 1: din("w_br_dn", [DEPTH, 512, D]),
              2: din("w_br_ret", [DEPTH, 512, D]), 3: din("w_br_mem", [DEPTH, 256, D])}
    w_out_d = din("w_out", [DEPTH, D, D])
    normg_d = din("norm_g", [DEPTH, 128, 8])
    memg_d = din("mem_norm_g", [DEPTH, 128, 8])
    bgate_d = din("b_gate", [DEPTH, 128, 32])
    fing_d = din("final_norm_g", [128, 8])
    cst_d = {n: din("c_" + n, [128, 128]) for n in CONST_NAMES}
    dnc_d = din("c_dn", [128, 4, 128])
    dncw_d = din("dn_conv_w", [DEPTH, 128, 12, 4])
    dng_d = din("dn_norm_g", [128, DEPTH])
    dnal_d = din("dn_alog", [DEPTH, 128, 128])
    dndt_d = din("dn_dtb", [DEPTH, 128, 128])
    pos_d = din("pos", [128, S], I32)
    retg_d = din("ret_norm_g", [DEPTH, 128, 4])
    rtdt_d = din("c_rt_dt", [128, 4, 128])
    rtsd_d = din("c_rt_sd", [128, 4])
    rtcd_d = din("c_rt_cd", [128, 2, 128])
    rtrp_d = din("c_rt_rp", [128, 4])
    rtcn_d = din("c_rt_cn", [128, 128])
    outT_d = nc.dram_tensor("outT", [D, S], F32, kind="ExternalOutput").ap()
    dbg_d = {}
    if debug:
        dbg_d["hT"] = nc.dram_tensor("dbg_hT", [D, S], F32, kind="ExternalOutput").ap()
        dbg_d["omem"] = nc.dram_tensor("dbg_omem", [256, S], F32, kind="ExternalOutput").ap()
        for nm in ["osb", "odn", "ort"]:
            dbg_d[nm] = nc.dram_tensor("dbg_" + nm, [512, S], F32, kind="ExternalOutput").ap()

    with ExitStack() as st:
        P = Prog(nc, st)

        def sb(name, shape, dt):
            return st.enter_context(nc.sbuf_tensor(_u + "s_" + name, list(shape), dt))

        xT = sb("xT", [128, 8, S], F32)
        hT = sb("hT", [128, 8, S], BF16)
        mT = sb("mT", [128, 8, S], BF16)
        obT = sb("obT", [128, 4, S], BF16)
        wst = [sb(f"wst{i}", [128, 8, 128], F32) for i in range(2)]
        NWB = 5
        wbp = [sb(f"wb{i}", [128, 8, 128], BF16) for i in range(NWB)]
        cst = {n: sb("k_" + n, [128, 128], BF16) for n in CONST_NAMES}
        cstf = sb("cstf", [128, 128], F32)
        normg = sb("normg", [128, DEPTH, 8], F32)
        memg = sb("memg", [128, DEPTH, 8], F32)
        bgate = sb("bgate", [128, DEPTH, 32], F32)
        fing = sb("fing", [128, 8], F32)
        ps = [st.enter_context(nc.psum_tensor(f"ps{i}", [128, 512], F32)) for i in range(8)]
        PS = [("ps", i) for i in range(8)]

        wcount = [0]
        wbcount = [0]

        def load_w(src, n=128, rows=8, eng="pool"):
            i = wcount[0] % 2
            wcount[0] += 1
            j = wbcount[0] % NWB
            wbcount[0] += 1
            stg, wb = wst[i], wbp[j]
            P.dma(stg[:, 0:rows, 0:n], src.rearrange("(k p) n -> p k n", p=128),
                  writes=[("wst", i)], stream=f"w{i}", queue="sp")
            fn = lambda e, o=wb[:, 0:rows, 0:n], a=stg[:, 0:rows, 0:n]: e.tensor_copy(out=o, in_=a)
            P.op(eng, fn, reads=[("wst", i)], writes=[("wb", j)])
            return wb, ("wb", j)

        for n in CONST_NAMES:
            P.dma(cstf[:, :], cst_d[n][:, :], writes=["cstf"], stream="c")
            P.op("dve", lambda e, o=cst[n][:, :]: e.tensor_copy(out=o, in_=cstf[:, :]),
                 reads=["cstf"], writes=["k_" + n])
        for l in range(DEPTH):
            P.dma(normg[:, l, :], normg_d[l, :, :], writes=["normg"], stream="c")
            P.dma(memg[:, l, :], memg_d[l, :, :], writes=["memg"], stream="c")
            P.dma(bgate[:, l, :], bgate_d[l, :, :], writes=["bgate"], stream="c")
        P.dma(fing[:, :], fing_d[:, :], writes=["fing"], stream="c")
        for k in range(8):
            P.dma(xT[:, k, :], xT_d[k * 128:(k + 1) * 128, :], writes=[("xT", k)], stream="x")

        rtdt = sb("rtdt", [128, 4, 128], F32)
        rtsd = sb("rtsd", [128, 4], F32)
        rtcd = sb("rtcd", [128, 2, 128], F32)
        rtrp = sb("rtrp", [128, 4], F32)
        rtcn = sb("rtcn", [128, 128], BF16)
        retg = sb("retg", [128, DEPTH, 4], F32)
        P.dma(rtdt[:, :, :], rtdt_d[:, :, :], writes=["rtdt"], stream="c")
        P.dma(rtsd[:, :], rtsd_d[:, :], writes=["rtsd"], stream="c")
        P.dma(rtcd[:, :, :], rtcd_d[:, :, :], writes=["rtcd"], stream="c")
        P.dma(rtrp[:, :], rtrp_d[:, :], writes=["rtrp"], stream="c")
        P.dma(cstf[:, :], rtcn_d[:, :], writes=["cstf"], stream="c")
        P.op("dve", lambda e: e.tensor_copy(out=rtcn[:, :], in_=cstf[:, :]), reads=["cstf"], writes=["rtcn"])
        for l in range(DEPTH):
            P.dma(retg[:, l, :], retg_d[l, :, :], writes=["retg"], stream="c")
        dnc = sb("dnc", [128, 4, 128], F32)
        dncw = sb("dncw", [128, DEPTH, 12, 4], F32)
        dng = sb("dng", [128, DEPTH], F32)
        P.dma(dnc[:, :, :], dnc_d[:, :, :], writes=["dnc"], stream="c")
        P.dma(dng[:, :], dng_d[:, :], writes=["dng"], stream="c")
        for l in range(DEPTH):
            P.dma(dncw[:, l, :, :], dncw_d[l, :, :, :], writes=["dncw"], stream="c")
        def rms_stats(src_tile, src_keys, nk, ncols, c0, psum_i, sq_tile, sq_key, rstd_tile, rstd_key, extra):
            for k in range(nk):
                P.op("act", lambda e, k=k: e.activation(out=sq_tile[:, k % 2, 0:ncols], in_=src_tile[:, k, c0:c0 + ncols],
                                                        func=AF.Square),
                     reads=[src_keys[k]], writes=[(sq_key, k % 2)])
                P.op("pe", lambda e, k=k: e.matmul(ps[psum_i][:, 0:ncols], lhsT=cst["ones"][:, :],
                                                   rhs=sq_tile[:, k % 2, 0:ncols], start=(k == 0), stop=(k == nk - 1)),
                     reads=[(sq_key, k % 2), "k_ones"], writes=[PS[psum_i]])
            P.op("act", lambda e: e.activation(out=rstd_tile[:, 0:ncols], in_=ps[psum_i][:, 0:ncols], func=AF.Ln,
                                               bias=epsb[:, 0:1], scale=1.0),
                 reads=[PS[psum_i], "epsb"], writes=[rstd_key])
            P.op("act", lambda e: e.activation(out=rstd_tile[:, 0:ncols], in_=rstd_tile[:, 0:ncols], func=AF.Exp,
                                               scale=-0.5),
                 reads=[rstd_key], writes=[rstd_key])

        epsb = sb("epsb", [128, 1], F32)
        P.op("dve", lambda e: e.memset(epsb[:, :], float(D * EPS)), writes=["epsb"])
        sqt = sb("sqt", [128, 2, TC], BF16)
        rstd = sb("rstd", [128, TC], F32)
        g32 = sb("g32", [128, 8], F32)

        def norm_to(dst_fn, gsrc, layer_tag):
            P.op("dve", lambda e: e.tensor_scalar(out=g32[:, :], in0=gsrc, scalar1=float(math.sqrt(D)), scalar2=None,
                                                  op0=ALU.mult), reads=["normg", "fing"], writes=["g32"])
            for tc in range(NT):
                rms_stats(xT, [("xT", k) for k in range(8)], 8, TC, tc * TC, 0, sqt, "sqt", rstd, "rstd", None)
                for k in range(8):
                    ap, key = dst_fn(k, tc)
                    P.op("dve", lambda e, k=k, tc=tc, ap=ap: e.scalar_tensor_tensor(
                        out=ap, in0=xT[:, k, tc * TC:(tc + 1) * TC], scalar=g32[:, k:k + 1], in1=rstd[:, :],
                        op0=ALU.mult, op1=ALU.mult),
                        reads=[("xT", k), "g32", "rstd"], writes=[key])

        def proj_fm(l, col0, ncols, evac, wsrc=None):
            src = (w_in_d[l, :, col0:col0 + ncols] if wsrc is None else wsrc)
            wb, wkey = load_w(src, n=ncols)
            for tc in range(NT):
                pi = 1 + (tc % 2)
                for k in range(8):
                    P.op("pe", lambda e, k=k, tc=tc, pi=pi: e.matmul(
                        ps[pi][0:ncols, :], lhsT=wb[:, k, 0:ncols], rhs=hT[:, k, tc * TC:(tc + 1) * TC],
                        start=(k == 0), stop=(k == 7)),
                        reads=[wkey, ("hT", k)], writes=[PS[pi]], inc=(k == 7))
                evac(tc, ps[pi][0:ncols, :], PS[pi])

        def proj_tm(l, col0, ncols, evac, ntok=128):
            wb, wkey = load_w(w_in_d[l, :, col0:col0 + ncols], n=ncols)
            for tt in range(S // ntok):
                pi = 1 + (tt % 2)
                for k in range(8):
                    P.op("pe", lambda e: e.matmul(ps[pi][0:ntok, 0:ncols], lhsT=hT[:, k, tt * ntok:(tt + 1) * ntok],
                                                  rhs=wb[:, k, 0:ncols], start=(k == 0), stop=(k == 7)),
                         reads=[wkey, ("hT", k)], writes=[PS[pi]], inc=(k == 7))
                evac(tt, ps[pi][0:ntok, 0:ncols], PS[pi])

        oneb = sb("oneb", [128, 1], F32)
        P.op("dve", lambda e: e.memset(oneb[:, :], 1.0), writes=["oneb"])

        def run_streams(gens, stagger=None):
            gens = list(gens)
            if stagger:
                for g, k in zip(gens, stagger):
                    for _ in range(k):
                        next(g)
            while gens:
                for g in list(gens):
                    try:
                        next(g)
                    except StopIteration:
                        gens.remove(g)

        def sb_attention(l):
            for hp in range(4):
                with ExitStack() as ph:
                    def sbp(name, shape, dt):
                        return ph.enter_context(nc.sbuf_tensor(_u + ("s_" + name), list(shape), dt))
                    qT = sbp("sbq", [128, S], BF16)
                    kT = sbp("sbk", [128, S], BF16)
                    vtm = sbp("sbv", [128, 16, 128], BF16)
                    proj_fm(l, C_SBQ + hp * 128, 128, lambda tc, pap, pkey: P.op(
                        "act", lambda e: e.activation(out=qT[:, tc * TC:(tc + 1) * TC], in_=pap, func=AF.Copy, scale=0.125),
                        reads=[pkey], writes=["sbq"]))
                    proj_fm(l, C_SBK + hp * 128, 128, lambda tc, pap, pkey: P.op(
                        "dve", lambda e: e.tensor_copy(out=kT[:, tc * TC:(tc + 1) * TC], in_=pap),
                        reads=[pkey], writes=["sbk"]))
                    proj_tm(l, C_SBV + hp * 128, 128, lambda tt, pap, pkey: P.op(
                        "dve", lambda e: e.tensor_copy(out=vtm[:, tt, :], in_=pap),
                        reads=[pkey], writes=[("sbv", tt)]))
                    with ExitStack() as ph2:
                        def sbw(name, shape, dt):
                            return ph2.enter_context(nc.sbuf_tensor(_u + ("s_" + name), list(shape), dt))

                        import os as _os

                        def stream(sid, hh, qcs):
                            ez = sbw(f"ez{sid}", [128, TC], F32)
                            spb = sbw(f"spb{sid}", [128, TC], BF16)
                            eg = sbw(f"eg{sid}", [128, TC], BF16)
                            Gb = sbw(f"Gb{sid}", [128, TC], BF16)
                            wv = eg
                            K_wv = ("eg", sid)
                            K_ez, K_spb, K_eg, K_Gb = ("ez", sid), ("spb", sid), ("eg", sid), ("Gb", sid)
                            rs = slice(hh * 64, hh * 64 + 64)
                            pzg, po = 2 * sid, 2 * sid + 1
                            pgg = (4 + 2 * sid) if _os.environ.get('SB_SEPG') else pzg
                            yield
                            for qc in qcs:
                                q0 = qc * TC
                                kmax = qc * 4 + 3
                                pc0 = None
                                P.op("pool", lambda e: e.memset(wv[:, 0:384], 0.0), writes=[K_wv])
                                for kb in range(kmax, -1, -1):
                                    if kmax - kb >= int(_os.environ.get("SB_MAXIT", "99")):
                                        continue
                                    j = kb - qc * 4
                                    c0 = 128 * j if j >= 0 else 0
                                    cs = slice(c0, TC)
                                    tsl = slice(q0 + c0, q0 + TC)
                                    ksl = slice(kb * 128, (kb + 1) * 128)
                                    P.op("pe", lambda e: e.matmul(ps[pzg][:, cs], lhsT=kT[rs, ksl], rhs=qT[rs, tsl],
                                                                  start=True, stop=True),
                                         reads=["sbk", "sbq"], writes=[PS[pzg]])
                                    yield
                                    P.op("act", lambda e: e.activation(out=ez[:, cs], in_=ps[pzg][:, cs], func=AF.Exp),
                                         reads=[PS[pzg]], writes=[K_ez])
                                    yield
                                    P.op("act", lambda e: e.activation(out=spb[:, cs], in_=ez[:, cs], func=AF.Ln,
                                                                       bias=oneb[:, 0:1], scale=1.0),
                                         reads=[K_ez, "oneb"], writes=[K_spb])
                                    yield
                                    if j >= 0:
                                        P.op("dve", lambda e: e.tensor_tensor(out=spb[:, c0:c0 + 128], in0=spb[:, c0:c0 + 128],
                                                                              in1=cst["masksb"][:, :], op=ALU.mult),
                                             reads=[K_spb, "k_masksb"], writes=[K_spb])
                                    P.op("pe", lambda e: e.matmul(ps[pgg][:, cs], lhsT=cst["trineg"][:, :], rhs=spb[:, cs],
                                                                  start=True, stop=(pc0 is None)),
                                         reads=["k_trineg", K_spb], writes=[PS[pgg]], inc=(pc0 is None))
                                    if pc0 is not None:
                                        pcs = slice(pc0, TC)
                                        P.op("pe", lambda e: e.matmul(ps[pgg][:, pcs], lhsT=cst["e0"][:, :], rhs=Gb[:, pcs],
                                                                      start=False, stop=True),
                                             reads=["k_e0", K_Gb], writes=[PS[pgg]])
                                    yield
                                    P.op("act", lambda e: e.activation(out=eg[:, cs], in_=ps[pgg][:, cs], func=AF.Exp),
                                         reads=[PS[pgg]], writes=[K_eg])
                                    if kb > 0:
                                        P.op("dve", lambda e: e.tensor_copy(out=Gb[:, cs], in_=ps[pgg][:, cs]),
                                             reads=[PS[pgg], K_eg], writes=[K_Gb])
                                    yield
                                    P.op("dve", lambda e: e.tensor_tensor(out=wv[:, cs], in0=ez[:, cs], in1=eg[:, cs], op=ALU.mult),
                                         reads=[K_ez, K_eg], writes=[K_wv])
                                    if j >= 0:
                                        P.op("dve", lambda e: e.tensor_tensor(out=wv[:, c0:c0 + 128], in0=wv[:, c0:c0 + 128],
                                                                              in1=cst["masksb"][:, :], op=ALU.mult),
                                             reads=[K_wv, "k_masksb"], writes=[K_wv])
                                    yield
                                    P.op("pe", lambda e: e.matmul(ps[po][rs, :], lhsT=vtm[:, kb, rs], rhs=wv[:, :],
                                                                  start=(kb == kmax), stop=(kb == 0)),
                                         reads=[("sbv", kb), K_wv], writes=[PS[po]])
                                    pc0 = c0
                                    yield
                                P.op("dve", lambda e: e.tensor_copy(out=obT[rs, hp, q0:q0 + TC], in_=ps[po][rs, :]),
                                     reads=[PS[po]], writes=[("obT", hp)])
                                yield

                        _mode = _os.environ.get("SB_MODE", "4")
                        if _mode == "1":
                            run_streams([stream(0, 0, [3, 0, 2, 1])])
                            run_streams([stream(1, 1, [3, 0, 2, 1])])
                        elif _mode == "2":
                            run_streams([stream(0, 0, [3, 0, 2, 1]), stream(1, 1, [3, 0, 2, 1])])
                        else:
                            run_streams([stream(0, 0, [3, 0]), stream(1, 1, [3, 0]), stream(2, 0, [2, 1]), stream(3, 1, [2, 1])],
                                        stagger=[int(x) for x in _os.environ.get('SB_STAG', '0,2,4,6').split(',')])
                        P.barrier()
                    with ExitStack() as ph3:
                        zs = [ph3.enter_context(nc.sbuf_tensor(_u + f"s_sbzs{i}", [128, TC], BF16)) for i in range(2)]

                        def evz(tc, pap, pkey):
                            b = tc % 2
                            P.op("act", lambda e: e.activation(out=zs[b][:, :], in_=pap, func=AF.Silu),
                                 reads=[pkey], writes=[("sbzs", b)])
                            tsl = slice(tc * TC, (tc + 1) * TC)
                            P.op("dve", lambda e: e.tensor_tensor(out=obT[:, hp, tsl], in0=obT[:, hp, tsl], in1=zs[b][:, :], op=ALU.mult),
                                 reads=[("sbzs", b)], writes=[("obT", hp)])
                        proj_fm(l, C_SBZ + hp * 128, 128, evz)
                        P.barrier()

        def retention(l):
            TWO_PI = 2.0 * math.pi
            C1 = 6.28125
            C2 = TWO_PI - C1
            with ExitStack() as ph0:
                def sb0(name, shape, dt):
                    return ph0.enter_context(nc.sbuf_tensor(_u + ("s_" + name), list(shape), dt))
                qrT = sb0("rqr", [128, 2, S], BF16)
                krT = sb0("rkr", [128, 2, S], BF16)
                with ExitStack() as ph1:
                    def sb1(name, shape, dt):
                        return ph1.enter_context(nc.sbuf_tensor(_u + ("s_" + name), list(shape), dt))
                    COS2 = sb1("rcos", [128, S], BF16)
                    SIN2 = sb1("rsin", [128, S], BF16)
                    pint = sb1("rpint", [128, TC], I32)
                    ta = sb1("rta", [128, TC], F32)
                    tk = sb1("rtk", [128, TC], F32)
                    tm_full = sb1("rtm", [128, TC], F32)
                    tm = tm_full[:, 0:256]
                    ki = tm_full[:, 256:512].bitcast(I32)
                    ta_f, tk_f, pint_f = ta, tk, pint
                    for tc in range(8):
                        tsl = slice(tc * 256, (tc + 1) * 256)
                        ta, tk, pint = ta_f[:, 0:256], tk_f[:, 0:256], pint_f[:, 0:256]
                        P.dma(pint, pos_d[:, tsl], writes=["rpint"], stream="x")
                        P.op("dve", lambda e: e.tensor_copy(out=ta, in_=pint), reads=["rpint"], writes=["rta"])
                        P.op("dve", lambda e: e.tensor_scalar(out=ta, in0=ta, scalar1=rtrp[:, 0:1], scalar2=None,
                                                              op0=ALU.mult), reads=["rta", "rtrp"], writes=["rta"])
                        P.op("dve", lambda e: e.tensor_scalar(out=ki, in0=ta, scalar1=float(1.0 / TWO_PI),
                                                              scalar2=None, op0=ALU.mult), reads=["rta"], writes=["rki"])
                        P.op("dve", lambda e: e.tensor_copy(out=tk, in_=ki), reads=["rki"], writes=["rtk"])
                        P.op("dve", lambda e: e.scalar_tensor_tensor(out=ta, in0=tk, scalar=-C1, in1=ta,
                                                                     op0=ALU.mult, op1=ALU.add),
                             reads=["rtk", "rta"], writes=["rta"])
                        P.op("dve", lambda e: e.scalar_tensor_tensor(out=ta, in0=tk, scalar=-C2, in1=ta,
                                                                     op0=ALU.mult, op1=ALU.add),
                             reads=["rtk", "rta"], writes=["rta"])
                        P.op("dve", lambda e: e.tensor_single_scalar(out=tm, in_=ta, scalar=float(math.pi),
                                                                     op=ALU.is_gt), reads=["rta"], writes=["rtm"])
                        P.op("dve", lambda e: e.scalar_tensor_tensor(out=ta, in0=tm, scalar=-TWO_PI, in1=ta,
                                                                     op0=ALU.mult, op1=ALU.add),
                             reads=["rtm", "rta"], writes=["rta"])
                        P.op("dve", lambda e: e.tensor_single_scalar(out=tm, in_=ta, scalar=float(-math.pi),
                                                                     op=ALU.is_lt), reads=["rta"], writes=["rtm"])
                        P.op("dve", lambda e: e.scalar_tensor_tensor(out=ta, in0=tm, scalar=TWO_PI, in1=ta,
                                                                     op0=ALU.mult, op1=ALU.add),
                             reads=["rtm", "rta"], writes=["rta"])
                        P.op("act", lambda e: e.activation(out=tk, in_=ta, func=AF.Sin),
                             reads=["rta"], writes=["rtk"])
                        P.op("dve", lambda e: e.tensor_scalar(out=SIN2[:, tsl], in0=tk, scalar1=rtrp[:, 1:2], scalar2=None,
                                                              op0=ALU.mult), reads=["rtk", "rtrp"], writes=["rsin"])
                        P.op("dve", lambda e: e.tensor_single_scalar(out=tm, in_=ta, scalar=float(math.pi / 2),
                                                                     op=ALU.is_gt), reads=["rta"], writes=["rtm"])
                        P.op("dve", lambda e: e.scalar_tensor_tensor(out=ta, in0=tm, scalar=-TWO_PI, in1=ta,
                                                                     op0=ALU.mult, op1=ALU.add),
                             reads=["rtm", "rta"], writes=["rta"])
                        P.op("act", lambda e: e.activation(out=COS2[:, tsl], in_=ta, func=AF.Sin, bias=rtrp[:, 2:3],
                                                           scale=1.0), reads=["rta", "rtrp"], writes=["rcos"])
                    ta, tk, pint = ta_f, tk_f, pint_f
                    for which, (c_base, dstT, scl) in enumerate([(C_RTQ, qrT, 1.0), (C_RTK, krT, 0.125)]):
                        for hp in range(2):
                            wb, wkey = load_w(w_in_d[l, :, c_base + hp * 128:c_base + (hp + 1) * 128])
                            jsw = wbcount[0] % NWB
                            wbcount[0] += 1
                            wsw = wbp[jsw]
                            for hh in range(2):
                                o = hh * 64
                                P.op("pool", lambda e: e.tensor_copy(out=wsw[:, :, o:o + 32], in_=wb[:, :, o + 32:o + 64]),
                                     reads=[wkey], writes=[("wb", jsw)])
                                P.op("pool", lambda e: e.tensor_copy(out=wsw[:, :, o + 32:o + 64], in_=wb[:, :, o:o + 32]),
                                     reads=[wkey], writes=[("wb", jsw)])
                            for tc in range(NT):
                                tsl = slice(tc * TC, (tc + 1) * TC)
                                for k in range(8):
                                    P.op("pe", lambda e: e.matmul(ps[1][:, :], lhsT=wb[:, k, :], rhs=hT[:, k, tsl],
                                                                  start=(k == 0), stop=(k == 7)),
                                         reads=[wkey, ("hT", k)], writes=[PS[1]], inc=(k == 7))
                                for k in range(8):
                                    P.op("pe", lambda e: e.matmul(ps[2][:, :], lhsT=wsw[:, k, :], rhs=hT[:, k, tsl],
                                                                  start=(k == 0), stop=(k == 7)),
                                         reads=[("wb", jsw), ("hT", k)], writes=[PS[2]], inc=(k == 7))
                                P.op("dve", lambda e: e.scalar_tensor_tensor(out=ta[:, :], in0=ps[1][:, :], scalar=float(scl),
                                                                             in1=COS2[:, tsl], op0=ALU.mult, op1=ALU.mult),
                                     reads=[PS[1], "rcos"], writes=["rta"])
                                P.op("dve", lambda e: e.scalar_tensor_tensor(out=tk[:, :], in0=ps[2][:, :], scalar=float(scl),
                                                                             in1=SIN2[:, tsl], op0=ALU.mult, op1=ALU.mult),
                                     reads=[PS[2], "rsin"], writes=["rtk"])
                                P.op("dve", lambda e: e.tensor_tensor(out=dstT[:, hp, tsl], in0=ta[:, :], in1=tk[:, :], op=ALU.add),
                                     reads=["rta", "rtk"], writes=[("rq", which, hp)])
                    P.barrier()
                for hp in range(2):
                    with ExitStack() as ph2:
                        def sb2(name, shape, dt):
                            return ph2.enter_context(nc.sbuf_tensor(_u + ("s_" + name), list(shape), dt))
                        kdtm = sb2("rkd", [128, 16, 128], BF16)
                        vtm = sb2("rv", [128, 16, 128], BF16)
                        zsT = sb2("rz", [128, S], BF16)
                        qc = [sb2(f"rqc{i}", [128, 128], BF16) for i in range(2)]
                        scm = [sb2(f"rscm{i}", [128, 128], BF16) for i in range(2)]
                        Sf = sb2("rSf", [128, 128], F32)
                        Sb = sb2("rSb", [128, 128], BF16)
                        ob = sb2("rob", [128, TC], BF16)
                        sq = sb2("rsq", [128, TC], BF16)
                        rs_t = rstd
                        tt_t = sb2("rtt", [128, TC], F32)
                        for n in range(16):
                            nsl = slice(n * 128, (n + 1) * 128)
                            pk = ps[3][:, 0:64].bitcast(BF16)
                            P.op("pe", lambda e: e.transpose(pk, krT[:, hp, nsl], cst["ident"][:, :]),
                                 reads=[("rq", 1, hp), "k_ident"], writes=[PS[3]])
                            for hh in range(2):
                                cs = slice(hh * 64, (hh + 1) * 64)
                                h = 2 * hp + hh
                                P.op("dve", lambda e: e.tensor_scalar(out=kdtm[:, n, cs], in0=pk[:, cs], scalar1=rtsd[:, h:h + 1],
                                                                      scalar2=None, op0=ALU.mult),
                                     reads=[PS[3], "rtsd"], writes=[("rkd", n)])
                        for hh in range(2):
                            h = 2 * hp + hh
                            rs = slice(hh * 64, (hh + 1) * 64)
                            proj_tm(l, C_RTV + h * 128, 128, lambda tt, pap, pkey: P.op(
                                "dve", lambda e: e.tensor_copy(out=vtm[:, tt, :], in_=pap), reads=[pkey], writes=[("rv", tt)]))
                            proj_fm(l, C_RTZ + h * 128, 128, lambda tc, pap, pkey: P.op(
                                "act", lambda e: e.activation(out=zsT[:, tc * TC:(tc + 1) * TC], in_=pap, func=AF.Silu),
                                reads=[pkey], writes=["rz"]))
                            gch = float((1.0 - 2.0 ** (-5.0 - h)) ** 128)
                            SBANK = [7, 3]

                            def emit_sc(n):
                                nsl = slice(n * 128, (n + 1) * 128)
                                b = n % 2
                                P.op("pe", lambda e: e.matmul(ps[4][:, 0:128], lhsT=krT[rs, hp, nsl], rhs=qrT[rs, hp, nsl],
                                                              start=True, stop=True),
                                     reads=[("rq", 1, hp), ("rq", 0, hp)], writes=[PS[4]])
                                P.op("dve", lambda e: e.tensor_tensor(out=scm[b][:, :], in0=ps[4][:, 0:128], in1=rtdt[:, h, :],
                                                                      op=ALU.mult),
                                     reads=[PS[4], "rtdt"], writes=[("rscm", b)])
                                if n > 0:
                                    P.op("dve", lambda e: e.tensor_tensor(out=qc[b][rs, :], in0=qrT[rs, hp, nsl], in1=rtcd[rs, hp, :],
                                                                          op=ALU.mult),
                                         reads=[("rq", 0, hp), "rtcd"], writes=[("rqc", b)])

                            def emit_smm(n):
                                sbk = SBANK[n % 2]
                                P.op("pe", lambda e: e.matmul(ps[sbk][rs, 0:128], lhsT=kdtm[:, n, rs], rhs=vtm[:, n, :],
                                                              start=True, stop=True),
                                     reads=[("rkd", n), ("rv", n)], writes=[PS[sbk]])

                            emit_sc(0)
                            emit_smm(0)
                            for n in range(16):
                                nsl = slice(n * 128, (n + 1) * 128)
                                b = n % 2
                                csl = slice((n % 4) * 128, (n % 4 + 1) * 128)
                                po = 5 + (n // 4) % 2
                                if n + 1 < 15:
                                    emit_smm(n + 1)
                                P.op("pe", lambda e: e.matmul(ps[po][:, csl], lhsT=vtm[:, n, :], rhs=scm[b][:, :],
                                                              start=True, stop=(n == 0)),
                                     reads=[("rv", n), ("rscm", b)], writes=[PS[po]], inc=(n == 0))
                                if n > 0:
                                    P.op("pe", lambda e: e.matmul(ps[po][:, csl], lhsT=Sb[rs, :], rhs=qc[b][rs, :],
                                                                  start=False, stop=True),
                                         reads=["rSb", ("rqc", b)], writes=[PS[po]])
                                if n < 15:
                                    sbk = SBANK[n % 2]
                                    if n == 0:
                                        P.op("dve", lambda e: e.tensor_copy(out=Sf[rs, :], in_=ps[sbk][rs, 0:128]),
                                             reads=[PS[sbk]], writes=["rSf"])
                                    else:
                                        P.op("dve", lambda e: e.scalar_tensor_tensor(out=Sf[rs, :], in0=Sf[rs, :], scalar=gch,
                                                                                     in1=ps[sbk][rs, 0:128], op0=ALU.mult,
                                                                                     op1=ALU.add),
                                             reads=[PS[sbk], "rSf"], writes=["rSf"])
                                    P.op("act", lambda e: e.activation(out=Sb[rs, :], in_=Sf[rs, :], func=AF.Copy),
                                         reads=["rSf"], writes=["rSb"])
                                if n + 1 < 16:
                                    emit_sc(n + 1)
                                if n % 4 == 3:
                                    tc = n // 4
                                    tsl = slice(tc * TC, (tc + 1) * TC)
                                    P.op("act", lambda e: e.activation(out=ob[:, :], in_=ps[po][:, :], func=AF.Copy),
                                         reads=[PS[po]], writes=["rob"])
                                    P.op("pe", lambda e: e.matmul(ps[1][:, :], lhsT=rtcn[:, :], rhs=ob[:, :], start=True, stop=True),
                                         reads=["rtcn", "rob"], writes=[PS[1]])
                                    P.op("act", lambda e: e.activation(out=sq[:, :], in_=ps[1][:, :], func=AF.Square),
                                         reads=[PS[1]], writes=["rsq"])
                                    P.op("pe", lambda e: e.matmul(ps[2][:, :], lhsT=cst["ones"][:, :], rhs=sq[:, :],
                                                                  start=True, stop=True),
                                         reads=["k_ones", "rsq"], writes=[PS[2]])
                                    P.op("act", lambda e: e.activation(out=rs_t[:, :], in_=ps[2][:, :], func=AF.Ln,
                                                                       bias=rtrp[:, 3:4], scale=float(1.0 / 128.0)),
                                         reads=[PS[2], "rtrp"], writes=["rstd"])
                                    P.op("act", lambda e: e.activation(out=rs_t[:, :], in_=rs_t[:, :], func=AF.Exp, scale=-0.5),
                                         reads=["rstd"], writes=["rstd"])
                                    P.op("dve", lambda e: e.scalar_tensor_tensor(out=tt_t[:, :], in0=ps[1][:, :],
                                                                                 scalar=retg[:, l, h:h + 1], in1=rs_t[:, :],
                                                                                 op0=ALU.mult, op1=ALU.mult),
                                         reads=[PS[1], "retg", "rstd"], writes=["rtt"])
                                    P.op("dve", lambda e: e.tensor_tensor(out=obT[:, h, tsl], in0=tt_t[:, :], in1=zsT[:, tsl],
                                                                          op=ALU.mult),
                                         reads=["rtt", "rz"], writes=[("obT", h)])
                        P.barrier()

        def deltanet(l):
            P.barrier(full=True)
            identf, onesf = dnc[:, 0, :], dnc[:, 1, :]
            mincl, mstrict = dnc[0:64, 2, 0:64], dnc[0:64, 3, 0:64]
            H = slice(0, 64)
            with ExitStack() as ph0:
                def sb0(name, shape, dt):
                    return ph0.enter_context(nc.sbuf_tensor(_u + ("s_" + name), list(shape), dt))
                abraw = sb0("dab", [64, 32, 8], F32)
                rep = sb0("drep", [64, 128], F32)
                t1 = sb0("dt1", [64, 32, 4], F32)
                g_tm = sb0("dg", [64, 32, 4], F32)
                beta_tm = sb0("dbeta", [64, 32, 4], F32)
                gc_tm = sb0("dgc", [64, 32, 4], F32)
                egl = sb0("degl", [128, 32, 4], F32)
                bg_tm = sb0("dbg", [64, 32, 4], F32)
                ed_tm = sb0("ded", [64, 32, 4], F32)
                proj_tm(l, C_DNA, 8, lambda tt, pap, pkey: P.op(
                    "dve", lambda e: e.tensor_copy(out=abraw[:, tt, :], in_=pap), reads=[pkey], writes=["dab"]), ntok=64)
                fl = lambda t: t[:, :, :].rearrange("p a b -> p (a b)")
                P.dma(rep[:, :], dndt_d[l, 0:64, :], writes=["drep"], stream="c")
                P.op("dve", lambda e: e.tensor_tensor(out=t1[:, :, :], in0=abraw[:, :, 0:4],
                                                      in1=rep[:, :].rearrange("p (a b) -> p a b", b=4), op=ALU.add),
                     reads=["dab", "drep"], writes=["dt1"])
                P.op("act", lambda e: e.activation(out=t1[:, :, :], in_=t1[:, :, :], func=AF.Exp), reads=["dt1"], writes=["dt1"])
                P.op("act", lambda e: e.activation(out=t1[:, :, :], in_=t1[:, :, :], func=AF.Ln, bias=oneb[0:64, 0:1], scale=1.0),
                     reads=["dt1", "oneb"], writes=["dt1"])
                P.dma(rep[:, :], dnal_d[l, 0:64, :], writes=["drep"], stream="c")
                P.op("act", lambda e: e.activation(out=rep[:, :], in_=rep[:, :], func=AF.Exp), reads=["drep"], writes=["drep"])
                P.op("dve", lambda e: e.scalar_tensor_tensor(out=g_tm[:, :, :], in0=t1[:, :, :], scalar=-1.0,
                                                             in1=rep[:, :].rearrange("p (a b) -> p a b", b=4),
                                                             op0=ALU.mult, op1=ALU.mult),
                     reads=["dt1", "drep"], writes=["dg"])
                P.op("act", lambda e: e.activation(out=beta_tm[:, :, :], in_=abraw[:, :, 4:8], func=AF.Sigmoid),
                     reads=["dab"], writes=["dbeta"])
                P.op("pe", lambda e: e.matmul(ps[0][0:64, 0:128], lhsT=dnc[0:64, 2, 0:64], rhs=fl(g_tm), start=True, stop=True),
                     reads=["dnc", "dg"], writes=[PS[0]])
                P.op("dve", lambda e: e.tensor_copy(out=fl(gc_tm), in_=ps[0][0:64, 0:128]), reads=[PS[0]], writes=["dgc"])
                P.op("pe", lambda e: e.matmul(ps[0][:, 128:256], lhsT=dnc[0:64, 1, :], rhs=fl(g_tm), start=True, stop=True),
                     reads=["dnc", "dg"], writes=[PS[0]])
                P.op("dve", lambda e: e.tensor_tensor(out=fl(ed_tm), in0=ps[0][0:64, 128:256], in1=fl(gc_tm), op=ALU.subtract),
                     reads=[PS[0], "dgc"], writes=["ded"])
                P.op("act", lambda e: e.activation(out=fl(ed_tm), in_=fl(ed_tm), func=AF.Exp), reads=["ded"], writes=["ded"])
                P.op("act", lambda e: e.activation(out=fl(egl), in_=ps[0][:, 128:256], func=AF.Exp), reads=[PS[0]], writes=["degl"])
                P.op("act", lambda e: e.activation(out=fl(bg_tm), in_=fl(gc_tm), func=AF.Exp), reads=["dgc"], writes=["dbg"])
                P.op("dve", lambda e: e.tensor_tensor(out=fl(bg_tm), in0=fl(bg_tm), in1=fl(beta_tm), op=ALU.mult),
                     reads=["dbg", "dbeta"], writes=["dbg"])
                P.barrier()
                for h in range(4):
                    with ExitStack() as ph1:
                        def sb1(name, shape, dt):
                            return ph1.enter_context(nc.sbuf_tensor(_u + ("s_" + name), list(shape), dt))
                        qT, kT, vT, zsT = mT[:, 0, :], mT[:, 1, :], mT[:, 2, :], mT[:, 3, :]
                        xpad = mT[:, 4:7, :].rearrange("p a b -> p (a b)").bitcast(F32)[:, 0:S + 3]
                        with ExitStack() as ph2:
                            acc = ph2.enter_context(nc.sbuf_tensor(_u + "s_dacc", [128, S], F32))
                            P.op("dve", lambda e: e.memset(xpad[:, 0:3], 0.0), writes=["dxpad0"])
                            for xi, (c_base, dstT) in enumerate([(C_DNQ, qT), (C_DNK, kT), (C_DNV, vT)]):
                                proj_fm(l, c_base + h * 128, 128, lambda tc, pap, pkey: P.op(
                                    "act", lambda e: e.activation(out=xpad[:, 3 + tc * TC:3 + (tc + 1) * TC], in_=pap, func=AF.Copy),
                                    reads=[pkey], writes=[("dxpad", tc)]))
                                allx = [("dxpad", tc) for tc in range(NT)] + ["dxpad0"]
                                cw = dncw[:, l, xi * 4 + h, :]
                                P.op("dve", lambda e: e.tensor_scalar(out=acc[:, :], in0=xpad[:, 3:3 + S], scalar1=cw[:, 3:4],
                                                                      scalar2=None, op0=ALU.mult),
                                     reads=allx + ["dncw"], writes=["dacc"])
                                for j in range(3):
                                    P.op("dve", lambda e: e.scalar_tensor_tensor(out=acc[:, :], in0=xpad[:, j:j + S],
                                                                                 scalar=cw[:, j:j + 1], in1=acc[:, :],
                                                                                 op0=ALU.mult, op1=ALU.add),
                                         reads=allx + ["dncw", "dacc"], writes=["dacc"])
                                P.op("act", lambda e: e.activation(out=acc[:, :], in_=acc[:, :], func=AF.Silu),
                                     reads=["dacc"], writes=["dacc"])
                                if xi == 2:
                                    P.op("dve", lambda e: e.tensor_copy(out=vT, in_=acc[:, :]), reads=["dacc"], writes=["dvT"])
                                else:
                                    for tc in range(NT):
                                        tsl = slice(tc * TC, (tc + 1) * TC)
                                        P.op("act", lambda e: e.activation(out=sqt[:, 0, :], in_=acc[:, tsl], func=AF.Square),
                                             reads=["dacc"], writes=[("sqt", 0)])
                                        P.op("pe", lambda e: e.matmul(ps[0][:, :], lhsT=cst["ones"][:, :], rhs=sqt[:, 0, :],
                                                                      start=True, stop=True),
                                             reads=[("sqt", 0), "k_ones"], writes=[PS[0]])
                                        P.op("act", lambda e: e.activation(out=rstd[:, :], in_=ps[0][:, :], func=AF.Ln,
                                                                           bias=rtrp[:, 3:4], scale=1.0),
                                             reads=[PS[0], "rtrp"], writes=["rstd"])
                                        P.op("act", lambda e: e.activation(out=rstd[:, :], in_=rstd[:, :], func=AF.Exp, scale=-0.5),
                                             reads=["rstd"], writes=["rstd"])
                                        sc = float(128.0 ** -0.5) if xi == 0 else 1.0
                                        P.op("dve", lambda e: e.scalar_tensor_tensor(out=dstT[:, tsl], in0=acc[:, tsl], scalar=sc,
                                                                                     in1=rstd[:, :], op0=ALU.mult, op1=ALU.mult),
                                             reads=["dacc", "rstd"], writes=["dqT" if xi == 0 else "dkT"])
                            P.barrier()
                        proj_fm(l, C_DNZ + h * 128, 128, lambda tc, pap, pkey: P.op(
                            "act", lambda e: e.activation(out=zsT[:, tc * TC:(tc + 1) * TC], in_=pap, func=AF.Silu),
                            reads=[pkey], writes=["dz"]))
                        NB = 8
                        B3 = [64, NB, 64]
                        gb = sb1("dgb", [64, NB, 64], F32)
                        dd = sb1("ddd", [64, NB, 64], F32)
                        LT = sb1("dLT", [64, NB, 64], F32)
                        egcb = sb1("degcb", [128, NB * 64], F32)
                        bs = sb1("dbs", [64, NB, 64], BF16)
                        Pm = sb1("dPm", [64, NB, 64], BF16)
                        PTm = sb1("dPTm", [64, NB, 64], BF16)
                        Tt = [sb1(f"dTt{i}", [64, NB, 64], BF16) for i in range(2)]
                        kbg = mT[0:64, 7, 0:1024].rearrange("p (a b) -> p a b", b=128)
                        vb = mT[0:64, 7, 1024:2048].rearrange("p (a b) -> p a b", b=128)
                        aT = sb1("daT", [64, NB, 64], BF16)
                        aTt = sb1("daTt", [64, NB, 64], BF16)
                        qg = sb1("dqg", [128, NB * 64], BF16)
                        u_sb = sb1("du", [64, NB, 128], BF16)
                        wT = sb1("dwT", [128, NB, 64], BF16)
                        kd = sb1("dkd", [64, NB, 128], BF16)
                        vnew = [sb1(f"dvnew{i}", [64, 128], BF16) for i in range(2)]
                        Sf = sb1("dSf", [128, 128], F32)
                        Sb = sb1("dSb", [128, 128], BF16)
                        nsq = sb1("dnsq", [128, TC], BF16)
                        ntmp = sb1("dntmp", [128, TC], F32)
                        identb = cst["ident"]
                        id64 = identb[0:64, 0:64]
                        fl3 = lambda t: t[:, :, :].rearrange("p a b -> p (a b)")
                        p3 = lambda i, w=64: ps[i][H, 0:NB * w].rearrange("p (a b) -> p a b", b=w)
                        bcast = lambda ap2: ap2.unsqueeze(1).broadcast_to(B3)

                        def stage_a(bt):
                            n0 = bt * NB
                            bsl = slice(n0 * 64, (n0 + NB) * 64)
                            nsl = slice(n0, n0 + NB)
                            sc_g = g_tm[:, nsl, h:h + 1].broadcast_to(B3)
                            sc_b = beta_tm[:, nsl, h:h + 1].broadcast_to(B3)
                            sc_gc = gc_tm[:, nsl, h:h + 1].broadcast_to(B3)
                            P.op("dve", lambda e: e.tensor_tensor(out=gb[:, :, :], in0=bcast(mincl), in1=sc_g, op=ALU.mult),
                                 reads=["dnc", "dg"], writes=["dgb"])
                            P.op("pe", lambda e: e.matmul(ps[0][:, :], lhsT=onesf[0:64, :], rhs=fl3(gb), start=True, stop=True),
                                 reads=["dnc", "dgb"], writes=[PS[0]])
                            yield
                            P.op("dve", lambda e: e.tensor_tensor(out=gb[:, :, :], in0=bcast(identf[0:64, 0:64]), in1=sc_b, op=ALU.mult),
                                 reads=["dnc", "dbeta"], writes=["dgb"])
                            P.op("pe", lambda e: e.matmul(ps[1][H, :], lhsT=onesf[0:64, 0:64], rhs=fl3(gb), start=True, stop=True),
                                 reads=["dnc", "dgb"], writes=[PS[1]])
                            yield
                            P.op("act", lambda e: e.activation(out=egcb[:, :], in_=ps[0][:, :], func=AF.Exp),
                                 reads=[PS[0]], writes=["degcb"])
                            P.op("dve", lambda e: e.tensor_tensor(out=dd[:, :, :], in0=p3(0), in1=sc_gc, op=ALU.subtract),
                                 reads=[PS[0], "dgc", "degcb"], writes=["ddd"])
                            yield
                            P.op("act", lambda e: e.activation(out=fl3(dd), in_=fl3(dd), func=AF.Exp), reads=["ddd"], writes=["ddd"])
                            P.op("dve", lambda e: e.tensor_tensor(out=bs[:, :, :], in0=p3(1), in1=bcast(mstrict), op=ALU.mult),
                                 reads=[PS[1], "dnc"], writes=["dbs"])
                            yield
                            P.op("dve", lambda e: e.scalar_tensor_tensor(out=dd[:, :, :], in0=dd[:, :, :], scalar=1.0, in1=bcast(mincl),
                                                                         op0=ALU.min, op1=ALU.mult),
                                 reads=["ddd", "dnc"], writes=["ddd"])
                            for i in range(NB):
                                csl = slice((n0 + i) * 64, (n0 + i + 1) * 64)
                                P.op("pe", lambda e: e.matmul(ps[2][H, i * 64:(i + 1) * 64], lhsT=kT[:, csl], rhs=kT[:, csl], start=True, stop=True),
                                     reads=["dkT"], writes=[PS[2]], inc=(i == NB - 1))
                            for i in range(NB):
                                csl = slice((n0 + i) * 64, (n0 + i + 1) * 64)
                                P.op("pe", lambda e: e.matmul(ps[3][H, i * 64:(i + 1) * 64], lhsT=kT[:, csl], rhs=qT[:, csl], start=True, stop=True),
                                     reads=["dkT", "dqT"], writes=[PS[3]], inc=(i == NB - 1))
                            yield
                            P.op("dve", lambda e: e.tensor_tensor(out=LT[:, :, :], in0=p3(2), in1=dd[:, :, :], op=ALU.mult),
                                 reads=[PS[2], "ddd"], writes=["dLT"])
                            P.op("dve", lambda e: e.tensor_tensor(out=aTt[:, :, :], in0=p3(3), in1=dd[:, :, :], op=ALU.mult),
                                 reads=[PS[3], "ddd"], writes=["daTt"])
                            yield
                            P.op("dve", lambda e: e.scalar_tensor_tensor(out=Pm[:, :, :], in0=LT[:, :, :], scalar=-1.0, in1=bs[:, :, :],
                                                                         op0=ALU.mult, op1=ALU.mult),
                                 reads=["dLT", "dbs"], writes=["dPm"])
                            yield
                            ptv = ps[4][H, 0:NB * 32].bitcast(BF16)
                            for i in range(NB):
                                P.op("pe", lambda e: e.transpose(ptv[:, i * 64:(i + 1) * 64], Pm[:, i, :], id64),
                                     reads=["dPm", "k_ident"], writes=[PS[4]], inc=(i == NB - 1))
                            P.op("dve", lambda e: e.tensor_tensor(out=Tt[0][:, :, :], in0=Pm[:, :, :], in1=bcast(id64), op=ALU.add),
                                 reads=["dPm", "k_ident"], writes=[("dTt", 0)])
                            yield
                            P.op("act", lambda e: e.activation(out=fl3(PTm), in_=ptv, func=AF.Copy), reads=[PS[4]], writes=["dPTm"])
                            yield
                            cur = 0
                            for lev in range(1, 6):
                                if lev < 5:
                                    for i in range(NB):
                                        P.op("pe", lambda e: e.matmul(ps[2][H, i * 64:(i + 1) * 64], lhsT=PTm[:, i, :], rhs=Pm[:, i, :],
                                                                      start=True, stop=True),
                                             reads=["dPm", "dPTm"], writes=[PS[2]], inc=(i == NB - 1))
                                for i in range(NB):
                                    P.op("pe", lambda e: e.matmul(ps[3][H, i * 64:(i + 1) * 64], lhsT=Pm[:, i, :], rhs=PTm[:, i, :],
                                                                  start=True, stop=True),
                                         reads=["dPm", "dPTm"], writes=[PS[3]], inc=(i == NB - 1))
                                yield
                                if lev < 5:
                                    P.op("act", lambda e: e.activation(out=fl3(Pm), in_=ps[2][H, :], func=AF.Copy), reads=[PS[2]], writes=["dPm"])
                                P.op("dve", lambda e: e.tensor_copy(out=fl3(PTm), in_=ps[3][H, :]), reads=[PS[3]], writes=["dPTm"])
                                yield
                                for i in range(NB):
                                    P.op("pe", lambda e: e.matmul(ps[4][H, i * 64:(i + 1) * 64], lhsT=PTm[:, i, :], rhs=Tt[cur][:, i, :],
                                                                  start=True, stop=False),
                                         reads=["dPTm", ("dTt", cur)], writes=[PS[4]], inc=False)
                                    P.op("pe", lambda e: e.matmul(ps[4][H, i * 64:(i + 1) * 64], lhsT=id64, rhs=Tt[cur][:, i, :],
                                                                  start=False, stop=True),
                                         reads=["k_ident", ("dTt", cur)], writes=[PS[4]], inc=(i == NB - 1))
                                yield
                                cur = 1 - cur
                                P.op("dve", lambda e: e.tensor_copy(out=fl3(Tt[cur]), in_=ps[4][H, :]), reads=[PS[4]], writes=[("dTt", cur)])
                                yield
                            kv = ps[0][H, :].bitcast(BF16)
                            vv = ps[1][H, :].bitcast(BF16)
                            for i in range(NB):
                                csl = slice((n0 + i) * 64, (n0 + i + 1) * 64)
                                P.op("pe", lambda e: e.transpose(kv[:, i * 128:(i + 1) * 128], kT[:, csl], identb[:, :]),
                                     reads=["dkT", "k_ident"], writes=[PS[0]], inc=(i == NB - 1))
                            for i in range(NB):
                                csl = slice((n0 + i) * 64, (n0 + i + 1) * 64)
                                P.op("pe", lambda e: e.transpose(vv[:, i * 128:(i + 1) * 128], vT[:, csl], identb[:, :]),
                                     reads=["dvT", "k_ident"], writes=[PS[1]], inc=(i == NB - 1))
                            yield
                            B3w = [64, NB, 128]
                            kv3 = kv.rearrange("p (a b) -> p a b", b=128)
                            vv3 = vv.rearrange("p (a b) -> p a b", b=128)
                            P.op("dve", lambda e: e.tensor_tensor(out=kbg[:, :, :], in0=kv3, in1=bg_tm[:, nsl, h:h + 1].broadcast_to(B3w), op=ALU.mult),
                                 reads=[PS[0], "dbg"], writes=["dkbg"])
                            P.op("dve", lambda e: e.tensor_tensor(out=vb[:, :, :], in0=vv3, in1=beta_tm[:, nsl, h:h + 1].broadcast_to(B3w), op=ALU.mult),
                                 reads=[PS[1], "dbeta"], writes=["dvb"])
                            yield
                            for i in range(NB):
                                pb = 2 + i // 4
                                P.op("pe", lambda e: e.matmul(ps[pb][H, (i % 4) * 128:(i % 4 + 1) * 128], lhsT=Tt[cur][:, i, :], rhs=vb[:, i, :],
                                                              start=True, stop=True),
                                     reads=[("dTt", cur), "dvb"], writes=[PS[pb]], inc=(i % 4 == 3))
                            for i in range(NB):
                                P.op("pe", lambda e: e.matmul(ps[4][:, i * 64:(i + 1) * 64], lhsT=kbg[:, i, :], rhs=Tt[cur][:, i, :],
                                                              start=True, stop=True),
                                     reads=[("dTt", cur), "dkbg"], writes=[PS[4]], inc=(i == NB - 1))
                            yield "OUT"
                            P.op("dve", lambda e: e.tensor_tensor(out=kd[:, :, :], in0=kv3, in1=ed_tm[:, nsl, h:h + 1].broadcast_to(B3w), op=ALU.mult),
                                 reads=[PS[0], "ded"], writes=["dkd"])
                            P.op("dve", lambda e: e.tensor_tensor(out=qg[:, :], in0=qT[:, bsl], in1=egcb[:, :], op=ALU.mult),
                                 reads=["dqT", "degcb"], writes=["dqg"])
                            P.op("pool", lambda e: e.tensor_copy(out=aT[:, :, :], in_=aTt[:, :, :]), reads=["daTt"], writes=["daT"])
                            for hb in range(2):
                                P.op("act", lambda e: e.activation(out=u_sb[:, hb * 4:(hb + 1) * 4, :].rearrange("p a b -> p (a b)"),
                                                                   in_=ps[2 + hb][H, :], func=AF.Copy),
                                     reads=[PS[2 + hb]], writes=["du"])
                            P.op("dve", lambda e: e.tensor_copy(out=wT[:, :, :].rearrange("p a b -> p (a b)"), in_=ps[4][:, :]),
                                 reads=[PS[4]], writes=["dwT"])
                            yield

                        def stage_b(bt):
                            n0 = bt * NB
                            bsl = slice(n0 * 64, (n0 + NB) * 64)
                            for i in range(NB):
                                n = n0 + i
                                vn = vnew[n % 2]
                                kvn = ("dvnew", n % 2)
                                if n == 0:
                                    P.op("dve", lambda e: e.tensor_copy(out=vn[:, :], in_=u_sb[:, i, :]), reads=["du"], writes=[kvn])
                                else:
                                    P.op("pe", lambda e: e.matmul(ps[5][H, 0:128], lhsT=wT[:, i, :], rhs=Sb[:, :], start=True, stop=True),
                                         reads=["dwT", "dSb"], writes=[PS[5]])
                                    yield
                                    P.op("dve", lambda e: e.tensor_tensor(out=vn[:, :], in0=u_sb[:, i, :], in1=ps[5][H, 0:128], op=ALU.subtract),
                                         reads=["du", PS[5]], writes=[kvn])
                                yield
                                osl = slice(i * 64, (i + 1) * 64)
                                if n < 31:
                                    P.op("pe", lambda e: e.matmul(ps[6][:, 0:128], lhsT=kd[:, i, :], rhs=vn[:, :], start=True, stop=True),
                                         reads=["dkd", kvn], writes=[PS[6]])
                                if n > 0:
                                    P.op("pe", lambda e: e.matmul(ps[7][:, osl], lhsT=Sb[:, :], rhs=qg[:, osl], start=True, stop=False),
                                         reads=["dSb", "dqg"], writes=[PS[7]], inc=False)
                                P.op("pe", lambda e: e.matmul(ps[7][:, osl], lhsT=vn[:, :], rhs=aT[:, i, :], start=(n == 0), stop=True),
                                     reads=[kvn, "daT"], writes=[PS[7]])
                                yield
                                if n < 31:
                                    if n == 0:
                                        P.op("dve", lambda e: e.tensor_copy(out=Sf[:, :], in_=ps[6][:, 0:128]), reads=[PS[6]], writes=["dSf"])
                                    else:
                                        P.op("dve", lambda e: e.scalar_tensor_tensor(out=Sf[:, :], in0=Sf[:, :], scalar=egl[:, n, h:h + 1],
                                                                                     in1=ps[6][:, 0:128], op0=ALU.mult, op1=ALU.add),
                                             reads=[PS[6], "dSf", "degl"], writes=["dSf"])
                                    yield
                                    P.op("act", lambda e: e.activation(out=Sb[:, :], in_=Sf[:, :], func=AF.Copy), reads=["dSf"], writes=["dSb"])
                                    yield
                            P.op("act", lambda e: e.activation(out=nsq[:, :], in_=ps[7][:, :], func=AF.Square),
                                 reads=[PS[7]], writes=["dnsq"])
                            yield
                            P.op("pe", lambda e: e.matmul(ps[5][:, :], lhsT=cst["ones"][:, :], rhs=nsq[:, :], start=True, stop=True),
                                 reads=["k_ones", "dnsq"], writes=[PS[5]])
                            yield
                            P.op("act", lambda e: e.activation(out=rstd[:, :], in_=ps[5][:, :], func=AF.Ln, bias=rtrp[:, 3:4],
                                                               scale=float(1.0 / 128.0)), reads=[PS[5], "rtrp"], writes=["rstd"])
                            P.op("act", lambda e: e.activation(out=rstd[:, :], in_=rstd[:, :], func=AF.Exp, scale=-0.5),
                                 reads=["rstd"], writes=["rstd"])
                            yield
                            P.op("dve", lambda e: e.scalar_tensor_tensor(out=ntmp[:, :], in0=ps[7][:, :], scalar=dng[:, l:l + 1],
                                                                         in1=rstd[:, :], op0=ALU.mult, op1=ALU.mult),
                                 reads=[PS[7], "dng", "rstd"], writes=["dntmp"])
                            P.op("dve", lambda e: e.tensor_tensor(out=obT[:, h, bsl], in0=ntmp[:, :], in1=zsT[:, bsl], op=ALU.mult),
                                 reads=["dntmp", "dz"], writes=[("obT", h)])
                            yield

                        import os as _os2
                        _stop = int(_os2.environ.get("DN_STOP", "-1"))
                        if _stop >= 0:
                            g_ = stage_a(0)
                            for _ in range(_stop):
                                next(g_)
                        else:
                            def drive(b_gen, a_gen):
                                a_wait, a_done, b_done = False, a_gen is None, b_gen is None
                                while not (b_done and (a_done or a_wait)):
                                    if not b_done:
                                        try:
                                            next(b_gen)
                                        except StopIteration:
                                            b_done = True
                                    if not a_done and not a_wait:
                                        try:
                                            if next(a_gen) == "OUT":
                                                a_wait = True
                                        except StopIteration:
                                            a_done = True
                                if not a_done:
                                    for _ in a_gen:
                                        pass

                            drive(None, stage_a(0))
                            for bt in range(4):
                                drive(stage_b(bt), stage_a(bt + 1) if bt + 1 < 4 else None)
                        P.barrier()

        def dump(nm, tile, nchunks, keyname, nokey=False):
            if nokey:
                P.barrier()
            with nc.sbuf_tensor(_u + "s_dbgf_" + nm, [128, S], F32) as dbgf:
                for k in range(nchunks):
                    P.op("dve", lambda e: e.tensor_copy(out=dbgf[:, :], in_=tile[:, k, :]),
                         reads=[(keyname, k)], writes=["dbgf"])
                    P.dma(dbg_d[nm][k * 128:(k + 1) * 128, :], dbgf[:, :], reads=["dbgf"], writes=["dbg_" + nm], stream="o")
                P.barrier()

        for l in range(n_layers):
            norm_to(lambda k, tc: (hT[:, k, tc * TC:(tc + 1) * TC], ("hT", k)), normg[:, l, :], l)
            if debug and l == 0:
                with nc.sbuf_tensor(_u + "dbgf", [128, S], F32) as dbgf:
                    for k in range(8):
                        P.op("dve", lambda e, k=k: e.tensor_copy(out=dbgf[:, :], in_=hT[:, k, :]),
                             reads=[("hT", k)], writes=["dbgf"])
                        P.dma(dbg_d["hT"][k * 128:(k + 1) * 128, :], dbgf[:, :], reads=["dbgf"], writes=["dbg_hT"],
                              stream="o")
                    P.barrier()

            def mem_attention(l):
                ph = ExitStack()
                def sbp(name, shape, dt):
                    return ph.enter_context(nc.sbuf_tensor(_u + "s_" + name, list(shape), dt))
                memT = sbp("memT", [128, 8, MEM_LEN], F32)
                memn = sbp("memn", [128, 8, MEM_LEN], BF16)
                kmT = sbp("kmT", [128, 2, MEM_LEN], BF16)
                vm = sbp("vm", [128, 2, 256], BF16)
                qmT = sbp("qmT", [128, 2, S], BF16)
                pT = [sbp(f"pT{i}", [128, TC], BF16) for i in range(2)]
                rden = sbp("rden", [128, TC], F32)
                for k in range(8):
                    P.dma(memT[:, k, :], memT_d[k * 128:(k + 1) * 128, :], writes=[("memT", k)], stream="x")
                P.op("dve", lambda e: e.tensor_scalar(out=g32[:, :], in0=memg[:, l, :], scalar1=float(math.sqrt(D)),
                                                      scalar2=None, op0=ALU.mult), reads=["memg"], writes=["g32"])
                rms_stats(memT, [("memT", k) for k in range(8)], 8, MEM_LEN, 0, 0, sqt, "sqt", rstd, "rstd", None)
                for k in range(8):
                    P.op("dve", lambda e, k=k: e.scalar_tensor_tensor(
                        out=memn[:, k, :], in0=memT[:, k, :], scalar=g32[:, k:k + 1], in1=rstd[:, 0:MEM_LEN],
                        op0=ALU.mult, op1=ALU.mult), reads=[("memT", k), "g32", "rstd"], writes=[("memn", k)])
                for ec in range(2):
                    wb, wkey = load_w(w_kv_d[l, :, ec * 128:(ec + 1) * 128])
                    for k in range(8):
                        P.op("pe", lambda e, k=k, wb=wb: e.matmul(ps[1][:, 0:MEM_LEN], lhsT=wb[:, k, :], rhs=memn[:, k, :],
                                                                  start=(k == 0), stop=(k == 7)),
                             reads=[wkey, ("memn", k)], writes=[PS[1]], inc=(k == 7))
                    P.op("dve", lambda e, ec=ec: e.tensor_copy(out=kmT[:, ec, :], in_=ps[1][:, 0:MEM_LEN]),
                         reads=[PS[1]], writes=[("kmT", ec)])
                for vc in range(2):
                    wb, wkey = load_w(w_kv_d[l, :, 256 + vc * 128:256 + (vc + 1) * 128])
                    for mt in range(2):
                        for k in range(8):
                            P.op("pe", lambda e, k=k, wb=wb, mt=mt: e.matmul(
                                ps[2][:, 0:128], lhsT=memn[:, k, mt * 128:(mt + 1) * 128], rhs=wb[:, k, :],
                                start=(k == 0), stop=(k == 7)),
                                reads=[wkey, ("memn", k)], writes=[PS[2]], inc=(k == 7))
                        P.op("dve", lambda e, mt=mt, vc=vc: e.tensor_copy(out=vm[:, mt, vc * 128:(vc + 1) * 128],
                                                                          in_=ps[2][:, 0:128]),
                             reads=[PS[2]], writes=[("vm", mt)])
                for ec in range(2):
                    def ev(tc, pap, pkey, ec=ec):
                        P.op("act", lambda e: e.activation(out=qmT[:, ec, tc * TC:(tc + 1) * TC], in_=pap,
                                                           func=AF.Copy, scale=0.125),
                             reads=[pkey], writes=[("qmT", ec)])
                    proj_fm(l, C_MQ + ec * 128, 128, ev)
                for h in range(4):
                    ec, r0 = h // 2, (h % 2) * 64
                    for tc in range(NT):
                        tsl = slice(tc * TC, (tc + 1) * TC)
                        for mb in range(2):
                            pi = 3 + mb
                            P.op("pe", lambda e, mb=mb, pi=pi: e.matmul(
                                ps[pi][:, :], lhsT=kmT[r0:r0 + 64, ec, mb * 128:(mb + 1) * 128],
                                rhs=qmT[r0:r0 + 64, ec, tsl], start=True, stop=True),
                                reads=[("kmT", ec), ("qmT", ec)], writes=[PS[pi]])
                            P.op("act", lambda e, mb=mb, pi=pi: e.activation(out=pT[mb][:, :], in_=ps[pi][:, :],
                                                                             func=AF.Exp),
                                 reads=[PS[pi]], writes=[("pT", mb)])
                        for mb in range(2):
                            P.op("pe", lambda e, mb=mb: e.matmul(
                                ps[5][r0:r0 + 64, :], lhsT=vm[:, mb, h * 64:(h + 1) * 64], rhs=pT[mb][:, :],
                                start=(mb == 0), stop=(mb == 1)),
                                reads=[("vm", mb), ("pT", mb)], writes=[PS[5]], inc=(mb == 1))
                        for mb in range(2):
                            P.op("pe", lambda e, mb=mb: e.matmul(
                                ps[6][r0:r0 + 64, :], lhsT=cst["ones"][:, 0:64], rhs=pT[mb][:, :],
                                start=(mb == 0), stop=(mb == 1)),
                                reads=["k_ones", ("pT", mb)], writes=[PS[6]], inc=(mb == 1))
                        P.op("dve", lambda e: e.reciprocal(out=rden[r0:r0 + 64, :], in_=ps[6][r0:r0 + 64, :]),
                             reads=[PS[6]], writes=["rden"])
                        P.op("dve", lambda e, tsl=tsl: e.tensor_tensor(out=obT[r0:r0 + 64, ec, tsl], in0=ps[5][r0:r0 + 64, :],
                                                                      in1=rden[r0:r0 + 64, :], op=ALU.mult),
                             reads=[PS[5], "rden"], writes=[("obT", ec)])
                P.barrier()
                ph.close()

            def merge(br, nwc, first):
                with ExitStack() as ph:
                    sg = [ph.enter_context(nc.sbuf_tensor(_u + f"sg{i}", [128, TC], F32)) for i in range(2)]
                    tmp = [ph.enter_context(nc.sbuf_tensor(_u + f"mtmp{i}", [128, TC], F32)) for i in range(2)]
                    it = 0
                    for dc in range(8):
                        wg, wgk = load_w(w_in_d[l, :, C_G + br * D + dc * 128:C_G + br * D + (dc + 1) * 128])
                        wr, wrk = load_w(w_br_d[br][l, :, dc * 128:(dc + 1) * 128], rows=nwc)
                        for tc in range(NT):
                            tsl = slice(tc * TC, (tc + 1) * TC)
                            b = it % 2
                            it += 1
                            pg, pp = 1 + b, 3 + b
                            for k in range(8):
                                P.op("pe", lambda e, k=k, pg=pg, tsl=tsl, wg=wg: e.matmul(
                                    ps[pg][:, :], lhsT=wg[:, k, :], rhs=hT[:, k, tsl], start=(k == 0), stop=(k == 7)),
                                    reads=[wgk, ("hT", k)], writes=[PS[pg]], inc=(k == 7))
                            for k in range(nwc):
                                P.op("pe", lambda e, k=k, pp=pp, tsl=tsl, wr=wr: e.matmul(
                                    ps[pp][:, :], lhsT=wr[:, k, :], rhs=obT[:, k, tsl], start=(k == 0), stop=(k == nwc - 1)),
                                    reads=[wrk, ("obT", k)], writes=[PS[pp]], inc=(k == nwc - 1))
                            P.op("act", lambda e, b=b, pg=pg, dc=dc: e.activation(
                                out=sg[b][:, :], in_=ps[pg][:, :], func=AF.Sigmoid,
                                bias=bgate[:, l, br * 8 + dc:br * 8 + dc + 1], scale=1.0),
                                reads=[PS[pg], "bgate"], writes=[("sg", b)])
                            if first:
                                P.op("dve", lambda e, b=b, pp=pp, dc=dc, tsl=tsl: e.tensor_tensor(
                                    out=mT[:, dc, tsl], in0=ps[pp][:, :], in1=sg[b][:, :], op=ALU.mult),
                                    reads=[PS[pp], ("sg", b)], writes=[("mT", dc, tc)])
                            else:
                                P.op("dve", lambda e, b=b, pp=pp: e.tensor_tensor(
                                    out=tmp[b][:, :], in0=ps[pp][:, :], in1=sg[b][:, :], op=ALU.mult),
                                    reads=[PS[pp], ("sg", b)], writes=[("mtmp", b)])
                                P.op("dve", lambda e, b=b, dc=dc, tsl=tsl: e.tensor_tensor(
                                    out=mT[:, dc, tsl], in0=mT[:, dc, tsl], in1=tmp[b][:, :], op=ALU.add),
                                    reads=[("mtmp", b), ("mT", dc, tc)], writes=[("mT", dc, tc)])
                    P.barrier()

            if 1 in branches:
                deltanet(l)
                if debug and l == 0:
                    dump('odn', obT, 4, 'obT')
                merge(1, 4, True)
            mem_attention(l)
            if debug and l == 0:
                dump('omem', obT, 2, 'obT')
            merge(3, 2, 1 not in branches)
            if 0 in branches:
                sb_attention(l)
                if debug and l == 0:
                    dump('osb', obT, 4, 'obT')
                merge(0, 4, False)
            if 2 in branches:
                retention(l)
                if debug and l == 0:
                    dump('ort', obT, 4, 'obT')
                merge(2, 4, False)

            for ec in range(8):
                wo, wok = load_w(w_out_d[l, :, ec * 128:(ec + 1) * 128])
                for tc in range(NT):
                    tsl = slice(tc * TC, (tc + 1) * TC)
                    pi = 1 + tc % 2
                    for dc in range(8):
                        P.op("pe", lambda e, dc=dc, pi=pi, tsl=tsl, wo=wo: e.matmul(
                            ps[pi][:, :], lhsT=wo[:, dc, :], rhs=mT[:, dc, tsl], start=(dc == 0), stop=(dc == 7)),
                            reads=[wok, ("mT", dc, tc)], writes=[PS[pi]], inc=(dc == 7))
                    P.op("dve", lambda e, ec=ec, pi=pi, tsl=tsl: e.tensor_tensor(
                        out=xT[:, ec, tsl], in0=xT[:, ec, tsl], in1=ps[pi][:, :], op=ALU.add),
                        reads=[PS[pi], ("xT", ec)], writes=[("xT", ec)])
            P.barrier()

        with ExitStack() as ph:
            ot = [ph.enter_context(nc.sbuf_tensor(_u + f"ot{i}", [128, TC], F32)) for i in range(2)]
            cnt = [0]

            def dst(k, tc):
                b = cnt[0] % 2
                cnt[0] += 1
                return ot[b][:, :], ("ot", b)
            P.op("dve", lambda e: e.tensor_scalar(out=g32[:, :], in0=fing[:, :], scalar1=float(math.sqrt(D)), scalar2=None,
                                                  op0=ALU.mult), reads=["fing"], writes=["g32"])
            for tc in range(NT):
                rms_stats(xT, [("xT", k) for k in range(8)], 8, TC, tc * TC, 0, sqt, "sqt", rstd, "rstd", None)
                for k in range(8):
                    ap, key = dst(k, tc)
                    P.op("dve", lambda e, k=k, tc=tc, ap=ap: e.scalar_tensor_tensor(
                        out=ap, in0=xT[:, k, tc * TC:(tc + 1) * TC], scalar=g32[:, k:k + 1], in1=rstd[:, :],
                        op0=ALU.mult, op1=ALU.mult),
                        reads=[("xT", k), "g32", "rstd"], writes=[key])
                    P.dma(outT_d[k * 128:(k + 1) * 128, tc * TC:(tc + 1) * TC], ap, reads=[key], writes=["outT"],
                          stream="o")
            P.finish(["outT", "dbg_hT", "dbg_omem", "dbg_osb", "dbg_odn", "dbg_ort"])
            P.barrier(full=True)
        P.emit()
        print("instructions recorded:", P.nins, {n: len(P.q[n]) for n in P.names})
    return nc


_NC_CACHE = {}


def _prep_inputs(inputs, b):
    f = np.float32
    m = {}
    m["xT"] = np.ascontiguousarray(inputs["x"][b].T.astype(f))
    m["memT"] = np.ascontiguousarray(inputs["mem"][b].T.astype(f))
    m["w_in"] = np.ascontiguousarray(inputs["w_in"], dtype=f)
    m["w_mem_kv"] = np.ascontiguousarray(inputs["w_mem_kv"], dtype=f)
    for n in ["w_br_sb", "w_br_dn", "w_br_ret", "w_br_mem", "w_out"]:
        m[n] = np.ascontiguousarray(inputs[n], dtype=f)
    m["norm_g"] = np.ascontiguousarray(inputs["norm_g"].reshape(DEPTH, 8, 128).transpose(0, 2, 1), dtype=f)
    m["mem_norm_g"] = np.ascontiguousarray(inputs["mem_norm_g"].reshape(DEPTH, 8, 128).transpose(0, 2, 1), dtype=f)
    m["b_gate"] = np.ascontiguousarray(inputs["b_gate"].reshape(DEPTH, 32, 128).transpose(0, 2, 1), dtype=f)
    m["final_norm_g"] = np.ascontiguousarray(inputs["final_norm_g"].reshape(8, 128).T, dtype=f)
    for n, v in _consts().items():
        m["c_" + n] = v
    for n, v in _rt_consts().items():
        m["c_" + n] = v
    m["c_dn"] = _dn_consts()
    m["dn_conv_w"] = np.ascontiguousarray(inputs["dn_conv_w"].reshape(DEPTH, 4, 12, 128).transpose(0, 3, 2, 1), dtype=f)
    m["dn_norm_g"] = np.ascontiguousarray(inputs["dn_norm_g"].T, dtype=f)
    m["dn_alog"] = np.ascontiguousarray(np.broadcast_to(np.tile(inputs["dn_a_log"], (1, 32))[:, None, :], (DEPTH, 128, 128)), dtype=f)
    m["dn_dtb"] = np.ascontiguousarray(np.broadcast_to(np.tile(inputs["dn_dt_bias"], (1, 32))[:, None, :], (DEPTH, 128, 128)), dtype=f)
    m["pos"] = np.ascontiguousarray(np.broadcast_to(inputs["positions"][b].astype(np.int32)[None, :], (128, S)))
    m["ret_norm_g"] = np.ascontiguousarray(inputs["ret_norm_g"].reshape(DEPTH, 4, 128).transpose(0, 2, 1), dtype=f)
    return m


def kernel(**inputs):
    inputs = {k: np.asarray(v) for k, v in inputs.items()}
    if "nc" not in _NC_CACHE:
        _NC_CACHE["nc"] = build()
    nc = _NC_CACHE["nc"]
    in_maps = [_prep_inputs(inputs, b) for b in range(8)]
    res = run_bass_kernel_spmd(nc, in_maps, core_ids=list(range(8)))
    out = np.stack([np.ascontiguousarray(res.results[b]["outT"].T) for b in range(8)], axis=0)
    return out.astype(np.float32)
```

```python
import math
from contextlib import ExitStack
import numpy as np
import concourse.bass as bass
import concourse.mybir as mybir
from concourse.bass_utils import run_bass_kernel_spmd

F32 = mybir.dt.float32
BF16 = mybir.dt.bfloat16
I32 = mybir.dt.int32
AF = mybir.ActivationFunctionType
ALU = mybir.AluOpType

D = 1024
S = 2048
DEPTH = 2
MEM_LEN = 256
EPS = 1e-6
IN_COLS = 9992
C_SBQ, C_SBK, C_SBV, C_SBZ = 0, 512, 1024, 1536
C_DNQ, C_DNK, C_DNV, C_DNZ, C_DNA, C_DNB = 2048, 2560, 3072, 3584, 4096, 4100
C_RTQ, C_RTK, C_RTV, C_RTZ = 4104, 4360, 4616, 5128
C_MQ = 5640
C_G = 5896
NT = 4
TC = 512


class _Uniq:
    def __init__(self):
        self.n = 0

    def __add__(self, name):
        self.n += 1
        return f"{name}_{self.n}"


class _Rec:
    def __getattr__(self, name):
        return lambda *a, **k: (name, a, k)


_REC = _Rec()


class Prog:
    LIMIT = 30000

    def __init__(self, nc, stack):
        self.nc = nc
        self.stack = stack
        self.names = ["pe", "act", "dve", "pool", "sp"]
        self.q = {n: [] for n in self.names}
        self.cnt = {n: 0 for n in self.names}
        self.sems = {n: [] for n in self.names}
        self.seen = {n: {} for n in self.names}
        self.bufs = {}
        self.dma_sem = {}
        self.dma_cnt = {}
        self.same_sync = True
        self.dma_i = 0
        self.nins = 0

    def _eng_sem(self, eng, g):
        ep = (g - 1) // self.LIMIT
        while len(self.sems[eng]) <= ep:
            s = self.stack.enter_context(self.nc.semaphore(f"s_{eng}_{len(self.sems[eng])}"))
            self.sems[eng].append(s)
        return self.sems[eng][ep], (g - 1) % self.LIMIT + 1

    def _tok_sem(self, tok):
        kind, g = tok
        if kind.startswith("dma:"):
            return self.dma_sem[kind], 16 * g
        return self._eng_sem(kind, g)

    def _need(self, eng, tok):
        kind, g = tok
        if kind == eng:
            if eng in ("pe", "sp") or not self.same_sync:
                return False
        return self.seen[eng].get(kind, 0) < g

    def _collect(self, eng, reads, writes):
        toks = []
        for k in reads:
            b = self.bufs.get(k)
            if b and b[0] is not None:
                toks.append(b[0])
            if b and isinstance(k, tuple) and k[0] == "ps" and eng in ("act", "dve"):
                other = "dve" if eng == "act" else "act"
                if other in b[1]:
                    toks.append((other, b[1][other]))
        for k in writes:
            b = self.bufs.get(k)
            if b:
                if b[0] is not None:
                    toks.append(b[0])
                toks.extend(b[1].items())
        need = {}
        for t in toks:
            if self._need(eng, t):
                need[t[0]] = max(need.get(t[0], 0), t[1])
        return list(need.items())

    def _update(self, tok, reads, writes):
        for k in writes:
            self.bufs[k] = [tok, {}]
        for k in reads:
            b = self.bufs.setdefault(k, [None, {}])
            if k in writes:
                continue
            b[1][tok[0]] = max(b[1].get(tok[0], 0), tok[1])

    def op(self, eng, fn, reads=(), writes=(), inc=True):
        call = fn(_REC)
        fn = lambda e, c=call: getattr(e, c[0])(*c[1], **c[2])
        waits = self._collect(eng, reads, writes)
        for t in waits:
            self.seen[eng][t[0]] = t[1]
        ws = [self._tok_sem(t) for t in waits]
        for w in ws[1:]:
            self.q[eng].append(("wait", w[0], w[1]))
        if inc:
            self.cnt[eng] += 1
            tok = (eng, self.cnt[eng])
            sem, val = self._eng_sem(eng, self.cnt[eng])
            self.q[eng].append(("ins", fn, ws[0] if ws else None, (sem, 1)))
        else:
            tok = (eng, self.cnt[eng] + 1)
            self.q[eng].append(("ins", fn, ws[0] if ws else None, None))
        self._update(tok, reads, writes)
        self.nins += 1
        return tok

    NDS = 16

    def dma(self, out, in_, reads=(), writes=(), stream="d0", queue="act"):
        j = self.dma_i % self.NDS
        self.dma_i += 1
        kind = f"dma:{j}"
        if kind not in self.dma_sem:
            self.dma_sem[kind] = self.stack.enter_context(self.nc.semaphore(f"sd_{j}"))
            self.dma_cnt[kind] = 0
        waits = self._collect(queue, reads, writes)
        if self.dma_cnt[kind] > 0 and self.seen[queue].get(kind, 0) < self.dma_cnt[kind]:
            waits = [w for w in waits if w[0] != kind] + [(kind, self.dma_cnt[kind])]
        for t in waits:
            self.seen[queue][t[0]] = t[1]
        for t in waits:
            s, v = self._tok_sem(t)
            self.q[queue].append(("wait", s, v))
        self.dma_cnt[kind] += 1
        tok = (kind, self.dma_cnt[kind])
        self.q[queue].append(("ins", lambda e, o=out, i=in_: e.dma_start(out=o, in_=i), None,
                              (self.dma_sem[kind], 16)))
        self._update(tok, reads, writes)
        self.nins += 1
        return tok

    PERSIST = {"cstf", "normg", "memg", "bgate", "fing", "xT", "hT", "mT", "obT", "wst", "wb", "epsb", "g32", "sqt",
               "rstd", "oneb", "rtdt", "rtsd", "rtcd", "rtrp", "rtcn", "retg", "dnc", "dncw", "dng", "ps", "outT"}

    def _persistent(self, k):
        h = k[0] if isinstance(k, tuple) else k
        return h in self.PERSIST or (isinstance(h, str) and (h.startswith("k_") or h.startswith("dbg_")))

    def barrier(self, full=False):
        if full:
            toks = [(n, self.cnt[n]) for n in self.names if self.cnt[n] > 0]
            toks += [(k, c) for k, c in self.dma_cnt.items() if c > 0]
            engines = self.names
        else:
            need = {}
            for k, b in self.bufs.items():
                if self._persistent(k):
                    continue
                if b[0] is not None:
                    need[b[0][0]] = max(need.get(b[0][0], 0), b[0][1])
                for kind, c in b[1].items():
                    need[kind] = max(need.get(kind, 0), c)
            toks = list(need.items())
            engines = [n for n in self.names if n != "sp"]
        for eng in engines:
            for t in toks:
                if t[0] == eng and eng in ("pe", "sp"):
                    continue
                if self.seen[eng].get(t[0], 0) < t[1]:
                    self.seen[eng][t[0]] = t[1]
                    s, v = self._tok_sem(t)
                    self.q[eng].append(("wait", s, v))
        if full:
            self.bufs = {}
        else:
            self.bufs = {k: b for k, b in self.bufs.items() if self._persistent(k)}

    def finish(self, final_keys):
        toks = []
        for k in final_keys:
            b = self.bufs.get(k)
            if b and b[0] is not None:
                toks.append(b[0])
        for t in toks:
            s, v = self._tok_sem(t)
            self.q["act"].append(("wait", s, v))

    def simulate(self):
        sem = {}
        pos = {n: 0 for n in self.names}
        def ok(w):
            return w is None or sem.get(id(w[0]), 0) >= w[1]
        progress = True
        while progress:
            progress = False
            for n in self.names:
                q = self.q[n]
                while pos[n] < len(q):
                    ent = q[pos[n]]
                    if ent[0] == "wait":
                        if not ok((ent[1], ent[2])):
                            break
                    else:
                        if not ok(ent[2]):
                            break
                        if ent[3] is not None:
                            sem[id(ent[3][0])] = sem.get(id(ent[3][0]), 0) + ent[3][1]
                    pos[n] += 1
                    progress = True
        stuck = {n: (pos[n], len(self.q[n])) for n in self.names if pos[n] < len(self.q[n])}
        if stuck:
            msg = []
            for n, (p, ln) in stuck.items():
                ent = self.q[n][p]
                w = (ent[1], ent[2]) if ent[0] == "wait" else ent[2]
                msg.append(f"{n}@{p}/{ln} waits {w} have {sem.get(id(w[0]), 0)} kind={ent[0]} tag={ent[4] if len(ent) > 4 else None}")
            raise RuntimeError("DEADLOCK in recorded program: " + "; ".join(msg))

    def emit(self):
        self.simulate()
        nc = self.nc
        with nc.Block() as block:
            def replay(e, name):
                for ent in self.q[name]:
                    if ent[0] == "wait":
                        e.wait_ge(ent[1], ent[2])
                    else:
                        _, fn, w, inc = ent
                        ins = fn(e)
                        if w is not None:
                            ins._wait_ge(w[0], w[1])
                        if inc is not None:
                            ins.then_inc(inc[0], inc[1])

            @block.sync
            def _(e):
                replay(e, "sp")

            @block.scalar
            def _(e):
                replay(e, "act")

            @block.vector
            def _(e):
                replay(e, "dve")

            @block.tensor
            def _(e):
                replay(e, "pe")

            @block.gpsimd
            def _(e):
                replay(e, "pool")


def _consts():
    c = {}
    i = np.arange(128)
    c["ident"] = np.eye(128, dtype=np.float32)
    c["ones"] = np.ones((128, 128), np.float32)
    c["trineg"] = -(i[:, None] >= i[None, :]).astype(np.float32)
    e0 = np.zeros((128, 128), np.float32)
    e0[0, :] = 1.0
    c["e0"] = e0
    c["masksb"] = (i[:, None] < i[None, :]).astype(np.float32)
    return c


def _rt_consts():
    gam = [1.0 - 2.0 ** (-5.0 - h) for h in range(4)]
    i = np.arange(128)
    dt = np.zeros((128, 4, 128), np.float64)
    sd = np.zeros((128, 4), np.float64)
    cd = np.zeros((128, 2, 128), np.float64)
    for h in range(4):
        rel = i[None, :] - i[:, None]
        dt[:, h, :] = np.where(rel >= 0, gam[h] ** np.maximum(rel, 0), 0.0)
        sd[:, h] = gam[h] ** (127 - i)
    for p in range(2):
        for hh in range(2):
            cd[hh * 64:(hh + 1) * 64, p, :] = (gam[2 * p + hh] ** (i + 1.0))[None, :]
    inv = 10000.0 ** (-(np.arange(32, dtype=np.float32)) / np.float32(32))
    rp = np.zeros((128, 4), np.float32)
    rp[:, 0] = np.tile(inv.astype(np.float32), 4)
    rp[:, 1] = np.where((i % 64) < 32, -1.0, 1.0)
    rp[:, 2] = np.float32(math.pi / 2)
    rp[:, 3] = EPS
    cn = np.eye(128) - 1.0 / 128.0
    return {"rt_dt": dt.astype(np.float32), "rt_sd": sd.astype(np.float32), "rt_cd": cd.astype(np.float32),
            "rt_rp": rp, "rt_cn": cn.astype(np.float32)}


def _dn_consts():
    i = np.arange(64)
    c = np.zeros((128, 4, 128), np.float32)
    c[:, 0, :] = np.eye(128)
    c[:, 1, :] = 1.0
    c[0:64, 2, 0:64] = (i[:, None] <= i[None, :])
    c[0:64, 3, 0:64] = (i[:, None] < i[None, :])
    return c


CONST_NAMES = ["ident", "ones", "trineg", "e0", "masksb"]


def build(n_layers=DEPTH, debug=False, branches=(0, 1, 2, 3)):
    _u = _Uniq()
    nc = bass.Bass("TRN2", target_bir_lowering=False)
    dr = {}

    def din(name, shape, dt=F32):
        dr[name] = nc.dram_tensor(name, list(shape), dt, kind="ExternalInput").ap()
        return dr[name]

    xT_d = din("xT", [D, S])
    memT_d = din("memT", [D, MEM_LEN])
    w_in_d = din("w_in", [DEPTH, D, IN_COLS])
    w_kv_d = din("w_mem_kv", [DEPTH, D, 512])
    w_br_d = {0: din("w_br_sb", [DEPTH, 512, D]), 1: din("w_br_dn", [DEPTH, 512, D]),
              2: din("w_br_ret", [DEPTH, 512, D]), 3: din("w_br_mem", [DEPTH, 256, D])}
    w_out_d = din("w_out", [DEPTH, D, D])
    normg_d = din("norm_g", [DEPTH, 128, 8])
    memg_d = din("mem_norm_g", [DEPTH, 128, 8])
    bgate_d = din("b_gate", [DEPTH, 128, 32])
    fing_d = din("final_norm_g", [128, 8])
    cst_d = {n: din("c_" + n, [128, 128]) for n in CONST_NAMES}
    dnc_d = din("c_dn", [128, 4, 128])
    dncw_d = din("dn_conv_w", [DEPTH, 128, 12, 4])
    dng_d = din("dn_norm_g", [128, DEPTH])
    dnal_d = din("dn_alog", [DEPTH, 128, 128])
    dndt_d = din("dn_dtb", [DEPTH, 128, 128])
    pos_d = din("pos", [128, S], I32)
    retg_d = din("ret_norm_g", [DEPTH, 128, 4])
    rtdt_d = din("c_rt_dt", [128, 4, 128])
    rtsd_d = din("c_rt_sd", [128, 4])
    rtcd_d = din("c_rt_cd", [128, 2, 128])
    rtrp_d = din("c_rt_rp", [128, 4])
    rtcn_d = din("c_rt_cn", [128, 128])
    outT_d = nc.dram_tensor("outT", [D, S], F32, kind="ExternalOutput").ap()
    dbg_d = {}
    if debug:
        dbg_d["hT"] = nc.dram_tensor("dbg_hT", [D, S], F32, kind="ExternalOutput").ap()
        dbg_d["omem"] = nc.dram_tensor("dbg_omem", [256, S], F32, kind="ExternalOutput").ap()
        for nm in ["osb", "odn", "ort"]:
            dbg_d[nm] = nc.dram_tensor("dbg_" + nm, [512, S], F32, kind="ExternalOutput").ap()

    with ExitStack() as st:
        P = Prog(nc, st)

        def sb(name, shape, dt):
            return st.enter_context(nc.sbuf_tensor(_u + "s_" + name, list(shape), dt))

        xT = sb("xT", [128, 8, S], F32)
        hT = sb("hT", [128, 8, S], BF16)
        mT = sb("mT", [128, 8, S], BF16)
        obT = sb("obT", [128, 4, S], BF16)
        wst = [sb(f"wst{i}", [128, 8, 128], F32) for i in range(2)]
        NWB = 5
        wbp = [sb(f"wb{i}", [128, 8, 128], BF16) for i in range(NWB)]
        cst = {n: sb("k_" + n, [128, 128], BF16) for n in CONST_NAMES}
        cstf = sb("cstf", [128, 128], F32)
        normg = sb("normg", [128, DEPTH, 8], F32)
        memg = sb("memg", [128, DEPTH, 8], F32)
        bgate = sb("bgate", [128, DEPTH, 32], F32)
        fing = sb("fing", [128, 8], F32)
        ps = [st.enter_context(nc.psum_tensor(f"ps{i}", [128, 512], F32)) for i in range(8)]
        PS = [("ps", i) for i in range(8)]

        wcount = [0]
        wbcount = [0]

        def load_w(src, n=128, rows=8, eng="pool"):
            i = wcount[0] % 2
            wcount[0] += 1
            j = wbcount[0] % NWB
            wbcount[0] += 1
            stg, wb = wst[i], wbp[j]
            P.dma(stg[:, 0:rows, 0:n], src.rearrange("(k p) n -> p k n", p=128),
                  writes=[("wst", i)], stream=f"w{i}", queue="sp")
            fn = lambda e, o=wb[:, 0:rows, 0:n], a=stg[:, 0:rows, 0:n]: e.tensor_copy(out=o, in_=a)
            P.op(eng, fn, reads=[("wst", i)], writes=[("wb", j)])
            return wb, ("wb", j)

        for n in CONST_NAMES:
            P.dma(cstf[:, :], cst_d[n][:, :], writes=["cstf"], stream="c")
            P.op("dve", lambda e, o=cst[n][:, :]: e.tensor_copy(out=o, in_=cstf[:, :]),
                 reads=["cstf"], writes=["k_" + n])
        for l in range(DEPTH):
            P.dma(normg[:, l, :], normg_d[l, :, :], writes=["normg"], stream="c")
            P.dma(memg[:, l, :], memg_d[l, :, :], writes=["memg"], stream="c")
            P.dma(bgate[:, l, :], bgate_d[l, :, :], writes=["bgate"], stream="c")
        P.dma(fing[:, :], fing_d[:, :], writes=["fing"], stream="c")
        for k in range(8):
            P.dma(xT[:, k, :], xT_d[k * 128:(k + 1) * 128, :], writes=[("xT", k)], stream="x")

        rtdt = sb("rtdt", [128, 4, 128], F32)
        rtsd = sb("rtsd", [128, 4], F32)
        rtcd = sb("rtcd", [128, 2, 128], F32)
        rtrp = sb("rtrp", [128, 4], F32)
        rtcn = sb("rtcn", [128, 128], BF16)
        retg = sb("retg", [128, DEPTH, 4], F32)
        P.dma(rtdt[:, :, :], rtdt_d[:, :, :], writes=["rtdt"], stream="c")
        P.dma(rtsd[:, :], rtsd_d[:, :], writes=["rtsd"], stream="c")
        P.dma(rtcd[:, :, :], rtcd_d[:, :, :], writes=["rtcd"], stream="c")
        P.dma(rtrp[:, :], rtrp_d[:, :], writes=["rtrp"], stream="c")
        P.dma(cstf[:, :], rtcn_d[:, :], writes=["cstf"], stream="c")
        P.op("dve", lambda e: e.tensor_copy(out=rtcn[:, :], in_=cstf[:, :]), reads=["cstf"], writes=["rtcn"])
        for l in range(DEPTH):
            P.dma(retg[:, l, :], retg_d[l, :, :], writes=["retg"], stream="c")
        dnc = sb("dnc", [128, 4, 128], F32)
        dncw = sb("dncw", [128, DEPTH, 12, 4], F32)
        dng = sb("dng", [128, DEPTH], F32)
        P.dma(dnc[:, :, :], dnc_d[:, :, :], writes=["dnc"], stream="c")
        P.dma(dng[:, :], dng_d[:, :], writes=["dng"], stream="c")
        for l in range(DEPTH):
            P.dma(dncw[:, l, :, :], dncw_d[l, :, :, :], writes=["dncw"], stream="c")
        def rms_stats(src_tile, src_keys, nk, ncols, c0, psum_i, sq_tile, sq_key, rstd_tile, rstd_key, extra):
            for k in range(nk):
                P.op("act", lambda e, k=k: e.activation(out=sq_tile[:, k % 2, 0:ncols], in_=src_tile[:, k, c0:c0 + ncols],
                                                        func=AF.Square),
                     reads=[src_keys[k]], writes=[(sq_key, k % 2)])
                P.op("pe", lambda e, k=k: e.matmul(ps[psum_i][:, 0:ncols], lhsT=cst["ones"][:, :],
                                                   rhs=sq_tile[:, k % 2, 0:ncols], start=(k == 0), stop=(k == nk - 1)),
                     reads=[(sq_key, k % 2), "k_ones"], writes=[PS[psum_i]])
            P.op("act", lambda e: e.activation(out=rstd_tile[:, 0:ncols], in_=ps[psum_i][:, 0:ncols], func=AF.Ln,
                                               bias=epsb[:, 0:1], scale=1.0),
                 reads=[PS[psum_i], "epsb"], writes=[rstd_key])
            P.op("act", lambda e: e.activation(out=rstd_tile[:, 0:ncols], in_=rstd_tile[:, 0:ncols], func=AF.Exp,
                                               scale=-0.5),
                 reads=[rstd_key], writes=[rstd_key])

        epsb = sb("epsb", [128, 1], F32)
        P.op("dve", lambda e: e.memset(epsb[:, :], float(D * EPS)), writes=["epsb"])
        sqt = sb("sqt", [128, 2, TC], BF16)
        rstd = sb("rstd", [128, TC], F32)
        g32 = sb("g32", [128, 8], F32)

        def norm_to(dst_fn, gsrc, layer_tag):
            P.op("dve", lambda e: e.tensor_scalar(out=g32[:, :], in0=gsrc, scalar1=float(math.sqrt(D)), scalar2=None,
                                                  op0=ALU.mult), reads=["normg", "fing"], writes=["g32"])
            for tc in range(NT):
                rms_stats(xT, [("xT", k) for k in range(8)], 8, TC, tc * TC, 0, sqt, "sqt", rstd, "rstd", None)
                for k in range(8):
                    ap, key = dst_fn(k, tc)
                    P.op("dve", lambda e, k=k, tc=tc, ap=ap: e.scalar_tensor_tensor(
                        out=ap, in0=xT[:, k, tc * TC:(tc + 1) * TC], scalar=g32[:, k:k + 1], in1=rstd[:, :],
                        op0=ALU.mult, op1=ALU.mult),
                        reads=[("xT", k), "g32", "rstd"], writes=[key])

        def proj_fm(l, col0, ncols, evac, wsrc=None):
            src = (w_in_d[l, :, col0:col0 + ncols] if wsrc is None else wsrc)
            wb, wkey = load_w(src, n=ncols)
            for tc in range(NT):
                pi = 1 + (tc % 2)
                for k in range(8):
                    P.op("pe", lambda e, k=k, tc=tc, pi=pi: e.matmul(
                        ps[pi][0:ncols, :], lhsT=wb[:, k, 0:ncols], rhs=hT[:, k, tc * TC:(tc + 1) * TC],
                        start=(k == 0), stop=(k == 7)),
                        reads=[wkey, ("hT", k)], writes=[PS[pi]], inc=(k == 7))
                evac(tc, ps[pi][0:ncols, :], PS[pi])

        def proj_tm(l, col0, ncols, evac, ntok=128):
            wb, wkey = load_w(w_in_d[l, :, col0:col0 + ncols], n=ncols)
            for tt in range(S // ntok):
                pi = 1 + (tt % 2)
                for k in range(8):
                    P.op("pe", lambda e: e.matmul(ps[pi][0:ntok, 0:ncols], lhsT=hT[:, k, tt * ntok:(tt + 1) * ntok],
                                                  rhs=wb[:, k, 0:ncols], start=(k == 0), stop=(k == 7)),
                         reads=[wkey, ("hT", k)], writes=[PS[pi]], inc=(k == 7))
                evac(tt, ps[pi][0:ntok, 0:ncols], PS[pi])

        oneb = sb("oneb", [128, 1], F32)
        P.op("dve", lambda e: e.memset(oneb[:, :], 1.0), writes=["oneb"])

        def run_streams(gens, stagger=None):
            gens = list(gens)
            if stagger:
                for g, k in zip(gens, stagger):
                    for _ in range(k):
                        next(g)
            while gens:
                for g in list(gens):
                    try:
                        next(g)
                    except StopIteration:
                        gens.remove(g)

        def sb_attention(l):
            for hp in range(4):
                with ExitStack() as ph:
                    def sbp(name, shape, dt):
                        return ph.enter_context(nc.sbuf_tensor(_u + ("s_" + name), list(shape), dt))
                    qT = sbp("sbq", [128, S], BF16)
                    kT = sbp("sbk", [128, S], BF16)
                    vtm = sbp("sbv", [128, 16, 128], BF16)
                    proj_fm(l, C_SBQ + hp * 128, 128, lambda tc, pap, pkey: P.op(
                        "act", lambda e: e.activation(out=qT[:, tc * TC:(tc + 1) * TC], in_=pap, func=AF.Copy, scale=0.125),
                        reads=[pkey], writes=["sbq"]))
                    proj_fm(l, C_SBK + hp * 128, 128, lambda tc, pap, pkey: P.op(
                        "dve", lambda e: e.tensor_copy(out=kT[:, tc * TC:(tc + 1) * TC], in_=pap),
                        reads=[pkey], writes=["sbk"]))
                    proj_tm(l, C_SBV + hp * 128, 128, lambda tt, pap, pkey: P.op(
                        "dve", lambda e: e.tensor_copy(out=vtm[:, tt, :], in_=pap),
                        reads=[pkey], writes=[("sbv", tt)]))
                    with ExitStack() as ph2:
                        def sbw(name, shape, dt):
                            return ph2.enter_context(nc.sbuf_tensor(_u + ("s_" + name), list(shape), dt))

                        import os as _os

                        def stream(sid, hh, qcs):
                            ez = sbw(f"ez{sid}", [128, TC], F32)
                            spb = sbw(f"spb{sid}", [128, TC], BF16)
                            eg = sbw(f"eg{sid}", [128, TC], BF16)
                            Gb = sbw(f"Gb{sid}", [128, TC], BF16)
                            wv = eg
                            K_wv = ("eg", sid)
                            K_ez, K_spb, K_eg, K_Gb = ("ez", sid), ("spb", sid), ("eg", sid), ("Gb", sid)
                            rs = slice(hh * 64, hh * 64 + 64)
                            pzg, po = 2 * sid, 2 * sid + 1
                            pgg = (4 + 2 * sid) if _os.environ.get('SB_SEPG') else pzg
                            yield
                            for qc in qcs:
                                q0 = qc * TC
                                kmax = qc * 4 + 3
                                pc0 = None
                                P.op("pool", lambda e: e.memset(wv[:, 0:384], 0.0), writes=[K_wv])
                                for kb in range(kmax, -1, -1):
                                    if kmax - kb >= int(_os.environ.get("SB_MAXIT", "99")):
                                        continue
                                    j = kb - qc * 4
                                    c0 = 128 * j if j >= 0 else 0
                                    cs = slice(c0, TC)
                                    tsl = slice(q0 + c0, q0 + TC)
                                    ksl = slice(kb * 128, (kb + 1) * 128)
                                    P.op("pe", lambda e: e.matmul(ps[pzg][:, cs], lhsT=kT[rs, ksl], rhs=qT[rs, tsl],
                                                                  start=True, stop=True),
                                         reads=["sbk", "sbq"], writes=[PS[pzg]])
                                    yield
                                    P.op("act", lambda e: e.activation(out=ez[:, cs], in_=ps[pzg][:, cs], func=AF.Exp),
                                         reads=[PS[pzg]], writes=[K_ez])
                                    yield
                                    P.op("act", lambda e: e.activation(out=spb[:, cs], in_=ez[:, cs], func=AF.Ln,
                                                                       bias=oneb[:, 0:1], scale=1.0),
                                         reads=[K_ez, "oneb"], writes=[K_spb])
                                    yield
                                    if j >= 0:
                                        P.op("dve", lambda e: e.tensor_tensor(out=spb[:, c0:c0 + 128], in0=spb[:, c0:c0 + 128],
                                                                              in1=cst["masksb"][:, :], op=ALU.mult),
                                             reads=[K_spb, "k_masksb"], writes=[K_spb])
                                    P.op("pe", lambda e: e.matmul(ps[pgg][:, cs], lhsT=cst["trineg"][:, :], rhs=spb[:, cs],
                                                                  start=True, stop=(pc0 is None)),
                                         reads=["k_trineg", K_spb], writes=[PS[pgg]], inc=(pc0 is None))
                                    if pc0 is not None:
                                        pcs = slice(pc0, TC)
                                        P.op("pe", lambda e: e.matmul(ps[pgg][:, pcs], lhsT=cst["e0"][:, :], rhs=Gb[:, pcs],
                                                                      start=False, stop=True),
                                             reads=["k_e0", K_Gb], writes=[PS[pgg]])
                                    yield
                                    P.op("act", lambda e: e.activation(out=eg[:, cs], in_=ps[pgg][:, cs], func=AF.Exp),
                                         reads=[PS[pgg]], writes=[K_eg])
                                    if kb > 0:
                                        P.op("dve", lambda e: e.tensor_copy(out=Gb[:, cs], in_=ps[pgg][:, cs]),
                                             reads=[PS[pgg], K_eg], writes=[K_Gb])
                                    yield
                                    P.op("dve", lambda e: e.tensor_tensor(out=wv[:, cs], in0=ez[:, cs], in1=eg[:, cs], op=ALU.mult),
                                         reads=[K_ez, K_eg], writes=[K_wv])
                                    if j >= 0:
                                        P.op("dve", lambda e: e.tensor_tensor(out=wv[:, c0:c0 + 128], in0=wv[:, c0:c0 + 128],
                                                                              in1=cst["masksb"][:, :], op=ALU.mult),
                                             reads=[K_wv, "k_masksb"], writes=[K_wv])
                                    yield
                                    P.op("pe", lambda e: e.matmul(ps[po][rs, :], lhsT=vtm[:, kb, rs], rhs=wv[:, :],
                                                                  start=(kb == kmax), stop=(kb == 0)),
                                         reads=[("sbv", kb), K_wv], writes=[PS[po]])
                                    pc0 = c0
                                    yield
                                P.op("dve", lambda e: e.tensor_copy(out=obT[rs, hp, q0:q0 + TC], in_=ps[po][rs, :]),
                                     reads=[PS[po]], writes=[("obT", hp)])
                                yield

                        _mode = _os.environ.get("SB_MODE", "4")
                        if _mode == "1":
                            run_streams([stream(0, 0, [3, 0, 2, 1])])
                            run_streams([stream(1, 1, [3, 0, 2, 1])])
                        elif _mode == "2":
                            run_streams([stream(0, 0, [3, 0, 2, 1]), stream(1, 1, [3, 0, 2, 1])])
                        else:
                            run_streams([stream(0, 0, [3, 0]), stream(1, 1, [3, 0]), stream(2, 0, [2, 1]), stream(3, 1, [2, 1])],
                                        stagger=[int(x) for x in _os.environ.get('SB_STAG', '0,2,4,6').split(',')])
                        P.barrier()
                    with ExitStack() as ph3:
                        zs = [ph3.enter_context(nc.sbuf_tensor(_u + f"s_sbzs{i}", [128, TC], BF16)) for i in range(2)]

                        def evz(tc, pap, pkey):
                            b = tc % 2
                            P.op("act", lambda e: e.activation(out=zs[b][:, :], in_=pap, func=AF.Silu),
                                 reads=[pkey], writes=[("sbzs", b)])
                            tsl = slice(tc * TC, (tc + 1) * TC)
                            P.op("dve", lambda e: e.tensor_tensor(out=obT[:, hp, tsl], in0=obT[:, hp, tsl], in1=zs[b][:, :], op=ALU.mult),
                                 reads=[("sbzs", b)], writes=[("obT", hp)])
                        proj_fm(l, C_SBZ + hp * 128, 128, evz)
                        P.barrier()

        def retention(l):
            TWO_PI = 2.0 * math.pi
            C1 = 6.28125
            C2 = TWO_PI - C1
            with ExitStack() as ph0:
                def sb0(name, shape, dt):
                    return ph0.enter_context(nc.sbuf_tensor(_u + ("s_" + name), list(shape), dt))
                qrT = sb0("rqr", [128, 2, S], BF16)
                krT = sb0("rkr", [128, 2, S], BF16)
                with ExitStack() as ph1:
                    def sb1(name, shape, dt):
                        return ph1.enter_context(nc.sbuf_tensor(_u + ("s_" + name), list(shape), dt))
                    COS2 = sb1("rcos", [128, S], BF16)
                    SIN2 = sb1("rsin", [128, S], BF16)
                    pint = sb1("rpint", [128, TC], I32)
                    ta = sb1("rta", [128, TC], F32)
                    tk = sb1("rtk", [128, TC], F32)
                    tm = sb1("rtm", [128, 256], F32)[:, :]
                    ki = sb1("rki", [128, 256], I32)[:, :]
                    ta_f, tk_f, pint_f = ta, tk, pint
                    for tc in range(8):
                        tsl = slice(tc * 256, (tc + 1) * 256)
                        ta, tk, pint = ta_f[:, 0:256], tk_f[:, 0:256], pint_f[:, 0:256]
                        P.dma(pint, pos_d[:, tsl], writes=["rpint"], stream="x")
                        P.op("dve", lambda e: e.tensor_scalar(out=ta, in0=pint, scalar1=rtrp[:, 0:1], scalar2=None,
                                                              op0=ALU.mult), reads=["rpint", "rtrp"], writes=["rta"])
                        P.op("dve", lambda e: e.tensor_scalar(out=ki, in0=ta, scalar1=float(1.0 / TWO_PI),
                                                              scalar2=None, op0=ALU.mult), reads=["rta"], writes=["rki"])
                        P.op("dve", lambda e: e.tensor_copy(out=tk, in_=ki), reads=["rki"], writes=["rtk"])
                        P.op("dve", lambda e: e.scalar_tensor_tensor(out=ta, in0=tk, scalar=-C1, in1=ta,
                                                                     op0=ALU.mult, op1=ALU.add),
                             reads=["rtk", "rta"], writes=["rta"])
                        P.op("dve", lambda e: e.scalar_tensor_tensor(out=ta, in0=tk, scalar=-C2, in1=ta,
                                                                     op0=ALU.mult, op1=ALU.add),
                             reads=["rtk", "rta"], writes=["rta"])
                        P.op("dve", lambda e: e.tensor_single_scalar(out=tm, in_=ta, scalar=float(math.pi),
                                                                     op=ALU.is_gt), reads=["rta"], writes=["rtm"])
                        P.op("dve", lambda e: e.scalar_tensor_tensor(out=ta, in0=tm, scalar=-TWO_PI, in1=ta,
                                                                     op0=ALU.mult, op1=ALU.add),
                             reads=["rtm", "rta"], writes=["rta"])
                        P.op("dve", lambda e: e.tensor_single_scalar(out=tm, in_=ta, scalar=float(-math.pi),
                                                                     op=ALU.is_lt), reads=["rta"], writes=["rtm"])
                        P.op("dve", lambda e: e.scalar_tensor_tensor(out=ta, in0=tm, scalar=TWO_PI, in1=ta,
                                                                     op0=ALU.mult, op1=ALU.add),
                             reads=["rtm", "rta"], writes=["rta"])
                        P.op("act", lambda e: e.activation(out=SIN2[:, tsl], in_=ta, func=AF.Sin, scale=rtrp[:, 1:2]),
                             reads=["rta", "rtrp"], writes=["rsin"])
                        P.op("dve", lambda e: e.tensor_single_scalar(out=tm, in_=ta, scalar=float(math.pi / 2),
                                                                     op=ALU.is_gt), reads=["rta"], writes=["rtm"])
                        P.op("dve", lambda e: e.scalar_tensor_tensor(out=ta, in0=tm, scalar=-TWO_PI, in1=ta,
                                                                     op0=ALU.mult, op1=ALU.add),
                             reads=["rtm", "rta"], writes=["rta"])
                        P.op("act", lambda e: e.activation(out=COS2[:, tsl], in_=ta, func=AF.Sin, bias=rtrp[:, 2:3],
                                                           scale=1.0), reads=["rta", "rtrp"], writes=["rcos"])
                    ta, tk, pint = ta_f, tk_f, pint_f
                    for which, (c_base, dstT, scl) in enumerate([(C_RTQ, qrT, 1.0), (C_RTK, krT, 0.125)]):
                        for hp in range(2):
                            wb, wkey = load_w(w_in_d[l, :, c_base + hp * 128:c_base + (hp + 1) * 128])
                            jsw = wbcount[0] % NWB
                            wbcount[0] += 1
                            wsw = wbp[jsw]
                            for hh in range(2):
                                o = hh * 64
                                P.op("pool", lambda e: e.tensor_copy(out=wsw[:, :, o:o + 32], in_=wb[:, :, o + 32:o + 64]),
                                     reads=[wkey], writes=[("wb", jsw)])
                                P.op("pool", lambda e: e.tensor_copy(out=wsw[:, :, o + 32:o + 64], in_=wb[:, :, o:o + 32]),
                                     reads=[wkey], writes=[("wb", jsw)])
                            for tc in range(NT):
                                tsl = slice(tc * TC, (tc + 1) * TC)
                                for k in range(8):
                                    P.op("pe", lambda e: e.matmul(ps[1][:, :], lhsT=wb[:, k, :], rhs=hT[:, k, tsl],
                                                                  start=(k == 0), stop=(k == 7)),
                                         reads=[wkey, ("hT", k)], writes=[PS[1]], inc=(k == 7))
                                for k in range(8):
                                    P.op("pe", lambda e: e.matmul(ps[2][:, :], lhsT=wsw[:, k, :], rhs=hT[:, k, tsl],
                                                                  start=(k == 0), stop=(k == 7)),
                                         reads=[("wb", jsw), ("hT", k)], writes=[PS[2]], inc=(k == 7))
                                P.op("dve", lambda e: e.scalar_tensor_tensor(out=ta[:, :], in0=ps[1][:, :], scalar=float(scl),
                                                                             in1=COS2[:, tsl], op0=ALU.mult, op1=ALU.mult),
                                     reads=[PS[1], "rcos"], writes=["rta"])
                                P.op("dve", lambda e: e.scalar_tensor_tensor(out=tk[:, :], in0=ps[2][:, :], scalar=float(scl),
                                                                             in1=SIN2[:, tsl], op0=ALU.mult, op1=ALU.mult),
                                     reads=[PS[2], "rsin"], writes=["rtk"])
                                P.op("dve", lambda e: e.tensor_tensor(out=dstT[:, hp, tsl], in0=ta[:, :], in1=tk[:, :], op=ALU.add),
                                     reads=["rta", "rtk"], writes=[("rq", which, hp)])
                    P.barrier()
                for hp in range(2):
                    with ExitStack() as ph2:
                        def sb2(name, shape, dt):
                            return ph2.enter_context(nc.sbuf_tensor(_u + ("s_" + name), list(shape), dt))
                        kdtm = sb2("rkd", [128, 16, 128], BF16)
                        vtm = sb2("rv", [128, 16, 128], BF16)
                        zsT = sb2("rz", [128, S], BF16)
                        qc = [sb2(f"rqc{i}", [128, 128], BF16) for i in range(2)]
                        scm = [sb2(f"rscm{i}", [128, 128], BF16) for i in range(2)]
                        Sf = sb2("rSf", [128, 128], F32)
                        Sb = sb2("rSb", [128, 128], BF16)
                        ob = sb2("rob", [128, TC], BF16)
                        sq = sb2("rsq", [128, TC], BF16)
                        rs_t = rstd
                        tt_t = sb2("rtt", [128, TC], F32)
                        for n in range(16):
                            nsl = slice(n * 128, (n + 1) * 128)
                            pk = ps[3][:, 0:64].bitcast(BF16)
                            P.op("pe", lambda e: e.transpose(pk, krT[:, hp, nsl], cst["ident"][:, :]),
                                 reads=[("rq", 1, hp), "k_ident"], writes=[PS[3]])
                            for hh in range(2):
                                cs = slice(hh * 64, (hh + 1) * 64)
                                h = 2 * hp + hh
                                P.op("dve", lambda e: e.tensor_scalar(out=kdtm[:, n, cs], in0=pk[:, cs], scalar1=rtsd[:, h:h + 1],
                                                                      scalar2=None, op0=ALU.mult),
                                     reads=[PS[3], "rtsd"], writes=[("rkd", n)])
                        for hh in range(2):
                            h = 2 * hp + hh
                            rs = slice(hh * 64, (hh + 1) * 64)
                            proj_tm(l, C_RTV + h * 128, 128, lambda tt, pap, pkey: P.op(
                                "dve", lambda e: e.tensor_copy(out=vtm[:, tt, :], in_=pap), reads=[pkey], writes=[("rv", tt)]))
                            proj_fm(l, C_RTZ + h * 128, 128, lambda tc, pap, pkey: P.op(
                                "act", lambda e: e.activation(out=zsT[:, tc * TC:(tc + 1) * TC], in_=pap, func=AF.Silu),
                                reads=[pkey], writes=["rz"]))
                            gch = float((1.0 - 2.0 ** (-5.0 - h)) ** 128)
                            SBANK = [7, 3]

                            def emit_sc(n):
                                nsl = slice(n * 128, (n + 1) * 128)
                                b = n % 2
                                P.op("pe", lambda e: e.matmul(ps[4][:, 0:128], lhsT=krT[rs, hp, nsl], rhs=qrT[rs, hp, nsl],
                                                              start=True, stop=True),
                                     reads=[("rq", 1, hp), ("rq", 0, hp)], writes=[PS[4]])
                                P.op("dve", lambda e: e.tensor_tensor(out=scm[b][:, :], in0=ps[4][:, 0:128], in1=rtdt[:, h, :],
                                                                      op=ALU.mult),
                                     reads=[PS[4], "rtdt"], writes=[("rscm", b)])
                                if n > 0:
                                    P.op("dve", lambda e: e.tensor_tensor(out=qc[b][rs, :], in0=qrT[rs, hp, nsl], in1=rtcd[rs, hp, :],
                                                                          op=ALU.mult),
                                         reads=[("rq", 0, hp), "rtcd"], writes=[("rqc", b)])

                            def emit_smm(n):
                                sbk = SBANK[n % 2]
                                P.op("pe", lambda e: e.matmul(ps[sbk][rs, 0:128], lhsT=kdtm[:, n, rs], rhs=vtm[:, n, :],
                                                              start=True, stop=True),
                                     reads=[("rkd", n), ("rv", n)], writes=[PS[sbk]])

                            emit_sc(0)
                            emit_smm(0)
                            for n in range(16):
                                nsl = slice(n * 128, (n + 1) * 128)
                                b = n % 2
                                csl = slice((n % 4) * 128, (n % 4 + 1) * 128)
                                po = 5 + (n // 4) % 2
                                if n + 1 < 15:
                                    emit_smm(n + 1)
                                P.op("pe", lambda e: e.matmul(ps[po][:, csl], lhsT=vtm[:, n, :], rhs=scm[b][:, :],
                                                              start=True, stop=(n == 0)),
                                     reads=[("rv", n), ("rscm", b)], writes=[PS[po]], inc=(n == 0))
                                if n > 0:
                                    P.op("pe", lambda e: e.matmul(ps[po][:, csl], lhsT=Sb[rs, :], rhs=qc[b][rs, :],
                                                                  start=False, stop=True),
                                         reads=["rSb", ("rqc", b)], writes=[PS[po]])
                                if n < 15:
                                    sbk = SBANK[n % 2]
                                    if n == 0:
                                        P.op("dve", lambda e: e.tensor_copy(out=Sf[rs, :], in_=ps[sbk][rs, 0:128]),
                                             reads=[PS[sbk]], writes=["rSf"])
                                    else:
                                        P.op("dve", lambda e: e.scalar_tensor_tensor(out=Sf[rs, :], in0=Sf[rs, :], scalar=gch,
                                                                                     in1=ps[sbk][rs, 0:128], op0=ALU.mult,
                                                                                     op1=ALU.add),
                                             reads=[PS[sbk], "rSf"], writes=["rSf"])
                                    P.op("act", lambda e: e.activation(out=Sb[rs, :], in_=Sf[rs, :], func=AF.Copy),
                                         reads=["rSf"], writes=["rSb"])
                                if n + 1 < 16:
                                    emit_sc(n + 1)
                                if n % 4 == 3:
                                    tc = n // 4
                                    tsl = slice(tc * TC, (tc + 1) * TC)
                                    P.op("act", lambda e: e.activation(out=ob[:, :], in_=ps[po][:, :], func=AF.Copy),
                                         reads=[PS[po]], writes=["rob"])
                                    P.op("pe", lambda e: e.matmul(ps[1][:, :], lhsT=rtcn[:, :], rhs=ob[:, :], start=True, stop=True),
                                         reads=["rtcn", "rob"], writes=[PS[1]])
                                    P.op("act", lambda e: e.activation(out=sq[:, :], in_=ps[1][:, :], func=AF.Square),
                                         reads=[PS[1]], writes=["rsq"])
                                    P.op("pe", lambda e: e.matmul(ps[2][:, :], lhsT=cst["ones"][:, :], rhs=sq[:, :],
                                                                  start=True, stop=True),
                                         reads=["k_ones", "rsq"], writes=[PS[2]])
                                    P.op("act", lambda e: e.activation(out=rs_t[:, :], in_=ps[2][:, :], func=AF.Ln,
                                                                       bias=rtrp[:, 3:4], scale=float(1.0 / 128.0)),
                                         reads=[PS[2], "rtrp"], writes=["rstd"])
                                    P.op("act", lambda e: e.activation(out=rs_t[:, :], in_=rs_t[:, :], func=AF.Exp, scale=-0.5),
                                         reads=["rstd"], writes=["rstd"])
                                    P.op("dve", lambda e: e.scalar_tensor_tensor(out=tt_t[:, :], in0=ps[1][:, :],
                                                                                 scalar=retg[:, l, h:h + 1], in1=rs_t[:, :],
                                                                                 op0=ALU.mult, op1=ALU.mult),
                                         reads=[PS[1], "retg", "rstd"], writes=["rtt"])
                                    P.op("dve", lambda e: e.tensor_tensor(out=obT[:, h, tsl], in0=tt_t[:, :], in1=zsT[:, tsl],
                                                                          op=ALU.mult),
                                         reads=["rtt", "rz"], writes=[("obT", h)])
                        P.barrier()

        def deltanet(l):
            P.barrier(full=True)
            identf, onesf = dnc[:, 0, :], dnc[:, 1, :]
            mincl, mstrict = dnc[0:64, 2, 0:64], dnc[0:64, 3, 0:64]
            H = slice(0, 64)
            with ExitStack() as ph0:
                def sb0(name, shape, dt):
                    return ph0.enter_context(nc.sbuf_tensor(_u + ("s_" + name), list(shape), dt))
                abraw = sb0("dab", [64, 32, 8], F32)
                rep = sb0("drep", [64, 128], F32)
                t1 = sb0("dt1", [64, 32, 4], F32)
                g_tm = sb0("dg", [64, 32, 4], F32)
                beta_tm = sb0("dbeta", [64, 32, 4], F32)
                gc_tm = sb0("dgc", [64, 32, 4], F32)
                egl = sb0("degl", [128, 32, 4], F32)
                bg_tm = sb0("dbg", [64, 32, 4], F32)
                ed_tm = sb0("ded", [64, 32, 4], F32)
                proj_tm(l, C_DNA, 8, lambda tt, pap, pkey: P.op(
                    "dve", lambda e: e.tensor_copy(out=abraw[:, tt, :], in_=pap), reads=[pkey], writes=["dab"]), ntok=64)
                fl = lambda t: t[:, :, :].rearrange("p a b -> p (a b)")
                P.dma(rep[:, :], dndt_d[l, 0:64, :], writes=["drep"], stream="c")
                P.op("dve", lambda e: e.tensor_tensor(out=t1[:, :, :], in0=abraw[:, :, 0:4],
                                                      in1=rep[:, :].rearrange("p (a b) -> p a b", b=4), op=ALU.add),
                     reads=["dab", "drep"], writes=["dt1"])
                P.op("act", lambda e: e.activation(out=t1[:, :, :], in_=t1[:, :, :], func=AF.Exp), reads=["dt1"], writes=["dt1"])
                P.op("act", lambda e: e.activation(out=t1[:, :, :], in_=t1[:, :, :], func=AF.Ln, bias=oneb[0:64, 0:1], scale=1.0),
                     reads=["dt1", "oneb"], writes=["dt1"])
                P.dma(rep[:, :], dnal_d[l, 0:64, :], writes=["drep"], stream="c")
                P.op("act", lambda e: e.activation(out=rep[:, :], in_=rep[:, :], func=AF.Exp), reads=["drep"], writes=["drep"])
                P.op("dve", lambda e: e.scalar_tensor_tensor(out=g_tm[:, :, :], in0=t1[:, :, :], scalar=-1.0,
                                                             in1=rep[:, :].rearrange("p (a b) -> p a b", b=4),
                                                             op0=ALU.mult, op1=ALU.mult),
                     reads=["dt1", "drep"], writes=["dg"])
                P.op("act", lambda e: e.activation(out=beta_tm[:, :, :], in_=abraw[:, :, 4:8], func=AF.Sigmoid),
                     reads=["dab"], writes=["dbeta"])
                P.op("pe", lambda e: e.matmul(ps[0][0:64, 0:128], lhsT=dnc[0:64, 2, 0:64], rhs=fl(g_tm), start=True, stop=True),
                     reads=["dnc", "dg"], writes=[PS[0]])
                P.op("dve", lambda e: e.tensor_copy(out=fl(gc_tm), in_=ps[0][0:64, 0:128]), reads=[PS[0]], writes=["dgc"])
                P.op("pe", lambda e: e.matmul(ps[0][:, 128:256], lhsT=dnc[0:64, 1, :], rhs=fl(g_tm), start=True, stop=True),
                     reads=["dnc", "dg"], writes=[PS[0]])
                P.op("dve", lambda e: e.tensor_tensor(out=fl(ed_tm), in0=ps[0][0:64, 128:256], in1=fl(gc_tm), op=ALU.subtract),
                     reads=[PS[0], "dgc"], writes=["ded"])
                P.op("act", lambda e: e.activation(out=fl(ed_tm), in_=fl(ed_tm), func=AF.Exp), reads=["ded"], writes=["ded"])
                P.op("act", lambda e: e.activation(out=fl(egl), in_=ps[0][:, 128:256], func=AF.Exp), reads=[PS[0]], writes=["degl"])
                P.op("act", lambda e: e.activation(out=fl(bg_tm), in_=fl(gc_tm), func=AF.Exp), reads=["dgc"], writes=["dbg"])
                P.op("dve", lambda e: e.tensor_tensor(out=fl(bg_tm), in0=fl(bg_tm), in1=fl(beta_tm), op=ALU.mult),
                     reads=["dbg", "dbeta"], writes=["dbg"])
                P.barrier()
                for h in range(4):
                    with ExitStack() as ph1:
                        def sb1(name, shape, dt):
                            return ph1.enter_context(nc.sbuf_tensor(_u + ("s_" + name), list(shape), dt))
                        qT, kT, vT, zsT = mT[:, 0, :], mT[:, 1, :], mT[:, 2, :], mT[:, 3, :]
                        xpad = mT[:, 4:7, :].rearrange("p a b -> p (a b)").bitcast(F32)[:, 0:S + 3]
                        with ExitStack() as ph2:
                            acc = ph2.enter_context(nc.sbuf_tensor(_u + "s_dacc", [128, S], F32))
                            P.op("dve", lambda e: e.memset(xpad[:, 0:3], 0.0), writes=["dxpad0"])
                            for xi, (c_base, dstT) in enumerate([(C_DNQ, qT), (C_DNK, kT), (C_DNV, vT)]):
                                proj_fm(l, c_base + h * 128, 128, lambda tc, pap, pkey: P.op(
                                    "act", lambda e: e.activation(out=xpad[:, 3 + tc * TC:3 + (tc + 1) * TC], in_=pap, func=AF.Copy),
                                    reads=[pkey], writes=[("dxpad", tc)]))
                                allx = [("dxpad", tc) for tc in range(NT)] + ["dxpad0"]
                                cw = dncw[:, l, xi * 4 + h, :]
                                P.op("dve", lambda e: e.tensor_scalar(out=acc[:, :], in0=xpad[:, 3:3 + S], scalar1=cw[:, 3:4],
                                                                      scalar2=None, op0=ALU.mult),
                                     reads=allx + ["dncw"], writes=["dacc"])
                                for j in range(3):
                                    P.op("dve", lambda e: e.scalar_tensor_tensor(out=acc[:, :], in0=xpad[:, j:j + S],
                                                                                 scalar=cw[:, j:j + 1], in1=acc[:, :],
                                                                                 op0=ALU.mult, op1=ALU.add),
                                         reads=allx + ["dncw", "dacc"], writes=["dacc"])
                                P.op("act", lambda e: e.activation(out=acc[:, :], in_=acc[:, :], func=AF.Silu),
                                     reads=["dacc"], writes=["dacc"])
                                if xi == 2:
                                    P.op("dve", lambda e: e.tensor_copy(out=vT, in_=acc[:, :]), reads=["dacc"], writes=["dvT"])
                                else:
                                    for tc in range(NT):
                                        tsl = slice(tc * TC, (tc + 1) * TC)
                                        P.op("act", lambda e: e.activation(out=sqt[:, 0, :], in_=acc[:, tsl], func=AF.Square),
                                             reads=["dacc"], writes=[("sqt", 0)])
                                        P.op("pe", lambda e: e.matmul(ps[0][:, :], lhsT=cst["ones"][:, :], rhs=sqt[:, 0, :],
                                                                      start=True, stop=True),
                                             reads=[("sqt", 0), "k_ones"], writes=[PS[0]])
                                        P.op("act", lambda e: e.activation(out=rstd[:, :], in_=ps[0][:, :], func=AF.Ln,
                                                                           bias=rtrp[:, 3:4], scale=1.0),
                                             reads=[PS[0], "rtrp"], writes=["rstd"])
                                        P.op("act", lambda e: e.activation(out=rstd[:, :], in_=rstd[:, :], func=AF.Exp, scale=-0.5),
                                             reads=["rstd"], writes=["rstd"])
                                        sc = float(128.0 ** -0.5) if xi == 0 else 1.0
                                        P.op("dve", lambda e: e.scalar_tensor_tensor(out=dstT[:, tsl], in0=acc[:, tsl], scalar=sc,
                                                                                     in1=rstd[:, :], op0=ALU.mult, op1=ALU.mult),
                                             reads=["dacc", "rstd"], writes=["dqT" if xi == 0 else "dkT"])
                            P.barrier()
                        proj_fm(l, C_DNZ + h * 128, 128, lambda tc, pap, pkey: P.op(
                            "act", lambda e: e.activation(out=zsT[:, tc * TC:(tc + 1) * TC], in_=pap, func=AF.Silu),
                            reads=[pkey], writes=["dz"]))
                        NB = 8
                        B3 = [64, NB, 64]
                        gb = sb1("dgb", [64, NB, 64], F32)
                        dd = sb1("ddd", [64, NB, 64], F32)
                        LT = sb1("dLT", [64, NB, 64], F32)
                        egcb = sb1("degcb", [128, NB * 64], F32)
                        bs = sb1("dbs", [64, NB, 64], BF16)
                        Pm = sb1("dPm", [64, NB, 64], BF16)
                        PTm = sb1("dPTm", [64, NB, 64], BF16)
                        Tt = [sb1(f"dTt{i}", [64, NB, 64], BF16) for i in range(2)]
                        kbg = mT[0:64, 7, 0:1024].rearrange("p (a b) -> p a b", b=128)
                        vb = mT[0:64, 7, 1024:2048].rearrange("p (a b) -> p a b", b=128)
                        aT = sb1("daT", [64, NB, 64], BF16)
                        aTt = sb1("daTt", [64, NB, 64], BF16)
                        qg = sb1("dqg", [128, NB * 64], BF16)
                        u_sb = sb1("du", [64, NB, 128], BF16)
                        wT = sb1("dwT", [128, NB, 64], BF16)
                        kd = sb1("dkd", [64, NB, 128], BF16)
                        vnew = [sb1(f"dvnew{i}", [64, 128], BF16) for i in range(2)]
                        Sf = sb1("dSf", [128, 128], F32)
                        Sb = sb1("dSb", [128, 128], BF16)
                        nsq = sb1("dnsq", [128, TC], BF16)
                        ntmp = sb1("dntmp", [128, TC], F32)
                        identb = cst["ident"]
                        id64 = identb[0:64, 0:64]
                        fl3 = lambda t: t[:, :, :].rearrange("p a b -> p (a b)")
                        p3 = lambda i, w=64: ps[i][H, 0:NB * w].rearrange("p (a b) -> p a b", b=w)
                        bcast = lambda ap2: ap2.unsqueeze(1).broadcast_to(B3)

                        def stage_a(bt):
                            n0 = bt * NB
                            bsl = slice(n0 * 64, (n0 + NB) * 64)
                            nsl = slice(n0, n0 + NB)
                            sc_g = g_tm[:, nsl, h:h + 1].broadcast_to(B3)
                            sc_b = beta_tm[:, nsl, h:h + 1].broadcast_to(B3)
                            sc_gc = gc_tm[:, nsl, h:h + 1].broadcast_to(B3)
                            P.op("dve", lambda e: e.tensor_tensor(out=gb[:, :, :], in0=bcast(mincl), in1=sc_g, op=ALU.mult),
                                 reads=["dnc", "dg"], writes=["dgb"])
                            P.op("pe", lambda e: e.matmul(ps[0][:, :], lhsT=onesf[0:64, :], rhs=fl3(gb), start=True, stop=True),
                                 reads=["dnc", "dgb"], writes=[PS[0]])
                            yield
                            P.op("dve", lambda e: e.tensor_tensor(out=gb[:, :, :], in0=bcast(identf[0:64, 0:64]), in1=sc_b, op=ALU.mult),
                                 reads=["dnc", "dbeta"], writes=["dgb"])
                            P.op("pe", lambda e: e.matmul(ps[1][H, :], lhsT=onesf[0:64, 0:64], rhs=fl3(gb), start=True, stop=True),
                                 reads=["dnc", "dgb"], writes=[PS[1]])
                            yield
                            P.op("act", lambda e: e.activation(out=egcb[:, :], in_=ps[0][:, :], func=AF.Exp),
                                 reads=[PS[0]], writes=["degcb"])
                            P.op("dve", lambda e: e.tensor_tensor(out=dd[:, :, :], in0=p3(0), in1=sc_gc, op=ALU.subtract),
                                 reads=[PS[0], "dgc", "degcb"], writes=["ddd"])
                            yield
                            P.op("act", lambda e: e.activation(out=fl3(dd), in_=fl3(dd), func=AF.Exp), reads=["ddd"], writes=["ddd"])
                            P.op("dve", lambda e: e.tensor_tensor(out=bs[:, :, :], in0=p3(1), in1=bcast(mstrict), op=ALU.mult),
                                 reads=[PS[1], "dnc"], writes=["dbs"])
                            yield
                            P.op("dve", lambda e: e.scalar_tensor_tensor(out=dd[:, :, :], in0=dd[:, :, :], scalar=1.0, in1=bcast(mincl),
                                                                         op0=ALU.min, op1=ALU.mult),
                                 reads=["ddd", "dnc"], writes=["ddd"])
                            for i in range(NB):
                                csl = slice((n0 + i) * 64, (n0 + i + 1) * 64)
                                P.op("pe", lambda e: e.matmul(ps[2][H, i * 64:(i + 1) * 64], lhsT=kT[:, csl], rhs=kT[:, csl], start=True, stop=True),
                                     reads=["dkT"], writes=[PS[2]], inc=(i == NB - 1))
                            for i in range(NB):
                                csl = slice((n0 + i) * 64, (n0 + i + 1) * 64)
                                P.op("pe", lambda e: e.matmul(ps[3][H, i * 64:(i + 1) * 64], lhsT=kT[:, csl], rhs=qT[:, csl], start=True, stop=True),
                                     reads=["dkT", "dqT"], writes=[PS[3]], inc=(i == NB - 1))
                            yield
                            P.op("dve", lambda e: e.tensor_tensor(out=LT[:, :, :], in0=p3(2), in1=dd[:, :, :], op=ALU.mult),
                                 reads=[PS[2], "ddd"], writes=["dLT"])
                            P.op("dve", lambda e: e.tensor_tensor(out=aTt[:, :, :], in0=p3(3), in1=dd[:, :, :], op=ALU.mult),
                                 reads=[PS[3], "ddd"], writes=["daTt"])
                            yield
                            P.op("dve", lambda e: e.scalar_tensor_tensor(out=Pm[:, :, :], in0=LT[:, :, :], scalar=-1.0, in1=bs[:, :, :],
                                                                         op0=ALU.mult, op1=ALU.mult),
                                 reads=["dLT", "dbs"], writes=["dPm"])
                            yield
                            ptv = ps[4][H, 0:NB * 32].bitcast(BF16)
                            for i in range(NB):
                                P.op("pe", lambda e: e.transpose(ptv[:, i * 64:(i + 1) * 64], Pm[:, i, :], id64),
                                     reads=["dPm", "k_ident"], writes=[PS[4]], inc=(i == NB - 1))
                            P.op("dve", lambda e: e.tensor_tensor(out=Tt[0][:, :, :], in0=Pm[:, :, :], in1=bcast(id64), op=ALU.add),
                                 reads=["dPm", "k_ident"], writes=[("dTt", 0)])
                            yield
                            P.op("act", lambda e: e.activation(out=fl3(PTm), in_=ptv, func=AF.Copy), reads=[PS[4]], writes=["dPTm"])
                            yield
                            cur = 0
                            for lev in range(1, 6):
                                if lev < 5:
                                    for i in range(NB):
                                        P.op("pe", lambda e: e.matmul(ps[2][H, i * 64:(i + 1) * 64], lhsT=PTm[:, i, :], rhs=Pm[:, i, :],
                                                                      start=True, stop=True),
                                             reads=["dPm", "dPTm"], writes=[PS[2]], inc=(i == NB - 1))
                                for i in range(NB):
                                    P.op("pe", lambda e: e.matmul(ps[3][H, i * 64:(i + 1) * 64], lhsT=Pm[:, i, :], rhs=PTm[:, i, :],
                                                                  start=True, stop=True),
                                         reads=["dPm", "dPTm"], writes=[PS[3]], inc=(i == NB - 1))
                                yield
                                if lev < 5:
                                    P.op("act", lambda e: e.activation(out=fl3(Pm), in_=ps[2][H, :], func=AF.Copy), reads=[PS[2]], writes=["dPm"])
                                P.op("dve", lambda e: e.tensor_copy(out=fl3(PTm), in_=ps[3][H, :]), reads=[PS[3]], writes=["dPTm"])
                                yield
                                for i in range(NB):
                                    P.op("pe", lambda e: e.matmul(ps[4][H, i * 64:(i + 1) * 64], lhsT=PTm[:, i, :], rhs=Tt[cur][:, i, :],
                                                                  start=True, stop=False),
                                         reads=["dPTm", ("dTt", cur)], writes=[PS[4]], inc=False)
                                    P.op("pe", lambda e: e.matmul(ps[4][H, i * 64:(i + 1) * 64], lhsT=id64, rhs=Tt[cur][:, i, :],
                                                                  start=False, stop=True),
                                         reads=["k_ident", ("dTt", cur)], writes=[PS[4]], inc=(i == NB - 1))
                                yield
                                cur = 1 - cur
                                P.op("dve", lambda e: e.tensor_copy(out=fl3(Tt[cur]), in_=ps[4][H, :]), reads=[PS[4]], writes=[("dTt", cur)])
                                yield
                            kv = ps[0][H, :].bitcast(BF16)
                            vv = ps[1][H, :].bitcast(BF16)
                            for i in range(NB):
                                csl = slice((n0 + i) * 64, (n0 + i + 1) * 64)
                                P.op("pe", lambda e: e.transpose(kv[:, i * 128:(i + 1) * 128], kT[:, csl], identb[:, :]),
                                     reads=["dkT", "k_ident"], writes=[PS[0]], inc=(i == NB - 1))
                            for i in range(NB):
                                csl = slice((n0 + i) * 64, (n0 + i + 1) * 64)
                                P.op("pe", lambda e: e.transpose(vv[:, i * 128:(i + 1) * 128], vT[:, csl], identb[:, :]),
                                     reads=["dvT", "k_ident"], writes=[PS[1]], inc=(i == NB - 1))
                            yield
                            B3w = [64, NB, 128]
                            kv3 = kv.rearrange("p (a b) -> p a b", b=128)
                            vv3 = vv.rearrange("p (a b) -> p a b", b=128)
                            P.op("dve", lambda e: e.tensor_tensor(out=kbg[:, :, :], in0=kv3, in1=bg_tm[:, nsl, h:h + 1].broadcast_to(B3w), op=ALU.mult),
                                 reads=[PS[0], "dbg"], writes=["dkbg"])
                            P.op("dve", lambda e: e.tensor_tensor(out=vb[:, :, :], in0=vv3, in1=beta_tm[:, nsl, h:h + 1].broadcast_to(B3w), op=ALU.mult),
                                 reads=[PS[1], "dbeta"], writes=["dvb"])
                            yield
                            for i in range(NB):
                                pb = 2 + i // 4
                                P.op("pe", lambda e: e.matmul(ps[pb][H, (i % 4) * 128:(i % 4 + 1) * 128], lhsT=Tt[cur][:, i, :], rhs=vb[:, i, :],
                                                              start=True, stop=True),
                                     reads=[("dTt", cur), "dvb"], writes=[PS[pb]], inc=(i % 4 == 3))
                            for i in range(NB):
                                P.op("pe", lambda e: e.matmul(ps[4][:, i * 64:(i + 1) * 64], lhsT=kbg[:, i, :], rhs=Tt[cur][:, i, :],
                                                              start=True, stop=True),
                                     reads=[("dTt", cur), "dkbg"], writes=[PS[4]], inc=(i == NB - 1))
                            yield "OUT"
                            P.op("dve", lambda e: e.tensor_tensor(out=kd[:, :, :], in0=kv3, in1=ed_tm[:, nsl, h:h + 1].broadcast_to(B3w), op=ALU.mult),
                                 reads=[PS[0], "ded"], writes=["dkd"])
                            P.op("dve", lambda e: e.tensor_tensor(out=qg[:, :], in0=qT[:, bsl], in1=egcb[:, :], op=ALU.mult),
                                 reads=["dqT", "degcb"], writes=["dqg"])
                            P.op("pool", lambda e: e.tensor_copy(out=aT[:, :, :], in_=aTt[:, :, :]), reads=["daTt"], writes=["daT"])
                            for hb in range(2):
                                P.op("act", lambda e: e.activation(out=u_sb[:, hb * 4:(hb + 1) * 4, :].rearrange("p a b -> p (a b)"),
                                                                   in_=ps[2 + hb][H, :], func=AF.Copy),
                                     reads=[PS[2 + hb]], writes=["du"])
                            P.op("dve", lambda e: e.tensor_copy(out=wT[:, :, :].rearrange("p a b -> p (a b)"), in_=ps[4][:, :]),
                                 reads=[PS[4]], writes=["dwT"])
                            yield

                        def stage_b(bt):
                            n0 = bt * NB
                            bsl = slice(n0 * 64, (n0 + NB) * 64)
                            for i in range(NB):
                                n = n0 + i
                                vn = vnew[n % 2]
                                kvn = ("dvnew", n % 2)
                                if n == 0:
                                    P.op("dve", lambda e: e.tensor_copy(out=vn[:, :], in_=u_sb[:, i, :]), reads=["du"], writes=[kvn])
                                else:
                                    P.op("pe", lambda e: e.matmul(ps[5][H, 0:128], lhsT=wT[:, i, :], rhs=Sb[:, :], start=True, stop=True),
                                         reads=["dwT", "dSb"], writes=[PS[5]])
                                    yield
                                    P.op("dve", lambda e: e.tensor_tensor(out=vn[:, :], in0=u_sb[:, i, :], in1=ps[5][H, 0:128], op=ALU.subtract),
                                         reads=["du", PS[5]], writes=[kvn])
                                yield
                                osl = slice(i * 64, (i + 1) * 64)
                                if n < 31:
                                    P.op("pe", lambda e: e.matmul(ps[6][:, 0:128], lhsT=kd[:, i, :], rhs=vn[:, :], start=True, stop=True),
                                         reads=["dkd", kvn], writes=[PS[6]])
                                if n > 0:
                                    P.op("pe", lambda e: e.matmul(ps[7][:, osl], lhsT=Sb[:, :], rhs=qg[:, osl], start=True, stop=False),
                                         reads=["dSb", "dqg"], writes=[PS[7]], inc=False)
                                P.op("pe", lambda e: e.matmul(ps[7][:, osl], lhsT=vn[:, :], rhs=aT[:, i, :], start=(n == 0), stop=True),
                                     reads=[kvn, "daT"], writes=[PS[7]])
                                yield
                                if n < 31:
                                    if n == 0:
                                        P.op("dve", lambda e: e.tensor_copy(out=Sf[:, :], in_=ps[6][:, 0:128]), reads=[PS[6]], writes=["dSf"])
                                    else:
                                        P.op("dve", lambda e: e.scalar_tensor_tensor(out=Sf[:, :], in0=Sf[:, :], scalar=egl[:, n, h:h + 1],
                                                                                     in1=ps[6][:, 0:128], op0=ALU.mult, op1=ALU.add),
                                             reads=[PS[6], "dSf", "degl"], writes=["dSf"])
                                    yield
                                    P.op("act", lambda e: e.activation(out=Sb[:, :], in_=Sf[:, :], func=AF.Copy), reads=["dSf"], writes=["dSb"])
                                    yield
                            P.op("act", lambda e: e.activation(out=nsq[:, :], in_=ps[7][:, :], func=AF.Square),
                                 reads=[PS[7]], writes=["dnsq"])
                            yield
                            P.op("pe", lambda e: e.matmul(ps[5][:, :], lhsT=cst["ones"][:, :], rhs=nsq[:, :], start=True, stop=True),
                                 reads=["k_ones", "dnsq"], writes=[PS[5]])
                            yield
                            P.op("act", lambda e: e.activation(out=rstd[:, :], in_=ps[5][:, :], func=AF.Ln, bias=rtrp[:, 3:4],
                                                               scale=float(1.0 / 128.0)), reads=[PS[5], "rtrp"], writes=["rstd"])
                            P.op("act", lambda e: e.activation(out=rstd[:, :], in_=rstd[:, :], func=AF.Exp, scale=-0.5),
                                 reads=["rstd"], writes=["rstd"])
                            yield
                            P.op("dve", lambda e: e.scalar_tensor_tensor(out=ntmp[:, :], in0=ps[7][:, :], scalar=dng[:, l:l + 1],
                                                                         in1=rstd[:, :], op0=ALU.mult, op1=ALU.mult),
                                 reads=[PS[7], "dng", "rstd"], writes=["dntmp"])
                            P.op("dve", lambda e: e.tensor_tensor(out=obT[:, h, bsl], in0=ntmp[:, :], in1=zsT[:, bsl], op=ALU.mult),
                                 reads=["dntmp", "dz"], writes=[("obT", h)])
                            yield

                        import os as _os2
                        _stop = int(_os2.environ.get("DN_STOP", "-1"))
                        if _stop >= 0:
                            g_ = stage_a(0)
                            for _ in range(_stop):
                                next(g_)
                        else:
                            def drive(b_gen, a_gen):
                                a_wait, a_done, b_done = False, a_gen is None, b_gen is None
                                while not (b_done and (a_done or a_wait)):
                                    if not b_done:
                                        try:
                                            next(b_gen)
                                        except StopIteration:
                                            b_done = True
                                    if not a_done and not a_wait:
                                        try:
                                            if next(a_gen) == "OUT":
                                                a_wait = True
                                        except StopIteration:
                                            a_done = True
                                if not a_done:
                                    for _ in a_gen:
                                        pass

                            drive(None, stage_a(0))
                            for bt in range(4):
                                drive(stage_b(bt), stage_a(bt + 1) if bt + 1 < 4 else None)
                        P.barrier()

        def dump(nm, tile, nchunks, keyname, nokey=False):
            if nokey:
                P.barrier()
            with nc.sbuf_tensor(_u + "s_dbgf_" + nm, [128, S], F32) as dbgf:
                for k in range(nchunks):
                    P.op("dve", lambda e: e.tensor_copy(out=dbgf[:, :], in_=tile[:, k, :]),
                         reads=[(keyname, k)], writes=["dbgf"])
                    P.dma(dbg_d[nm][k * 128:(k + 1) * 128, :], dbgf[:, :], reads=["dbgf"], writes=["dbg_" + nm], stream="o")
                P.barrier()

        for l in range(n_layers):
            norm_to(lambda k, tc: (hT[:, k, tc * TC:(tc + 1) * TC], ("hT", k)), normg[:, l, :], l)
            if debug and l == 0:
                with nc.sbuf_tensor(_u + "dbgf", [128, S], F32) as dbgf:
                    for k in range(8):
                        P.op("dve", lambda e, k=k: e.tensor_copy(out=dbgf[:, :], in_=hT[:, k, :]),
                             reads=[("hT", k)], writes=["dbgf"])
                        P.dma(dbg_d["hT"][k * 128:(k + 1) * 128, :], dbgf[:, :], reads=["dbgf"], writes=["dbg_hT"],
                              stream="o")
                    P.barrier()

            def mem_attention(l):
                ph = ExitStack()
                def sbp(name, shape, dt):
                    return ph.enter_context(nc.sbuf_tensor(_u + "s_" + name, list(shape), dt))
                memT = sbp("memT", [128, 8, MEM_LEN], F32)
                memn = sbp("memn", [128, 8, MEM_LEN], BF16)
                kmT = sbp("kmT", [128, 2, MEM_LEN], BF16)
                vm = sbp("vm", [128, 2, 256], BF16)
                qmT = sbp("qmT", [128, 2, S], BF16)
                pT = [sbp(f"pT{i}", [128, TC], BF16) for i in range(2)]
                rden = sbp("rden", [128, TC], F32)
                for k in range(8):
                    P.dma(memT[:, k, :], memT_d[k * 128:(k + 1) * 128, :], writes=[("memT", k)], stream="x")
                P.op("dve", lambda e: e.tensor_scalar(out=g32[:, :], in0=memg[:, l, :], scalar1=float(math.sqrt(D)),
                                                      scalar2=None, op0=ALU.mult), reads=["memg"], writes=["g32"])
                rms_stats(memT, [("memT", k) for k in range(8)], 8, MEM_LEN, 0, 0, sqt, "sqt", rstd, "rstd", None)
                for k in range(8):
                    P.op("dve", lambda e, k=k: e.scalar_tensor_tensor(
                        out=memn[:, k, :], in0=memT[:, k, :], scalar=g32[:, k:k + 1], in1=rstd[:, 0:MEM_LEN],
                        op0=ALU.mult, op1=ALU.mult), reads=[("memT", k), "g32", "rstd"], writes=[("memn", k)])
                for ec in range(2):
                    wb, wkey = load_w(w_kv_d[l, :, ec * 128:(ec + 1) * 128])
                    for k in range(8):
                        P.op("pe", lambda e, k=k, wb=wb: e.matmul(ps[1][:, 0:MEM_LEN], lhsT=wb[:, k, :], rhs=memn[:, k, :],
                                                                  start=(k == 0), stop=(k == 7)),
                             reads=[wkey, ("memn", k)], writes=[PS[1]], inc=(k == 7))
                    P.op("dve", lambda e, ec=ec: e.tensor_copy(out=kmT[:, ec, :], in_=ps[1][:, 0:MEM_LEN]),
                         reads=[PS[1]], writes=[("kmT", ec)])
                for vc in range(2):
                    wb, wkey = load_w(w_kv_d[l, :, 256 + vc * 128:256 + (vc + 1) * 128])
                    for mt in range(2):
                        for k in range(8):
                            P.op("pe", lambda e, k=k, wb=wb, mt=mt: e.matmul(
                                ps[2][:, 0:128], lhsT=memn[:, k, mt * 128:(mt + 1) * 128], rhs=wb[:, k, :],
                                start=(k == 0), stop=(k == 7)),
                                reads=[wkey, ("memn", k)], writes=[PS[2]], inc=(k == 7))
                        P.op("dve", lambda e, mt=mt, vc=vc: e.tensor_copy(out=vm[:, mt, vc * 128:(vc + 1) * 128],
                                                                          in_=ps[2][:, 0:128]),
                             reads=[PS[2]], writes=[("vm", mt)])
                for ec in range(2):
                    def ev(tc, pap, pkey, ec=ec):
                        P.op("act", lambda e: e.activation(out=qmT[:, ec, tc * TC:(tc + 1) * TC], in_=pap,
                                                           func=AF.Copy, scale=0.125),
                             reads=[pkey], writes=[("qmT", ec)])
                    proj_fm(l, C_MQ + ec * 128, 128, ev)
                for h in range(4):
                    ec, r0 = h // 2, (h % 2) * 64
                    for tc in range(NT):
                        tsl = slice(tc * TC, (tc + 1) * TC)
                        for mb in range(2):
                            pi = 3 + mb
                            P.op("pe", lambda e, mb=mb, pi=pi: e.matmul(
                                ps[pi][:, :], lhsT=kmT[r0:r0 + 64, ec, mb * 128:(mb + 1) * 128],
                                rhs=qmT[r0:r0 + 64, ec, tsl], start=True, stop=True),
                                reads=[("kmT", ec), ("qmT", ec)], writes=[PS[pi]])
                            P.op("act", lambda e, mb=mb, pi=pi: e.activation(out=pT[mb][:, :], in_=ps[pi][:, :],
                                                                             func=AF.Exp),
                                 reads=[PS[pi]], writes=[("pT", mb)])
                        for mb in range(2):
                            P.op("pe", lambda e, mb=mb: e.matmul(
                                ps[5][r0:r0 + 64, :], lhsT=vm[:, mb, h * 64:(h + 1) * 64], rhs=pT[mb][:, :],
                                start=(mb == 0), stop=(mb == 1)),
                                reads=[("vm", mb), ("pT", mb)], writes=[PS[5]], inc=(mb == 1))
                        for mb in range(2):
                            P.op("pe", lambda e, mb=mb: e.matmul(
                                ps[6][r0:r0 + 64, :], lhsT=cst["ones"][:, 0:64], rhs=pT[mb][:, :],
                                start=(mb == 0), stop=(mb == 1)),
                                reads=["k_ones", ("pT", mb)], writes=[PS[6]], inc=(mb == 1))
                        P.op("dve", lambda e: e.reciprocal(out=rden[r0:r0 + 64, :], in_=ps[6][r0:r0 + 64, :]),
                             reads=[PS[6]], writes=["rden"])
                        P.op("dve", lambda e, tsl=tsl: e.tensor_tensor(out=obT[r0:r0 + 64, ec, tsl], in0=ps[5][r0:r0 + 64, :],
                                                                      in1=rden[r0:r0 + 64, :], op=ALU.mult),
                             reads=[PS[5], "rden"], writes=[("obT", ec)])
                P.barrier()
                ph.close()

            def merge(br, nwc, first):
                with ExitStack() as ph:
                    sg = [ph.enter_context(nc.sbuf_tensor(_u + f"sg{i}", [128, TC], F32)) for i in range(2)]
                    tmp = [ph.enter_context(nc.sbuf_tensor(_u + f"mtmp{i}", [128, TC], F32)) for i in range(2)]
                    it = 0
                    for dc in range(8):
                        wg, wgk = load_w(w_in_d[l, :, C_G + br * D + dc * 128:C_G + br * D + (dc + 1) * 128])
                        wr, wrk = load_w(w_br_d[br][l, :, dc * 128:(dc + 1) * 128], rows=nwc)
                        for tc in range(NT):
                            tsl = slice(tc * TC, (tc + 1) * TC)
                            b = it % 2
                            it += 1
                            pg, pp = 1 + b, 3 + b
                            for k in range(8):
                                P.op("pe", lambda e, k=k, pg=pg, tsl=tsl, wg=wg: e.matmul(
                                    ps[pg][:, :], lhsT=wg[:, k, :], rhs=hT[:, k, tsl], start=(k == 0), stop=(k == 7)),
                                    reads=[wgk, ("hT", k)], writes=[PS[pg]], inc=(k == 7))
                            for k in range(nwc):
                                P.op("pe", lambda e, k=k, pp=pp, tsl=tsl, wr=wr: e.matmul(
                                    ps[pp][:, :], lhsT=wr[:, k, :], rhs=obT[:, k, tsl], start=(k == 0), stop=(k == nwc - 1)),
                                    reads=[wrk, ("obT", k)], writes=[PS[pp]], inc=(k == nwc - 1))
                            P.op("act", lambda e, b=b, pg=pg, dc=dc: e.activation(
                                out=sg[b][:, :], in_=ps[pg][:, :], func=AF.Sigmoid,
                                bias=bgate[:, l, br * 8 + dc:br * 8 + dc + 1], scale=1.0),
                                reads=[PS[pg], "bgate"], writes=[("sg", b)])
                            if first:
                                P.op("dve", lambda e, b=b, pp=pp, dc=dc, tsl=tsl: e.tensor_tensor(
                                    out=mT[:, dc, tsl], in0=ps[pp][:, :], in1=sg[b][:, :], op=ALU.mult),
                                    reads=[PS[pp], ("sg", b)], writes=[("mT", dc, tc)])
                            else:
                                P.op("dve", lambda e, b=b, pp=pp: e.tensor_tensor(
                                    out=tmp[b][:, :], in0=ps[pp][:, :], in1=sg[b][:, :], op=ALU.mult),
                                    reads=[PS[pp], ("sg", b)], writes=[("mtmp", b)])
                                P.op("dve", lambda e, b=b, dc=dc, tsl=tsl: e.tensor_tensor(
                                    out=mT[:, dc, tsl], in0=mT[:, dc, tsl], in1=tmp[b][:, :], op=ALU.add),
                                    reads=[("mtmp", b), ("mT", dc, tc)], writes=[("mT", dc, tc)])
                    P.barrier()

            if 1 in branches:
                deltanet(l)
                if debug and l == 0:
                    dump('odn', obT, 4, 'obT')
                merge(1, 4, True)
            mem_attention(l)
            if debug and l == 0:
                dump('omem', obT, 2, 'obT')
            merge(3, 2, 1 not in branches)
            if 0 in branches:
                sb_attention(l)
                if debug and l == 0:
                    dump('osb', obT, 4, 'obT')
                merge(0, 4, False)
            if 2 in branches:
                retention(l)
                if debug and l == 0:
                    dump('ort', obT, 4, 'obT')
                merge(2, 4, False)

            for ec in range(8):
                wo, wok = load_w(w_out_d[l, :, ec * 128:(ec + 1) * 128])
                for tc in range(NT):
                    tsl = slice(tc * TC, (tc + 1) * TC)
                    pi = 1 + tc % 2
                    for dc in range(8):
                        P.op("pe", lambda e, dc=dc, pi=pi, tsl=tsl, wo=wo: e.matmul(
                            ps[pi][:, :], lhsT=wo[:, dc, :], rhs=mT[:, dc, tsl], start=(dc == 0), stop=(dc == 7)),
                            reads=[wok, ("mT", dc, tc)], writes=[PS[pi]], inc=(dc == 7))
                    P.op("dve", lambda e, ec=ec, pi=pi, tsl=tsl: e.tensor_tensor(
                        out=xT[:, ec, tsl], in0=xT[:, ec, tsl], in1=ps[pi][:, :], op=ALU.add),
                        reads=[PS[pi], ("xT", ec)], writes=[("xT", ec)])
            P.barrier()

        with ExitStack() as ph:
            ot = [ph.enter_context(nc.sbuf_tensor(_u + f"ot{i}", [128, TC], F32)) for i in range(2)]
            cnt = [0]

            def dst(k, tc):
                b = cnt[0] % 2
                cnt[0] += 1
                return ot[b][:, :], ("ot", b)
            P.op("dve", lambda e: e.tensor_scalar(out=g32[:, :], in0=fing[:, :], scalar1=float(math.sqrt(D)), scalar2=None,
                                                  op0=ALU.mult), reads=["fing"], writes=["g32"])
            for tc in range(NT):
                rms_stats(xT, [("xT", k) for k in range(8)], 8, TC, tc * TC, 0, sqt, "sqt", rstd, "rstd", None)
                for k in range(8):
                    ap, key = dst(k, tc)
                    P.op("dve", lambda e, k=k, tc=tc, ap=ap: e.scalar_tensor_tensor(
                        out=ap, in0=xT[:, k, tc * TC:(tc + 1) * TC], scalar=g32[:, k:k + 1], in1=rstd[:, :],
                        op0=ALU.mult, op1=ALU.mult),
                        reads=[("xT", k), "g32", "rstd"], writes=[key])
                    P.dma(outT_d[k * 128:(k + 1) * 128, tc * TC:(tc + 1) * TC], ap, reads=[key], writes=["outT"],
                          stream="o")
            P.finish(["outT", "dbg_hT", "dbg_omem", "dbg_osb", "dbg_odn", "dbg_ort"])
            P.barrier(full=True)
        P.emit()
        print("instructions recorded:", P.nins, {n: len(P.q[n]) for n in P.names})
    return nc


_NC_CACHE = {}


def _prep_inputs(inputs, b):
    f = np.float32
    m = {}
    m["xT"] = np.ascontiguousarray(inputs["x"][b].T.astype(f))
    m["memT"] = np.ascontiguousarray(inputs["mem"][b].T.astype(f))
    m["w_in"] = np.ascontiguousarray(inputs["w_in"], dtype=f)
    m["w_mem_kv"] = np.ascontiguousarray(inputs["w_mem_kv"], dtype=f)
    for n in ["w_br_sb", "w_br_dn", "w_br_ret", "w_br_mem", "w_out"]:
        m[n] = np.ascontiguousarray(inputs[n], dtype=f)
    m["norm_g"] = np.ascontiguousarray(inputs["norm_g"].reshape(DEPTH, 8, 128).transpose(0, 2, 1), dtype=f)
    m["mem_norm_g"] = np.ascontiguousarray(inputs["mem_norm_g"].reshape(DEPTH, 8, 128).transpose(0, 2, 1), dtype=f)
    m["b_gate"] = np.ascontiguousarray(inputs["b_gate"].reshape(DEPTH, 32, 128).transpose(0, 2, 1), dtype=f)
    m["final_norm_g"] = np.ascontiguousarray(inputs["final_norm_g"].reshape(8, 128).T, dtype=f)
    for n, v in _consts().items():
        m["c_" + n] = v
    for n, v in _rt_consts().items():
        m["c_" + n] = v
    m["c_dn"] = _dn_consts()
    m["dn_conv_w"] = np.ascontiguousarray(inputs["dn_conv_w"].reshape(DEPTH, 4, 12, 128).transpose(0, 3, 2, 1), dtype=f)
    m["dn_norm_g"] = np.ascontiguousarray(inputs["dn_norm_g"].T, dtype=f)
    m["dn_alog"] = np.ascontiguousarray(np.broadcast_to(np.tile(inputs["dn_a_log"], (1, 32))[:, None, :], (DEPTH, 128, 128)), dtype=f)
    m["dn_dtb"] = np.ascontiguousarray(np.broadcast_to(np.tile(inputs["dn_dt_bias"], (1, 32))[:, None, :], (DEPTH, 128, 128)), dtype=f)
    m["pos"] = np.ascontiguousarray(np.broadcast_to(inputs["positions"][b].astype(np.int32)[None, :], (128, S)))
    m["ret_norm_g"] = np.ascontiguousarray(inputs["ret_norm_g"].reshape(DEPTH, 4, 128).transpose(0, 2, 1), dtype=f)
    return m


def kernel(**inputs):
    inputs = {k: np.asarray(v) for k, v in inputs.items()}
    if "nc" not in _NC_CACHE:
        _NC_CACHE["nc"] = build()
    nc = _NC_CACHE["nc"]
    in_maps = [_prep_inputs(inputs, b) for b in range(8)]
    res = run_bass_kernel_spmd(nc, in_maps, core_ids=list(range(8)))
    out = np.stack([np.ascontiguousarray(res.results[b]["outT"].T) for b in range(8)], axis=0)
    return out.astype(np.float32)
```

```python
import math
from contextlib import ExitStack
import numpy as np
import concourse.bass as bass
import concourse.mybir as mybir
from concourse.bass_utils import run_bass_kernel_spmd

F32 = mybir.dt.float32
BF16 = mybir.dt.bfloat16
I32 = mybir.dt.int32
AF = mybir.ActivationFunctionType
ALU = mybir.AluOpType

D = 1024
S = 2048
DEPTH = 2
MEM_LEN = 256
EPS = 1e-6
IN_COLS = 9992
C_SBQ, C_SBK, C_SBV, C_SBZ = 0, 512, 1024, 1536
C_DNQ, C_DNK, C_DNV, C_DNZ, C_DNA, C_DNB = 2048, 2560, 3072, 3584, 4096, 4100
C_RTQ, C_RTK, C_RTV, C_RTZ = 4104, 4360, 4616, 5128
C_MQ = 5640
C_G = 5896
NT = 4
TC = 512


class _Uniq:
    def __init__(self):
        self.n = 0

    def __add__(self, name):
        self.n += 1
        return f"{name}_{self.n}"


class _Rec:
    def __getattr__(self, name):
        return lambda *a, **k: (name, a, k)


_REC = _Rec()


class Prog:
    LIMIT = 30000

    def __init__(self, nc, stack):
        self.nc = nc
        self.stack = stack
        self.names = ["pe", "act", "dve", "pool", "sp"]
        self.q = {n: [] for n in self.names}
        self.cnt = {n: 0 for n in self.names}
        self.sems = {n: [] for n in self.names}
        self.seen = {n: {} for n in self.names}
        self.bufs = {}
        self.dma_sem = {}
        self.dma_cnt = {}
        self.same_sync = True
        self.dma_i = 0
        self.nins = 0

    def _eng_sem(self, eng, g):
        ep = (g - 1) // self.LIMIT
        while len(self.sems[eng]) <= ep:
            s = self.stack.enter_context(self.nc.semaphore(f"s_{eng}_{len(self.sems[eng])}"))
            self.sems[eng].append(s)
        return self.sems[eng][ep], (g - 1) % self.LIMIT + 1

    def _tok_sem(self, tok):
        kind, g = tok
        if kind.startswith("dma:"):
            return self.dma_sem[kind], 16 * g
        return self._eng_sem(kind, g)

    def _need(self, eng, tok):
        kind, g = tok
        if kind == eng:
            if eng in ("pe", "sp") or not self.same_sync:
                return False
        return self.seen[eng].get(kind, 0) < g

    def _collect(self, eng, reads, writes):
        toks = []
        for k in reads:
            b = self.bufs.get(k)
            if b and b[0] is not None:
                toks.append(b[0])
            if b and isinstance(k, tuple) and k[0] == "ps" and eng in ("act", "dve"):
                other = "dve" if eng == "act" else "act"
                if other in b[1]:
                    toks.append((other, b[1][other]))
        for k in writes:
            b = self.bufs.get(k)
            if b:
                if b[0] is not None:
                    toks.append(b[0])
                toks.extend(b[1].items())
        need = {}
        for t in toks:
            if self._need(eng, t):
                need[t[0]] = max(need.get(t[0], 0), t[1])
        return list(need.items())

    def _update(self, tok, reads, writes):
        for k in writes:
            self.bufs[k] = [tok, {}]
        for k in reads:
            b = self.bufs.setdefault(k, [None, {}])
            if k in writes:
                continue
            b[1][tok[0]] = max(b[1].get(tok[0], 0), tok[1])

    def op(self, eng, fn, reads=(), writes=(), inc=True):
        call = fn(_REC)
        fn = lambda e, c=call: getattr(e, c[0])(*c[1], **c[2])
        waits = self._collect(eng, reads, writes)
        for t in waits:
            self.seen[eng][t[0]] = t[1]
        ws = [self._tok_sem(t) for t in waits]
        for w in ws[1:]:
            self.q[eng].append(("wait", w[0], w[1]))
        if inc:
            self.cnt[eng] += 1
            tok = (eng, self.cnt[eng])
            sem, val = self._eng_sem(eng, self.cnt[eng])
            self.q[eng].append(("ins", fn, ws[0] if ws else None, (sem, 1)))
        else:
            tok = (eng, self.cnt[eng] + 1)
            self.q[eng].append(("ins", fn, ws[0] if ws else None, None))
        self._update(tok, reads, writes)
        self.nins += 1
        return tok

    NDS = 16

    def dma(self, out, in_, reads=(), writes=(), stream="d0", queue="act"):
        if not self.SOFT:
            queue = "sp"
        j = self.dma_i % self.NDS
        self.dma_i += 1
        kind = f"dma:{j}"
        if kind not in self.dma_sem:
            self.dma_sem[kind] = self.stack.enter_context(self.nc.semaphore(f"sd_{j}"))
            self.dma_cnt[kind] = 0
        waits = self._collect(queue, reads, writes)
        if self.dma_cnt[kind] > 0 and self.seen[queue].get(kind, 0) < self.dma_cnt[kind]:
            waits = [w for w in waits if w[0] != kind] + [(kind, self.dma_cnt[kind])]
        for t in waits:
            self.seen[queue][t[0]] = t[1]
        for t in waits:
            s, v = self._tok_sem(t)
            self.q[queue].append(("wait", s, v))
        self.dma_cnt[kind] += 1
        tok = (kind, self.dma_cnt[kind])
        self.q[queue].append(("ins", lambda e, o=out, i=in_: e.dma_start(out=o, in_=i), None,
                              (self.dma_sem[kind], 16)))
        self._update(tok, reads, writes)
        self.nins += 1
        return tok

    PERSIST = {"cstf", "normg", "memg", "bgate", "fing", "xT", "hT", "mT", "obT", "wst", "wb", "epsb", "g32", "sqt",
               "rstd", "oneb", "rtsd", "rtrp", "rtcn", "retg", "dncw", "dng", "ps", "outT"}

    def _persistent(self, k):
        h = k[0] if isinstance(k, tuple) else k
        return h in self.PERSIST or (isinstance(h, str) and (h.startswith("k_") or h.startswith("dbg_")))

    SOFT = False

    def barrier(self, full=False):
        full = full or not self.SOFT
        if full:
            toks = [(n, self.cnt[n]) for n in self.names if self.cnt[n] > 0]
            toks += [(k, c) for k, c in self.dma_cnt.items() if c > 0]
            engines = self.names
        else:
            need = {}
            for k, b in self.bufs.items():
                if self._persistent(k):
                    continue
                if b[0] is not None:
                    need[b[0][0]] = max(need.get(b[0][0], 0), b[0][1])
                for kind, c in b[1].items():
                    need[kind] = max(need.get(kind, 0), c)
            toks = list(need.items())
            engines = [n for n in self.names if n != "sp"]
        for eng in engines:
            for t in toks:
                if t[0] == eng and eng in ("pe", "sp"):
                    continue
                if self.seen[eng].get(t[0], 0) < t[1]:
                    self.seen[eng][t[0]] = t[1]
                    s, v = self._tok_sem(t)
                    self.q[eng].append(("wait", s, v))
        if full:
            self.bufs = {}
        else:
            self.bufs = {k: b for k, b in self.bufs.items() if self._persistent(k)}

    def finish(self, final_keys):
        toks = []
        for k in final_keys:
            b = self.bufs.get(k)
            if b and b[0] is not None:
                toks.append(b[0])
        for t in toks:
            s, v = self._tok_sem(t)
            self.q["act" if self.SOFT else "sp"].append(("wait", s, v))

    def simulate(self):
        sem = {}
        pos = {n: 0 for n in self.names}
        def ok(w):
            return w is None or sem.get(id(w[0]), 0) >= w[1]
        progress = True
        while progress:
            progress = False
            for n in self.names:
                q = self.q[n]
                while pos[n] < len(q):
                    ent = q[pos[n]]
                    if ent[0] == "wait":
                        if not ok((ent[1], ent[2])):
                            break
                    else:
                        if not ok(ent[2]):
                            break
                        if ent[3] is not None:
                            sem[id(ent[3][0])] = sem.get(id(ent[3][0]), 0) + ent[3][1]
                    pos[n] += 1
                    progress = True
        stuck = {n: (pos[n], len(self.q[n])) for n in self.names if pos[n] < len(self.q[n])}
        if stuck:
            msg = []
            for n, (p, ln) in stuck.items():
                ent = self.q[n][p]
                w = (ent[1], ent[2]) if ent[0] == "wait" else ent[2]
                msg.append(f"{n}@{p}/{ln} waits {w} have {sem.get(id(w[0]), 0)} kind={ent[0]} tag={ent[4] if len(ent) > 4 else None}")
            raise RuntimeError("DEADLOCK in recorded program: " + "; ".join(msg))

    def emit(self):
        self.simulate()
        nc = self.nc
        with nc.Block() as block:
            def replay(e, name):
                for ent in self.q[name]:
                    if ent[0] == "wait":
                        e.wait_ge(ent[1], ent[2])
                    else:
                        _, fn, w, inc = ent
                        ins = fn(e)
                        if w is not None:
                            ins._wait_ge(w[0], w[1])
                        if inc is not None:
                            ins.then_inc(inc[0], inc[1])

            @block.sync
            def _(e):
                replay(e, "sp")

            @block.scalar
            def _(e):
                replay(e, "act")

            @block.vector
            def _(e):
                replay(e, "dve")

            @block.tensor
            def _(e):
                replay(e, "pe")

            @block.gpsimd
            def _(e):
                replay(e, "pool")


def _consts():
    c = {}
    i = np.arange(128)
    c["ident"] = np.eye(128, dtype=np.float32)
    c["ones"] = np.ones((128, 128), np.float32)
    c["trineg"] = -(i[:, None] >= i[None, :]).astype(np.float32)
    e0 = np.zeros((128, 128), np.float32)
    e0[0, :] = 1.0
    c["e0"] = e0
    c["masksb"] = (i[:, None] < i[None, :]).astype(np.float32)
    return c


def _rt_consts():
    gam = [1.0 - 2.0 ** (-5.0 - h) for h in range(4)]
    i = np.arange(128)
    dt = np.zeros((128, 4, 128), np.float64)
    sd = np.zeros((128, 4), np.float64)
    cd = np.zeros((128, 2, 128), np.float64)
    for h in range(4):
        rel = i[None, :] - i[:, None]
        dt[:, h, :] = np.where(rel >= 0, gam[h] ** np.maximum(rel, 0), 0.0)
        sd[:, h] = gam[h] ** (127 - i)
    for p in range(2):
        for hh in range(2):
            cd[hh * 64:(hh + 1) * 64, p, :] = (gam[2 * p + hh] ** (i + 1.0))[None, :]
    inv = 10000.0 ** (-(np.arange(32, dtype=np.float32)) / np.float32(32))
    rp = np.zeros((128, 4), np.float32)
    rp[:, 0] = np.tile(inv.astype(np.float32), 4)
    rp[:, 1] = np.where((i % 64) < 32, -1.0, 1.0)
    rp[:, 2] = np.float32(math.pi / 2)
    rp[:, 3] = EPS
    cn = np.eye(128) - 1.0 / 128.0
    return {"rt_dt": dt.astype(np.float32), "rt_sd": sd.astype(np.float32), "rt_cd": cd.astype(np.float32),
            "rt_rp": rp, "rt_cn": cn.astype(np.float32)}


def _dn_consts():
    i = np.arange(64)
    c = np.zeros((128, 4, 128), np.float32)
    c[:, 0, :] = np.eye(128)
    c[:, 1, :] = 1.0
    c[0:64, 2, 0:64] = (i[:, None] <= i[None, :])
    c[0:64, 3, 0:64] = (i[:, None] < i[None, :])
    return c


CONST_NAMES = ["ident", "ones", "trineg", "e0", "masksb"]


def build(n_layers=DEPTH, debug=False, branches=(0, 1, 2, 3)):
    _u = _Uniq()
    nc = bass.Bass("TRN2", target_bir_lowering=False)
    dr = {}

    def din(name, shape, dt=F32):
        dr[name] = nc.dram_tensor(name, list(shape), dt, kind="ExternalInput").ap()
        return dr[name]

    xT_d = din("xT", [D, S])
    memT_d = din("memT", [D, MEM_LEN])
    w_in_d = din("w_in", [DEPTH, D, IN_COLS])
    w_kv_d = din("w_mem_kv", [DEPTH, D, 512])
    w_br_d = {0: din("w_br_sb", [DEPTH, 512, D]), 1: din("w_br_dn", [DEPTH, 512, D]),
              2: din("w_br_ret", [DEPTH, 512, D]), 3: din("w_br_mem", [DEPTH, 256, D])}
    w_out_d = din("w_out", [DEPTH, D, D])
    normg_d = din("norm_g", [DEPTH, 128, 8])
    memg_d = din("mem_norm_g", [DEPTH, 128, 8])
    bgate_d = din("b_gate", [DEPTH, 128, 32])
    fing_d = din("final_norm_g", [128, 8])
    cst_d = {n: din("c_" + n, [128, 128]) for n in CONST_NAMES}
    dnc_d = din("c_dn", [128, 4, 128])
    dncw_d = din("dn_conv_w", [DEPTH, 128, 12, 4])
    dng_d = din("dn_norm_g", [128, DEPTH])
    dnal_d = din("dn_alog", [DEPTH, 128, 128])
    dndt_d = din("dn_dtb", [DEPTH, 128, 128])
    pos_d = din("pos", [128, S], I32)
    retg_d = din("ret_norm_g", [DEPTH, 128, 4])
    rtdt_d = din("c_rt_dt", [128, 4, 128])
    rtsd_d = din("c_rt_sd", [128, 4])
    rtcd_d = din("c_rt_cd", [128, 2, 128])
    rtrp_d = din("c_rt_rp", [128, 4])
    rtcn_d = din("c_rt_cn", [128, 128])
    outT_d = nc.dram_tensor("outT", [D, S], F32, kind="ExternalOutput").ap()
    dbg_d = {}
    if debug:
        dbg_d["hT"] = nc.dram_tensor("dbg_hT", [D, S], F32, kind="ExternalOutput").ap()
        dbg_d["omem"] = nc.dram_tensor("dbg_omem", [256, S], F32, kind="ExternalOutput").ap()
        for nm in ["osb", "odn", "ort"]:
            dbg_d[nm] = nc.dram_tensor("dbg_" + nm, [512, S], F32, kind="ExternalOutput").ap()

    with ExitStack() as st:
        P = Prog(nc, st)

        def sb(name, shape, dt):
            return st.enter_context(nc.sbuf_tensor(_u + "s_" + name, list(shape), dt))

        xT = sb("xT", [128, 8, S], F32)
        hT = sb("hT", [128, 8, S], BF16)
        mT = sb("mT", [128, 8, S], BF16)
        obT = sb("obT", [128, 4, S], BF16)
        wst = [sb(f"wst{i}", [128, 8, 128], F32) for i in range(2)]
        NWB = 4
        wbp = [sb(f"wb{i}", [128, 8, 128], BF16) for i in range(NWB)]
        cst = {n: sb("k_" + n, [128, 128], BF16) for n in CONST_NAMES}
        cstf = sb("cstf", [128, 128], F32)
        normg = sb("normg", [128, DEPTH, 8], F32)
        memg = sb("memg", [128, DEPTH, 8], F32)
        bgate = sb("bgate", [128, DEPTH, 32], F32)
        fing = sb("fing", [128, 8], F32)
        ps = [st.enter_context(nc.psum_tensor(f"ps{i}", [128, 512], F32)) for i in range(8)]
        PS = [("ps", i) for i in range(8)]

        wcount = [0]
        wbcount = [0]

        def load_w(src, n=128, rows=8, eng="pool"):
            i = wcount[0] % 2
            wcount[0] += 1
            j = wbcount[0] % NWB
            wbcount[0] += 1
            stg, wb = wst[i], wbp[j]
            P.dma(stg[:, 0:rows, 0:n], src.rearrange("(k p) n -> p k n", p=128),
                  writes=[("wst", i)], stream=f"w{i}", queue="sp")
            fn = lambda e, o=wb[:, 0:rows, 0:n], a=stg[:, 0:rows, 0:n]: e.tensor_copy(out=o, in_=a)
            P.op(eng, fn, reads=[("wst", i)], writes=[("wb", j)])
            return wb, ("wb", j)

        for n in CONST_NAMES:
            P.dma(cstf[:, :], cst_d[n][:, :], writes=["cstf"], stream="c")
            P.op("dve", lambda e, o=cst[n][:, :]: e.tensor_copy(out=o, in_=cstf[:, :]),
                 reads=["cstf"], writes=["k_" + n])
        for l in range(DEPTH):
            P.dma(normg[:, l, :], normg_d[l, :, :], writes=["normg"], stream="c")
            P.dma(memg[:, l, :], memg_d[l, :, :], writes=["memg"], stream="c")
            P.dma(bgate[:, l, :], bgate_d[l, :, :], writes=["bgate"], stream="c")
        P.dma(fing[:, :], fing_d[:, :], writes=["fing"], stream="c")
        for k in range(8):
            P.dma(xT[:, k, :], xT_d[k * 128:(k + 1) * 128, :], writes=[("xT", k)], stream="x")

        rtsd = sb("rtsd", [128, 4], F32)
        rtrp = sb("rtrp", [128, 4], F32)
        rtcn = sb("rtcn", [128, 128], BF16)
        retg = sb("retg", [128, DEPTH, 4], F32)
        P.dma(rtsd[:, :], rtsd_d[:, :], writes=["rtsd"], stream="c")
        P.dma(rtrp[:, :], rtrp_d[:, :], writes=["rtrp"], stream="c")
        P.dma(cstf[:, :], rtcn_d[:, :], writes=["cstf"], stream="c")
        P.op("dve", lambda e: e.tensor_copy(out=rtcn[:, :], in_=cstf[:, :]), reads=["cstf"], writes=["rtcn"])
        for l in range(DEPTH):
            P.dma(retg[:, l, :], retg_d[l, :, :], writes=["retg"], stream="c")
        dncw = sb("dncw", [128, DEPTH, 12, 4], F32)
        dng = sb("dng", [128, DEPTH], F32)
        P.dma(dng[:, :], dng_d[:, :], writes=["dng"], stream="c")
        for l in range(DEPTH):
            P.dma(dncw[:, l, :, :], dncw_d[l, :, :, :], writes=["dncw"], stream="c")
        def rms_stats(src_tile, src_keys, nk, ncols, c0, psum_i, sq_tile, sq_key, rstd_tile, rstd_key, extra):
            for k in range(nk):
                P.op("act", lambda e, k=k: e.activation(out=sq_tile[:, k % 2, 0:ncols], in_=src_tile[:, k, c0:c0 + ncols],
                                                        func=AF.Square),
                     reads=[src_keys[k]], writes=[(sq_key, k % 2)])
                P.op("pe", lambda e, k=k: e.matmul(ps[psum_i][:, 0:ncols], lhsT=cst["ones"][:, :],
                                                   rhs=sq_tile[:, k % 2, 0:ncols], start=(k == 0), stop=(k == nk - 1)),
                     reads=[(sq_key, k % 2), "k_ones"], writes=[PS[psum_i]])
            P.op("act", lambda e: e.activation(out=rstd_tile[:, 0:ncols], in_=ps[psum_i][:, 0:ncols], func=AF.Ln,
                                               bias=epsb[:, 0:1], scale=1.0),
                 reads=[PS[psum_i], "epsb"], writes=[rstd_key])
            P.op("act", lambda e: e.activation(out=rstd_tile[:, 0:ncols], in_=rstd_tile[:, 0:ncols], func=AF.Exp,
                                               scale=-0.5),
                 reads=[rstd_key], writes=[rstd_key])

        epsb = sb("epsb", [128, 1], F32)
        P.op("dve", lambda e: e.memset(epsb[:, :], float(D * EPS)), writes=["epsb"])
        sqt = sb("sqt", [128, 2, TC], BF16)
        rstd = sb("rstd", [128, TC], F32)
        g32 = sb("g32", [128, 8], F32)

        def norm_to(dst_fn, gsrc, layer_tag):
            P.op("dve", lambda e: e.tensor_scalar(out=g32[:, :], in0=gsrc, scalar1=float(math.sqrt(D)), scalar2=None,
                                                  op0=ALU.mult), reads=["normg", "fing"], writes=["g32"])
            for tc in range(NT):
                rms_stats(xT, [("xT", k) for k in range(8)], 8, TC, tc * TC, 0, sqt, "sqt", rstd, "rstd", None)
                for k in range(8):
                    ap, key = dst_fn(k, tc)
                    P.op("dve", lambda e, k=k, tc=tc, ap=ap: e.scalar_tensor_tensor(
                        out=ap, in0=xT[:, k, tc * TC:(tc + 1) * TC], scalar=g32[:, k:k + 1], in1=rstd[:, :],
                        op0=ALU.mult, op1=ALU.mult),
                        reads=[("xT", k), "g32", "rstd"], writes=[key])

        def proj_fm(l, col0, ncols, evac, wsrc=None):
            src = (w_in_d[l, :, col0:col0 + ncols] if wsrc is None else wsrc)
            wb, wkey = load_w(src, n=ncols)
            for tc in range(NT):
                pi = 1 + (tc % 2)
                for k in range(8):
                    P.op("pe", lambda e, k=k, tc=tc, pi=pi: e.matmul(
                        ps[pi][0:ncols, :], lhsT=wb[:, k, 0:ncols], rhs=hT[:, k, tc * TC:(tc + 1) * TC],
                        start=(k == 0), stop=(k == 7)),
                        reads=[wkey, ("hT", k)], writes=[PS[pi]], inc=(k == 7))
                evac(tc, ps[pi][0:ncols, :], PS[pi])

        def proj_tm(l, col0, ncols, evac, ntok=128):
            wb, wkey = load_w(w_in_d[l, :, col0:col0 + ncols], n=ncols)
            for tt in range(S // ntok):
                pi = 1 + (tt % 2)
                for k in range(8):
                    P.op("pe", lambda e: e.matmul(ps[pi][0:ntok, 0:ncols], lhsT=hT[:, k, tt * ntok:(tt + 1) * ntok],
                                                  rhs=wb[:, k, 0:ncols], start=(k == 0), stop=(k == 7)),
                         reads=[wkey, ("hT", k)], writes=[PS[pi]], inc=(k == 7))
                evac(tt, ps[pi][0:ntok, 0:ncols], PS[pi])

        oneb = sb("oneb", [128, 1], F32)
        P.op("dve", lambda e: e.memset(oneb[:, :], 1.0), writes=["oneb"])

        def run_streams(gens, stagger=None):
            gens = list(gens)
            if stagger:
                for g, k in zip(gens, stagger):
                    for _ in range(k):
                        next(g)
            while gens:
                for g in list(gens):
                    try:
                        next(g)
                    except StopIteration:
                        gens.remove(g)

        def sb_attention(l):
            for hp in range(4):
                with ExitStack() as ph:
                    def sbp(name, shape, dt):
                        return ph.enter_context(nc.sbuf_tensor(_u + ("s_" + name), list(shape), dt))
                    qT = sbp("sbq", [128, S], BF16)
                    kTm = [sbp(f"sbk{i}", [128, S], BF16) for i in range(2)]
                    P.op("pool", lambda e: e.memset(kTm[0][64:128, :], 0.0), writes=["sbk"])
                    P.op("pool", lambda e: e.memset(kTm[1][0:64, :], 0.0), writes=["sbk"])
                    vtm = sbp("sbv", [128, 16, 128], BF16)
                    proj_fm(l, C_SBQ + hp * 128, 128, lambda tc, pap, pkey: P.op(
                        "act", lambda e: e.activation(out=qT[:, tc * TC:(tc + 1) * TC], in_=pap, func=AF.Copy, scale=0.125),
                        reads=[pkey], writes=["sbq"]))
                    def evk(tc, pap, pkey):
                        P.op("dve", lambda e: e.tensor_copy(out=kTm[0][0:64, tc * TC:(tc + 1) * TC], in_=pap[0:64, :]),
                             reads=[pkey], writes=["sbk"])
                        P.op("act", lambda e: e.activation(out=kTm[1][64:128, tc * TC:(tc + 1) * TC], in_=pap[64:128, :], func=AF.Copy),
                             reads=[pkey], writes=["sbk"])
                    proj_fm(l, C_SBK + hp * 128, 128, evk)
                    proj_tm(l, C_SBV + hp * 128, 128, lambda tt, pap, pkey: P.op(
                        "dve", lambda e: e.tensor_copy(out=vtm[:, tt, :], in_=pap),
                        reads=[pkey], writes=[("sbv", tt)]))
                    with ExitStack() as ph2:
                        def sbw(name, shape, dt):
                            return ph2.enter_context(nc.sbuf_tensor(_u + ("s_" + name), list(shape), dt))

                        import os as _os

                        def stream(sid, hh, qcs):
                            ez = sbw(f"ez{sid}", [128, TC], F32)
                            spb = sbw(f"spb{sid}", [128, TC], BF16)
                            eg = sbw(f"eg{sid}", [128, TC], BF16)
                            Gb = sbw(f"Gb{sid}", [128, TC], BF16)
                            wv = eg
                            K_wv = ("eg", sid)
                            K_ez, K_spb, K_eg, K_Gb = ("ez", sid), ("spb", sid), ("eg", sid), ("Gb", sid)
                            rs = slice(hh * 64, hh * 64 + 64)
                            pzg, po = 2 * sid, 2 * sid + 1
                            pgg = (4 + 2 * sid) if _os.environ.get('SB_SEPG') else pzg
                            yield
                            for qc in qcs:
                                q0 = qc * TC
                                kmax = qc * 4 + 3
                                pc0 = None
                                P.op("pool", lambda e: e.memset(wv[:, 0:384], 0.0), writes=[K_wv])
                                for kb in range(kmax, -1, -1):
                                    if kmax - kb >= int(_os.environ.get("SB_MAXIT", "99")):
                                        continue
                                    j = kb - qc * 4
                                    c0 = 128 * j if j >= 0 else 0
                                    cs = slice(c0, TC)
                                    tsl = slice(q0 + c0, q0 + TC)
                                    ksl = slice(kb * 128, (kb + 1) * 128)
                                    P.op("pe", lambda e: e.matmul(ps[pzg][:, cs], lhsT=kTm[hh][:, ksl], rhs=qT[:, tsl],
                                                                  start=True, stop=True),
                                         reads=["sbk", "sbq"], writes=[PS[pzg]])
                                    yield
                                    P.op("act", lambda e: e.activation(out=ez[:, cs], in_=ps[pzg][:, cs], func=AF.Exp),
                                         reads=[PS[pzg]], writes=[K_ez])
                                    yield
                                    P.op("act", lambda e: e.activation(out=spb[:, cs], in_=ez[:, cs], func=AF.Ln,
                                                                       bias=oneb[:, 0:1], scale=1.0),
                                         reads=[K_ez, "oneb"], writes=[K_spb])
                                    yield
                                    if j >= 0:
                                        P.op("dve", lambda e: e.tensor_tensor(out=spb[:, c0:c0 + 128], in0=spb[:, c0:c0 + 128],
                                                                              in1=cst["masksb"][:, :], op=ALU.mult),
                                             reads=[K_spb, "k_masksb"], writes=[K_spb])
                                    P.op("pe", lambda e: e.matmul(ps[pgg][:, cs], lhsT=cst["trineg"][:, :], rhs=spb[:, cs],
                                                                  start=True, stop=(pc0 is None)),
                                         reads=["k_trineg", K_spb], writes=[PS[pgg]], inc=(pc0 is None))
                                    if pc0 is not None:
                                        pcs = slice(pc0, TC)
                                        P.op("pe", lambda e: e.matmul(ps[pgg][:, pcs], lhsT=cst["e0"][:, :], rhs=Gb[:, pcs],
                                                                      start=False, stop=True),
                                             reads=["k_e0", K_Gb], writes=[PS[pgg]])
                                    yield
                                    P.op("act", lambda e: e.activation(out=eg[:, cs], in_=ps[pgg][:, cs], func=AF.Exp),
                                         reads=[PS[pgg]], writes=[K_eg])
                                    if kb > 0:
                                        P.op("dve", lambda e: e.tensor_copy(out=Gb[:, cs], in_=ps[pgg][:, cs]),
                                             reads=[PS[pgg], K_eg], writes=[K_Gb])
                                    yield
                                    P.op("dve", lambda e: e.tensor_tensor(out=wv[:, cs], in0=ez[:, cs], in1=eg[:, cs], op=ALU.mult),
                                         reads=[K_ez, K_eg], writes=[K_wv])
                                    if j >= 0:
                                        P.op("dve", lambda e: e.tensor_tensor(out=wv[:, c0:c0 + 128], in0=wv[:, c0:c0 + 128],
                                                                              in1=cst["masksb"][:, :], op=ALU.mult),
                                             reads=[K_wv, "k_masksb"], writes=[K_wv])
                                    yield
                                    P.op("pe", lambda e: e.matmul(ps[po][:, :], lhsT=vtm[:, kb, :], rhs=wv[:, :],
                                                                  start=(kb == kmax), stop=(kb == 0)),
                                         reads=[("sbv", kb), K_wv], writes=[PS[po]])
                                    pc0 = c0
                                    yield
                                P.op("dve", lambda e: e.tensor_copy(out=obT[rs, hp, q0:q0 + TC], in_=ps[po][rs, :]),
                                     reads=[PS[po]], writes=[("obT", hp)])
                                yield

                        _mode = _os.environ.get("SB_MODE", "4")
                        if _mode == "1":
                            run_streams([stream(0, 0, [3, 0, 2, 1])])
                            run_streams([stream(1, 1, [3, 0, 2, 1])])
                        elif _mode == "2":
                            run_streams([stream(0, 0, [3, 0, 2, 1]), stream(1, 1, [3, 0, 2, 1])])
                        else:
                            run_streams([stream(0, 0, [3, 0]), stream(1, 1, [3, 0]), stream(2, 0, [2, 1]), stream(3, 1, [2, 1])],
                                        stagger=[int(x) for x in _os.environ.get('SB_STAG', '0,2,4,6').split(',')])
                        P.barrier()
                    with ExitStack() as ph3:
                        zs = [ph3.enter_context(nc.sbuf_tensor(_u + f"s_sbzs{i}", [128, TC], BF16)) for i in range(2)]

                        def evz(tc, pap, pkey):
                            b = tc % 2
                            P.op("act", lambda e: e.activation(out=zs[b][:, :], in_=pap, func=AF.Silu),
                                 reads=[pkey], writes=[("sbzs", b)])
                            tsl = slice(tc * TC, (tc + 1) * TC)
                            P.op("dve", lambda e: e.tensor_tensor(out=obT[:, hp, tsl], in0=obT[:, hp, tsl], in1=zs[b][:, :], op=ALU.mult),
                                 reads=[("sbzs", b)], writes=[("obT", hp)])
                        proj_fm(l, C_SBZ + hp * 128, 128, evz)
                        P.barrier()

        def retention(l):
            TWO_PI = 2.0 * math.pi
            C1 = 6.28125
            C2 = TWO_PI - C1
            with ExitStack() as ph0:
                def sb0(name, shape, dt):
                    return ph0.enter_context(nc.sbuf_tensor(_u + ("s_" + name), list(shape), dt))
                qrT = sb0("rqr", [128, 2, S], BF16)
                krT = sb0("rkr", [128, 2, S], BF16)
                rtdt = sb0("rtdt", [128, 4, 128], F32)
                rtcd = sb0("rtcd", [128, 2, 128], F32)
                P.dma(rtdt[:, :, :], rtdt_d[:, :, :], writes=["rtdt"], stream="c")
                P.dma(rtcd[:, :, :], rtcd_d[:, :, :], writes=["rtcd"], stream="c")
                with ExitStack() as ph1:
                    def sb1(name, shape, dt):
                        return ph1.enter_context(nc.sbuf_tensor(_u + ("s_" + name), list(shape), dt))
                    COS2 = sb1("rcos", [128, S], BF16)
                    SIN2 = sb1("rsin", [128, S], BF16)
                    pint = sb1("rpint", [128, TC], I32)
                    ta = sb1("rta", [128, TC], F32)
                    tk = sb1("rtk", [128, TC], F32)
                    tm = sb1("rtm", [128, 256], F32)[:, :]
                    ki = sb1("rki", [128, 256], I32)[:, :]
                    ta_f, tk_f, pint_f = ta, tk, pint
                    for tc in range(8):
                        tsl = slice(tc * 256, (tc + 1) * 256)
                        ta, tk, pint = ta_f[:, 0:256], tk_f[:, 0:256], pint_f[:, 0:256]
                        P.dma(pint, pos_d[:, tsl], writes=["rpint"], stream="x")
                        P.op("dve", lambda e: e.tensor_scalar(out=ta, in0=pint, scalar1=rtrp[:, 0:1], scalar2=None,
                                                              op0=ALU.mult), reads=["rpint", "rtrp"], writes=["rta"])
                        P.op("dve", lambda e: e.tensor_scalar(out=ki, in0=ta, scalar1=float(1.0 / TWO_PI),
                                                              scalar2=None, op0=ALU.mult), reads=["rta"], writes=["rki"])
                        P.op("dve", lambda e: e.tensor_copy(out=tk, in_=ki), reads=["rki"], writes=["rtk"])
                        P.op("dve", lambda e: e.scalar_tensor_tensor(out=ta, in0=tk, scalar=-C1, in1=ta,
                                                                     op0=ALU.mult, op1=ALU.add),
                             reads=["rtk", "rta"], writes=["rta"])
                        P.op("dve", lambda e: e.scalar_tensor_tensor(out=ta, in0=tk, scalar=-C2, in1=ta,
                                                                     op0=ALU.mult, op1=ALU.add),
                             reads=["rtk", "rta"], writes=["rta"])
                        P.op("dve", lambda e: e.tensor_single_scalar(out=tm, in_=ta, scalar=float(math.pi),
                                                                     op=ALU.is_gt), reads=["rta"], writes=["rtm"])
                        P.op("dve", lambda e: e.scalar_tensor_tensor(out=ta, in0=tm, scalar=-TWO_PI, in1=ta,
                                                                     op0=ALU.mult, op1=ALU.add),
                             reads=["rtm", "rta"], writes=["rta"])
                        P.op("dve", lambda e: e.tensor_single_scalar(out=tm, in_=ta, scalar=float(-math.pi),
                                                                     op=ALU.is_lt), reads=["rta"], writes=["rtm"])
                        P.op("dve", lambda e: e.scalar_tensor_tensor(out=ta, in0=tm, scalar=TWO_PI, in1=ta,
                                                                     op0=ALU.mult, op1=ALU.add),
                             reads=["rtm", "rta"], writes=["rta"])
                        P.op("act", lambda e: e.activation(out=SIN2[:, tsl], in_=ta, func=AF.Sin, scale=rtrp[:, 1:2]),
                             reads=["rta", "rtrp"], writes=["rsin"])
                        P.op("dve", lambda e: e.tensor_single_scalar(out=tm, in_=ta, scalar=float(math.pi / 2),
                                                                     op=ALU.is_gt), reads=["rta"], writes=["rtm"])
                        P.op("dve", lambda e: e.scalar_tensor_tensor(out=ta, in0=tm, scalar=-TWO_PI, in1=ta,
                                                                     op0=ALU.mult, op1=ALU.add),
                             reads=["rtm", "rta"], writes=["rta"])
                        P.op("act", lambda e: e.activation(out=COS2[:, tsl], in_=ta, func=AF.Sin, bias=rtrp[:, 2:3],
                                                           scale=1.0), reads=["rta", "rtrp"], writes=["rcos"])
                    ta, tk, pint = ta_f, tk_f, pint_f
                    for which, (c_base, dstT, scl) in enumerate([(C_RTQ, qrT, 1.0), (C_RTK, krT, 0.125)]):
                        for hp in range(2):
                            wb, wkey = load_w(w_in_d[l, :, c_base + hp * 128:c_base + (hp + 1) * 128])
                            jsw = wbcount[0] % NWB
                            wbcount[0] += 1
                            wsw = wbp[jsw]
                            for hh in range(2):
                                o = hh * 64
                                P.op("pool", lambda e: e.tensor_copy(out=wsw[:, :, o:o + 32], in_=wb[:, :, o + 32:o + 64]),
                                     reads=[wkey], writes=[("wb", jsw)])
                                P.op("pool", lambda e: e.tensor_copy(out=wsw[:, :, o + 32:o + 64], in_=wb[:, :, o:o + 32]),
                                     reads=[wkey], writes=[("wb", jsw)])
                            for tc in range(NT):
                                tsl = slice(tc * TC, (tc + 1) * TC)
                                for k in range(8):
                                    P.op("pe", lambda e: e.matmul(ps[1][:, :], lhsT=wb[:, k, :], rhs=hT[:, k, tsl],
                                                                  start=(k == 0), stop=(k == 7)),
                                         reads=[wkey, ("hT", k)], writes=[PS[1]], inc=(k == 7))
                                for k in range(8):
                                    P.op("pe", lambda e: e.matmul(ps[2][:, :], lhsT=wsw[:, k, :], rhs=hT[:, k, tsl],
                                                                  start=(k == 0), stop=(k == 7)),
                                         reads=[("wb", jsw), ("hT", k)], writes=[PS[2]], inc=(k == 7))
                                P.op("dve", lambda e: e.scalar_tensor_tensor(out=ta[:, :], in0=ps[1][:, :], scalar=float(scl),
                                                                             in1=COS2[:, tsl], op0=ALU.mult, op1=ALU.mult),
                                     reads=[PS[1], "rcos"], writes=["rta"])
                                P.op("dve", lambda e: e.scalar_tensor_tensor(out=tk[:, :], in0=ps[2][:, :], scalar=float(scl),
                                                                             in1=SIN2[:, tsl], op0=ALU.mult, op1=ALU.mult),
                                     reads=[PS[2], "rsin"], writes=["rtk"])
                                P.op("dve", lambda e: e.tensor_tensor(out=dstT[:, hp, tsl], in0=ta[:, :], in1=tk[:, :], op=ALU.add),
                                     reads=["rta", "rtk"], writes=[("rq", which, hp)])
                    P.barrier()
                for hp in range(2):
                    with ExitStack() as ph2:
                        def sb2(name, shape, dt):
                            return ph2.enter_context(nc.sbuf_tensor(_u + ("s_" + name), list(shape), dt))
                        kdtm = sb2("rkd", [128, 16, 128], BF16)
                        vtm = sb2("rv", [128, 16, 128], BF16)
                        zsT = sb2("rz", [128, S], BF16)
                        qc = [sb2(f"rqc{i}", [128, 128], BF16) for i in range(2)]
                        scm = [sb2(f"rscm{i}", [128, 128], BF16) for i in range(2)]
                        Sf = sb2("rSf", [128, 128], F32)
                        Sb = sb2("rSb", [128, 128], BF16)
                        ob = sb2("rob", [128, TC], BF16)
                        sq = sb2("rsq", [128, TC], BF16)
                        rs_t = rstd
                        tt_t = sb2("rtt", [128, TC], F32)
                        for n in range(16):
                            nsl = slice(n * 128, (n + 1) * 128)
                            pk = ps[3][:, 0:64].bitcast(BF16)
                            P.op("pe", lambda e: e.transpose(pk, krT[:, hp, nsl], cst["ident"][:, :]),
                                 reads=[("rq", 1, hp), "k_ident"], writes=[PS[3]])
                            for hh in range(2):
                                cs = slice(hh * 64, (hh + 1) * 64)
                                h = 2 * hp + hh
                                P.op("dve", lambda e: e.tensor_scalar(out=kdtm[:, n, cs], in0=pk[:, cs], scalar1=rtsd[:, h:h + 1],
                                                                      scalar2=None, op0=ALU.mult),
                                     reads=[PS[3], "rtsd"], writes=[("rkd", n)])
                        for hh in range(2):
                            h = 2 * hp + hh
                            rs = slice(hh * 64, (hh + 1) * 64)
                            proj_tm(l, C_RTV + h * 128, 128, lambda tt, pap, pkey: P.op(
                                "dve", lambda e: e.tensor_copy(out=vtm[:, tt, :], in_=pap), reads=[pkey], writes=[("rv", tt)]))
                            proj_fm(l, C_RTZ + h * 128, 128, lambda tc, pap, pkey: P.op(
                                "act", lambda e: e.activation(out=zsT[:, tc * TC:(tc + 1) * TC], in_=pap, func=AF.Silu),
                                reads=[pkey], writes=["rz"]))
                            gch = float((1.0 - 2.0 ** (-5.0 - h)) ** 128)
                            SBANK = [7, 3]

                            def emit_sc(n):
                                nsl = slice(n * 128, (n + 1) * 128)
                                b = n % 2
                                P.op("pe", lambda e: e.matmul(ps[4][:, 0:128], lhsT=krT[rs, hp, nsl], rhs=qrT[rs, hp, nsl],
                                                              start=True, stop=True),
                                     reads=[("rq", 1, hp), ("rq", 0, hp)], writes=[PS[4]])
                                P.op("dve", lambda e: e.tensor_tensor(out=scm[b][:, :], in0=ps[4][:, 0:128], in1=rtdt[:, h, :],
                                                                      op=ALU.mult),
                                     reads=[PS[4], "rtdt"], writes=[("rscm", b)])
                                if n > 0:
                                    P.op("dve", lambda e: e.tensor_tensor(out=qc[b][rs, :], in0=qrT[rs, hp, nsl], in1=rtcd[rs, hp, :],
                                                                          op=ALU.mult),
                                         reads=[("rq", 0, hp), "rtcd"], writes=[("rqc", b)])

                            def emit_smm(n):
                                sbk = SBANK[n % 2]
                                P.op("pe", lambda e: e.matmul(ps[sbk][rs, 0:128], lhsT=kdtm[:, n, rs], rhs=vtm[:, n, :],
                                                              start=True, stop=True),
                                     reads=[("rkd", n), ("rv", n)], writes=[PS[sbk]])

                            emit_sc(0)
                            emit_smm(0)
                            for n in range(16):
                                nsl = slice(n * 128, (n + 1) * 128)
                                b = n % 2
                                csl = slice((n % 4) * 128, (n % 4 + 1) * 128)
                                po = 5 + (n // 4) % 2
                                if n + 1 < 15:
                                    emit_smm(n + 1)
                                P.op("pe", lambda e: e.matmul(ps[po][:, csl], lhsT=vtm[:, n, :], rhs=scm[b][:, :],
                                                              start=True, stop=(n == 0)),
                                     reads=[("rv", n), ("rscm", b)], writes=[PS[po]], inc=(n == 0))
                                if n > 0:
                                    P.op("pe", lambda e: e.matmul(ps[po][:, csl], lhsT=Sb[rs, :], rhs=qc[b][rs, :],
                                                                  start=False, stop=True),
                                         reads=["rSb", ("rqc", b)], writes=[PS[po]])
                                if n < 15:
                                    sbk = SBANK[n % 2]
                                    if n == 0:
                                        P.op("dve", lambda e: e.tensor_copy(out=Sf[rs, :], in_=ps[sbk][rs, 0:128]),
                                             reads=[PS[sbk]], writes=["rSf"])
                                    else:
                                        P.op("dve", lambda e: e.scalar_tensor_tensor(out=Sf[rs, :], in0=Sf[rs, :], scalar=gch,
                                                                                     in1=ps[sbk][rs, 0:128], op0=ALU.mult,
                                                                                     op1=ALU.add),
                                             reads=[PS[sbk], "rSf"], writes=["rSf"])
                                    P.op("act", lambda e: e.activation(out=Sb[rs, :], in_=Sf[rs, :], func=AF.Copy),
                                         reads=["rSf"], writes=["rSb"])
                                if n + 1 < 16:
                                    emit_sc(n + 1)
                                if n % 4 == 3:
                                    tc = n // 4
                                    tsl = slice(tc * TC, (tc + 1) * TC)
                                    P.op("act", lambda e: e.activation(out=ob[:, :], in_=ps[po][:, :], func=AF.Copy),
                                         reads=[PS[po]], writes=["rob"])
                                    P.op("pe", lambda e: e.matmul(ps[1][:, :], lhsT=rtcn[:, :], rhs=ob[:, :], start=True, stop=True),
                                         reads=["rtcn", "rob"], writes=[PS[1]])
                                    P.op("act", lambda e: e.activation(out=sq[:, :], in_=ps[1][:, :], func=AF.Square),
                                         reads=[PS[1]], writes=["rsq"])
                                    P.op("pe", lambda e: e.matmul(ps[2][:, :], lhsT=cst["ones"][:, :], rhs=sq[:, :],
                                                                  start=True, stop=True),
                                         reads=["k_ones", "rsq"], writes=[PS[2]])
                                    P.op("act", lambda e: e.activation(out=rs_t[:, :], in_=ps[2][:, :], func=AF.Ln,
                                                                       bias=rtrp[:, 3:4], scale=float(1.0 / 128.0)),
                                         reads=[PS[2], "rtrp"], writes=["rstd"])
                                    P.op("act", lambda e: e.activation(out=rs_t[:, :], in_=rs_t[:, :], func=AF.Exp, scale=-0.5),
                                         reads=["rstd"], writes=["rstd"])
                                    P.op("dve", lambda e: e.scalar_tensor_tensor(out=tt_t[:, :], in0=ps[1][:, :],
                                                                                 scalar=retg[:, l, h:h + 1], in1=rs_t[:, :],
                                                                                 op0=ALU.mult, op1=ALU.mult),
                                         reads=[PS[1], "retg", "rstd"], writes=["rtt"])
                                    P.op("dve", lambda e: e.tensor_tensor(out=obT[:, h, tsl], in0=tt_t[:, :], in1=zsT[:, tsl],
                                                                          op=ALU.mult),
                                         reads=["rtt", "rz"], writes=[("obT", h)])
                        P.barrier()

        def deltanet(l):
            P.barrier(full=True)
            H = slice(0, 64)
            with ExitStack() as ph0:
                def sb0(name, shape, dt):
                    return ph0.enter_context(nc.sbuf_tensor(_u + ("s_" + name), list(shape), dt))
                dnc = sb0("dnc", [128, 4, 128], F32)
                P.dma(dnc[:, :, :], dnc_d[:, :, :], writes=["dnc"], stream="c")
                identf, onesf = dnc[:, 0, :], dnc[:, 1, :]
                mincl, mstrict = dnc[0:64, 2, 0:64], dnc[0:64, 3, 0:64]
                abraw = sb0("dab", [64, 32, 8], F32)
                rep = sb0("drep", [64, 128], F32)
                t1 = sb0("dt1", [64, 32, 4], F32)
                g_tm = sb0("dg", [64, 32, 4], F32)
                beta_tm = sb0("dbeta", [64, 32, 4], F32)
                gc_tm = sb0("dgc", [64, 32, 4], F32)
                egl = sb0("degl", [128, 32, 4], F32)
                bg_tm = sb0("dbg", [64, 32, 4], F32)
                ed_tm = sb0("ded", [64, 32, 4], F32)
                proj_tm(l, C_DNA, 8, lambda tt, pap, pkey: P.op(
                    "dve", lambda e: e.tensor_copy(out=abraw[:, tt, :], in_=pap), reads=[pkey], writes=["dab"]), ntok=64)
                fl = lambda t: t[:, :, :].rearrange("p a b -> p (a b)")
                P.dma(rep[:, :], dndt_d[l, 0:64, :], writes=["drep"], stream="c")
                P.op("dve", lambda e: e.tensor_tensor(out=t1[:, :, :], in0=abraw[:, :, 0:4],
                                                      in1=rep[:, :].rearrange("p (a b) -> p a b", b=4), op=ALU.add),
                     reads=["dab", "drep"], writes=["dt1"])
                P.op("act", lambda e: e.activation(out=t1[:, :, :], in_=t1[:, :, :], func=AF.Exp), reads=["dt1"], writes=["dt1"])
                P.op("act", lambda e: e.activation(out=t1[:, :, :], in_=t1[:, :, :], func=AF.Ln, bias=oneb[0:64, 0:1], scale=1.0),
                     reads=["dt1", "oneb"], writes=["dt1"])
                P.dma(rep[:, :], dnal_d[l, 0:64, :], writes=["drep"], stream="c")
                P.op("act", lambda e: e.activation(out=rep[:, :], in_=rep[:, :], func=AF.Exp), reads=["drep"], writes=["drep"])
                P.op("dve", lambda e: e.scalar_tensor_tensor(out=g_tm[:, :, :], in0=t1[:, :, :], scalar=-1.0,
                                                             in1=rep[:, :].rearrange("p (a b) -> p a b", b=4),
                                                             op0=ALU.mult, op1=ALU.mult),
                     reads=["dt1", "drep"], writes=["dg"])
                P.op("act", lambda e: e.activation(out=beta_tm[:, :, :], in_=abraw[:, :, 4:8], func=AF.Sigmoid),
                     reads=["dab"], writes=["dbeta"])
                P.op("pe", lambda e: e.matmul(ps[0][0:64, 0:128], lhsT=dnc[0:64, 2, 0:64], rhs=fl(g_tm), start=True, stop=True),
                     reads=["dnc", "dg"], writes=[PS[0]])
                P.op("dve", lambda e: e.tensor_copy(out=fl(gc_tm), in_=ps[0][0:64, 0:128]), reads=[PS[0]], writes=["dgc"])
                P.op("pe", lambda e: e.matmul(ps[0][:, 128:256], lhsT=dnc[0:64, 1, :], rhs=fl(g_tm), start=True, stop=True),
                     reads=["dnc", "dg"], writes=[PS[0]])
                P.op("dve", lambda e: e.tensor_tensor(out=fl(ed_tm), in0=ps[0][0:64, 128:256], in1=fl(gc_tm), op=ALU.subtract),
                     reads=[PS[0], "dgc"], writes=["ded"])
                P.op("act", lambda e: e.activation(out=fl(ed_tm), in_=fl(ed_tm), func=AF.Exp), reads=["ded"], writes=["ded"])
                P.op("act", lambda e: e.activation(out=fl(egl), in_=ps[0][:, 128:256], func=AF.Exp), reads=[PS[0]], writes=["degl"])
                P.op("act", lambda e: e.activation(out=fl(bg_tm), in_=fl(gc_tm), func=AF.Exp), reads=["dgc"], writes=["dbg"])
                P.op("dve", lambda e: e.tensor_tensor(out=fl(bg_tm), in0=fl(bg_tm), in1=fl(beta_tm), op=ALU.mult),
                     reads=["dbg", "dbeta"], writes=["dbg"])
                P.barrier()
                for h in range(4):
                    with ExitStack() as ph1:
                        def sb1(name, shape, dt):
                            return ph1.enter_context(nc.sbuf_tensor(_u + ("s_" + name), list(shape), dt))
                        qT, kT, vT, zsT = mT[:, 0, :], mT[:, 1, :], mT[:, 2, :], mT[:, 3, :]
                        with ExitStack() as ph2:
                            xpads = [mT[:, 3:6, :].rearrange("p a b -> p (a b)").bitcast(F32)[:, 0:S + 3],
                                     ph2.enter_context(nc.sbuf_tensor(_u + "s_dxpad1", [128, S + 3], F32))[:, :],
                                     ph2.enter_context(nc.sbuf_tensor(_u + "s_dxpad2", [128, S + 3], F32))[:, :]]
                            accs = [ph2.enter_context(nc.sbuf_tensor(_u + "s_dacc0", [128, S], F32))[:, :],
                                    ph2.enter_context(nc.sbuf_tensor(_u + "s_dacc1", [128, S], F32))[:, :],
                                    mT[:, 6:8, :].rearrange("p a b -> p (a b)").bitcast(F32)]

                            def d1_stream(xi, c_base, dstT):
                                xpad, acc = xpads[xi], accs[xi]
                                kacc = ("dacc", xi)
                                pbank = [1 + 2 * xi, 2 + 2 * xi]
                                pn = 0 if xi == 0 else 7
                                P.op("dve", lambda e: e.memset(xpad[:, 0:3], 0.0), writes=[("dxpad0", xi)])
                                wb, wkey = load_w(w_in_d[l, :, c_base + h * 128:c_base + (h + 1) * 128])
                                yield
                                for tc in range(NT):
                                    pi = pbank[tc % 2]
                                    for k in range(8):
                                        P.op("pe", lambda e: e.matmul(ps[pi][:, :], lhsT=wb[:, k, :], rhs=hT[:, k, tc * TC:(tc + 1) * TC],
                                                                      start=(k == 0), stop=(k == 7)),
                                             reads=[wkey, ("hT", k)], writes=[PS[pi]], inc=(k == 7))
                                    yield
                                    P.op("act", lambda e: e.activation(out=xpad[:, 3 + tc * TC:3 + (tc + 1) * TC], in_=ps[pi][:, :], func=AF.Copy),
                                         reads=[PS[pi]], writes=[("dxpad", xi, tc)])
                                    yield
                                allx = [("dxpad", xi, tc) for tc in range(NT)] + [("dxpad0", xi)]
                                cw = dncw[:, l, xi * 4 + h, :]
                                P.op("dve", lambda e: e.tensor_scalar(out=acc, in0=xpad[:, 3:3 + S], scalar1=cw[:, 3:4],
                                                                      scalar2=None, op0=ALU.mult),
                                     reads=allx + ["dncw"], writes=[kacc])
                                yield
                                for j in range(3):
                                    P.op("dve", lambda e: e.scalar_tensor_tensor(out=acc, in0=xpad[:, j:j + S],
                                                                                 scalar=cw[:, j:j + 1], in1=acc,
                                                                                 op0=ALU.mult, op1=ALU.add),
                                         reads=allx + ["dncw", kacc], writes=[kacc])
                                    yield
                                P.op("act", lambda e: e.activation(out=acc, in_=acc, func=AF.Silu), reads=[kacc], writes=[kacc])
                                yield
                                if xi == 2:
                                    P.op("dve", lambda e: e.tensor_copy(out=vT, in_=acc), reads=[kacc], writes=["dvT"])
                                    yield
                                    return
                                sqi = sqt[:, xi, :]
                                ksq = ("sqt", xi)
                                for tc in range(NT):
                                    tsl = slice(tc * TC, (tc + 1) * TC)
                                    P.op("act", lambda e: e.activation(out=sqi, in_=acc[:, tsl], func=AF.Square),
                                         reads=[kacc], writes=[ksq])
                                    yield
                                    P.op("pe", lambda e: e.matmul(ps[pn][:, :], lhsT=cst["ones"][:, :], rhs=sqi, start=True, stop=True),
                                         reads=[ksq, "k_ones"], writes=[PS[pn]])
                                    yield
                                    P.op("act", lambda e: e.activation(out=rstd[:, :], in_=ps[pn][:, :], func=AF.Ln,
                                                                       bias=rtrp[:, 3:4], scale=1.0),
                                         reads=[PS[pn], "rtrp"], writes=["rstd"])
                                    P.op("act", lambda e: e.activation(out=rstd[:, :], in_=rstd[:, :], func=AF.Exp, scale=-0.5),
                                         reads=["rstd"], writes=["rstd"])
                                    sc = float(128.0 ** -0.5) if xi == 0 else 1.0
                                    P.op("dve", lambda e: e.scalar_tensor_tensor(out=dstT[:, tsl], in0=acc[:, tsl], scalar=sc,
                                                                                 in1=rstd[:, :], op0=ALU.mult, op1=ALU.mult),
                                         reads=[kacc, "rstd"], writes=["dqT" if xi == 0 else "dkT"])
                                    yield

                            run_streams([d1_stream(0, C_DNQ, qT), d1_stream(1, C_DNK, kT), d1_stream(2, C_DNV, vT)])
                            P.barrier()
                        proj_fm(l, C_DNZ + h * 128, 128, lambda tc, pap, pkey: P.op(
                            "act", lambda e: e.activation(out=zsT[:, tc * TC:(tc + 1) * TC], in_=pap, func=AF.Silu),
                            reads=[pkey], writes=["dz"]))
                        NB = 8
                        B3 = [64, NB, 64]
                        gb = sb1("dgb", [64, NB, 64], F32)
                        dd = sb1("ddd", [64, NB, 64], F32)
                        LT = sb1("dLT", [64, NB, 64], F32)
                        egcb = sb1("degcb", [128, NB * 64], F32)
                        bs = sb1("dbs", [64, NB, 64], BF16)
                        Pm = sb1("dPm", [64, NB, 64], BF16)
                        PTm = sb1("dPTm", [64, NB, 64], BF16)
                        Tt = [sb1(f"dTt{i}", [64, NB, 64], BF16) for i in range(2)]
                        kbg = mT[0:64, 7, 0:1024].rearrange("p (a b) -> p a b", b=128)
                        vb = mT[0:64, 7, 1024:2048].rearrange("p (a b) -> p a b", b=128)
                        aT = sb1("daT", [64, NB, 64], BF16)
                        aTt = sb1("daTt", [64, NB, 64], BF16)
                        qg = sb1("dqg", [128, NB * 64], BF16)
                        u_sb = sb1("du", [64, NB, 128], BF16)
                        wT = sb1("dwT", [128, NB, 64], BF16)
                        kd = sb1("dkd", [64, NB, 128], BF16)
                        vnew = [sb1(f"dvnew{i}", [64, 128], BF16) for i in range(2)]
                        Sf = sb1("dSf", [128, 128], F32)
                        Sb = sb1("dSb", [128, 128], BF16)
                        nsq = sb1("dnsq", [128, TC], BF16)
                        ntmp = sb1("dntmp", [128, TC], F32)
                        identb = cst["ident"]
                        id64 = identb[0:64, 0:64]
                        fl3 = lambda t: t[:, :, :].rearrange("p a b -> p (a b)")
                        p3 = lambda i, w=64: ps[i][H, 0:NB * w].rearrange("p (a b) -> p a b", b=w)
                        bcast = lambda ap2: ap2.unsqueeze(1).broadcast_to(B3)

                        def stage_a(bt):
                            n0 = bt * NB
                            bsl = slice(n0 * 64, (n0 + NB) * 64)
                            nsl = slice(n0, n0 + NB)
                            sc_g = g_tm[:, nsl, h:h + 1].broadcast_to(B3)
                            sc_b = beta_tm[:, nsl, h:h + 1].broadcast_to(B3)
                            sc_gc = gc_tm[:, nsl, h:h + 1].broadcast_to(B3)
                            P.op("dve", lambda e: e.tensor_tensor(out=gb[:, :, :], in0=bcast(mincl), in1=sc_g, op=ALU.mult),
                                 reads=["dnc", "dg"], writes=["dgb"])
                            P.op("pe", lambda e: e.matmul(ps[0][:, :], lhsT=onesf[0:64, :], rhs=fl3(gb), start=True, stop=True),
                                 reads=["dnc", "dgb"], writes=[PS[0]])
                            yield
                            P.op("dve", lambda e: e.tensor_tensor(out=gb[:, :, :], in0=bcast(identf[0:64, 0:64]), in1=sc_b, op=ALU.mult),
                                 reads=["dnc", "dbeta"], writes=["dgb"])
                            P.op("pe", lambda e: e.matmul(ps[1][H, :], lhsT=onesf[0:64, 0:64], rhs=fl3(gb), start=True, stop=True),
                                 reads=["dnc", "dgb"], writes=[PS[1]])
                            yield
                            P.op("act", lambda e: e.activation(out=egcb[:, :], in_=ps[0][:, :], func=AF.Exp),
                                 reads=[PS[0]], writes=["degcb"])
                            P.op("dve", lambda e: e.tensor_tensor(out=dd[:, :, :], in0=p3(0), in1=sc_gc, op=ALU.subtract),
                                 reads=[PS[0], "dgc", "degcb"], writes=["ddd"])
                            yield
                            P.op("act", lambda e: e.activation(out=fl3(dd), in_=fl3(dd), func=AF.Exp), reads=["ddd"], writes=["ddd"])
                            P.op("dve", lambda e: e.tensor_tensor(out=bs[:, :, :], in0=p3(1), in1=bcast(mstrict), op=ALU.mult),
                                 reads=[PS[1], "dnc"], writes=["dbs"])
                            yield
                            P.op("dve", lambda e: e.scalar_tensor_tensor(out=dd[:, :, :], in0=dd[:, :, :], scalar=1.0, in1=bcast(mincl),
                                                                         op0=ALU.min, op1=ALU.mult),
                                 reads=["ddd", "dnc"], writes=["ddd"])
                            for i in range(NB):
                                csl = slice((n0 + i) * 64, (n0 + i + 1) * 64)
                                P.op("pe", lambda e: e.matmul(ps[2][H, i * 64:(i + 1) * 64], lhsT=kT[:, csl], rhs=kT[:, csl], start=True, stop=True),
                                     reads=["dkT"], writes=[PS[2]], inc=(i == NB - 1))
                            for i in range(NB):
                                csl = slice((n0 + i) * 64, (n0 + i + 1) * 64)
                                P.op("pe", lambda e: e.matmul(ps[3][H, i * 64:(i + 1) * 64], lhsT=kT[:, csl], rhs=qT[:, csl], start=True, stop=True),
                                     reads=["dkT", "dqT"], writes=[PS[3]], inc=(i == NB - 1))
                            yield
                            P.op("dve", lambda e: e.tensor_tensor(out=LT[:, :, :], in0=p3(2), in1=dd[:, :, :], op=ALU.mult),
                                 reads=[PS[2], "ddd"], writes=["dLT"])
                            P.op("dve", lambda e: e.tensor_tensor(out=aTt[:, :, :], in0=p3(3), in1=dd[:, :, :], op=ALU.mult),
                                 reads=[PS[3], "ddd"], writes=["daTt"])
                            yield
                            P.op("dve", lambda e: e.scalar_tensor_tensor(out=Pm[:, :, :], in0=LT[:, :, :], scalar=-1.0, in1=bs[:, :, :],
                                                                         op0=ALU.mult, op1=ALU.mult),
                                 reads=["dLT", "dbs"], writes=["dPm"])
                            yield
                            ptv = ps[4][H, 0:NB * 32].bitcast(BF16)
                            for i in range(NB):
                                P.op("pe", lambda e: e.transpose(ptv[:, i * 64:(i + 1) * 64], Pm[:, i, :], id64),
                                     reads=["dPm", "k_ident"], writes=[PS[4]], inc=(i == NB - 1))
                            P.op("dve", lambda e: e.tensor_tensor(out=Tt[0][:, :, :], in0=Pm[:, :, :], in1=bcast(id64), op=ALU.add),
                                 reads=["dPm", "k_ident"], writes=[("dTt", 0)])
                            yield
                            P.op("act", lambda e: e.activation(out=fl3(PTm), in_=ptv, func=AF.Copy), reads=[PS[4]], writes=["dPTm"])
                            yield
                            cur = 0
                            for lev in range(1, 6):
                                if lev < 5:
                                    for i in range(NB):
                                        P.op("pe", lambda e: e.matmul(ps[2][H, i * 64:(i + 1) * 64], lhsT=PTm[:, i, :], rhs=Pm[:, i, :],
                                                                      start=True, stop=True),
                                             reads=["dPm", "dPTm"], writes=[PS[2]], inc=(i == NB - 1))
                                for i in range(NB):
                                    P.op("pe", lambda e: e.matmul(ps[3][H, i * 64:(i + 1) * 64], lhsT=Pm[:, i, :], rhs=PTm[:, i, :],
                                                                  start=True, stop=True),
                                         reads=["dPm", "dPTm"], writes=[PS[3]], inc=(i == NB - 1))
                                yield
                                if lev < 5:
                                    P.op("act", lambda e: e.activation(out=fl3(Pm), in_=ps[2][H, :], func=AF.Copy), reads=[PS[2]], writes=["dPm"])
                                P.op("dve", lambda e: e.tensor_copy(out=fl3(PTm), in_=ps[3][H, :]), reads=[PS[3]], writes=["dPTm"])
                                yield
                                for i in range(NB):
                                    P.op("pe", lambda e: e.matmul(ps[4][H, i * 64:(i + 1) * 64], lhsT=PTm[:, i, :], rhs=Tt[cur][:, i, :],
                                                                  start=True, stop=False),
                                         reads=["dPTm", ("dTt", cur)], writes=[PS[4]], inc=False)
                                    P.op("pe", lambda e: e.matmul(ps[4][H, i * 64:(i + 1) * 64], lhsT=id64, rhs=Tt[cur][:, i, :],
                                                                  start=False, stop=True),
                                         reads=["k_ident", ("dTt", cur)], writes=[PS[4]], inc=(i == NB - 1))
                                yield
                                cur = 1 - cur
                                P.op("dve", lambda e: e.tensor_copy(out=fl3(Tt[cur]), in_=ps[4][H, :]), reads=[PS[4]], writes=[("dTt", cur)])
                                yield
                            kv = ps[0][H, :].bitcast(BF16)
                            vv = ps[1][H, :].bitcast(BF16)
                            for i in range(NB):
                                csl = slice((n0 + i) * 64, (n0 + i + 1) * 64)
                                P.op("pe", lambda e: e.transpose(kv[:, i * 128:(i + 1) * 128], kT[:, csl], identb[:, :]),
                                     reads=["dkT", "k_ident"], writes=[PS[0]], inc=(i == NB - 1))
                            for i in range(NB):
                                csl = slice((n0 + i) * 64, (n0 + i + 1) * 64)
                                P.op("pe", lambda e: e.transpose(vv[:, i * 128:(i + 1) * 128], vT[:, csl], identb[:, :]),
                                     reads=["dvT", "k_ident"], writes=[PS[1]], inc=(i == NB - 1))
                            yield
                            B3w = [64, NB, 128]
                            kv3 = kv.rearrange("p (a b) -> p a b", b=128)
                            vv3 = vv.rearrange("p (a b) -> p a b", b=128)
                            P.op("dve", lambda e: e.tensor_tensor(out=kbg[:, :, :], in0=kv3, in1=bg_tm[:, nsl, h:h + 1].broadcast_to(B3w), op=ALU.mult),
                                 reads=[PS[0], "dbg"], writes=["dkbg"])
                            P.op("dve", lambda e: e.tensor_tensor(out=vb[:, :, :], in0=vv3, in1=beta_tm[:, nsl, h:h + 1].broadcast_to(B3w), op=ALU.mult),
                                 reads=[PS[1], "dbeta"], writes=["dvb"])
                            yield
                            for i in range(NB):
                                pb = 2 + i // 4
                                P.op("pe", lambda e: e.matmul(ps[pb][H, (i % 4) * 128:(i % 4 + 1) * 128], lhsT=Tt[cur][:, i, :], rhs=vb[:, i, :],
                                                              start=True, stop=True),
                                     reads=[("dTt", cur), "dvb"], writes=[PS[pb]], inc=(i % 4 == 3))
                            for i in range(NB):
                                P.op("pe", lambda e: e.matmul(ps[4][:, i * 64:(i + 1) * 64], lhsT=kbg[:, i, :], rhs=Tt[cur][:, i, :],
                                                              start=True, stop=True),
                                     reads=[("dTt", cur), "dkbg"], writes=[PS[4]], inc=(i == NB - 1))
                            yield "OUT"
                            P.op("dve", lambda e: e.tensor_tensor(out=kd[:, :, :], in0=kv3, in1=ed_tm[:, nsl, h:h + 1].broadcast_to(B3w), op=ALU.mult),
                                 reads=[PS[0], "ded"], writes=["dkd"])
                            P.op("dve", lambda e: e.tensor_tensor(out=qg[:, :], in0=qT[:, bsl], in1=egcb[:, :], op=ALU.mult),
                                 reads=["dqT", "degcb"], writes=["dqg"])
                            P.op("pool", lambda e: e.tensor_copy(out=aT[:, :, :], in_=aTt[:, :, :]), reads=["daTt"], writes=["daT"])
                            for hb in range(2):
                                P.op("act", lambda e: e.activation(out=u_sb[:, hb * 4:(hb + 1) * 4, :].rearrange("p a b -> p (a b)"),
                                                                   in_=ps[2 + hb][H, :], func=AF.Copy),
                                     reads=[PS[2 + hb]], writes=["du"])
                            P.op("dve", lambda e: e.tensor_copy(out=wT[:, :, :].rearrange("p a b -> p (a b)"), in_=ps[4][:, :]),
                                 reads=[PS[4]], writes=["dwT"])
                            yield

                        def stage_b(bt):
                            n0 = bt * NB
                            bsl = slice(n0 * 64, (n0 + NB) * 64)
                            for i in range(NB):
                                n = n0 + i
                                vn = vnew[n % 2]
                                kvn = ("dvnew", n % 2)
                                if n == 0:
                                    P.op("dve", lambda e: e.tensor_copy(out=vn[:, :], in_=u_sb[:, i, :]), reads=["du"], writes=[kvn])
                                else:
                                    P.op("pe", lambda e: e.matmul(ps[5][H, 0:128], lhsT=wT[:, i, :], rhs=Sb[:, :], start=True, stop=True),
                                         reads=["dwT", "dSb"], writes=[PS[5]])
                                    yield
                                    P.op("dve", lambda e: e.tensor_tensor(out=vn[:, :], in0=u_sb[:, i, :], in1=ps[5][H, 0:128], op=ALU.subtract),
                                         reads=["du", PS[5]], writes=[kvn])
                                yield
                                osl = slice(i * 64, (i + 1) * 64)
                                if n < 31:
                                    P.op("pe", lambda e: e.matmul(ps[6][:, 0:128], lhsT=kd[:, i, :], rhs=vn[:, :], start=True, stop=True),
                                         reads=["dkd", kvn], writes=[PS[6]])
                                if n > 0:
                                    P.op("pe", lambda e: e.matmul(ps[7][:, osl], lhsT=Sb[:, :], rhs=qg[:, osl], start=True, stop=False),
                                         reads=["dSb", "dqg"], writes=[PS[7]], inc=False)
                                P.op("pe", lambda e: e.matmul(ps[7][:, osl], lhsT=vn[:, :], rhs=aT[:, i, :], start=(n == 0), stop=True),
                                     reads=[kvn, "daT"], writes=[PS[7]])
                                yield
                                if n < 31:
                                    if n == 0:
                                        P.op("dve", lambda e: e.tensor_copy(out=Sf[:, :], in_=ps[6][:, 0:128]), reads=[PS[6]], writes=["dSf"])
                                    else:
                                        P.op("dve", lambda e: e.scalar_tensor_tensor(out=Sf[:, :], in0=Sf[:, :], scalar=egl[:, n, h:h + 1],
                                                                                     in1=ps[6][:, 0:128], op0=ALU.mult, op1=ALU.add),
                                             reads=[PS[6], "dSf", "degl"], writes=["dSf"])
                                    yield
                                    P.op("act", lambda e: e.activation(out=Sb[:, :], in_=Sf[:, :], func=AF.Copy), reads=["dSf"], writes=["dSb"])
                                    yield
                            P.op("act", lambda e: e.activation(out=nsq[:, :], in_=ps[7][:, :], func=AF.Square),
                                 reads=[PS[7]], writes=["dnsq"])
                            yield
                            P.op("pe", lambda e: e.matmul(ps[5][:, :], lhsT=cst["ones"][:, :], rhs=nsq[:, :], start=True, stop=True),
                                 reads=["k_ones", "dnsq"], writes=[PS[5]])
                            yield
                            P.op("act", lambda e: e.activation(out=rstd[:, :], in_=ps[5][:, :], func=AF.Ln, bias=rtrp[:, 3:4],
                                                               scale=float(1.0 / 128.0)), reads=[PS[5], "rtrp"], writes=["rstd"])
                            P.op("act", lambda e: e.activation(out=rstd[:, :], in_=rstd[:, :], func=AF.Exp, scale=-0.5),
                                 reads=["rstd"], writes=["rstd"])
                            yield
                            P.op("dve", lambda e: e.scalar_tensor_tensor(out=ntmp[:, :], in0=ps[7][:, :], scalar=dng[:, l:l + 1],
                                                                         in1=rstd[:, :], op0=ALU.mult, op1=ALU.mult),
                                 reads=[PS[7], "dng", "rstd"], writes=["dntmp"])
                            P.op("dve", lambda e: e.tensor_tensor(out=obT[:, h, bsl], in0=ntmp[:, :], in1=zsT[:, bsl], op=ALU.mult),
                                 reads=["dntmp", "dz"], writes=[("obT", h)])
                            yield

                        import os as _os2
                        _stop = int(_os2.environ.get("DN_STOP", "-1"))
                        if _stop >= 0:
                            g_ = stage_a(0)
                            for _ in range(_stop):
                                next(g_)
                        else:
                            def drive(b_gen, a_gen):
                                a_wait, a_done, b_done = False, a_gen is None, b_gen is None
                                while not (b_done and (a_done or a_wait)):
                                    if not b_done:
                                        try:
                                            next(b_gen)
                                        except StopIteration:
                                            b_done = True
                                    if not a_done and not a_wait:
                                        try:
                                            if next(a_gen) == "OUT":
                                                a_wait = True
                                        except StopIteration:
                                            a_done = True
                                if not a_done:
                                    for _ in a_gen:
                                        pass

                            drive(None, stage_a(0))
                            for bt in range(4):
                                drive(stage_b(bt), stage_a(bt + 1) if bt + 1 < 4 else None)
                        P.barrier()

        def dump(nm, tile, nchunks, keyname, nokey=False):
            if nokey:
                P.barrier()
            with nc.sbuf_tensor(_u + "s_dbgf_" + nm, [128, S], F32) as dbgf:
                for k in range(nchunks):
                    P.op("dve", lambda e: e.tensor_copy(out=dbgf[:, :], in_=tile[:, k, :]),
                         reads=[(keyname, k)], writes=["dbgf"])
                    P.dma(dbg_d[nm][k * 128:(k + 1) * 128, :], dbgf[:, :], reads=["dbgf"], writes=["dbg_" + nm], stream="o")
                P.barrier()

        for l in range(n_layers):
            norm_to(lambda k, tc: (hT[:, k, tc * TC:(tc + 1) * TC], ("hT", k)), normg[:, l, :], l)
            if debug and l == 0:
                with nc.sbuf_tensor(_u + "dbgf", [128, S], F32) as dbgf:
                    for k in range(8):
                        P.op("dve", lambda e, k=k: e.tensor_copy(out=dbgf[:, :], in_=hT[:, k, :]),
                             reads=[("hT", k)], writes=["dbgf"])
                        P.dma(dbg_d["hT"][k * 128:(k + 1) * 128, :], dbgf[:, :], reads=["dbgf"], writes=["dbg_hT"],
                              stream="o")
                    P.barrier()

            def mem_attention(l):
                ph = ExitStack()
                def sbp(name, shape, dt):
                    return ph.enter_context(nc.sbuf_tensor(_u + "s_" + name, list(shape), dt))
                memT = sbp("memT", [128, 8, MEM_LEN], F32)
                memn = sbp("memn", [128, 8, MEM_LEN], BF16)
                kmT = sbp("kmT", [128, 2, MEM_LEN], BF16)
                vm = sbp("vm", [128, 2, 256], BF16)
                qmT = sbp("qmT", [128, 2, S], BF16)
                pT = [sbp(f"pT{i}", [128, TC], BF16) for i in range(2)]
                rden = sbp("rden", [128, TC], F32)
                for k in range(8):
                    P.dma(memT[:, k, :], memT_d[k * 128:(k + 1) * 128, :], writes=[("memT", k)], stream="x")
                P.op("dve", lambda e: e.tensor_scalar(out=g32[:, :], in0=memg[:, l, :], scalar1=float(math.sqrt(D)),
                                                      scalar2=None, op0=ALU.mult), reads=["memg"], writes=["g32"])
                rms_stats(memT, [("memT", k) for k in range(8)], 8, MEM_LEN, 0, 0, sqt, "sqt", rstd, "rstd", None)
                for k in range(8):
                    P.op("dve", lambda e, k=k: e.scalar_tensor_tensor(
                        out=memn[:, k, :], in0=memT[:, k, :], scalar=g32[:, k:k + 1], in1=rstd[:, 0:MEM_LEN],
                        op0=ALU.mult, op1=ALU.mult), reads=[("memT", k), "g32", "rstd"], writes=[("memn", k)])
                for ec in range(2):
                    wb, wkey = load_w(w_kv_d[l, :, ec * 128:(ec + 1) * 128])
                    for k in range(8):
                        P.op("pe", lambda e, k=k, wb=wb: e.matmul(ps[1][:, 0:MEM_LEN], lhsT=wb[:, k, :], rhs=memn[:, k, :],
                                                                  start=(k == 0), stop=(k == 7)),
                             reads=[wkey, ("memn", k)], writes=[PS[1]], inc=(k == 7))
                    P.op("dve", lambda e, ec=ec: e.tensor_copy(out=kmT[:, ec, :], in_=ps[1][:, 0:MEM_LEN]),
                         reads=[PS[1]], writes=[("kmT", ec)])
                for vc in range(2):
                    wb, wkey = load_w(w_kv_d[l, :, 256 + vc * 128:256 + (vc + 1) * 128])
                    for mt in range(2):
                        for k in range(8):
                            P.op("pe", lambda e, k=k, wb=wb, mt=mt: e.matmul(
                                ps[2][:, 0:128], lhsT=memn[:, k, mt * 128:(mt + 1) * 128], rhs=wb[:, k, :],
                                start=(k == 0), stop=(k == 7)),
                                reads=[wkey, ("memn", k)], writes=[PS[2]], inc=(k == 7))
                        P.op("dve", lambda e, mt=mt, vc=vc: e.tensor_copy(out=vm[:, mt, vc * 128:(vc + 1) * 128],
                                                                          in_=ps[2][:, 0:128]),
                             reads=[PS[2]], writes=[("vm", mt)])
                for ec in range(2):
                    def ev(tc, pap, pkey, ec=ec):
                        P.op("act", lambda e: e.activation(out=qmT[:, ec, tc * TC:(tc + 1) * TC], in_=pap,
                                                           func=AF.Copy, scale=0.125),
                             reads=[pkey], writes=[("qmT", ec)])
                    proj_fm(l, C_MQ + ec * 128, 128, ev)
                for h in range(4):
                    ec, r0 = h // 2, (h % 2) * 64
                    for tc in range(NT):
                        tsl = slice(tc * TC, (tc + 1) * TC)
                        for mb in range(2):
                            pi = 3 + mb
                            P.op("pe", lambda e, mb=mb, pi=pi: e.matmul(
                                ps[pi][:, :], lhsT=kmT[r0:r0 + 64, ec, mb * 128:(mb + 1) * 128],
                                rhs=qmT[r0:r0 + 64, ec, tsl], start=True, stop=True),
                                reads=[("kmT", ec), ("qmT", ec)], writes=[PS[pi]])
                            P.op("act", lambda e, mb=mb, pi=pi: e.activation(out=pT[mb][:, :], in_=ps[pi][:, :],
                                                                             func=AF.Exp),
                                 reads=[PS[pi]], writes=[("pT", mb)])
                        for mb in range(2):
                            P.op("pe", lambda e, mb=mb: e.matmul(
                                ps[5][r0:r0 + 64, :], lhsT=vm[:, mb, h * 64:(h + 1) * 64], rhs=pT[mb][:, :],
                                start=(mb == 0), stop=(mb == 1)),
                                reads=[("vm", mb), ("pT", mb)], writes=[PS[5]], inc=(mb == 1))
                        for mb in range(2):
                            P.op("pe", lambda e, mb=mb: e.matmul(
                                ps[6][r0:r0 + 64, :], lhsT=cst["ones"][:, 0:64], rhs=pT[mb][:, :],
                                start=(mb == 0), stop=(mb == 1)),
                                reads=["k_ones", ("pT", mb)], writes=[PS[6]], inc=(mb == 1))
                        P.op("dve", lambda e: e.reciprocal(out=rden[r0:r0 + 64, :], in_=ps[6][r0:r0 + 64, :]),
                             reads=[PS[6]], writes=["rden"])
                        P.op("dve", lambda e, tsl=tsl: e.tensor_tensor(out=obT[r0:r0 + 64, ec, tsl], in0=ps[5][r0:r0 + 64, :],
                                                                      in1=rden[r0:r0 + 64, :], op=ALU.mult),
                             reads=[PS[5], "rden"], writes=[("obT", ec)])
                P.barrier()
                ph.close()

            def merge(br, nwc, first):
                with ExitStack() as ph:
                    sg = [ph.enter_context(nc.sbuf_tensor(_u + f"sg{i}", [128, TC], F32)) for i in range(2)]
                    tmp = [ph.enter_context(nc.sbuf_tensor(_u + f"mtmp{i}", [128, TC], F32)) for i in range(2)]
                    it = 0
                    for dc in range(8):
                        wg, wgk = load_w(w_in_d[l, :, C_G + br * D + dc * 128:C_G + br * D + (dc + 1) * 128])
                        wr, wrk = load_w(w_br_d[br][l, :, dc * 128:(dc + 1) * 128], rows=nwc)
                        for tc in range(NT):
                            tsl = slice(tc * TC, (tc + 1) * TC)
                            b = it % 2
                            it += 1
                            pg, pp = 1 + b, 3 + b
                            for k in range(8):
                                P.op("pe", lambda e, k=k, pg=pg, tsl=tsl, wg=wg: e.matmul(
                                    ps[pg][:, :], lhsT=wg[:, k, :], rhs=hT[:, k, tsl], start=(k == 0), stop=(k == 7)),
                                    reads=[wgk, ("hT", k)], writes=[PS[pg]], inc=(k == 7))
                            for k in range(nwc):
                                P.op("pe", lambda e, k=k, pp=pp, tsl=tsl, wr=wr: e.matmul(
                                    ps[pp][:, :], lhsT=wr[:, k, :], rhs=obT[:, k, tsl], start=(k == 0), stop=(k == nwc - 1)),
                                    reads=[wrk, ("obT", k)], writes=[PS[pp]], inc=(k == nwc - 1))
                            P.op("act", lambda e, b=b, pg=pg, dc=dc: e.activation(
                                out=sg[b][:, :], in_=ps[pg][:, :], func=AF.Sigmoid,
                                bias=bgate[:, l, br * 8 + dc:br * 8 + dc + 1], scale=1.0),
                                reads=[PS[pg], "bgate"], writes=[("sg", b)])
                            if first:
                                P.op("dve", lambda e, b=b, pp=pp, dc=dc, tsl=tsl: e.tensor_tensor(
                                    out=mT[:, dc, tsl], in0=ps[pp][:, :], in1=sg[b][:, :], op=ALU.mult),
                                    reads=[PS[pp], ("sg", b)], writes=[("mT", dc, tc)])
                            else:
                                P.op("dve", lambda e, b=b, pp=pp: e.tensor_tensor(
                                    out=tmp[b][:, :], in0=ps[pp][:, :], in1=sg[b][:, :], op=ALU.mult),
                                    reads=[PS[pp], ("sg", b)], writes=[("mtmp", b)])
                                P.op("dve", lambda e, b=b, dc=dc, tsl=tsl: e.tensor_tensor(
                                    out=mT[:, dc, tsl], in0=mT[:, dc, tsl], in1=tmp[b][:, :], op=ALU.add),
                                    reads=[("mtmp", b), ("mT", dc, tc)], writes=[("mT", dc, tc)])
                    P.barrier()

            if 1 in branches:
                deltanet(l)
                if debug and l == 0:
                    dump('odn', obT, 4, 'obT')
                merge(1, 4, True)
            mem_attention(l)
            if debug and l == 0:
                dump('omem', obT, 2, 'obT')
            merge(3, 2, 1 not in branches)
            if 0 in branches:
                sb_attention(l)
                if debug and l == 0:
                    dump('osb', obT, 4, 'obT')
                merge(0, 4, False)
            if 2 in branches:
                retention(l)
                if debug and l == 0:
                    dump('ort', obT, 4, 'obT')
                merge(2, 4, False)

            for ec in range(8):
                wo, wok = load_w(w_out_d[l, :, ec * 128:(ec + 1) * 128])
                for tc in range(NT):
                    tsl = slice(tc * TC, (tc + 1) * TC)
                    pi = 1 + tc % 2
                    for dc in range(8):
                        P.op("pe", lambda e, dc=dc, pi=pi, tsl=tsl, wo=wo: e.matmul(
                            ps[pi][:, :], lhsT=wo[:, dc, :], rhs=mT[:, dc, tsl], start=(dc == 0), stop=(dc == 7)),
                            reads=[wok, ("mT", dc, tc)], writes=[PS[pi]], inc=(dc == 7))
                    P.op("dve", lambda e, ec=ec, pi=pi, tsl=tsl: e.tensor_tensor(
                        out=xT[:, ec, tsl], in0=xT[:, ec, tsl], in1=ps[pi][:, :], op=ALU.add),
                        reads=[PS[pi], ("xT", ec)], writes=[("xT", ec)])
            P.barrier()

        with ExitStack() as ph:
            ot = [ph.enter_context(nc.sbuf_tensor(_u + f"ot{i}", [128, TC], F32)) for i in range(2)]
            cnt = [0]

            def dst(k, tc):
                b = cnt[0] % 2
                cnt[0] += 1
                return ot[b][:, :], ("ot", b)
            P.op("dve", lambda e: e.tensor_scalar(out=g32[:, :], in0=fing[:, :], scalar1=float(math.sqrt(D)), scalar2=None,
                                                  op0=ALU.mult), reads=["fing"], writes=["g32"])
            for tc in range(NT):
                rms_stats(xT, [("xT", k) for k in range(8)], 8, TC, tc * TC, 0, sqt, "sqt", rstd, "rstd", None)
                for k in range(8):
                    ap, key = dst(k, tc)
                    P.op("dve", lambda e, k=k, tc=tc, ap=ap: e.scalar_tensor_tensor(
                        out=ap, in0=xT[:, k, tc * TC:(tc + 1) * TC], scalar=g32[:, k:k + 1], in1=rstd[:, :],
                        op0=ALU.mult, op1=ALU.mult),
                        reads=[("xT", k), "g32", "rstd"], writes=[key])
                    P.dma(outT_d[k * 128:(k + 1) * 128, tc * TC:(tc + 1) * TC], ap, reads=[key], writes=["outT"],
                          stream="o")
            P.finish(["outT", "dbg_hT", "dbg_omem", "dbg_osb", "dbg_odn", "dbg_ort"])
            P.barrier(full=True)
        P.emit()
        print("instructions recorded:", P.nins, {n: len(P.q[n]) for n in P.names})
    return nc


_NC_CACHE = {}


def _prep_inputs(inputs, b):
    f = np.float32
    m = {}
    m["xT"] = np.ascontiguousarray(inputs["x"][b].T.astype(f))
    m["memT"] = np.ascontiguousarray(inputs["mem"][b].T.astype(f))
    m["w_in"] = np.ascontiguousarray(inputs["w_in"], dtype=f)
    m["w_mem_kv"] = np.ascontiguousarray(inputs["w_mem_kv"], dtype=f)
    for n in ["w_br_sb", "w_br_dn", "w_br_ret", "w_br_mem", "w_out"]:
        m[n] = np.ascontiguousarray(inputs[n], dtype=f)
    m["norm_g"] = np.ascontiguousarray(inputs["norm_g"].reshape(DEPTH, 8, 128).transpose(0, 2, 1), dtype=f)
    m["mem_norm_g"] = np.ascontiguousarray(inputs["mem_norm_g"].reshape(DEPTH, 8, 128).transpose(0, 2, 1), dtype=f)
    m["b_gate"] = np.ascontiguousarray(inputs["b_gate"].reshape(DEPTH, 32, 128).transpose(0, 2, 1), dtype=f)
    m["final_norm_g"] = np.ascontiguousarray(inputs["final_norm_g"].reshape(8, 128).T, dtype=f)
    for n, v in _consts().items():
        m["c_" + n] = v
    for n, v in _rt_consts().items():
        m["c_" + n] = v
    m["c_dn"] = _dn_consts()
    m["dn_conv_w"] = np.ascontiguousarray(inputs["dn_conv_w"].reshape(DEPTH, 4, 12, 128).transpose(0, 3, 2, 1), dtype=f)
    m["dn_norm_g"] = np.ascontiguousarray(inputs["dn_norm_g"].T, dtype=f)
    m["dn_alog"] = np.ascontiguousarray(np.broadcast_to(np.tile(inputs["dn_a_log"], (1, 32))[:, None, :], (DEPTH, 128, 128)), dtype=f)
    m["dn_dtb"] = np.ascontiguousarray(np.broadcast_to(np.tile(inputs["dn_dt_bias"], (1, 32))[:, None, :], (DEPTH, 128, 128)), dtype=f)
    m["pos"] = np.ascontiguousarray(np.broadcast_to(inputs["positions"][b].astype(np.int32)[None, :], (128, S)))
    m["ret_norm_g"] = np.ascontiguousarray(inputs["ret_norm_g"].reshape(DEPTH, 4, 128).transpose(0, 2, 1), dtype=f)
    return m


def kernel(**inputs):
    inputs = {k: np.asarray(v) for k, v in inputs.items()}
    if "nc" not in _NC_CACHE:
        _NC_CACHE["nc"] = build()
    nc = _NC_CACHE["nc"]
    in_maps = [_prep_inputs(inputs, b) for b in range(8)]
    res = run_bass_kernel_spmd(nc, in_maps, core_ids=list(range(8)))
    out = np.stack([np.ascontiguousarray(res.results[b]["outT"].T) for b in range(8)], axis=0)
    return out.astype(np.float32)
```

```python
import math
from contextlib import ExitStack
import numpy as np
import concourse.bass as bass
import concourse.mybir as mybir
from concourse.bass_utils import run_bass_kernel_spmd

F32 = mybir.dt.float32
BF16 = mybir.dt.bfloat16
I32 = mybir.dt.int32
AF = mybir.ActivationFunctionType
ALU = mybir.AluOpType

D = 1024
S = 2048
DEPTH = 2
MEM_LEN = 256
EPS = 1e-6
IN_COLS = 9992
C_SBQ, C_SBK, C_SBV, C_SBZ = 0, 512, 1024, 1536
C_DNQ, C_DNK, C_DNV, C_DNZ, C_DNA, C_DNB = 2048, 2560, 3072, 3584, 4096, 4100
C_RTQ, C_RTK, C_RTV, C_RTZ = 4104, 4360, 4616, 5128
C_MQ = 5640
C_G = 5896
NT = 4
TC = 512


class _Uniq:
    def __init__(self):
        self.n = 0

    def __add__(self, name):
        self.n += 1
        return f"{name}_{self.n}"


class _Rec:
    def __getattr__(self, name):
        return lambda *a, **k: (name, a, k)


_REC = _Rec()


class Prog:
    LIMIT = 30000

    def __init__(self, nc, stack):
        self.nc = nc
        self.stack = stack
        self.names = ["pe", "act", "dve", "pool", "sp"]
        self.q = {n: [] for n in self.names}
        self.cnt = {n: 0 for n in self.names}
        self.sems = {n: [] for n in self.names}
        self.seen = {n: {} for n in self.names}
        self.bufs = {}
        self.dma_sem = {}
        self.dma_cnt = {}
        self.same_sync = True
        self.dma_i = 0
        self.nins = 0

    def _eng_sem(self, eng, g):
        ep = (g - 1) // self.LIMIT
        while len(self.sems[eng]) <= ep:
            s = self.stack.enter_context(self.nc.semaphore(f"s_{eng}_{len(self.sems[eng])}"))
            self.sems[eng].append(s)
        return self.sems[eng][ep], (g - 1) % self.LIMIT + 1

    def _tok_sem(self, tok):
        kind, g = tok
        if kind.startswith("dma:"):
            return self.dma_sem[kind], 16 * g
        return self._eng_sem(kind, g)

    def _need(self, eng, tok):
        kind, g = tok
        if kind == eng:
            if eng in ("pe", "sp") or not self.same_sync:
                return False
        return self.seen[eng].get(kind, 0) < g

    def _collect(self, eng, reads, writes):
        toks = []
        for k in reads:
            b = self.bufs.get(k)
            if b and b[0] is not None:
                toks.append(b[0])
            if b and isinstance(k, tuple) and k[0] == "ps" and eng in ("act", "dve"):
                other = "dve" if eng == "act" else "act"
                if other in b[1]:
                    toks.append((other, b[1][other]))
        for k in writes:
            b = self.bufs.get(k)
            if b:
                if b[0] is not None:
                    toks.append(b[0])
                toks.extend(b[1].items())
        need = {}
        for t in toks:
            if self._need(eng, t):
                need[t[0]] = max(need.get(t[0], 0), t[1])
        return list(need.items())

    def _update(self, tok, reads, writes):
        for k in writes:
            self.bufs[k] = [tok, {}]
        for k in reads:
            b = self.bufs.setdefault(k, [None, {}])
            if k in writes:
                continue
            b[1][tok[0]] = max(b[1].get(tok[0], 0), tok[1])

    def op(self, eng, fn, reads=(), writes=(), inc=True):
        call = fn(_REC)
        fn = lambda e, c=call: getattr(e, c[0])(*c[1], **c[2])
        waits = self._collect(eng, reads, writes)
        for t in waits:
            self.seen[eng][t[0]] = t[1]
        ws = [self._tok_sem(t) for t in waits]
        for w in ws[1:]:
            self.q[eng].append(("wait", w[0], w[1]))
        if inc:
            self.cnt[eng] += 1
            tok = (eng, self.cnt[eng])
            sem, val = self._eng_sem(eng, self.cnt[eng])
            self.q[eng].append(("ins", fn, ws[0] if ws else None, (sem, 1)))
        else:
            tok = (eng, self.cnt[eng] + 1)
            self.q[eng].append(("ins", fn, ws[0] if ws else None, None))
        self._update(tok, reads, writes)
        self.nins += 1
        return tok

    NDS = 16

    def dma(self, out, in_, reads=(), writes=(), stream="d0", queue="act"):
        if not self.SOFT:
            queue = "sp"
        j = self.dma_i % self.NDS
        self.dma_i += 1
        kind = f"dma:{j}"
        if kind not in self.dma_sem:
            self.dma_sem[kind] = self.stack.enter_context(self.nc.semaphore(f"sd_{j}"))
            self.dma_cnt[kind] = 0
        waits = self._collect(queue, reads, writes)
        if self.dma_cnt[kind] > 0 and self.seen[queue].get(kind, 0) < self.dma_cnt[kind]:
            waits = [w for w in waits if w[0] != kind] + [(kind, self.dma_cnt[kind])]
        for t in waits:
            self.seen[queue][t[0]] = t[1]
        for t in waits:
            s, v = self._tok_sem(t)
            self.q[queue].append(("wait", s, v))
        self.dma_cnt[kind] += 1
        tok = (kind, self.dma_cnt[kind])
        self.q[queue].append(("ins", lambda e, o=out, i=in_: e.dma_start(out=o, in_=i), None,
                              (self.dma_sem[kind], 16)))
        self._update(tok, reads, writes)
        self.nins += 1
        return tok

    PERSIST = {"cstf", "normg", "memg", "bgate", "fing", "xT", "hT", "mT", "obT", "wst", "wb", "epsb", "g32", "sqt",
               "rstd", "oneb", "rtsd", "rtrp", "rtcn", "retg", "dncw", "dng", "ps", "outT"}

    def _persistent(self, k):
        h = k[0] if isinstance(k, tuple) else k
        return h in self.PERSIST or (isinstance(h, str) and (h.startswith("k_") or h.startswith("dbg_")))

    SOFT = False

    def barrier(self, full=False):
        full = full or not self.SOFT
        if full:
            toks = [(n, self.cnt[n]) for n in self.names if self.cnt[n] > 0]
            toks += [(k, c) for k, c in self.dma_cnt.items() if c > 0]
            engines = self.names
        else:
            need = {}
            for k, b in self.bufs.items():
                if self._persistent(k):
                    continue
                if b[0] is not None:
                    need[b[0][0]] = max(need.get(b[0][0], 0), b[0][1])
                for kind, c in b[1].items():
                    need[kind] = max(need.get(kind, 0), c)
            toks = list(need.items())
            engines = [n for n in self.names if n != "sp"]
        for eng in engines:
            for t in toks:
                if t[0] == eng and eng in ("pe", "sp"):
                    continue
                if self.seen[eng].get(t[0], 0) < t[1]:
                    self.seen[eng][t[0]] = t[1]
                    s, v = self._tok_sem(t)
                    self.q[eng].append(("wait", s, v))
        if full:
            self.bufs = {}
        else:
            self.bufs = {k: b for k, b in self.bufs.items() if self._persistent(k)}

    def finish(self, final_keys):
        toks = []
        for k in final_keys:
            b = self.bufs.get(k)
            if b and b[0] is not None:
                toks.append(b[0])
        for t in toks:
            s, v = self._tok_sem(t)
            self.q["act" if self.SOFT else "sp"].append(("wait", s, v))

    def simulate(self):
        sem = {}
        pos = {n: 0 for n in self.names}
        def ok(w):
            return w is None or sem.get(id(w[0]), 0) >= w[1]
        progress = True
        while progress:
            progress = False
            for n in self.names:
                q = self.q[n]
                while pos[n] < len(q):
                    ent = q[pos[n]]
                    if ent[0] == "wait":
                        if not ok((ent[1], ent[2])):
                            break
                    else:
                        if not ok(ent[2]):
                            break
                        if ent[3] is not None:
                            sem[id(ent[3][0])] = sem.get(id(ent[3][0]), 0) + ent[3][1]
                    pos[n] += 1
                    progress = True
        stuck = {n: (pos[n], len(self.q[n])) for n in self.names if pos[n] < len(self.q[n])}
        if stuck:
            msg = []
            for n, (p, ln) in stuck.items():
                ent = self.q[n][p]
                w = (ent[1], ent[2]) if ent[0] == "wait" else ent[2]
                msg.append(f"{n}@{p}/{ln} waits {w} have {sem.get(id(w[0]), 0)} kind={ent[0]} tag={ent[4] if len(ent) > 4 else None}")
            raise RuntimeError("DEADLOCK in recorded program: " + "; ".join(msg))

    def emit(self):
        self.simulate()
        nc = self.nc
        with nc.Block() as block:
            def replay(e, name):
                for ent in self.q[name]:
                    if ent[0] == "wait":
                        e.wait_ge(ent[1], ent[2])
                    else:
                        _, fn, w, inc = ent
                        ins = fn(e)
                        if w is not None:
                            ins._wait_ge(w[0], w[1])
                        if inc is not None:
                            ins.then_inc(inc[0], inc[1])

            @block.sync
            def _(e):
                replay(e, "sp")

            @block.scalar
            def _(e):
                replay(e, "act")

            @block.vector
            def _(e):
                replay(e, "dve")

            @block.tensor
            def _(e):
                replay(e, "pe")

            @block.gpsimd
            def _(e):
                replay(e, "pool")


def _consts():
    c = {}
    i = np.arange(128)
    c["ident"] = np.eye(128, dtype=np.float32)
    c["ones"] = np.ones((128, 128), np.float32)
    c["trineg"] = -(i[:, None] >= i[None, :]).astype(np.float32)
    e0 = np.zeros((128, 128), np.float32)
    e0[0, :] = 1.0
    c["e0"] = e0
    c["masksb"] = (i[:, None] < i[None, :]).astype(np.float32)
    return c


def _rt_consts():
    gam = [1.0 - 2.0 ** (-5.0 - h) for h in range(4)]
    i = np.arange(128)
    dt = np.zeros((128, 4, 128), np.float64)
    sd = np.zeros((128, 4), np.float64)
    cd = np.zeros((128, 2, 128), np.float64)
    for h in range(4):
        rel = i[None, :] - i[:, None]
        dt[:, h, :] = np.where(rel >= 0, gam[h] ** np.maximum(rel, 0), 0.0)
        sd[:, h] = gam[h] ** (127 - i)
    for p in range(2):
        for hh in range(2):
            cd[hh * 64:(hh + 1) * 64, p, :] = (gam[2 * p + hh] ** (i + 1.0))[None, :]
    inv = 10000.0 ** (-(np.arange(32, dtype=np.float32)) / np.float32(32))
    rp = np.zeros((128, 4), np.float32)
    rp[:, 0] = np.tile(inv.astype(np.float32), 4)
    rp[:, 1] = np.where((i % 64) < 32, -1.0, 1.0)
    rp[:, 2] = np.float32(math.pi / 2)
    rp[:, 3] = EPS
    cn = np.eye(128) - 1.0 / 128.0
    return {"rt_dt": dt.astype(np.float32), "rt_sd": sd.astype(np.float32), "rt_cd": cd.astype(np.float32),
            "rt_rp": rp, "rt_cn": cn.astype(np.float32)}


def _dn_consts():
    i = np.arange(64)
    c = np.zeros((128, 4, 128), np.float32)
    c[:, 0, :] = np.eye(128)
    c[:, 1, :] = 1.0
    c[0:64, 2, 0:64] = (i[:, None] <= i[None, :])
    c[0:64, 3, 0:64] = (i[:, None] < i[None, :])
    return c


CONST_NAMES = ["ident", "ones", "trineg", "e0", "masksb"]


def build(n_layers=DEPTH, debug=False, branches=(0, 1, 2, 3), wseq="auto", _wrec=None):
    if wseq == "auto":
        rec = []
        build(n_layers, debug, branches, wseq=None, _wrec=rec)
        wseq = rec
    NPRE = 2
    _u = _Uniq()
    nc = bass.Bass("TRN2", target_bir_lowering=False)
    dr = {}

    def din(name, shape, dt=F32):
        dr[name] = nc.dram_tensor(name, list(shape), dt, kind="ExternalInput").ap()
        return dr[name]

    xT_d = din("xT", [D, S])
    memT_d = din("memT", [D, MEM_LEN])
    w_in_d = din("w_in", [DEPTH, D, IN_COLS])
    w_kv_d = din("w_mem_kv", [DEPTH, D, 512])
    w_br_d = {0: din("w_br_sb", [DEPTH, 512, D]), 1: din("w_br_dn", [DEPTH, 512, D]),
              2: din("w_br_ret", [DEPTH, 512, D]), 3: din("w_br_mem", [DEPTH, 256, D])}
    w_out_d = din("w_out", [DEPTH, D, D])
    normg_d = din("norm_g", [DEPTH, 128, 8])
    memg_d = din("mem_norm_g", [DEPTH, 128, 8])
    bgate_d = din("b_gate", [DEPTH, 128, 32])
    fing_d = din("final_norm_g", [128, 8])
    cst_d = {n: din("c_" + n, [128, 128]) for n in CONST_NAMES}
    dnc_d = din("c_dn", [128, 4, 128])
    dncw_d = din("dn_conv_w", [DEPTH, 128, 12, 4])
    dng_d = din("dn_norm_g", [128, DEPTH])
    dnal_d = din("dn_alog", [DEPTH, 128, 128])
    dndt_d = din("dn_dtb", [DEPTH, 128, 128])
    pos_d = din("pos", [128, S], I32)
    retg_d = din("ret_norm_g", [DEPTH, 128, 4])
    rtdt_d = din("c_rt_dt", [128, 4, 128])
    rtsd_d = din("c_rt_sd", [128, 4])
    rtcd_d = din("c_rt_cd", [128, 2, 128])
    rtrp_d = din("c_rt_rp", [128, 4])
    rtcn_d = din("c_rt_cn", [128, 128])
    outT_d = nc.dram_tensor("outT", [D, S], F32, kind="ExternalOutput").ap()
    dbg_d = {}
    if debug:
        dbg_d["hT"] = nc.dram_tensor("dbg_hT", [D, S], F32, kind="ExternalOutput").ap()
        dbg_d["omem"] = nc.dram_tensor("dbg_omem", [256, S], F32, kind="ExternalOutput").ap()
        for nm in ["osb", "odn", "ort"]:
            dbg_d[nm] = nc.dram_tensor("dbg_" + nm, [512, S], F32, kind="ExternalOutput").ap()

    with ExitStack() as st:
        P = Prog(nc, st)

        def sb(name, shape, dt):
            return st.enter_context(nc.sbuf_tensor(_u + "s_" + name, list(shape), dt))

        xT = sb("xT", [128, 8, S], F32)
        hT = sb("hT", [128, 8, S], BF16)
        mT = sb("mT", [128, 8, S], BF16)
        obT = sb("obT", [128, 4, S], BF16)
        wst = [sb(f"wst{i}", [128, 8, 128], F32) for i in range(2)]
        NWB = 4
        wbp = [sb(f"wb{i}", [128, 8, 128], BF16) for i in range(NWB)]
        cst = {n: sb("k_" + n, [128, 128], BF16) for n in CONST_NAMES}
        cstf = sb("cstf", [128, 128], F32)
        normg = sb("normg", [128, DEPTH, 8], F32)
        memg = sb("memg", [128, DEPTH, 8], F32)
        bgate = sb("bgate", [128, DEPTH, 32], F32)
        fing = sb("fing", [128, 8], F32)
        ps = [st.enter_context(nc.psum_tensor(f"ps{i}", [128, 512], F32)) for i in range(8)]
        PS = [("ps", i) for i in range(8)]

        wcount = [0]
        wbcount = [0]

        WT = {"w_in": w_in_d, "w_kv": w_kv_d, "w_out": w_out_d,
              "w_br0": w_br_d[0], "w_br1": w_br_d[1], "w_br2": w_br_d[2], "w_br3": w_br_d[3]}
        wcall = [0]
        wissued = [0]
        wpre = {}

        def _issue_w(rec, eng="pool"):
            tid, l_, c0, n, rows = rec
            src = WT[tid][l_, 0:rows * 128, c0:c0 + n]
            i = wcount[0] % 2
            wcount[0] += 1
            j = wbcount[0] % NWB
            wbcount[0] += 1
            stg, wb = wst[i], wbp[j]
            P.dma(stg[:, 0:rows, 0:n], src.rearrange("(k p) n -> p k n", p=128),
                  writes=[("wst", i)], stream=f"w{i}", queue="sp")
            fn = lambda e, o=wb[:, 0:rows, 0:n], a=stg[:, 0:rows, 0:n]: e.tensor_copy(out=o, in_=a)
            P.op(eng, fn, reads=[("wst", i)], writes=[("wb", j)])
            return wb, ("wb", j)

        def load_w(tid, l_, c0, n=128, rows=8):
            rec = (tid, l_, c0, n, rows)
            idx = wcall[0]
            wcall[0] += 1
            if _wrec is not None:
                _wrec.append(rec)
            if idx in wpre:
                assert wseq[idx] == rec, (idx, wseq[idx], rec)
                return wpre.pop(idx)
            assert wissued[0] == idx, (wissued[0], idx)
            wissued[0] = idx + 1
            return _issue_w(rec)

        def phase_barrier(full=False):
            if wseq is not None:
                while wissued[0] < min(wcall[0] + NPRE, len(wseq)):
                    wpre[wissued[0]] = _issue_w(wseq[wissued[0]])
                    wissued[0] += 1
            P.barrier(full=full)

        for n in CONST_NAMES:
            P.dma(cstf[:, :], cst_d[n][:, :], writes=["cstf"], stream="c")
            P.op("dve", lambda e, o=cst[n][:, :]: e.tensor_copy(out=o, in_=cstf[:, :]),
                 reads=["cstf"], writes=["k_" + n])
        for l in range(DEPTH):
            P.dma(normg[:, l, :], normg_d[l, :, :], writes=["normg"], stream="c")
            P.dma(memg[:, l, :], memg_d[l, :, :], writes=["memg"], stream="c")
            P.dma(bgate[:, l, :], bgate_d[l, :, :], writes=["bgate"], stream="c")
        P.dma(fing[:, :], fing_d[:, :], writes=["fing"], stream="c")
        for k in range(8):
            P.dma(xT[:, k, :], xT_d[k * 128:(k + 1) * 128, :], writes=[("xT", k)], stream="x")

        rtsd = sb("rtsd", [128, 4], F32)
        rtrp = sb("rtrp", [128, 4], F32)
        rtcn = sb("rtcn", [128, 128], BF16)
        retg = sb("retg", [128, DEPTH, 4], F32)
        P.dma(rtsd[:, :], rtsd_d[:, :], writes=["rtsd"], stream="c")
        P.dma(rtrp[:, :], rtrp_d[:, :], writes=["rtrp"], stream="c")
        P.dma(cstf[:, :], rtcn_d[:, :], writes=["cstf"], stream="c")
        P.op("dve", lambda e: e.tensor_copy(out=rtcn[:, :], in_=cstf[:, :]), reads=["cstf"], writes=["rtcn"])
        for l in range(DEPTH):
            P.dma(retg[:, l, :], retg_d[l, :, :], writes=["retg"], stream="c")
        dncw = sb("dncw", [128, DEPTH, 12, 4], F32)
        dng = sb("dng", [128, DEPTH], F32)
        P.dma(dng[:, :], dng_d[:, :], writes=["dng"], stream="c")
        for l in range(DEPTH):
            P.dma(dncw[:, l, :, :], dncw_d[l, :, :, :], writes=["dncw"], stream="c")
        def rms_stats(src_tile, src_keys, nk, ncols, c0, psum_i, sq_tile, sq_key, rstd_tile, rstd_key, extra):
            for k in range(nk):
                P.op("act", lambda e, k=k: e.activation(out=sq_tile[:, k % 2, 0:ncols], in_=src_tile[:, k, c0:c0 + ncols],
                                                        func=AF.Square),
                     reads=[src_keys[k]], writes=[(sq_key, k % 2)])
                P.op("pe", lambda e, k=k: e.matmul(ps[psum_i][:, 0:ncols], lhsT=cst["ones"][:, :],
                                                   rhs=sq_tile[:, k % 2, 0:ncols], start=(k == 0), stop=(k == nk - 1)),
                     reads=[(sq_key, k % 2), "k_ones"], writes=[PS[psum_i]])
            P.op("act", lambda e: e.activation(out=rstd_tile[:, 0:ncols], in_=ps[psum_i][:, 0:ncols], func=AF.Ln,
                                               bias=epsb[:, 0:1], scale=1.0),
                 reads=[PS[psum_i], "epsb"], writes=[rstd_key])
            P.op("act", lambda e: e.activation(out=rstd_tile[:, 0:ncols], in_=rstd_tile[:, 0:ncols], func=AF.Exp,
                                               scale=-0.5),
                 reads=[rstd_key], writes=[rstd_key])

        epsb = sb("epsb", [128, 1], F32)
        P.op("dve", lambda e: e.memset(epsb[:, :], float(D * EPS)), writes=["epsb"])
        sqt = sb("sqt", [128, 2, TC], BF16)
        rstd = sb("rstd", [128, TC], F32)
        g32 = sb("g32", [128, 8], F32)

        def norm_to(dst_fn, gsrc, layer_tag):
            P.op("dve", lambda e: e.tensor_scalar(out=g32[:, :], in0=gsrc, scalar1=float(math.sqrt(D)), scalar2=None,
                                                  op0=ALU.mult), reads=["normg", "fing"], writes=["g32"])
            for tc in range(NT):
                rms_stats(xT, [("xT", k) for k in range(8)], 8, TC, tc * TC, 0, sqt, "sqt", rstd, "rstd", None)
                for k in range(8):
                    ap, key = dst_fn(k, tc)
                    P.op("dve", lambda e, k=k, tc=tc, ap=ap: e.scalar_tensor_tensor(
                        out=ap, in0=xT[:, k, tc * TC:(tc + 1) * TC], scalar=g32[:, k:k + 1], in1=rstd[:, :],
                        op0=ALU.mult, op1=ALU.mult),
                        reads=[("xT", k), "g32", "rstd"], writes=[key])

        def proj_fm(l, col0, ncols, evac, wsrc=None):
            wb, wkey = load_w("w_in", l, col0, n=ncols)
            for tc in range(NT):
                pi = 1 + (tc % 2)
                for k in range(8):
                    P.op("pe", lambda e, k=k, tc=tc, pi=pi: e.matmul(
                        ps[pi][0:ncols, :], lhsT=wb[:, k, 0:ncols], rhs=hT[:, k, tc * TC:(tc + 1) * TC],
                        start=(k == 0), stop=(k == 7)),
                        reads=[wkey, ("hT", k)], writes=[PS[pi]], inc=(k == 7))
                evac(tc, ps[pi][0:ncols, :], PS[pi])

        def proj_tm(l, col0, ncols, evac, ntok=128):
            wb, wkey = load_w("w_in", l, col0, n=ncols)
            for tt in range(S // ntok):
                pi = 1 + (tt % 2)
                for k in range(8):
                    P.op("pe", lambda e: e.matmul(ps[pi][0:ntok, 0:ncols], lhsT=hT[:, k, tt * ntok:(tt + 1) * ntok],
                                                  rhs=wb[:, k, 0:ncols], start=(k == 0), stop=(k == 7)),
                         reads=[wkey, ("hT", k)], writes=[PS[pi]], inc=(k == 7))
                evac(tt, ps[pi][0:ntok, 0:ncols], PS[pi])

        oneb = sb("oneb", [128, 1], F32)
        P.op("dve", lambda e: e.memset(oneb[:, :], 1.0), writes=["oneb"])

        def run_streams(gens, stagger=None):
            gens = list(gens)
            if stagger:
                for g, k in zip(gens, stagger):
                    for _ in range(k):
                        next(g)
            while gens:
                for g in list(gens):
                    try:
                        next(g)
                    except StopIteration:
                        gens.remove(g)

        def sb_attention(l):
            for hp in range(4):
                with ExitStack() as ph:
                    def sbp(name, shape, dt):
                        return ph.enter_context(nc.sbuf_tensor(_u + ("s_" + name), list(shape), dt))
                    qT = sbp("sbq", [128, S], BF16)
                    kTm = [sbp(f"sbk{i}", [128, S], BF16) for i in range(2)]
                    P.op("pool", lambda e: e.memset(kTm[0][64:128, :], 0.0), writes=["sbk"])
                    P.op("pool", lambda e: e.memset(kTm[1][0:64, :], 0.0), writes=["sbk"])
                    vtm = sbp("sbv", [128, 16, 128], BF16)
                    proj_fm(l, C_SBQ + hp * 128, 128, lambda tc, pap, pkey: P.op(
                        "act", lambda e: e.activation(out=qT[:, tc * TC:(tc + 1) * TC], in_=pap, func=AF.Copy, scale=0.125),
                        reads=[pkey], writes=["sbq"]))
                    def evk(tc, pap, pkey):
                        P.op("dve", lambda e: e.tensor_copy(out=kTm[0][0:64, tc * TC:(tc + 1) * TC], in_=pap[0:64, :]),
                             reads=[pkey], writes=["sbk"])
                        P.op("act", lambda e: e.activation(out=kTm[1][64:128, tc * TC:(tc + 1) * TC], in_=pap[64:128, :], func=AF.Copy),
                             reads=[pkey], writes=["sbk"])
                    proj_fm(l, C_SBK + hp * 128, 128, evk)
                    proj_tm(l, C_SBV + hp * 128, 128, lambda tt, pap, pkey: P.op(
                        "dve", lambda e: e.tensor_copy(out=vtm[:, tt, :], in_=pap),
                        reads=[pkey], writes=[("sbv", tt)]))
                    with ExitStack() as ph2:
                        def sbw(name, shape, dt):
                            return ph2.enter_context(nc.sbuf_tensor(_u + ("s_" + name), list(shape), dt))

                        import os as _os

                        def stream(sid, hh, qcs):
                            ez = sbw(f"ez{sid}", [128, TC], F32)
                            spb = sbw(f"spb{sid}", [128, TC], BF16)
                            eg = sbw(f"eg{sid}", [128, TC], BF16)
                            Gb = sbw(f"Gb{sid}", [128, TC], BF16)
                            wv = eg
                            K_wv = ("eg", sid)
                            K_ez, K_spb, K_eg, K_Gb = ("ez", sid), ("spb", sid), ("eg", sid), ("Gb", sid)
                            rs = slice(hh * 64, hh * 64 + 64)
                            pzg, po = 2 * sid, 2 * sid + 1
                            pgg = (4 + 2 * sid) if _os.environ.get('SB_SEPG') else pzg
                            yield
                            for qc in qcs:
                                q0 = qc * TC
                                kmax = qc * 4 + 3
                                pc0 = None
                                P.op("pool", lambda e: e.memset(wv[:, 0:384], 0.0), writes=[K_wv])
                                for kb in range(kmax, -1, -1):
                                    if kmax - kb >= int(_os.environ.get("SB_MAXIT", "99")):
                                        continue
                                    j = kb - qc * 4
                                    c0 = 128 * j if j >= 0 else 0
                                    cs = slice(c0, TC)
                                    tsl = slice(q0 + c0, q0 + TC)
                                    ksl = slice(kb * 128, (kb + 1) * 128)
                                    P.op("pe", lambda e: e.matmul(ps[pzg][:, cs], lhsT=kTm[hh][:, ksl], rhs=qT[:, tsl],
                                                                  start=True, stop=True),
                                         reads=["sbk", "sbq"], writes=[PS[pzg]])
                                    yield
                                    P.op("act", lambda e: e.activation(out=ez[:, cs], in_=ps[pzg][:, cs], func=AF.Exp),
                                         reads=[PS[pzg]], writes=[K_ez])
                                    yield
                                    P.op("act", lambda e: e.activation(out=spb[:, cs], in_=ez[:, cs], func=AF.Ln,
                                                                       bias=oneb[:, 0:1], scale=1.0),
                                         reads=[K_ez, "oneb"], writes=[K_spb])
                                    yield
                                    if j >= 0:
                                        P.op("dve", lambda e: e.tensor_tensor(out=spb[:, c0:c0 + 128], in0=spb[:, c0:c0 + 128],
                                                                              in1=cst["masksb"][:, :], op=ALU.mult),
                                             reads=[K_spb, "k_masksb"], writes=[K_spb])
                                    P.op("pe", lambda e: e.matmul(ps[pgg][:, cs], lhsT=cst["trineg"][:, :], rhs=spb[:, cs],
                                                                  start=True, stop=(pc0 is None)),
                                         reads=["k_trineg", K_spb], writes=[PS[pgg]], inc=(pc0 is None))
                                    if pc0 is not None:
                                        pcs = slice(pc0, TC)
                                        P.op("pe", lambda e: e.matmul(ps[pgg][:, pcs], lhsT=cst["e0"][:, :], rhs=Gb[:, pcs],
                                                                      start=False, stop=True),
                                             reads=["k_e0", K_Gb], writes=[PS[pgg]])
                                    yield
                                    P.op("act", lambda e: e.activation(out=eg[:, cs], in_=ps[pgg][:, cs], func=AF.Exp),
                                         reads=[PS[pgg]], writes=[K_eg])
                                    if kb > 0:
                                        P.op("dve", lambda e: e.tensor_copy(out=Gb[:, cs], in_=ps[pgg][:, cs]),
                                             reads=[PS[pgg], K_eg], writes=[K_Gb])
                                    yield
                                    P.op("dve", lambda e: e.tensor_tensor(out=wv[:, cs], in0=ez[:, cs], in1=eg[:, cs], op=ALU.mult),
                                         reads=[K_ez, K_eg], writes=[K_wv])
                                    if j >= 0:
                                        P.op("dve", lambda e: e.tensor_tensor(out=wv[:, c0:c0 + 128], in0=wv[:, c0:c0 + 128],
                                                                              in1=cst["masksb"][:, :], op=ALU.mult),
                                             reads=[K_wv, "k_masksb"], writes=[K_wv])
                                    yield
                                    P.op("pe", lambda e: e.matmul(ps[po][:, :], lhsT=vtm[:, kb, :], rhs=wv[:, :],
                                                                  start=(kb == kmax), stop=(kb == 0)),
                                         reads=[("sbv", kb), K_wv], writes=[PS[po]])
                                    pc0 = c0
                                    yield
                                P.op("dve", lambda e: e.tensor_copy(out=obT[rs, hp, q0:q0 + TC], in_=ps[po][rs, :]),
                                     reads=[PS[po]], writes=[("obT", hp)])
                                yield

                        _mode = _os.environ.get("SB_MODE", "4")
                        if _mode == "1":
                            run_streams([stream(0, 0, [3, 0, 2, 1])])
                            run_streams([stream(1, 1, [3, 0, 2, 1])])
                        elif _mode == "2":
                            run_streams([stream(0, 0, [3, 0, 2, 1]), stream(1, 1, [3, 0, 2, 1])])
                        else:
                            run_streams([stream(0, 0, [3, 0]), stream(1, 1, [3, 0]), stream(2, 0, [2, 1]), stream(3, 1, [2, 1])],
                                        stagger=[int(x) for x in _os.environ.get('SB_STAG', '0,2,4,6').split(',')])
                        phase_barrier()
                    with ExitStack() as ph3:
                        zs = [ph3.enter_context(nc.sbuf_tensor(_u + f"s_sbzs{i}", [128, TC], BF16)) for i in range(2)]

                        def evz(tc, pap, pkey):
                            b = tc % 2
                            P.op("act", lambda e: e.activation(out=zs[b][:, :], in_=pap, func=AF.Silu),
                                 reads=[pkey], writes=[("sbzs", b)])
                            tsl = slice(tc * TC, (tc + 1) * TC)
                            P.op("dve", lambda e: e.tensor_tensor(out=obT[:, hp, tsl], in0=obT[:, hp, tsl], in1=zs[b][:, :], op=ALU.mult),
                                 reads=[("sbzs", b)], writes=[("obT", hp)])
                        proj_fm(l, C_SBZ + hp * 128, 128, evz)
                        phase_barrier()

        def retention(l):
            TWO_PI = 2.0 * math.pi
            C1 = 6.28125
            C2 = TWO_PI - C1
            with ExitStack() as ph0:
                def sb0(name, shape, dt):
                    return ph0.enter_context(nc.sbuf_tensor(_u + ("s_" + name), list(shape), dt))
                qrT = sb0("rqr", [128, 2, S], BF16)
                krT = sb0("rkr", [128, 2, S], BF16)
                rtdt = sb0("rtdt", [128, 4, 128], F32)
                rtcd = sb0("rtcd", [128, 2, 128], F32)
                P.dma(rtdt[:, :, :], rtdt_d[:, :, :], writes=["rtdt"], stream="c")
                P.dma(rtcd[:, :, :], rtcd_d[:, :, :], writes=["rtcd"], stream="c")
                with ExitStack() as ph1:
                    def sb1(name, shape, dt):
                        return ph1.enter_context(nc.sbuf_tensor(_u + ("s_" + name), list(shape), dt))
                    COS2 = sb1("rcos", [128, S], BF16)
                    SIN2 = sb1("rsin", [128, S], BF16)
                    pint = sb1("rpint", [128, TC], I32)
                    ta = sb1("rta", [128, TC], F32)
                    tk = sb1("rtk", [128, TC], F32)
                    tm = sb1("rtm", [128, 256], F32)[:, :]
                    ki = sb1("rki", [128, 256], I32)[:, :]
                    ta_f, tk_f, pint_f = ta, tk, pint
                    for tc in range(8):
                        tsl = slice(tc * 256, (tc + 1) * 256)
                        ta, tk, pint = ta_f[:, 0:256], tk_f[:, 0:256], pint_f[:, 0:256]
                        P.dma(pint, pos_d[:, tsl], writes=["rpint"], stream="x")
                        P.op("dve", lambda e: e.tensor_scalar(out=ta, in0=pint, scalar1=rtrp[:, 0:1], scalar2=None,
                                                              op0=ALU.mult), reads=["rpint", "rtrp"], writes=["rta"])
                        P.op("dve", lambda e: e.tensor_scalar(out=ki, in0=ta, scalar1=float(1.0 / TWO_PI),
                                                              scalar2=None, op0=ALU.mult), reads=["rta"], writes=["rki"])
                        P.op("dve", lambda e: e.tensor_copy(out=tk, in_=ki), reads=["rki"], writes=["rtk"])
                        P.op("dve", lambda e: e.scalar_tensor_tensor(out=ta, in0=tk, scalar=-C1, in1=ta,
                                                                     op0=ALU.mult, op1=ALU.add),
                             reads=["rtk", "rta"], writes=["rta"])
                        P.op("dve", lambda e: e.scalar_tensor_tensor(out=ta, in0=tk, scalar=-C2, in1=ta,
                                                                     op0=ALU.mult, op1=ALU.add),
                             reads=["rtk", "rta"], writes=["rta"])
                        P.op("dve", lambda e: e.tensor_single_scalar(out=tm, in_=ta, scalar=float(math.pi),
                                                                     op=ALU.is_gt), reads=["rta"], writes=["rtm"])
                        P.op("dve", lambda e: e.scalar_tensor_tensor(out=ta, in0=tm, scalar=-TWO_PI, in1=ta,
                                                                     op0=ALU.mult, op1=ALU.add),
                             reads=["rtm", "rta"], writes=["rta"])
                        P.op("dve", lambda e: e.tensor_single_scalar(out=tm, in_=ta, scalar=float(-math.pi),
                                                                     op=ALU.is_lt), reads=["rta"], writes=["rtm"])
                        P.op("dve", lambda e: e.scalar_tensor_tensor(out=ta, in0=tm, scalar=TWO_PI, in1=ta,
                                                                     op0=ALU.mult, op1=ALU.add),
                             reads=["rtm", "rta"], writes=["rta"])
                        P.op("act", lambda e: e.activation(out=SIN2[:, tsl], in_=ta, func=AF.Sin, scale=rtrp[:, 1:2]),
                             reads=["rta", "rtrp"], writes=["rsin"])
                        P.op("dve", lambda e: e.tensor_single_scalar(out=tm, in_=ta, scalar=float(math.pi / 2),
                                                                     op=ALU.is_gt), reads=["rta"], writes=["rtm"])
                        P.op("dve", lambda e: e.scalar_tensor_tensor(out=ta, in0=tm, scalar=-TWO_PI, in1=ta,
                                                                     op0=ALU.mult, op1=ALU.add),
                             reads=["rtm", "rta"], writes=["rta"])
                        P.op("act", lambda e: e.activation(out=COS2[:, tsl], in_=ta, func=AF.Sin, bias=rtrp[:, 2:3],
                                                           scale=1.0), reads=["rta", "rtrp"], writes=["rcos"])
                    ta, tk, pint = ta_f, tk_f, pint_f
                    for which, (c_base, dstT, scl) in enumerate([(C_RTQ, qrT, 1.0), (C_RTK, krT, 0.125)]):
                        for hp in range(2):
                            wb, wkey = load_w("w_in", l, c_base + hp * 128)
                            jsw = wbcount[0] % NWB
                            wbcount[0] += 1
                            wsw = wbp[jsw]
                            for hh in range(2):
                                o = hh * 64
                                P.op("pool", lambda e: e.tensor_copy(out=wsw[:, :, o:o + 32], in_=wb[:, :, o + 32:o + 64]),
                                     reads=[wkey], writes=[("wb", jsw)])
                                P.op("pool", lambda e: e.tensor_copy(out=wsw[:, :, o + 32:o + 64], in_=wb[:, :, o:o + 32]),
                                     reads=[wkey], writes=[("wb", jsw)])
                            for tc in range(NT):
                                tsl = slice(tc * TC, (tc + 1) * TC)
                                for k in range(8):
                                    P.op("pe", lambda e: e.matmul(ps[1][:, :], lhsT=wb[:, k, :], rhs=hT[:, k, tsl],
                                                                  start=(k == 0), stop=(k == 7)),
                                         reads=[wkey, ("hT", k)], writes=[PS[1]], inc=(k == 7))
                                for k in range(8):
                                    P.op("pe", lambda e: e.matmul(ps[2][:, :], lhsT=wsw[:, k, :], rhs=hT[:, k, tsl],
                                                                  start=(k == 0), stop=(k == 7)),
                                         reads=[("wb", jsw), ("hT", k)], writes=[PS[2]], inc=(k == 7))
                                P.op("dve", lambda e: e.scalar_tensor_tensor(out=ta[:, :], in0=ps[1][:, :], scalar=float(scl),
                                                                             in1=COS2[:, tsl], op0=ALU.mult, op1=ALU.mult),
                                     reads=[PS[1], "rcos"], writes=["rta"])
                                P.op("dve", lambda e: e.scalar_tensor_tensor(out=tk[:, :], in0=ps[2][:, :], scalar=float(scl),
                                                                             in1=SIN2[:, tsl], op0=ALU.mult, op1=ALU.mult),
                                     reads=[PS[2], "rsin"], writes=["rtk"])
                                P.op("dve", lambda e: e.tensor_tensor(out=dstT[:, hp, tsl], in0=ta[:, :], in1=tk[:, :], op=ALU.add),
                                     reads=["rta", "rtk"], writes=[("rq", which, hp)])
                    phase_barrier()
                for hp in range(2):
                    with ExitStack() as ph2:
                        def sb2(name, shape, dt):
                            return ph2.enter_context(nc.sbuf_tensor(_u + ("s_" + name), list(shape), dt))
                        kdtm = sb2("rkd", [128, 16, 128], BF16)
                        vtm = sb2("rv", [128, 16, 128], BF16)
                        zsT = sb2("rz", [128, S], BF16)
                        qc = [sb2(f"rqc{i}", [128, 128], BF16) for i in range(2)]
                        scm = [sb2(f"rscm{i}", [128, 128], BF16) for i in range(2)]
                        Sf = sb2("rSf", [128, 128], F32)
                        Sb = sb2("rSb", [128, 128], BF16)
                        ob = sb2("rob", [128, TC], BF16)
                        sq = sb2("rsq", [128, TC], BF16)
                        rs_t = rstd
                        tt_t = sb2("rtt", [128, TC], F32)
                        for n in range(16):
                            nsl = slice(n * 128, (n + 1) * 128)
                            pk = ps[3][:, 0:64].bitcast(BF16)
                            P.op("pe", lambda e: e.transpose(pk, krT[:, hp, nsl], cst["ident"][:, :]),
                                 reads=[("rq", 1, hp), "k_ident"], writes=[PS[3]])
                            for hh in range(2):
                                cs = slice(hh * 64, (hh + 1) * 64)
                                h = 2 * hp + hh
                                P.op("dve", lambda e: e.tensor_scalar(out=kdtm[:, n, cs], in0=pk[:, cs], scalar1=rtsd[:, h:h + 1],
                                                                      scalar2=None, op0=ALU.mult),
                                     reads=[PS[3], "rtsd"], writes=[("rkd", n)])
                        for hh in range(2):
                            h = 2 * hp + hh
                            rs = slice(hh * 64, (hh + 1) * 64)
                            proj_tm(l, C_RTV + h * 128, 128, lambda tt, pap, pkey: P.op(
                                "dve", lambda e: e.tensor_copy(out=vtm[:, tt, :], in_=pap), reads=[pkey], writes=[("rv", tt)]))
                            proj_fm(l, C_RTZ + h * 128, 128, lambda tc, pap, pkey: P.op(
                                "act", lambda e: e.activation(out=zsT[:, tc * TC:(tc + 1) * TC], in_=pap, func=AF.Silu),
                                reads=[pkey], writes=["rz"]))
                            gch = float((1.0 - 2.0 ** (-5.0 - h)) ** 128)
                            SBANK = [7, 3]

                            def emit_sc(n):
                                nsl = slice(n * 128, (n + 1) * 128)
                                b = n % 2
                                P.op("pe", lambda e: e.matmul(ps[4][:, 0:128], lhsT=krT[rs, hp, nsl], rhs=qrT[rs, hp, nsl],
                                                              start=True, stop=True),
                                     reads=[("rq", 1, hp), ("rq", 0, hp)], writes=[PS[4]])
                                P.op("dve", lambda e: e.tensor_tensor(out=scm[b][:, :], in0=ps[4][:, 0:128], in1=rtdt[:, h, :],
                                                                      op=ALU.mult),
                                     reads=[PS[4], "rtdt"], writes=[("rscm", b)])
                                if n > 0:
                                    P.op("dve", lambda e: e.tensor_tensor(out=qc[b][rs, :], in0=qrT[rs, hp, nsl], in1=rtcd[rs, hp, :],
                                                                          op=ALU.mult),
                                         reads=[("rq", 0, hp), "rtcd"], writes=[("rqc", b)])

                            def emit_smm(n):
                                sbk = SBANK[n % 2]
                                P.op("pe", lambda e: e.matmul(ps[sbk][rs, 0:128], lhsT=kdtm[:, n, rs], rhs=vtm[:, n, :],
                                                              start=True, stop=True),
                                     reads=[("rkd", n), ("rv", n)], writes=[PS[sbk]])

                            emit_sc(0)
                            emit_smm(0)
                            for n in range(16):
                                nsl = slice(n * 128, (n + 1) * 128)
                                b = n % 2
                                csl = slice((n % 4) * 128, (n % 4 + 1) * 128)
                                po = 5 + (n // 4) % 2
                                if n + 1 < 15:
                                    emit_smm(n + 1)
                                P.op("pe", lambda e: e.matmul(ps[po][:, csl], lhsT=vtm[:, n, :], rhs=scm[b][:, :],
                                                              start=True, stop=(n == 0)),
                                     reads=[("rv", n), ("rscm", b)], writes=[PS[po]], inc=(n == 0))
                                if n > 0:
                                    P.op("pe", lambda e: e.matmul(ps[po][:, csl], lhsT=Sb[rs, :], rhs=qc[b][rs, :],
                                                                  start=False, stop=True),
                                         reads=["rSb", ("rqc", b)], writes=[PS[po]])
                                if n < 15:
                                    sbk = SBANK[n % 2]
                                    if n == 0:
                                        P.op("dve", lambda e: e.tensor_copy(out=Sf[rs, :], in_=ps[sbk][rs, 0:128]),
                                             reads=[PS[sbk]], writes=["rSf"])
                                    else:
                                        P.op("dve", lambda e: e.scalar_tensor_tensor(out=Sf[rs, :], in0=Sf[rs, :], scalar=gch,
                                                                                     in1=ps[sbk][rs, 0:128], op0=ALU.mult,
                                                                                     op1=ALU.add),
                                             reads=[PS[sbk], "rSf"], writes=["rSf"])
                                    P.op("act", lambda e: e.activation(out=Sb[rs, :], in_=Sf[rs, :], func=AF.Copy),
                                         reads=["rSf"], writes=["rSb"])
                                if n + 1 < 16:
                                    emit_sc(n + 1)
                                if n % 4 == 3:
                                    tc = n // 4
                                    tsl = slice(tc * TC, (tc + 1) * TC)
                                    P.op("act", lambda e: e.activation(out=ob[:, :], in_=ps[po][:, :], func=AF.Copy),
                                         reads=[PS[po]], writes=["rob"])
                                    P.op("pe", lambda e: e.matmul(ps[1][:, :], lhsT=rtcn[:, :], rhs=ob[:, :], start=True, stop=True),
                                         reads=["rtcn", "rob"], writes=[PS[1]])
                                    P.op("act", lambda e: e.activation(out=sq[:, :], in_=ps[1][:, :], func=AF.Square),
                                         reads=[PS[1]], writes=["rsq"])
                                    P.op("pe", lambda e: e.matmul(ps[2][:, :], lhsT=cst["ones"][:, :], rhs=sq[:, :],
                                                                  start=True, stop=True),
                                         reads=["k_ones", "rsq"], writes=[PS[2]])
                                    P.op("act", lambda e: e.activation(out=rs_t[:, :], in_=ps[2][:, :], func=AF.Ln,
                                                                       bias=rtrp[:, 3:4], scale=float(1.0 / 128.0)),
                                         reads=[PS[2], "rtrp"], writes=["rstd"])
                                    P.op("act", lambda e: e.activation(out=rs_t[:, :], in_=rs_t[:, :], func=AF.Exp, scale=-0.5),
                                         reads=["rstd"], writes=["rstd"])
                                    P.op("dve", lambda e: e.scalar_tensor_tensor(out=tt_t[:, :], in0=ps[1][:, :],
                                                                                 scalar=retg[:, l, h:h + 1], in1=rs_t[:, :],
                                                                                 op0=ALU.mult, op1=ALU.mult),
                                         reads=[PS[1], "retg", "rstd"], writes=["rtt"])
                                    P.op("dve", lambda e: e.tensor_tensor(out=obT[:, h, tsl], in0=tt_t[:, :], in1=zsT[:, tsl],
                                                                          op=ALU.mult),
                                         reads=["rtt", "rz"], writes=[("obT", h)])
                        phase_barrier()

        def deltanet(l):
            phase_barrier(full=True)
            H = slice(0, 64)
            with ExitStack() as ph0:
                def sb0(name, shape, dt):
                    return ph0.enter_context(nc.sbuf_tensor(_u + ("s_" + name), list(shape), dt))
                dnc = sb0("dnc", [128, 4, 128], F32)
                P.dma(dnc[:, :, :], dnc_d[:, :, :], writes=["dnc"], stream="c")
                identf, onesf = dnc[:, 0, :], dnc[:, 1, :]
                mincl, mstrict = dnc[0:64, 2, 0:64], dnc[0:64, 3, 0:64]
                abraw = sb0("dab", [64, 32, 8], F32)
                rep = sb0("drep", [64, 128], F32)
                t1 = sb0("dt1", [64, 32, 4], F32)
                g_tm = sb0("dg", [64, 32, 4], F32)
                beta_tm = sb0("dbeta", [64, 32, 4], F32)
                gc_tm = sb0("dgc", [64, 32, 4], F32)
                egl = sb0("degl", [128, 32, 4], F32)
                bg_tm = sb0("dbg", [64, 32, 4], F32)
                ed_tm = sb0("ded", [64, 32, 4], F32)
                proj_tm(l, C_DNA, 8, lambda tt, pap, pkey: P.op(
                    "dve", lambda e: e.tensor_copy(out=abraw[:, tt, :], in_=pap), reads=[pkey], writes=["dab"]), ntok=64)
                fl = lambda t: t[:, :, :].rearrange("p a b -> p (a b)")
                P.dma(rep[:, :], dndt_d[l, 0:64, :], writes=["drep"], stream="c")
                P.op("dve", lambda e: e.tensor_tensor(out=t1[:, :, :], in0=abraw[:, :, 0:4],
                                                      in1=rep[:, :].rearrange("p (a b) -> p a b", b=4), op=ALU.add),
                     reads=["dab", "drep"], writes=["dt1"])
                P.op("act", lambda e: e.activation(out=t1[:, :, :], in_=t1[:, :, :], func=AF.Exp), reads=["dt1"], writes=["dt1"])
                P.op("act", lambda e: e.activation(out=t1[:, :, :], in_=t1[:, :, :], func=AF.Ln, bias=oneb[0:64, 0:1], scale=1.0),
                     reads=["dt1", "oneb"], writes=["dt1"])
                P.dma(rep[:, :], dnal_d[l, 0:64, :], writes=["drep"], stream="c")
                P.op("act", lambda e: e.activation(out=rep[:, :], in_=rep[:, :], func=AF.Exp), reads=["drep"], writes=["drep"])
                P.op("dve", lambda e: e.scalar_tensor_tensor(out=g_tm[:, :, :], in0=t1[:, :, :], scalar=-1.0,
                                                             in1=rep[:, :].rearrange("p (a b) -> p a b", b=4),
                                                             op0=ALU.mult, op1=ALU.mult),
                     reads=["dt1", "drep"], writes=["dg"])
                P.op("act", lambda e: e.activation(out=beta_tm[:, :, :], in_=abraw[:, :, 4:8], func=AF.Sigmoid),
                     reads=["dab"], writes=["dbeta"])
                P.op("pe", lambda e: e.matmul(ps[0][0:64, 0:128], lhsT=dnc[0:64, 2, 0:64], rhs=fl(g_tm), start=True, stop=True),
                     reads=["dnc", "dg"], writes=[PS[0]])
                P.op("dve", lambda e: e.tensor_copy(out=fl(gc_tm), in_=ps[0][0:64, 0:128]), reads=[PS[0]], writes=["dgc"])
                P.op("pe", lambda e: e.matmul(ps[0][:, 128:256], lhsT=dnc[0:64, 1, :], rhs=fl(g_tm), start=True, stop=True),
                     reads=["dnc", "dg"], writes=[PS[0]])
                P.op("dve", lambda e: e.tensor_tensor(out=fl(ed_tm), in0=ps[0][0:64, 128:256], in1=fl(gc_tm), op=ALU.subtract),
                     reads=[PS[0], "dgc"], writes=["ded"])
                P.op("act", lambda e: e.activation(out=fl(ed_tm), in_=fl(ed_tm), func=AF.Exp), reads=["ded"], writes=["ded"])
                P.op("act", lambda e: e.activation(out=fl(egl), in_=ps[0][:, 128:256], func=AF.Exp), reads=[PS[0]], writes=["degl"])
                P.op("act", lambda e: e.activation(out=fl(bg_tm), in_=fl(gc_tm), func=AF.Exp), reads=["dgc"], writes=["dbg"])
                P.op("dve", lambda e: e.tensor_tensor(out=fl(bg_tm), in0=fl(bg_tm), in1=fl(beta_tm), op=ALU.mult),
                     reads=["dbg", "dbeta"], writes=["dbg"])
                phase_barrier()
                for h in range(4):
                    with ExitStack() as ph1:
                        def sb1(name, shape, dt):
                            return ph1.enter_context(nc.sbuf_tensor(_u + ("s_" + name), list(shape), dt))
                        qT, kT, vT, zsT = mT[:, 0, :], mT[:, 1, :], mT[:, 2, :], mT[:, 3, :]
                        with ExitStack() as ph2:
                            xpads = [mT[:, 3:6, :].rearrange("p a b -> p (a b)").bitcast(F32)[:, 0:S + 3],
                                     ph2.enter_context(nc.sbuf_tensor(_u + "s_dxpad1", [128, S + 3], F32))[:, :],
                                     ph2.enter_context(nc.sbuf_tensor(_u + "s_dxpad2", [128, S + 3], F32))[:, :]]
                            accs = [ph2.enter_context(nc.sbuf_tensor(_u + "s_dacc0", [128, S], F32))[:, :],
                                    ph2.enter_context(nc.sbuf_tensor(_u + "s_dacc1", [128, S], F32))[:, :],
                                    mT[:, 6:8, :].rearrange("p a b -> p (a b)").bitcast(F32)]

                            def d1_stream(xi, c_base, dstT):
                                xpad, acc = xpads[xi], accs[xi]
                                kacc = ("dacc", xi)
                                pbank = [1 + 2 * xi, 2 + 2 * xi]
                                pn = 0 if xi == 0 else 7
                                P.op("dve", lambda e: e.memset(xpad[:, 0:3], 0.0), writes=[("dxpad0", xi)])
                                wb, wkey = load_w("w_in", l, c_base + h * 128)
                                yield
                                for tc in range(NT):
                                    pi = pbank[tc % 2]
                                    for k in range(8):
                                        P.op("pe", lambda e: e.matmul(ps[pi][:, :], lhsT=wb[:, k, :], rhs=hT[:, k, tc * TC:(tc + 1) * TC],
                                                                      start=(k == 0), stop=(k == 7)),
                                             reads=[wkey, ("hT", k)], writes=[PS[pi]], inc=(k == 7))
                                    yield
                                    P.op("act", lambda e: e.activation(out=xpad[:, 3 + tc * TC:3 + (tc + 1) * TC], in_=ps[pi][:, :], func=AF.Copy),
                                         reads=[PS[pi]], writes=[("dxpad", xi, tc)])
                                    yield
                                allx = [("dxpad", xi, tc) for tc in range(NT)] + [("dxpad0", xi)]
                                cw = dncw[:, l, xi * 4 + h, :]
                                P.op("dve", lambda e: e.tensor_scalar(out=acc, in0=xpad[:, 3:3 + S], scalar1=cw[:, 3:4],
                                                                      scalar2=None, op0=ALU.mult),
                                     reads=allx + ["dncw"], writes=[kacc])
                                yield
                                for j in range(3):
                                    P.op("dve", lambda e: e.scalar_tensor_tensor(out=acc, in0=xpad[:, j:j + S],
                                                                                 scalar=cw[:, j:j + 1], in1=acc,
                                                                                 op0=ALU.mult, op1=ALU.add),
                                         reads=allx + ["dncw", kacc], writes=[kacc])
                                    yield
                                P.op("act", lambda e: e.activation(out=acc, in_=acc, func=AF.Silu), reads=[kacc], writes=[kacc])
                                yield
                                if xi == 2:
                                    P.op("dve", lambda e: e.tensor_copy(out=vT, in_=acc), reads=[kacc], writes=["dvT"])
                                    yield
                                    return
                                sqi = sqt[:, xi, :]
                                ksq = ("sqt", xi)
                                for tc in range(NT):
                                    tsl = slice(tc * TC, (tc + 1) * TC)
                                    P.op("act", lambda e: e.activation(out=sqi, in_=acc[:, tsl], func=AF.Square),
                                         reads=[kacc], writes=[ksq])
                                    yield
                                    P.op("pe", lambda e: e.matmul(ps[pn][:, :], lhsT=cst["ones"][:, :], rhs=sqi, start=True, stop=True),
                                         reads=[ksq, "k_ones"], writes=[PS[pn]])
                                    yield
                                    P.op("act", lambda e: e.activation(out=rstd[:, :], in_=ps[pn][:, :], func=AF.Ln,
                                                                       bias=rtrp[:, 3:4], scale=1.0),
                                         reads=[PS[pn], "rtrp"], writes=["rstd"])
                                    P.op("act", lambda e: e.activation(out=rstd[:, :], in_=rstd[:, :], func=AF.Exp, scale=-0.5),
                                         reads=["rstd"], writes=["rstd"])
                                    sc = float(128.0 ** -0.5) if xi == 0 else 1.0
                                    P.op("dve", lambda e: e.scalar_tensor_tensor(out=dstT[:, tsl], in0=acc[:, tsl], scalar=sc,
                                                                                 in1=rstd[:, :], op0=ALU.mult, op1=ALU.mult),
                                         reads=[kacc, "rstd"], writes=["dqT" if xi == 0 else "dkT"])
                                    yield

                            run_streams([d1_stream(0, C_DNQ, qT), d1_stream(1, C_DNK, kT), d1_stream(2, C_DNV, vT)])
                            phase_barrier()
                        proj_fm(l, C_DNZ + h * 128, 128, lambda tc, pap, pkey: P.op(
                            "act", lambda e: e.activation(out=zsT[:, tc * TC:(tc + 1) * TC], in_=pap, func=AF.Silu),
                            reads=[pkey], writes=["dz"]))
                        NB = 8
                        B3 = [64, NB, 64]
                        gb = sb1("dgb", [64, NB, 64], F32)
                        dd = sb1("ddd", [64, NB, 64], F32)
                        LT = sb1("dLT", [64, NB, 64], F32)
                        egcb = sb1("degcb", [128, NB * 64], F32)
                        bs = sb1("dbs", [64, NB, 64], BF16)
                        Pm = sb1("dPm", [64, NB, 64], BF16)
                        PTm = sb1("dPTm", [64, NB, 64], BF16)
                        Tt = [sb1(f"dTt{i}", [64, NB, 64], BF16) for i in range(2)]
                        kbg = mT[0:64, 7, 0:1024].rearrange("p (a b) -> p a b", b=128)
                        vb = mT[0:64, 7, 1024:2048].rearrange("p (a b) -> p a b", b=128)
                        aT = sb1("daT", [64, NB, 64], BF16)
                        aTt = sb1("daTt", [64, NB, 64], BF16)
                        qg = sb1("dqg", [128, NB * 64], BF16)
                        u_sb = sb1("du", [64, NB, 128], BF16)
                        wT = sb1("dwT", [128, NB, 64], BF16)
                        kd = sb1("dkd", [64, NB, 128], BF16)
                        vnew = [sb1(f"dvnew{i}", [64, 128], BF16) for i in range(2)]
                        Sf = sb1("dSf", [128, 128], F32)
                        Sb = sb1("dSb", [128, 128], BF16)
                        nsq = sb1("dnsq", [128, TC], BF16)
                        ntmp = sb1("dntmp", [128, TC], F32)
                        identb = cst["ident"]
                        id64 = identb[0:64, 0:64]
                        fl3 = lambda t: t[:, :, :].rearrange("p a b -> p (a b)")
                        p3 = lambda i, w=64: ps[i][H, 0:NB * w].rearrange("p (a b) -> p a b", b=w)
                        bcast = lambda ap2: ap2.unsqueeze(1).broadcast_to(B3)

                        def stage_a(bt):
                            n0 = bt * NB
                            bsl = slice(n0 * 64, (n0 + NB) * 64)
                            nsl = slice(n0, n0 + NB)
                            sc_g = g_tm[:, nsl, h:h + 1].broadcast_to(B3)
                            sc_b = beta_tm[:, nsl, h:h + 1].broadcast_to(B3)
                            sc_gc = gc_tm[:, nsl, h:h + 1].broadcast_to(B3)
                            P.op("dve", lambda e: e.tensor_tensor(out=gb[:, :, :], in0=bcast(mincl), in1=sc_g, op=ALU.mult),
                                 reads=["dnc", "dg"], writes=["dgb"])
                            P.op("pe", lambda e: e.matmul(ps[0][:, :], lhsT=onesf[0:64, :], rhs=fl3(gb), start=True, stop=True),
                                 reads=["dnc", "dgb"], writes=[PS[0]])
                            yield
                            P.op("dve", lambda e: e.tensor_tensor(out=gb[:, :, :], in0=bcast(identf[0:64, 0:64]), in1=sc_b, op=ALU.mult),
                                 reads=["dnc", "dbeta"], writes=["dgb"])
                            P.op("pe", lambda e: e.matmul(ps[1][H, :], lhsT=onesf[0:64, 0:64], rhs=fl3(gb), start=True, stop=True),
                                 reads=["dnc", "dgb"], writes=[PS[1]])
                            yield
                            P.op("act", lambda e: e.activation(out=egcb[:, :], in_=ps[0][:, :], func=AF.Exp),
                                 reads=[PS[0]], writes=["degcb"])
                            P.op("dve", lambda e: e.tensor_tensor(out=dd[:, :, :], in0=p3(0), in1=sc_gc, op=ALU.subtract),
                                 reads=[PS[0], "dgc", "degcb"], writes=["ddd"])
                            yield
                            P.op("act", lambda e: e.activation(out=fl3(dd), in_=fl3(dd), func=AF.Exp), reads=["ddd"], writes=["ddd"])
                            P.op("dve", lambda e: e.tensor_tensor(out=bs[:, :, :], in0=p3(1), in1=bcast(mstrict), op=ALU.mult),
                                 reads=[PS[1], "dnc"], writes=["dbs"])
                            yield
                            P.op("dve", lambda e: e.scalar_tensor_tensor(out=dd[:, :, :], in0=dd[:, :, :], scalar=1.0, in1=bcast(mincl),
                                                                         op0=ALU.min, op1=ALU.mult),
                                 reads=["ddd", "dnc"], writes=["ddd"])
                            for i in range(NB):
                                csl = slice((n0 + i) * 64, (n0 + i + 1) * 64)
                                P.op("pe", lambda e: e.matmul(ps[2][H, i * 64:(i + 1) * 64], lhsT=kT[:, csl], rhs=kT[:, csl], start=True, stop=True),
                                     reads=["dkT"], writes=[PS[2]], inc=(i == NB - 1))
                            for i in range(NB):
                                csl = slice((n0 + i) * 64, (n0 + i + 1) * 64)
                                P.op("pe", lambda e: e.matmul(ps[3][H, i * 64:(i + 1) * 64], lhsT=kT[:, csl], rhs=qT[:, csl], start=True, stop=True),
                                     reads=["dkT", "dqT"], writes=[PS[3]], inc=(i == NB - 1))
                            yield
                            P.op("dve", lambda e: e.tensor_tensor(out=LT[:, :, :], in0=p3(2), in1=dd[:, :, :], op=ALU.mult),
                                 reads=[PS[2], "ddd"], writes=["dLT"])
                            P.op("dve", lambda e: e.tensor_tensor(out=aTt[:, :, :], in0=p3(3), in1=dd[:, :, :], op=ALU.mult),
                                 reads=[PS[3], "ddd"], writes=["daTt"])
                            yield
                            P.op("dve", lambda e: e.scalar_tensor_tensor(out=Pm[:, :, :], in0=LT[:, :, :], scalar=-1.0, in1=bs[:, :, :],
                                                                         op0=ALU.mult, op1=ALU.mult),
                                 reads=["dLT", "dbs"], writes=["dPm"])
                            yield
                            ptv = ps[4][H, 0:NB * 32].bitcast(BF16)
                            for i in range(NB):
                                P.op("pe", lambda e: e.transpose(ptv[:, i * 64:(i + 1) * 64], Pm[:, i, :], id64),
                                     reads=["dPm", "k_ident"], writes=[PS[4]], inc=(i == NB - 1))
                            P.op("dve", lambda e: e.tensor_tensor(out=Tt[0][:, :, :], in0=Pm[:, :, :], in1=bcast(id64), op=ALU.add),
                                 reads=["dPm", "k_ident"], writes=[("dTt", 0)])
                            yield
                            P.op("act", lambda e: e.activation(out=fl3(PTm), in_=ptv, func=AF.Copy), reads=[PS[4]], writes=["dPTm"])
                            yield
                            cur = 0
                            for lev in range(1, 6):
                                if lev < 5:
                                    for i in range(NB):
                                        P.op("pe", lambda e: e.matmul(ps[2][H, i * 64:(i + 1) * 64], lhsT=PTm[:, i, :], rhs=Pm[:, i, :],
                                                                      start=True, stop=True),
                                             reads=["dPm", "dPTm"], writes=[PS[2]], inc=(i == NB - 1))
                                for i in range(NB):
                                    P.op("pe", lambda e: e.matmul(ps[3][H, i * 64:(i + 1) * 64], lhsT=Pm[:, i, :], rhs=PTm[:, i, :],
                                                                  start=True, stop=True),
                                         reads=["dPm", "dPTm"], writes=[PS[3]], inc=(i == NB - 1))
                                yield
                                if lev < 5:
                                    P.op("act", lambda e: e.activation(out=fl3(Pm), in_=ps[2][H, :], func=AF.Copy), reads=[PS[2]], writes=["dPm"])
                                P.op("dve", lambda e: e.tensor_copy(out=fl3(PTm), in_=ps[3][H, :]), reads=[PS[3]], writes=["dPTm"])
                                yield
                                for i in range(NB):
                                    P.op("pe", lambda e: e.matmul(ps[4][H, i * 64:(i + 1) * 64], lhsT=PTm[:, i, :], rhs=Tt[cur][:, i, :],
                                                                  start=True, stop=False),
                                         reads=["dPTm", ("dTt", cur)], writes=[PS[4]], inc=False)
                                    P.op("pe", lambda e: e.matmul(ps[4][H, i * 64:(i + 1) * 64], lhsT=id64, rhs=Tt[cur][:, i, :],
                                                                  start=False, stop=True),
                                         reads=["k_ident", ("dTt", cur)], writes=[PS[4]], inc=(i == NB - 1))
                                yield
                                cur = 1 - cur
                                P.op("dve", lambda e: e.tensor_copy(out=fl3(Tt[cur]), in_=ps[4][H, :]), reads=[PS[4]], writes=[("dTt", cur)])
                                yield
                            kv = ps[0][H, :].bitcast(BF16)
                            vv = ps[1][H, :].bitcast(BF16)
                            for i in range(NB):
                                csl = slice((n0 + i) * 64, (n0 + i + 1) * 64)
                                P.op("pe", lambda e: e.transpose(kv[:, i * 128:(i + 1) * 128], kT[:, csl], identb[:, :]),
                                     reads=["dkT", "k_ident"], writes=[PS[0]], inc=(i == NB - 1))
                            for i in range(NB):
                                csl = slice((n0 + i) * 64, (n0 + i + 1) * 64)
                                P.op("pe", lambda e: e.transpose(vv[:, i * 128:(i + 1) * 128], vT[:, csl], identb[:, :]),
                                     reads=["dvT", "k_ident"], writes=[PS[1]], inc=(i == NB - 1))
                            yield
                            B3w = [64, NB, 128]
                            kv3 = kv.rearrange("p (a b) -> p a b", b=128)
                            vv3 = vv.rearrange("p (a b) -> p a b", b=128)
                            P.op("dve", lambda e: e.tensor_tensor(out=kbg[:, :, :], in0=kv3, in1=bg_tm[:, nsl, h:h + 1].broadcast_to(B3w), op=ALU.mult),
                                 reads=[PS[0], "dbg"], writes=["dkbg"])
                            P.op("dve", lambda e: e.tensor_tensor(out=vb[:, :, :], in0=vv3, in1=beta_tm[:, nsl, h:h + 1].broadcast_to(B3w), op=ALU.mult),
                                 reads=[PS[1], "dbeta"], writes=["dvb"])
                            yield
                            for i in range(NB):
                                pb = 2 + i // 4
                                P.op("pe", lambda e: e.matmul(ps[pb][H, (i % 4) * 128:(i % 4 + 1) * 128], lhsT=Tt[cur][:, i, :], rhs=vb[:, i, :],
                                                              start=True, stop=True),
                                     reads=[("dTt", cur), "dvb"], writes=[PS[pb]], inc=(i % 4 == 3))
                            for i in range(NB):
                                P.op("pe", lambda e: e.matmul(ps[4][:, i * 64:(i + 1) * 64], lhsT=kbg[:, i, :], rhs=Tt[cur][:, i, :],
                                                              start=True, stop=True),
                                     reads=[("dTt", cur), "dkbg"], writes=[PS[4]], inc=(i == NB - 1))
                            yield "OUT"
                            P.op("dve", lambda e: e.tensor_tensor(out=kd[:, :, :], in0=kv3, in1=ed_tm[:, nsl, h:h + 1].broadcast_to(B3w), op=ALU.mult),
                                 reads=[PS[0], "ded"], writes=["dkd"])
                            P.op("dve", lambda e: e.tensor_tensor(out=qg[:, :], in0=qT[:, bsl], in1=egcb[:, :], op=ALU.mult),
                                 reads=["dqT", "degcb"], writes=["dqg"])
                            P.op("pool", lambda e: e.tensor_copy(out=aT[:, :, :], in_=aTt[:, :, :]), reads=["daTt"], writes=["daT"])
                            for hb in range(2):
                                P.op("act", lambda e: e.activation(out=u_sb[:, hb * 4:(hb + 1) * 4, :].rearrange("p a b -> p (a b)"),
                                                                   in_=ps[2 + hb][H, :], func=AF.Copy),
                                     reads=[PS[2 + hb]], writes=["du"])
                            P.op("dve", lambda e: e.tensor_copy(out=wT[:, :, :].rearrange("p a b -> p (a b)"), in_=ps[4][:, :]),
                                 reads=[PS[4]], writes=["dwT"])
                            yield

                        def stage_b(bt):
                            n0 = bt * NB
                            bsl = slice(n0 * 64, (n0 + NB) * 64)
                            for i in range(NB):
                                n = n0 + i
                                vn = vnew[n % 2]
                                kvn = ("dvnew", n % 2)
                                if n == 0:
                                    P.op("dve", lambda e: e.tensor_copy(out=vn[:, :], in_=u_sb[:, i, :]), reads=["du"], writes=[kvn])
                                else:
                                    P.op("pe", lambda e: e.matmul(ps[5][H, 0:128], lhsT=wT[:, i, :], rhs=Sb[:, :], start=True, stop=True),
                                         reads=["dwT", "dSb"], writes=[PS[5]])
                                    yield
                                    P.op("dve", lambda e: e.tensor_tensor(out=vn[:, :], in0=u_sb[:, i, :], in1=ps[5][H, 0:128], op=ALU.subtract),
                                         reads=["du", PS[5]], writes=[kvn])
                                yield
                                osl = slice(i * 64, (i + 1) * 64)
                                if n < 31:
                                    P.op("pe", lambda e: e.matmul(ps[6][:, 0:128], lhsT=kd[:, i, :], rhs=vn[:, :], start=True, stop=True),
                                         reads=["dkd", kvn], writes=[PS[6]])
                                if n > 0:
                                    P.op("pe", lambda e: e.matmul(ps[7][:, osl], lhsT=Sb[:, :], rhs=qg[:, osl], start=True, stop=False),
                                         reads=["dSb", "dqg"], writes=[PS[7]], inc=False)
                                P.op("pe", lambda e: e.matmul(ps[7][:, osl], lhsT=vn[:, :], rhs=aT[:, i, :], start=(n == 0), stop=True),
                                     reads=[kvn, "daT"], writes=[PS[7]])
                                yield
                                if n < 31:
                                    if n == 0:
                                        P.op("dve", lambda e: e.tensor_copy(out=Sf[:, :], in_=ps[6][:, 0:128]), reads=[PS[6]], writes=["dSf"])
                                    else:
                                        P.op("dve", lambda e: e.scalar_tensor_tensor(out=Sf[:, :], in0=Sf[:, :], scalar=egl[:, n, h:h + 1],
                                                                                     in1=ps[6][:, 0:128], op0=ALU.mult, op1=ALU.add),
                                             reads=[PS[6], "dSf", "degl"], writes=["dSf"])
                                    yield
                                    P.op("act", lambda e: e.activation(out=Sb[:, :], in_=Sf[:, :], func=AF.Copy), reads=["dSf"], writes=["dSb"])
                                    yield
                            P.op("act", lambda e: e.activation(out=nsq[:, :], in_=ps[7][:, :], func=AF.Square),
                                 reads=[PS[7]], writes=["dnsq"])
                            yield
                            P.op("pe", lambda e: e.matmul(ps[5][:, :], lhsT=cst["ones"][:, :], rhs=nsq[:, :], start=True, stop=True),
                                 reads=["k_ones", "dnsq"], writes=[PS[5]])
                            yield
                            P.op("act", lambda e: e.activation(out=rstd[:, :], in_=ps[5][:, :], func=AF.Ln, bias=rtrp[:, 3:4],
                                                               scale=float(1.0 / 128.0)), reads=[PS[5], "rtrp"], writes=["rstd"])
                            P.op("act", lambda e: e.activation(out=rstd[:, :], in_=rstd[:, :], func=AF.Exp, scale=-0.5),
                                 reads=["rstd"], writes=["rstd"])
                            yield
                            P.op("dve", lambda e: e.scalar_tensor_tensor(out=ntmp[:, :], in0=ps[7][:, :], scalar=dng[:, l:l + 1],
                                                                         in1=rstd[:, :], op0=ALU.mult, op1=ALU.mult),
                                 reads=[PS[7], "dng", "rstd"], writes=["dntmp"])
                            P.op("dve", lambda e: e.tensor_tensor(out=obT[:, h, bsl], in0=ntmp[:, :], in1=zsT[:, bsl], op=ALU.mult),
                                 reads=["dntmp", "dz"], writes=[("obT", h)])
                            yield

                        import os as _os2
                        _stop = int(_os2.environ.get("DN_STOP", "-1"))
                        if _stop >= 0:
                            g_ = stage_a(0)
                            for _ in range(_stop):
                                next(g_)
                        else:
                            def drive(b_gen, a_gen):
                                a_wait, a_done, b_done = False, a_gen is None, b_gen is None
                                while not (b_done and (a_done or a_wait)):
                                    if not b_done:
                                        try:
                                            next(b_gen)
                                        except StopIteration:
                                            b_done = True
                                    if not a_done and not a_wait:
                                        try:
                                            if next(a_gen) == "OUT":
                                                a_wait = True
                                        except StopIteration:
                                            a_done = True
                                if not a_done:
                                    for _ in a_gen:
                                        pass

                            drive(None, stage_a(0))
                            for bt in range(4):
                                drive(stage_b(bt), stage_a(bt + 1) if bt + 1 < 4 else None)
                        phase_barrier()

        def dump(nm, tile, nchunks, keyname, nokey=False):
            if nokey:
                phase_barrier()
            with nc.sbuf_tensor(_u + "s_dbgf_" + nm, [128, S], F32) as dbgf:
                for k in range(nchunks):
                    P.op("dve", lambda e: e.tensor_copy(out=dbgf[:, :], in_=tile[:, k, :]),
                         reads=[(keyname, k)], writes=["dbgf"])
                    P.dma(dbg_d[nm][k * 128:(k + 1) * 128, :], dbgf[:, :], reads=["dbgf"], writes=["dbg_" + nm], stream="o")
                phase_barrier()

        for l in range(n_layers):
            norm_to(lambda k, tc: (hT[:, k, tc * TC:(tc + 1) * TC], ("hT", k)), normg[:, l, :], l)
            if debug and l == 0:
                with nc.sbuf_tensor(_u + "dbgf", [128, S], F32) as dbgf:
                    for k in range(8):
                        P.op("dve", lambda e, k=k: e.tensor_copy(out=dbgf[:, :], in_=hT[:, k, :]),
                             reads=[("hT", k)], writes=["dbgf"])
                        P.dma(dbg_d["hT"][k * 128:(k + 1) * 128, :], dbgf[:, :], reads=["dbgf"], writes=["dbg_hT"],
                              stream="o")
                    phase_barrier()

            def mem_attention(l):
                ph = ExitStack()
                def sbp(name, shape, dt):
                    return ph.enter_context(nc.sbuf_tensor(_u + "s_" + name, list(shape), dt))
                memT = sbp("memT", [128, 8, MEM_LEN], F32)
                memn = sbp("memn", [128, 8, MEM_LEN], BF16)
                kmT = sbp("kmT", [128, 2, MEM_LEN], BF16)
                vm = sbp("vm", [128, 2, 256], BF16)
                qmT = sbp("qmT", [128, 2, S], BF16)
                pT = [sbp(f"pT{i}", [128, TC], BF16) for i in range(2)]
                rden = sbp("rden", [128, TC], F32)
                for k in range(8):
                    P.dma(memT[:, k, :], memT_d[k * 128:(k + 1) * 128, :], writes=[("memT", k)], stream="x")
                P.op("dve", lambda e: e.tensor_scalar(out=g32[:, :], in0=memg[:, l, :], scalar1=float(math.sqrt(D)),
                                                      scalar2=None, op0=ALU.mult), reads=["memg"], writes=["g32"])
                rms_stats(memT, [("memT", k) for k in range(8)], 8, MEM_LEN, 0, 0, sqt, "sqt", rstd, "rstd", None)
                for k in range(8):
                    P.op("dve", lambda e, k=k: e.scalar_tensor_tensor(
                        out=memn[:, k, :], in0=memT[:, k, :], scalar=g32[:, k:k + 1], in1=rstd[:, 0:MEM_LEN],
                        op0=ALU.mult, op1=ALU.mult), reads=[("memT", k), "g32", "rstd"], writes=[("memn", k)])
                for ec in range(2):
                    wb, wkey = load_w("w_kv", l, ec * 128)
                    for k in range(8):
                        P.op("pe", lambda e, k=k, wb=wb: e.matmul(ps[1][:, 0:MEM_LEN], lhsT=wb[:, k, :], rhs=memn[:, k, :],
                                                                  start=(k == 0), stop=(k == 7)),
                             reads=[wkey, ("memn", k)], writes=[PS[1]], inc=(k == 7))
                    P.op("dve", lambda e, ec=ec: e.tensor_copy(out=kmT[:, ec, :], in_=ps[1][:, 0:MEM_LEN]),
                         reads=[PS[1]], writes=[("kmT", ec)])
                for vc in range(2):
                    wb, wkey = load_w("w_kv", l, 256 + vc * 128)
                    for mt in range(2):
                        for k in range(8):
                            P.op("pe", lambda e, k=k, wb=wb, mt=mt: e.matmul(
                                ps[2][:, 0:128], lhsT=memn[:, k, mt * 128:(mt + 1) * 128], rhs=wb[:, k, :],
                                start=(k == 0), stop=(k == 7)),
                                reads=[wkey, ("memn", k)], writes=[PS[2]], inc=(k == 7))
                        P.op("dve", lambda e, mt=mt, vc=vc: e.tensor_copy(out=vm[:, mt, vc * 128:(vc + 1) * 128],
                                                                          in_=ps[2][:, 0:128]),
                             reads=[PS[2]], writes=[("vm", mt)])
                for ec in range(2):
                    def ev(tc, pap, pkey, ec=ec):
                        P.op("act", lambda e: e.activation(out=qmT[:, ec, tc * TC:(tc + 1) * TC], in_=pap,
                                                           func=AF.Copy, scale=0.125),
                             reads=[pkey], writes=[("qmT", ec)])
                    proj_fm(l, C_MQ + ec * 128, 128, ev)
                for h in range(4):
                    ec, r0 = h // 2, (h % 2) * 64
                    for tc in range(NT):
                        tsl = slice(tc * TC, (tc + 1) * TC)
                        for mb in range(2):
                            pi = 3 + mb
                            P.op("pe", lambda e, mb=mb, pi=pi: e.matmul(
                                ps[pi][:, :], lhsT=kmT[r0:r0 + 64, ec, mb * 128:(mb + 1) * 128],
                                rhs=qmT[r0:r0 + 64, ec, tsl], start=True, stop=True),
                                reads=[("kmT", ec), ("qmT", ec)], writes=[PS[pi]])
                            P.op("act", lambda e, mb=mb, pi=pi: e.activation(out=pT[mb][:, :], in_=ps[pi][:, :],
                                                                             func=AF.Exp),
                                 reads=[PS[pi]], writes=[("pT", mb)])
                        for mb in range(2):
                            P.op("pe", lambda e, mb=mb: e.matmul(
                                ps[5][r0:r0 + 64, :], lhsT=vm[:, mb, h * 64:(h + 1) * 64], rhs=pT[mb][:, :],
                                start=(mb == 0), stop=(mb == 1)),
                                reads=[("vm", mb), ("pT", mb)], writes=[PS[5]], inc=(mb == 1))
                        for mb in range(2):
                            P.op("pe", lambda e, mb=mb: e.matmul(
                                ps[6][r0:r0 + 64, :], lhsT=cst["ones"][:, 0:64], rhs=pT[mb][:, :],
                                start=(mb == 0), stop=(mb == 1)),
                                reads=["k_ones", ("pT", mb)], writes=[PS[6]], inc=(mb == 1))
                        P.op("act", lambda e: e.activation(out=rden[r0:r0 + 64, :], in_=ps[6][r0:r0 + 64, :], func=AF.Ln),
                             reads=[PS[6]], writes=["rden"])
                        P.op("act", lambda e: e.activation(out=rden[r0:r0 + 64, :], in_=rden[r0:r0 + 64, :], func=AF.Exp, scale=-1.0),
                             reads=["rden"], writes=["rden"])
                        P.op("dve", lambda e, tsl=tsl: e.tensor_tensor(out=obT[r0:r0 + 64, ec, tsl], in0=ps[5][r0:r0 + 64, :],
                                                                      in1=rden[r0:r0 + 64, :], op=ALU.mult),
                             reads=[PS[5], "rden"], writes=[("obT", ec)])
                phase_barrier()
                ph.close()

            def merge(br, nwc, first):
                with ExitStack() as ph:
                    sg = [ph.enter_context(nc.sbuf_tensor(_u + f"sg{i}", [128, TC], F32)) for i in range(2)]
                    tmp = [ph.enter_context(nc.sbuf_tensor(_u + f"mtmp{i}", [128, TC], BF16)) for i in range(2)]
                    it = 0
                    for dc in range(8):
                        wg, wgk = load_w("w_in", l, C_G + br * D + dc * 128)
                        wr, wrk = load_w(f"w_br{br}", l, dc * 128, rows=nwc)
                        for tc in range(NT):
                            tsl = slice(tc * TC, (tc + 1) * TC)
                            b = it % 2
                            it += 1
                            pg, pp = 1 + b, 3 + b
                            for k in range(8):
                                P.op("pe", lambda e, k=k, pg=pg, tsl=tsl, wg=wg: e.matmul(
                                    ps[pg][:, :], lhsT=wg[:, k, :], rhs=hT[:, k, tsl], start=(k == 0), stop=(k == 7)),
                                    reads=[wgk, ("hT", k)], writes=[PS[pg]], inc=(k == 7))
                            for k in range(nwc):
                                P.op("pe", lambda e, k=k, pp=pp, tsl=tsl, wr=wr: e.matmul(
                                    ps[pp][:, :], lhsT=wr[:, k, :], rhs=obT[:, k, tsl], start=(k == 0), stop=(k == nwc - 1)),
                                    reads=[wrk, ("obT", k)], writes=[PS[pp]], inc=(k == nwc - 1))
                            P.op("act", lambda e, b=b, pg=pg, dc=dc: e.activation(
                                out=sg[b][:, :], in_=ps[pg][:, :], func=AF.Sigmoid,
                                bias=bgate[:, l, br * 8 + dc:br * 8 + dc + 1], scale=1.0),
                                reads=[PS[pg], "bgate"], writes=[("sg", b)])
                            if first:
                                P.op("dve", lambda e, b=b, pp=pp, dc=dc, tsl=tsl: e.tensor_tensor(
                                    out=mT[:, dc, tsl], in0=ps[pp][:, :], in1=sg[b][:, :], op=ALU.mult),
                                    reads=[PS[pp], ("sg", b)], writes=[("mT", dc, tc)])
                            else:
                                P.op("dve", lambda e, b=b, pp=pp: e.tensor_tensor(
                                    out=tmp[b][:, :], in0=ps[pp][:, :], in1=sg[b][:, :], op=ALU.mult),
                                    reads=[PS[pp], ("sg", b)], writes=[("mtmp", b)])
                                P.op("dve", lambda e, b=b, dc=dc, tsl=tsl: e.tensor_tensor(
                                    out=mT[:, dc, tsl], in0=mT[:, dc, tsl], in1=tmp[b][:, :], op=ALU.add),
                                    reads=[("mtmp", b), ("mT", dc, tc)], writes=[("mT", dc, tc)])
                    phase_barrier()

            if 1 in branches:
                deltanet(l)
                if debug and l == 0:
                    dump('odn', obT, 4, 'obT')
                merge(1, 4, True)
            mem_attention(l)
            if debug and l == 0:
                dump('omem', obT, 2, 'obT')
            merge(3, 2, 1 not in branches)
            if 0 in branches:
                sb_attention(l)
                if debug and l == 0:
                    dump('osb', obT, 4, 'obT')
                merge(0, 4, False)
            if 2 in branches:
                retention(l)
                if debug and l == 0:
                    dump('ort', obT, 4, 'obT')
                merge(2, 4, False)

            for ec in range(8):
                wo, wok = load_w("w_out", l, ec * 128)
                for tc in range(NT):
                    tsl = slice(tc * TC, (tc + 1) * TC)
                    pi = 1 + tc % 2
                    for dc in range(8):
                        P.op("pe", lambda e, dc=dc, pi=pi, tsl=tsl, wo=wo: e.matmul(
                            ps[pi][:, :], lhsT=wo[:, dc, :], rhs=mT[:, dc, tsl], start=(dc == 0), stop=(dc == 7)),
                            reads=[wok, ("mT", dc, tc)], writes=[PS[pi]], inc=(dc == 7))
                    P.op("dve", lambda e, ec=ec, pi=pi, tsl=tsl: e.tensor_tensor(
                        out=xT[:, ec, tsl], in0=xT[:, ec, tsl], in1=ps[pi][:, :], op=ALU.add),
                        reads=[PS[pi], ("xT", ec)], writes=[("xT", ec)])
            phase_barrier()

        with ExitStack() as ph:
            ot = [ph.enter_context(nc.sbuf_tensor(_u + f"ot{i}", [128, TC], F32)) for i in range(2)]
            cnt = [0]

            def dst(k, tc):
                b = cnt[0] % 2
                cnt[0] += 1
                return ot[b][:, :], ("ot", b)
            P.op("dve", lambda e: e.tensor_scalar(out=g32[:, :], in0=fing[:, :], scalar1=float(math.sqrt(D)), scalar2=None,
                                                  op0=ALU.mult), reads=["fing"], writes=["g32"])
            for tc in range(NT):
                rms_stats(xT, [("xT", k) for k in range(8)], 8, TC, tc * TC, 0, sqt, "sqt", rstd, "rstd", None)
                for k in range(8):
                    ap, key = dst(k, tc)
                    P.op("dve", lambda e, k=k, tc=tc, ap=ap: e.scalar_tensor_tensor(
                        out=ap, in0=xT[:, k, tc * TC:(tc + 1) * TC], scalar=g32[:, k:k + 1], in1=rstd[:, :],
                        op0=ALU.mult, op1=ALU.mult),
                        reads=[("xT", k), "g32", "rstd"], writes=[key])
                    P.dma(outT_d[k * 128:(k + 1) * 128, tc * TC:(tc + 1) * TC], ap, reads=[key], writes=["outT"],
                          stream="o")
            P.finish(["outT", "dbg_hT", "dbg_omem", "dbg_osb", "dbg_odn", "dbg_ort"])
            phase_barrier(full=True)
        P.emit()
        print("instructions recorded:", P.nins, {n: len(P.q[n]) for n in P.names})
    return nc


_NC_CACHE = {}


def _prep_inputs(inputs, b):
    f = np.float32
    m = {}
    m["xT"] = np.ascontiguousarray(inputs["x"][b].T.astype(f))
    m["memT"] = np.ascontiguousarray(inputs["mem"][b].T.astype(f))
    m["w_in"] = np.ascontiguousarray(inputs["w_in"], dtype=f)
    m["w_mem_kv"] = np.ascontiguousarray(inputs["w_mem_kv"], dtype=f)
    for n in ["w_br_sb", "w_br_dn", "w_br_ret", "w_br_mem", "w_out"]:
        m[n] = np.ascontiguousarray(inputs[n], dtype=f)
    m["norm_g"] = np.ascontiguousarray(inputs["norm_g"].reshape(DEPTH, 8, 128).transpose(0, 2, 1), dtype=f)
    m["mem_norm_g"] = np.ascontiguousarray(inputs["mem_norm_g"].reshape(DEPTH, 8, 128).transpose(0, 2, 1), dtype=f)
    m["b_gate"] = np.ascontiguousarray(inputs["b_gate"].reshape(DEPTH, 32, 128).transpose(0, 2, 1), dtype=f)
    m["final_norm_g"] = np.ascontiguousarray(inputs["final_norm_g"].reshape(8, 128).T, dtype=f)
    for n, v in _consts().items():
        m["c_" + n] = v
    for n, v in _rt_consts().items():
        m["c_" + n] = v
    m["c_dn"] = _dn_consts()
    m["dn_conv_w"] = np.ascontiguousarray(inputs["dn_conv_w"].reshape(DEPTH, 4, 12, 128).transpose(0, 3, 2, 1), dtype=f)
    m["dn_norm_g"] = np.ascontiguousarray(inputs["dn_norm_g"].T, dtype=f)
    m["dn_alog"] = np.ascontiguousarray(np.broadcast_to(np.tile(inputs["dn_a_log"], (1, 32))[:, None, :], (DEPTH, 128, 128)), dtype=f)
    m["dn_dtb"] = np.ascontiguousarray(np.broadcast_to(np.tile(inputs["dn_dt_bias"], (1, 32))[:, None, :], (DEPTH, 128, 128)), dtype=f)
    m["pos"] = np.ascontiguousarray(np.broadcast_to(inputs["positions"][b].astype(np.int32)[None, :], (128, S)))
    m["ret_norm_g"] = np.ascontiguousarray(inputs["ret_norm_g"].reshape(DEPTH, 4, 128).transpose(0, 2, 1), dtype=f)
    return m


def kernel(**inputs):
    inputs = {k: np.asarray(v) for k, v in inputs.items()}
    if "nc" not in _NC_CACHE:
        _NC_CACHE["nc"] = build()
    nc = _NC_CACHE["nc"]
    in_maps = [_prep_inputs(inputs, b) for b in range(8)]
    res = run_bass_kernel_spmd(nc, in_maps, core_ids=list(range(8)))
    out = np.stack([np.ascontiguousarray(res.results[b]["outT"].T) for b in range(8)], axis=0)
    return out.astype(np.float32)
```

```python
import math
from contextlib import ExitStack
import numpy as np
import concourse.bass as bass
import concourse.mybir as mybir
from concourse.bass_utils import run_bass_kernel_spmd

F32 = mybir.dt.float32
BF16 = mybir.dt.bfloat16
I32 = mybir.dt.int32
AF = mybir.ActivationFunctionType
ALU = mybir.AluOpType

D = 1024
S = 2048
DEPTH = 2
MEM_LEN = 256
EPS = 1e-6
IN_COLS = 9992
C_SBQ, C_SBK, C_SBV, C_SBZ = 0, 512, 1024, 1536
C_DNQ, C_DNK, C_DNV, C_DNZ, C_DNA, C_DNB = 2048, 2560, 3072, 3584, 4096, 4100
C_RTQ, C_RTK, C_RTV, C_RTZ = 4104, 4360, 4616, 5128
C_MQ = 5640
C_G = 5896
NT = 4
TC = 512


class _Uniq:
    def __init__(self):
        self.n = 0

    def __add__(self, name):
        self.n += 1
        return f"{name}_{self.n}"


class _Rec:
    def __getattr__(self, name):
        return lambda *a, **k: (name, a, k)


_REC = _Rec()


class Prog:
    LIMIT = 30000

    def __init__(self, nc, stack):
        self.nc = nc
        self.stack = stack
        self.names = ["pe", "act", "dve", "pool", "sp"]
        self.q = {n: [] for n in self.names}
        self.cnt = {n: 0 for n in self.names}
        self.sems = {n: [] for n in self.names}
        self.seen = {n: {} for n in self.names}
        self.bufs = {}
        self.dma_sem = {}
        self.dma_cnt = {}
        self.same_sync = True
        self.dma_i = 0
        self.nins = 0

    def _eng_sem(self, eng, g):
        ep = (g - 1) // self.LIMIT
        while len(self.sems[eng]) <= ep:
            s = self.stack.enter_context(self.nc.semaphore(f"s_{eng}_{len(self.sems[eng])}"))
            self.sems[eng].append(s)
        return self.sems[eng][ep], (g - 1) % self.LIMIT + 1

    def _tok_sem(self, tok):
        kind, g = tok
        if kind.startswith("dma:"):
            return self.dma_sem[kind], 16 * g
        return self._eng_sem(kind, g)

    def _need(self, eng, tok):
        kind, g = tok
        if kind == eng:
            if eng in ("pe", "sp") or not self.same_sync:
                return False
        return self.seen[eng].get(kind, 0) < g

    def _collect(self, eng, reads, writes):
        toks = []
        for k in reads:
            b = self.bufs.get(k)
            if b and b[0] is not None:
                toks.append(b[0])
            if b and isinstance(k, tuple) and k[0] == "ps" and eng in ("act", "dve"):
                other = "dve" if eng == "act" else "act"
                if other in b[1]:
                    toks.append((other, b[1][other]))
        for k in writes:
            b = self.bufs.get(k)
            if b:
                if b[0] is not None:
                    toks.append(b[0])
                toks.extend(b[1].items())
        need = {}
        for t in toks:
            if self._need(eng, t):
                need[t[0]] = max(need.get(t[0], 0), t[1])
        return list(need.items())

    def _update(self, tok, reads, writes):
        for k in writes:
            self.bufs[k] = [tok, {}]
        for k in reads:
            b = self.bufs.setdefault(k, [None, {}])
            if k in writes:
                continue
            b[1][tok[0]] = max(b[1].get(tok[0], 0), tok[1])

    def op(self, eng, fn, reads=(), writes=(), inc=True):
        call = fn(_REC)
        fn = lambda e, c=call: getattr(e, c[0])(*c[1], **c[2])
        waits = self._collect(eng, reads, writes)
        for t in waits:
            self.seen[eng][t[0]] = t[1]
        ws = [self._tok_sem(t) for t in waits]
        for w in ws[1:]:
            self.q[eng].append(("wait", w[0], w[1]))
        if inc:
            self.cnt[eng] += 1
            tok = (eng, self.cnt[eng])
            sem, val = self._eng_sem(eng, self.cnt[eng])
            self.q[eng].append(("ins", fn, ws[0] if ws else None, (sem, 1)))
        else:
            tok = (eng, self.cnt[eng] + 1)
            self.q[eng].append(("ins", fn, ws[0] if ws else None, None))
        self._update(tok, reads, writes)
        self.nins += 1
        return tok

    NDS = 16

    def dma(self, out, in_, reads=(), writes=(), stream="d0", queue="act"):
        if not self.SOFT:
            queue = "sp"
        j = self.dma_i % self.NDS
        self.dma_i += 1
        kind = f"dma:{j}"
        if kind not in self.dma_sem:
            self.dma_sem[kind] = self.stack.enter_context(self.nc.semaphore(f"sd_{j}"))
            self.dma_cnt[kind] = 0
        waits = self._collect(queue, reads, writes)
        if self.dma_cnt[kind] > 0 and self.seen[queue].get(kind, 0) < self.dma_cnt[kind]:
            waits = [w for w in waits if w[0] != kind] + [(kind, self.dma_cnt[kind])]
        for t in waits:
            self.seen[queue][t[0]] = t[1]
        for t in waits:
            s, v = self._tok_sem(t)
            self.q[queue].append(("wait", s, v))
        self.dma_cnt[kind] += 1
        tok = (kind, self.dma_cnt[kind])
        self.q[queue].append(("ins", lambda e, o=out, i=in_: e.dma_start(out=o, in_=i), None,
                              (self.dma_sem[kind], 16)))
        self._update(tok, reads, writes)
        self.nins += 1
        return tok

    PERSIST = {"cstf", "normg", "memg", "bgate", "fing", "xT", "hT", "mT", "obT", "wst", "wb", "epsb", "g32", "sqt",
               "rstd", "oneb", "rtsd", "rtrp", "rtcn", "retg", "dncw", "dng", "ps", "outT"}

    def _persistent(self, k):
        h = k[0] if isinstance(k, tuple) else k
        return h in self.PERSIST or (isinstance(h, str) and (h.startswith("k_") or h.startswith("dbg_")))

    SOFT = False

    def barrier(self, full=False):
        full = full or not self.SOFT
        if full:
            toks = [(n, self.cnt[n]) for n in self.names if self.cnt[n] > 0]
            toks += [(k, c) for k, c in self.dma_cnt.items() if c > 0]
            engines = self.names
        else:
            need = {}
            for k, b in self.bufs.items():
                if self._persistent(k):
                    continue
                if b[0] is not None:
                    need[b[0][0]] = max(need.get(b[0][0], 0), b[0][1])
                for kind, c in b[1].items():
                    need[kind] = max(need.get(kind, 0), c)
            toks = list(need.items())
            engines = [n for n in self.names if n != "sp"]
        for eng in engines:
            for t in toks:
                if t[0] == eng and eng in ("pe", "sp"):
                    continue
                if self.seen[eng].get(t[0], 0) < t[1]:
                    self.seen[eng][t[0]] = t[1]
                    s, v = self._tok_sem(t)
                    self.q[eng].append(("wait", s, v))
        if full:
            self.bufs = {}
        else:
            self.bufs = {k: b for k, b in self.bufs.items() if self._persistent(k)}

    def finish(self, final_keys):
        toks = []
        for k in final_keys:
            b = self.bufs.get(k)
            if b and b[0] is not None:
                toks.append(b[0])
        for t in toks:
            s, v = self._tok_sem(t)
            self.q["act" if self.SOFT else "sp"].append(("wait", s, v))

    def simulate(self):
        sem = {}
        pos = {n: 0 for n in self.names}
        def ok(w):
            return w is None or sem.get(id(w[0]), 0) >= w[1]
        progress = True
        while progress:
            progress = False
            for n in self.names:
                q = self.q[n]
                while pos[n] < len(q):
                    ent = q[pos[n]]
                    if ent[0] == "wait":
                        if not ok((ent[1], ent[2])):
                            break
                    else:
                        if not ok(ent[2]):
                            break
                        if ent[3] is not None:
                            sem[id(ent[3][0])] = sem.get(id(ent[3][0]), 0) + ent[3][1]
                    pos[n] += 1
                    progress = True
        stuck = {n: (pos[n], len(self.q[n])) for n in self.names if pos[n] < len(self.q[n])}
        if stuck:
            msg = []
            for n, (p, ln) in stuck.items():
                ent = self.q[n][p]
                w = (ent[1], ent[2]) if ent[0] == "wait" else ent[2]
                msg.append(f"{n}@{p}/{ln} waits {w} have {sem.get(id(w[0]), 0)} kind={ent[0]} tag={ent[4] if len(ent) > 4 else None}")
            raise RuntimeError("DEADLOCK in recorded program: " + "; ".join(msg))

    def emit(self):
        self.simulate()
        nc = self.nc
        with nc.Block() as block:
            def replay(e, name):
                for ent in self.q[name]:
                    if ent[0] == "wait":
                        e.wait_ge(ent[1], ent[2])
                    else:
                        _, fn, w, inc = ent
                        ins = fn(e)
                        if w is not None:
                            ins._wait_ge(w[0], w[1])
                        if inc is not None:
                            ins.then_inc(inc[0], inc[1])

            @block.sync
            def _(e):
                replay(e, "sp")

            @block.scalar
            def _(e):
                replay(e, "act")

            @block.vector
            def _(e):
                replay(e, "dve")

            @block.tensor
            def _(e):
                replay(e, "pe")

            @block.gpsimd
            def _(e):
                replay(e, "pool")


def _consts():
    c = {}
    i = np.arange(128)
    c["ident"] = np.eye(128, dtype=np.float32)
    c["ones"] = np.ones((128, 128), np.float32)
    c["trineg"] = -(i[:, None] >= i[None, :]).astype(np.float32)
    e0 = np.zeros((128, 128), np.float32)
    e0[0, :] = 1.0
    c["e0"] = e0
    c["masksb"] = (i[:, None] < i[None, :]).astype(np.float32)
    return c


def _rt_consts():
    gam = [1.0 - 2.0 ** (-5.0 - h) for h in range(4)]
    i = np.arange(128)
    dt = np.zeros((128, 4, 128), np.float64)
    sd = np.zeros((128, 4), np.float64)
    cd = np.zeros((128, 2, 128), np.float64)
    for h in range(4):
        rel = i[None, :] - i[:, None]
        dt[:, h, :] = np.where(rel >= 0, gam[h] ** np.maximum(rel, 0), 0.0)
        sd[:, h] = gam[h] ** (127 - i)
    for p in range(2):
        for hh in range(2):
            cd[hh * 64:(hh + 1) * 64, p, :] = (gam[2 * p + hh] ** (i + 1.0))[None, :]
    inv = 10000.0 ** (-(np.arange(32, dtype=np.float32)) / np.float32(32))
    rp = np.zeros((128, 4), np.float32)
    rp[:, 0] = np.tile(inv.astype(np.float32), 4)
    rp[:, 1] = np.where((i % 64) < 32, -1.0, 1.0)
    rp[:, 2] = np.float32(math.pi / 2)
    rp[:, 3] = EPS
    cn = np.eye(128) - 1.0 / 128.0
    return {"rt_dt": dt.astype(np.float32), "rt_sd": sd.astype(np.float32), "rt_cd": cd.astype(np.float32),
            "rt_rp": rp, "rt_cn": cn.astype(np.float32)}


def _dn_consts():
    i = np.arange(64)
    c = np.zeros((128, 4, 128), np.float32)
    c[:, 0, :] = np.eye(128)
    c[:, 1, :] = 1.0
    c[0:64, 2, 0:64] = (i[:, None] <= i[None, :])
    c[0:64, 3, 0:64] = (i[:, None] < i[None, :])
    return c


CONST_NAMES = ["ident", "ones", "trineg", "e0", "masksb"]


def build(n_layers=DEPTH, debug=False, branches=(0, 1, 2, 3), wseq="auto", _wrec=None):
    if wseq == "auto":
        rec = []
        build(n_layers, debug, branches, wseq=None, _wrec=rec)
        wseq = rec
    NPRE = 2
    _u = _Uniq()
    nc = bass.Bass("TRN2", target_bir_lowering=False)
    dr = {}

    def din(name, shape, dt=F32):
        dr[name] = nc.dram_tensor(name, list(shape), dt, kind="ExternalInput").ap()
        return dr[name]

    xT_d = din("xT", [D, S])
    memT_d = din("memT", [D, MEM_LEN])
    w_in_d = din("w_in", [DEPTH, D, IN_COLS])
    w_kv_d = din("w_mem_kv", [DEPTH, D, 512])
    w_br_d = {0: din("w_br_sb", [DEPTH, 512, D]), 1: din("w_br_dn", [DEPTH, 512, D]),
              2: din("w_br_ret", [DEPTH, 512, D]), 3: din("w_br_mem", [DEPTH, 256, D])}
    w_out_d = din("w_out", [DEPTH, D, D])
    normg_d = din("norm_g", [DEPTH, 128, 8])
    memg_d = din("mem_norm_g", [DEPTH, 128, 8])
    bgate_d = din("b_gate", [DEPTH, 128, 32])
    fing_d = din("final_norm_g", [128, 8])
    cst_d = {n: din("c_" + n, [128, 128]) for n in CONST_NAMES}
    dnc_d = din("c_dn", [128, 4, 128])
    dncw_d = din("dn_conv_w", [DEPTH, 128, 12, 4])
    dng_d = din("dn_norm_g", [128, DEPTH])
    dnal_d = din("dn_alog", [DEPTH, 128, 128])
    dndt_d = din("dn_dtb", [DEPTH, 128, 128])
    pos_d = din("pos", [128, S], I32)
    retg_d = din("ret_norm_g", [DEPTH, 128, 4])
    rtdt_d = din("c_rt_dt", [128, 4, 128])
    rtsd_d = din("c_rt_sd", [128, 4])
    rtcd_d = din("c_rt_cd", [128, 2, 128])
    rtrp_d = din("c_rt_rp", [128, 4])
    rtcn_d = din("c_rt_cn", [128, 128])
    outT_d = nc.dram_tensor("outT", [D, S], F32, kind="ExternalOutput").ap()
    dbg_d = {}
    if debug:
        dbg_d["hT"] = nc.dram_tensor("dbg_hT", [D, S], F32, kind="ExternalOutput").ap()
        dbg_d["omem"] = nc.dram_tensor("dbg_omem", [256, S], F32, kind="ExternalOutput").ap()
        for nm in ["osb", "odn", "ort"]:
            dbg_d[nm] = nc.dram_tensor("dbg_" + nm, [512, S], F32, kind="ExternalOutput").ap()

    with ExitStack() as st:
        P = Prog(nc, st)

        def sb(name, shape, dt):
            return st.enter_context(nc.sbuf_tensor(_u + "s_" + name, list(shape), dt))

        xT = sb("xT", [128, 8, S], F32)
        hT = sb("hT", [128, 8, S], BF16)
        mT = sb("mT", [128, 8, S], BF16)
        obT = sb("obT", [128, 4, S], BF16)
        wst = [sb(f"wst{i}", [128, 8, 128], F32) for i in range(2)]
        NWB = 4
        wbp = [sb(f"wb{i}", [128, 8, 128], BF16) for i in range(NWB)]
        cst = {n: sb("k_" + n, [128, 128], BF16) for n in CONST_NAMES}
        cstf = sb("cstf", [128, 128], F32)
        normg = sb("normg", [128, DEPTH, 8], F32)
        memg = sb("memg", [128, DEPTH, 8], F32)
        bgate = sb("bgate", [128, DEPTH, 32], F32)
        fing = sb("fing", [128, 8], F32)
        ps = [st.enter_context(nc.psum_tensor(f"ps{i}", [128, 512], F32)) for i in range(8)]
        PS = [("ps", i) for i in range(8)]

        wcount = [0]
        wbcount = [0]

        WT = {"w_in": w_in_d, "w_kv": w_kv_d, "w_out": w_out_d,
              "w_br0": w_br_d[0], "w_br1": w_br_d[1], "w_br2": w_br_d[2], "w_br3": w_br_d[3]}
        wcall = [0]
        wissued = [0]
        wpre = {}

        def _issue_w(rec, eng="pool"):
            tid, l_, c0, n, rows = rec
            src = WT[tid][l_, 0:rows * 128, c0:c0 + n]
            i = wcount[0] % 2
            wcount[0] += 1
            j = wbcount[0] % NWB
            wbcount[0] += 1
            stg, wb = wst[i], wbp[j]
            P.dma(stg[:, 0:rows, 0:n], src.rearrange("(k p) n -> p k n", p=128),
                  writes=[("wst", i)], stream=f"w{i}", queue="sp")
            fn = lambda e, o=wb[:, 0:rows, 0:n], a=stg[:, 0:rows, 0:n]: e.tensor_copy(out=o, in_=a)
            P.op(eng, fn, reads=[("wst", i)], writes=[("wb", j)])
            return wb, ("wb", j)

        def load_w(tid, l_, c0, n=128, rows=8):
            rec = (tid, l_, c0, n, rows)
            idx = wcall[0]
            wcall[0] += 1
            if _wrec is not None:
                _wrec.append(rec)
            if idx in wpre:
                assert wseq[idx] == rec, (idx, wseq[idx], rec)
                return wpre.pop(idx)
            assert wissued[0] == idx, (wissued[0], idx)
            wissued[0] = idx + 1
            return _issue_w(rec)

        def phase_barrier(full=False):
            if wseq is not None:
                while wissued[0] < min(wcall[0] + NPRE, len(wseq)):
                    wpre[wissued[0]] = _issue_w(wseq[wissued[0]])
                    wissued[0] += 1
            P.barrier(full=full)

        for n in CONST_NAMES:
            P.dma(cstf[:, :], cst_d[n][:, :], writes=["cstf"], stream="c")
            P.op("dve", lambda e, o=cst[n][:, :]: e.tensor_copy(out=o, in_=cstf[:, :]),
                 reads=["cstf"], writes=["k_" + n])
        for l in range(DEPTH):
            P.dma(normg[:, l, :], normg_d[l, :, :], writes=["normg"], stream="c")
            P.dma(memg[:, l, :], memg_d[l, :, :], writes=["memg"], stream="c")
            P.dma(bgate[:, l, :], bgate_d[l, :, :], writes=["bgate"], stream="c")
        P.dma(fing[:, :], fing_d[:, :], writes=["fing"], stream="c")
        for k in range(8):
            P.dma(xT[:, k, :], xT_d[k * 128:(k + 1) * 128, :], writes=[("xT", k)], stream="x")

        rtsd = sb("rtsd", [128, 4], F32)
        rtrp = sb("rtrp", [128, 4], F32)
        rtcn = sb("rtcn", [128, 128], BF16)
        retg = sb("retg", [128, DEPTH, 4], F32)
        P.dma(rtsd[:, :], rtsd_d[:, :], writes=["rtsd"], stream="c")
        P.dma(rtrp[:, :], rtrp_d[:, :], writes=["rtrp"], stream="c")
        P.dma(cstf[:, :], rtcn_d[:, :], writes=["cstf"], stream="c")
        P.op("dve", lambda e: e.tensor_copy(out=rtcn[:, :], in_=cstf[:, :]), reads=["cstf"], writes=["rtcn"])
        for l in range(DEPTH):
            P.dma(retg[:, l, :], retg_d[l, :, :], writes=["retg"], stream="c")
        dncw = sb("dncw", [128, DEPTH, 12, 4], F32)
        dng = sb("dng", [128, DEPTH], F32)
        P.dma(dng[:, :], dng_d[:, :], writes=["dng"], stream="c")
        for l in range(DEPTH):
            P.dma(dncw[:, l, :, :], dncw_d[l, :, :, :], writes=["dncw"], stream="c")
        def rms_stats(src_tile, src_keys, nk, ncols, c0, psum_i, sq_tile, sq_key, rstd_tile, rstd_key, extra):
            for k in range(nk):
                P.op("act", lambda e, k=k: e.activation(out=sq_tile[:, k % 2, 0:ncols], in_=src_tile[:, k, c0:c0 + ncols],
                                                        func=AF.Square),
                     reads=[src_keys[k]], writes=[(sq_key, k % 2)])
                P.op("pe", lambda e, k=k: e.matmul(ps[psum_i][:, 0:ncols], lhsT=cst["ones"][:, :],
                                                   rhs=sq_tile[:, k % 2, 0:ncols], start=(k == 0), stop=(k == nk - 1)),
                     reads=[(sq_key, k % 2), "k_ones"], writes=[PS[psum_i]])
            P.op("act", lambda e: e.activation(out=rstd_tile[:, 0:ncols], in_=ps[psum_i][:, 0:ncols], func=AF.Ln,
                                               bias=epsb[:, 0:1], scale=1.0),
                 reads=[PS[psum_i], "epsb"], writes=[rstd_key])
            P.op("act", lambda e: e.activation(out=rstd_tile[:, 0:ncols], in_=rstd_tile[:, 0:ncols], func=AF.Exp,
                                               scale=-0.5),
                 reads=[rstd_key], writes=[rstd_key])

        epsb = sb("epsb", [128, 1], F32)
        P.op("dve", lambda e: e.memset(epsb[:, :], float(D * EPS)), writes=["epsb"])
        sqt = sb("sqt", [128, 2, TC], BF16)
        rstd = sb("rstd", [128, TC], F32)
        g32 = sb("g32", [128, 8], F32)

        def norm_to(dst_fn, gsrc, layer_tag):
            P.op("dve", lambda e: e.tensor_scalar(out=g32[:, :], in0=gsrc, scalar1=float(math.sqrt(D)), scalar2=None,
                                                  op0=ALU.mult), reads=["normg", "fing"], writes=["g32"])
            for tc in range(NT):
                rms_stats(xT, [("xT", k) for k in range(8)], 8, TC, tc * TC, 0, sqt, "sqt", rstd, "rstd", None)
                for k in range(8):
                    ap, key = dst_fn(k, tc)
                    P.op("dve", lambda e, k=k, tc=tc, ap=ap: e.scalar_tensor_tensor(
                        out=ap, in0=xT[:, k, tc * TC:(tc + 1) * TC], scalar=g32[:, k:k + 1], in1=rstd[:, :],
                        op0=ALU.mult, op1=ALU.mult),
                        reads=[("xT", k), "g32", "rstd"], writes=[key])

        def proj_fm(l, col0, ncols, evac, wsrc=None):
            wb, wkey = load_w("w_in", l, col0, n=ncols)
            for tc in range(NT):
                pi = 1 + (tc % 2)
                for k in range(8):
                    P.op("pe", lambda e, k=k, tc=tc, pi=pi: e.matmul(
                        ps[pi][0:ncols, :], lhsT=wb[:, k, 0:ncols], rhs=hT[:, k, tc * TC:(tc + 1) * TC],
                        start=(k == 0), stop=(k == 7)),
                        reads=[wkey, ("hT", k)], writes=[PS[pi]], inc=(k == 7))
                evac(tc, ps[pi][0:ncols, :], PS[pi])

        def proj_tm(l, col0, ncols, evac, ntok=128):
            wb, wkey = load_w("w_in", l, col0, n=ncols)
            for tt in range(S // ntok):
                pi = 1 + (tt % 2)
                for k in range(8):
                    P.op("pe", lambda e: e.matmul(ps[pi][0:ntok, 0:ncols], lhsT=hT[:, k, tt * ntok:(tt + 1) * ntok],
                                                  rhs=wb[:, k, 0:ncols], start=(k == 0), stop=(k == 7)),
                         reads=[wkey, ("hT", k)], writes=[PS[pi]], inc=(k == 7))
                evac(tt, ps[pi][0:ntok, 0:ncols], PS[pi])

        oneb = sb("oneb", [128, 1], F32)
        P.op("dve", lambda e: e.memset(oneb[:, :], 1.0), writes=["oneb"])

        def run_streams(gens, stagger=None):
            gens = list(gens)
            if stagger:
                for g, k in zip(gens, stagger):
                    for _ in range(k):
                        next(g)
            while gens:
                for g in list(gens):
                    try:
                        next(g)
                    except StopIteration:
                        gens.remove(g)

        def sb_attention(l):
            for hp in range(4):
                with ExitStack() as ph:
                    def sbp(name, shape, dt):
                        return ph.enter_context(nc.sbuf_tensor(_u + ("s_" + name), list(shape), dt))
                    qT = sbp("sbq", [128, S], BF16)
                    kTm = [sbp(f"sbk{i}", [128, S], BF16) for i in range(2)]
                    P.op("pool", lambda e: e.memset(kTm[0][64:128, :], 0.0), writes=["sbk"])
                    P.op("pool", lambda e: e.memset(kTm[1][0:64, :], 0.0), writes=["sbk"])
                    vtm = sbp("sbv", [128, 16, 128], BF16)
                    proj_fm(l, C_SBQ + hp * 128, 128, lambda tc, pap, pkey: P.op(
                        "act", lambda e: e.activation(out=qT[:, tc * TC:(tc + 1) * TC], in_=pap, func=AF.Copy, scale=0.125),
                        reads=[pkey], writes=["sbq"]))
                    def evk(tc, pap, pkey):
                        P.op("dve", lambda e: e.tensor_copy(out=kTm[0][0:64, tc * TC:(tc + 1) * TC], in_=pap[0:64, :]),
                             reads=[pkey], writes=["sbk"])
                        P.op("act", lambda e: e.activation(out=kTm[1][64:128, tc * TC:(tc + 1) * TC], in_=pap[64:128, :], func=AF.Copy),
                             reads=[pkey], writes=["sbk"])
                    proj_fm(l, C_SBK + hp * 128, 128, evk)
                    proj_tm(l, C_SBV + hp * 128, 128, lambda tt, pap, pkey: P.op(
                        "dve", lambda e: e.tensor_copy(out=vtm[:, tt, :], in_=pap),
                        reads=[pkey], writes=[("sbv", tt)]))
                    with ExitStack() as ph2:
                        def sbw(name, shape, dt):
                            return ph2.enter_context(nc.sbuf_tensor(_u + ("s_" + name), list(shape), dt))

                        import os as _os

                        def stream(sid, hh, qcs):
                            ez = sbw(f"ez{sid}", [128, TC], F32)
                            spb = sbw(f"spb{sid}", [128, TC], BF16)
                            eg = sbw(f"eg{sid}", [128, TC], BF16)
                            Gb = sbw(f"Gb{sid}", [128, TC], BF16)
                            wv = eg
                            K_wv = ("eg", sid)
                            K_ez, K_spb, K_eg, K_Gb = ("ez", sid), ("spb", sid), ("eg", sid), ("Gb", sid)
                            rs = slice(hh * 64, hh * 64 + 64)
                            pzg, po = 2 * sid, 2 * sid + 1
                            pgg = (4 + 2 * sid) if _os.environ.get('SB_SEPG') else pzg
                            yield
                            for qc in qcs:
                                q0 = qc * TC
                                kmax = qc * 4 + 3
                                pc0 = None
                                P.op("pool", lambda e: e.memset(wv[:, 0:384], 0.0), writes=[K_wv])
                                for kb in range(kmax, -1, -1):
                                    if kmax - kb >= int(_os.environ.get("SB_MAXIT", "99")):
                                        continue
                                    j = kb - qc * 4
                                    c0 = 128 * j if j >= 0 else 0
                                    cs = slice(c0, TC)
                                    tsl = slice(q0 + c0, q0 + TC)
                                    ksl = slice(kb * 128, (kb + 1) * 128)
                                    P.op("pe", lambda e: e.matmul(ps[pzg][:, cs], lhsT=kTm[hh][:, ksl], rhs=qT[:, tsl],
                                                                  start=True, stop=True),
                                         reads=["sbk", "sbq"], writes=[PS[pzg]])
                                    yield
                                    P.op("act", lambda e: e.activation(out=ez[:, cs], in_=ps[pzg][:, cs], func=AF.Exp),
                                         reads=[PS[pzg]], writes=[K_ez])
                                    yield
                                    P.op("act", lambda e: e.activation(out=spb[:, cs], in_=ez[:, cs], func=AF.Ln,
                                                                       bias=oneb[:, 0:1], scale=1.0),
                                         reads=[K_ez, "oneb"], writes=[K_spb])
                                    yield
                                    if j >= 0:
                                        P.op("dve", lambda e: e.tensor_tensor(out=spb[:, c0:c0 + 128], in0=spb[:, c0:c0 + 128],
                                                                              in1=cst["masksb"][:, :], op=ALU.mult),
                                             reads=[K_spb, "k_masksb"], writes=[K_spb])
                                    P.op("pe", lambda e: e.matmul(ps[pgg][:, cs], lhsT=cst["trineg"][:, :], rhs=spb[:, cs],
                                                                  start=True, stop=(pc0 is None)),
                                         reads=["k_trineg", K_spb], writes=[PS[pgg]], inc=(pc0 is None))
                                    if pc0 is not None:
                                        pcs = slice(pc0, TC)
                                        P.op("pe", lambda e: e.matmul(ps[pgg][:, pcs], lhsT=cst["e0"][:, :], rhs=Gb[:, pcs],
                                                                      start=False, stop=True),
                                             reads=["k_e0", K_Gb], writes=[PS[pgg]])
                                    yield
                                    P.op("act", lambda e: e.activation(out=eg[:, cs], in_=ps[pgg][:, cs], func=AF.Exp),
                                         reads=[PS[pgg]], writes=[K_eg])
                                    if kb > 0:
                                        P.op("dve", lambda e: e.tensor_copy(out=Gb[:, cs], in_=ps[pgg][:, cs]),
                                             reads=[PS[pgg], K_eg], writes=[K_Gb])
                                    yield
                                    P.op("dve", lambda e: e.tensor_tensor(out=wv[:, cs], in0=ez[:, cs], in1=eg[:, cs], op=ALU.mult),
                                         reads=[K_ez, K_eg], writes=[K_wv])
                                    if j >= 0:
                                        P.op("dve", lambda e: e.tensor_tensor(out=wv[:, c0:c0 + 128], in0=wv[:, c0:c0 + 128],
                                                                              in1=cst["masksb"][:, :], op=ALU.mult),
                                             reads=[K_wv, "k_masksb"], writes=[K_wv])
                                    yield
                                    P.op("pe", lambda e: e.matmul(ps[po][:, :], lhsT=vtm[:, kb, :], rhs=wv[:, :],
                                                                  start=(kb == kmax), stop=(kb == 0)),
                                         reads=[("sbv", kb), K_wv], writes=[PS[po]])
                                    pc0 = c0
                                    yield
                                P.op("dve", lambda e: e.tensor_copy(out=obT[rs, hp, q0:q0 + TC], in_=ps[po][rs, :]),
                                     reads=[PS[po]], writes=[("obT", hp)])
                                yield

                        _mode = _os.environ.get("SB_MODE", "4")
                        if _mode == "1":
                            run_streams([stream(0, 0, [3, 0, 2, 1])])
                            run_streams([stream(1, 1, [3, 0, 2, 1])])
                        elif _mode == "2":
                            run_streams([stream(0, 0, [3, 0, 2, 1]), stream(1, 1, [3, 0, 2, 1])])
                        else:
                            run_streams([stream(0, 0, [3, 0]), stream(1, 1, [3, 0]), stream(2, 0, [2, 1]), stream(3, 1, [2, 1])],
                                        stagger=[int(x) for x in _os.environ.get('SB_STAG', '0,2,4,6').split(',')])
                        phase_barrier()
                    with ExitStack() as ph3:
                        zs = [ph3.enter_context(nc.sbuf_tensor(_u + f"s_sbzs{i}", [128, TC], BF16)) for i in range(2)]

                        def evz(tc, pap, pkey):
                            b = tc % 2
                            P.op("act", lambda e: e.activation(out=zs[b][:, :], in_=pap, func=AF.Silu),
                                 reads=[pkey], writes=[("sbzs", b)])
                            tsl = slice(tc * TC, (tc + 1) * TC)
                            P.op("dve", lambda e: e.tensor_tensor(out=obT[:, hp, tsl], in0=obT[:, hp, tsl], in1=zs[b][:, :], op=ALU.mult),
                                 reads=[("sbzs", b)], writes=[("obT", hp)])
                        proj_fm(l, C_SBZ + hp * 128, 128, evz)
                        phase_barrier()

        def retention(l):
            TWO_PI = 2.0 * math.pi
            C1 = 6.28125
            C2 = TWO_PI - C1
            with ExitStack() as ph0:
                def sb0(name, shape, dt):
                    return ph0.enter_context(nc.sbuf_tensor(_u + ("s_" + name), list(shape), dt))
                qrT = sb0("rqr", [128, 2, S], BF16)
                krT = sb0("rkr", [128, 2, S], BF16)
                rtdt = sb0("rtdt", [128, 4, 128], F32)
                rtcd = sb0("rtcd", [128, 2, 128], F32)
                P.dma(rtdt[:, :, :], rtdt_d[:, :, :], writes=["rtdt"], stream="c")
                P.dma(rtcd[:, :, :], rtcd_d[:, :, :], writes=["rtcd"], stream="c")
                with ExitStack() as ph1:
                    def sb1(name, shape, dt):
                        return ph1.enter_context(nc.sbuf_tensor(_u + ("s_" + name), list(shape), dt))
                    COS2 = sb1("rcos", [128, S], BF16)
                    SIN2 = sb1("rsin", [128, S], BF16)
                    pint = sb1("rpint", [128, TC], I32)
                    ta = sb1("rta", [128, TC], F32)
                    tk = sb1("rtk", [128, TC], F32)
                    tm = sb1("rtm", [128, 256], F32)[:, :]
                    ki = sb1("rki", [128, 256], I32)[:, :]
                    ta_f, tk_f, pint_f = ta, tk, pint
                    for tc in range(8):
                        tsl = slice(tc * 256, (tc + 1) * 256)
                        ta, tk, pint = ta_f[:, 0:256], tk_f[:, 0:256], pint_f[:, 0:256]
                        P.dma(pint, pos_d[:, tsl], writes=["rpint"], stream="x")
                        P.op("dve", lambda e: e.tensor_scalar(out=ta, in0=pint, scalar1=rtrp[:, 0:1], scalar2=None,
                                                              op0=ALU.mult), reads=["rpint", "rtrp"], writes=["rta"])
                        P.op("dve", lambda e: e.tensor_scalar(out=ki, in0=ta, scalar1=float(1.0 / TWO_PI),
                                                              scalar2=None, op0=ALU.mult), reads=["rta"], writes=["rki"])
                        P.op("dve", lambda e: e.tensor_copy(out=tk, in_=ki), reads=["rki"], writes=["rtk"])
                        P.op("dve", lambda e: e.scalar_tensor_tensor(out=ta, in0=tk, scalar=-C1, in1=ta,
                                                                     op0=ALU.mult, op1=ALU.add),
                             reads=["rtk", "rta"], writes=["rta"])
                        P.op("dve", lambda e: e.scalar_tensor_tensor(out=ta, in0=tk, scalar=-C2, in1=ta,
                                                                     op0=ALU.mult, op1=ALU.add),
                             reads=["rtk", "rta"], writes=["rta"])
                        P.op("dve", lambda e: e.tensor_single_scalar(out=tm, in_=ta, scalar=float(math.pi),
                                                                     op=ALU.is_gt), reads=["rta"], writes=["rtm"])
                        P.op("dve", lambda e: e.scalar_tensor_tensor(out=ta, in0=tm, scalar=-TWO_PI, in1=ta,
                                                                     op0=ALU.mult, op1=ALU.add),
                             reads=["rtm", "rta"], writes=["rta"])
                        P.op("dve", lambda e: e.tensor_single_scalar(out=tm, in_=ta, scalar=float(-math.pi),
                                                                     op=ALU.is_lt), reads=["rta"], writes=["rtm"])
                        P.op("dve", lambda e: e.scalar_tensor_tensor(out=ta, in0=tm, scalar=TWO_PI, in1=ta,
                                                                     op0=ALU.mult, op1=ALU.add),
                             reads=["rtm", "rta"], writes=["rta"])
                        P.op("act", lambda e: e.activation(out=SIN2[:, tsl], in_=ta, func=AF.Sin, scale=rtrp[:, 1:2]),
                             reads=["rta", "rtrp"], writes=["rsin"])
                        P.op("dve", lambda e: e.tensor_single_scalar(out=tm, in_=ta, scalar=float(math.pi / 2),
                                                                     op=ALU.is_gt), reads=["rta"], writes=["rtm"])
                        P.op("dve", lambda e: e.scalar_tensor_tensor(out=ta, in0=tm, scalar=-TWO_PI, in1=ta,
                                                                     op0=ALU.mult, op1=ALU.add),
                             reads=["rtm", "rta"], writes=["rta"])
                        P.op("act", lambda e: e.activation(out=COS2[:, tsl], in_=ta, func=AF.Sin, bias=rtrp[:, 2:3],
                                                           scale=1.0), reads=["rta", "rtrp"], writes=["rcos"])
                    ta, tk, pint = ta_f, tk_f, pint_f
                    for which, (c_base, dstT, scl) in enumerate([(C_RTQ, qrT, 1.0), (C_RTK, krT, 0.125)]):
                        for hp in range(2):
                            wb, wkey = load_w("w_in", l, c_base + hp * 128)
                            jsw = wbcount[0] % NWB
                            wbcount[0] += 1
                            wsw = wbp[jsw]
                            for hh in range(2):
                                o = hh * 64
                                P.op("pool", lambda e: e.tensor_copy(out=wsw[:, :, o:o + 32], in_=wb[:, :, o + 32:o + 64]),
                                     reads=[wkey], writes=[("wb", jsw)])
                                P.op("pool", lambda e: e.tensor_copy(out=wsw[:, :, o + 32:o + 64], in_=wb[:, :, o:o + 32]),
                                     reads=[wkey], writes=[("wb", jsw)])
                            for tc in range(NT):
                                tsl = slice(tc * TC, (tc + 1) * TC)
                                for k in range(8):
                                    P.op("pe", lambda e: e.matmul(ps[1][:, :], lhsT=wb[:, k, :], rhs=hT[:, k, tsl],
                                                                  start=(k == 0), stop=(k == 7)),
                                         reads=[wkey, ("hT", k)], writes=[PS[1]], inc=(k == 7))
                                for k in range(8):
                                    P.op("pe", lambda e: e.matmul(ps[2][:, :], lhsT=wsw[:, k, :], rhs=hT[:, k, tsl],
                                                                  start=(k == 0), stop=(k == 7)),
                                         reads=[("wb", jsw), ("hT", k)], writes=[PS[2]], inc=(k == 7))
                                P.op("dve", lambda e: e.scalar_tensor_tensor(out=ta[:, :], in0=ps[1][:, :], scalar=float(scl),
                                                                             in1=COS2[:, tsl], op0=ALU.mult, op1=ALU.mult),
                                     reads=[PS[1], "rcos"], writes=["rta"])
                                P.op("dve", lambda e: e.scalar_tensor_tensor(out=tk[:, :], in0=ps[2][:, :], scalar=float(scl),
                                                                             in1=SIN2[:, tsl], op0=ALU.mult, op1=ALU.mult),
                                     reads=[PS[2], "rsin"], writes=["rtk"])
                                P.op("dve", lambda e: e.tensor_tensor(out=dstT[:, hp, tsl], in0=ta[:, :], in1=tk[:, :], op=ALU.add),
                                     reads=["rta", "rtk"], writes=[("rq", which, hp)])
                    phase_barrier()
                for hp in range(2):
                    with ExitStack() as ph2:
                        def sb2(name, shape, dt):
                            return ph2.enter_context(nc.sbuf_tensor(_u + ("s_" + name), list(shape), dt))
                        kdtm = sb2("rkd", [128, 16, 128], BF16)
                        vtm = sb2("rv", [128, 16, 128], BF16)
                        zsT = sb2("rz", [128, S], BF16)
                        qc = [sb2(f"rqc{i}", [128, 128], BF16) for i in range(2)]
                        scm = [sb2(f"rscm{i}", [128, 128], BF16) for i in range(2)]
                        Sf = sb2("rSf", [128, 128], F32)
                        Sb = sb2("rSb", [128, 128], BF16)
                        ob = sb2("rob", [128, TC], BF16)
                        sq = sb2("rsq", [128, TC], BF16)
                        rs_t = rstd
                        tt_t = sb2("rtt", [128, TC], F32)
                        for n in range(16):
                            nsl = slice(n * 128, (n + 1) * 128)
                            pk = ps[3][:, 0:64].bitcast(BF16)
                            P.op("pe", lambda e: e.transpose(pk, krT[:, hp, nsl], cst["ident"][:, :]),
                                 reads=[("rq", 1, hp), "k_ident"], writes=[PS[3]])
                            for hh in range(2):
                                cs = slice(hh * 64, (hh + 1) * 64)
                                h = 2 * hp + hh
                                P.op("dve", lambda e: e.tensor_scalar(out=kdtm[:, n, cs], in0=pk[:, cs], scalar1=rtsd[:, h:h + 1],
                                                                      scalar2=None, op0=ALU.mult),
                                     reads=[PS[3], "rtsd"], writes=[("rkd", n)])
                        for hh in range(2):
                            h = 2 * hp + hh
                            rs = slice(hh * 64, (hh + 1) * 64)
                            proj_tm(l, C_RTV + h * 128, 128, lambda tt, pap, pkey: P.op(
                                "dve", lambda e: e.tensor_copy(out=vtm[:, tt, :], in_=pap), reads=[pkey], writes=[("rv", tt)]))
                            proj_fm(l, C_RTZ + h * 128, 128, lambda tc, pap, pkey: P.op(
                                "act", lambda e: e.activation(out=zsT[:, tc * TC:(tc + 1) * TC], in_=pap, func=AF.Silu),
                                reads=[pkey], writes=["rz"]))
                            gch = float((1.0 - 2.0 ** (-5.0 - h)) ** 128)
                            SBANK = [7, 3]

                            def emit_sc(n):
                                nsl = slice(n * 128, (n + 1) * 128)
                                b = n % 2
                                P.op("pe", lambda e: e.matmul(ps[4][:, 0:128], lhsT=krT[rs, hp, nsl], rhs=qrT[rs, hp, nsl],
                                                              start=True, stop=True),
                                     reads=[("rq", 1, hp), ("rq", 0, hp)], writes=[PS[4]])
                                P.op("dve", lambda e: e.tensor_tensor(out=scm[b][:, :], in0=ps[4][:, 0:128], in1=rtdt[:, h, :],
                                                                      op=ALU.mult),
                                     reads=[PS[4], "rtdt"], writes=[("rscm", b)])
                                if n > 0:
                                    P.op("dve", lambda e: e.tensor_tensor(out=qc[b][rs, :], in0=qrT[rs, hp, nsl], in1=rtcd[rs, hp, :],
                                                                          op=ALU.mult),
                                         reads=[("rq", 0, hp), "rtcd"], writes=[("rqc", b)])

                            def emit_smm(n):
                                sbk = SBANK[n % 2]
                                P.op("pe", lambda e: e.matmul(ps[sbk][rs, 0:128], lhsT=kdtm[:, n, rs], rhs=vtm[:, n, :],
                                                              start=True, stop=True),
                                     reads=[("rkd", n), ("rv", n)], writes=[PS[sbk]])

                            emit_sc(0)
                            emit_smm(0)
                            for n in range(16):
                                nsl = slice(n * 128, (n + 1) * 128)
                                b = n % 2
                                csl = slice((n % 4) * 128, (n % 4 + 1) * 128)
                                po = 5 + (n // 4) % 2
                                if n + 1 < 15:
                                    emit_smm(n + 1)
                                P.op("pe", lambda e: e.matmul(ps[po][:, csl], lhsT=vtm[:, n, :], rhs=scm[b][:, :],
                                                              start=True, stop=(n == 0)),
                                     reads=[("rv", n), ("rscm", b)], writes=[PS[po]], inc=(n == 0))
                                if n > 0:
                                    P.op("pe", lambda e: e.matmul(ps[po][:, csl], lhsT=Sb[rs, :], rhs=qc[b][rs, :],
                                                                  start=False, stop=True),
                                         reads=["rSb", ("rqc", b)], writes=[PS[po]])
                                if n < 15:
                                    sbk = SBANK[n % 2]
                                    if n == 0:
                                        P.op("dve", lambda e: e.tensor_copy(out=Sf[rs, :], in_=ps[sbk][rs, 0:128]),
                                             reads=[PS[sbk]], writes=["rSf"])
                                    else:
                                        P.op("dve", lambda e: e.scalar_tensor_tensor(out=Sf[rs, :], in0=Sf[rs, :], scalar=gch,
                                                                                     in1=ps[sbk][rs, 0:128], op0=ALU.mult,
                                                                                     op1=ALU.add),
                                             reads=[PS[sbk], "rSf"], writes=["rSf"])
                                    P.op("act", lambda e: e.activation(out=Sb[rs, :], in_=Sf[rs, :], func=AF.Copy),
                                         reads=["rSf"], writes=["rSb"])
                                if n + 1 < 16:
                                    emit_sc(n + 1)
                                if n % 4 == 3:
                                    tc = n // 4
                                    tsl = slice(tc * TC, (tc + 1) * TC)
                                    P.op("act", lambda e: e.activation(out=ob[:, :], in_=ps[po][:, :], func=AF.Copy),
                                         reads=[PS[po]], writes=["rob"])
                                    P.op("pe", lambda e: e.matmul(ps[1][:, :], lhsT=rtcn[:, :], rhs=ob[:, :], start=True, stop=True),
                                         reads=["rtcn", "rob"], writes=[PS[1]])
                                    P.op("act", lambda e: e.activation(out=sq[:, :], in_=ps[1][:, :], func=AF.Square),
                                         reads=[PS[1]], writes=["rsq"])
                                    P.op("pe", lambda e: e.matmul(ps[2][:, :], lhsT=cst["ones"][:, :], rhs=sq[:, :],
                                                                  start=True, stop=True),
                                         reads=["k_ones", "rsq"], writes=[PS[2]])
                                    P.op("act", lambda e: e.activation(out=rs_t[:, :], in_=ps[2][:, :], func=AF.Ln,
                                                                       bias=rtrp[:, 3:4], scale=float(1.0 / 128.0)),
                                         reads=[PS[2], "rtrp"], writes=["rstd"])
                                    P.op("act", lambda e: e.activation(out=rs_t[:, :], in_=rs_t[:, :], func=AF.Exp, scale=-0.5),
                                         reads=["rstd"], writes=["rstd"])
                                    P.op("dve", lambda e: e.scalar_tensor_tensor(out=tt_t[:, :], in0=ps[1][:, :],
                                                                                 scalar=retg[:, l, h:h + 1], in1=rs_t[:, :],
                                                                                 op0=ALU.mult, op1=ALU.mult),
                                         reads=[PS[1], "retg", "rstd"], writes=["rtt"])
                                    P.op("dve", lambda e: e.tensor_tensor(out=obT[:, h, tsl], in0=tt_t[:, :], in1=zsT[:, tsl],
                                                                          op=ALU.mult),
                                         reads=["rtt", "rz"], writes=[("obT", h)])
                        phase_barrier()

        def deltanet(l):
            phase_barrier(full=True)
            H = slice(0, 64)
            with ExitStack() as ph0:
                def sb0(name, shape, dt):
                    return ph0.enter_context(nc.sbuf_tensor(_u + ("s_" + name), list(shape), dt))
                dnc = sb0("dnc", [128, 4, 128], F32)
                P.dma(dnc[:, :, :], dnc_d[:, :, :], writes=["dnc"], stream="c")
                identf, onesf = dnc[:, 0, :], dnc[:, 1, :]
                mincl, mstrict = dnc[0:64, 2, 0:64], dnc[0:64, 3, 0:64]
                abraw = sb0("dab", [64, 32, 8], F32)
                rep = sb0("drep", [64, 128], F32)
                t1 = sb0("dt1", [64, 32, 4], F32)
                g_tm = sb0("dg", [64, 32, 4], F32)
                beta_tm = sb0("dbeta", [64, 32, 4], F32)
                gc_tm = sb0("dgc", [64, 32, 4], F32)
                egl = sb0("degl", [128, 32, 4], F32)
                bg_tm = sb0("dbg", [64, 32, 4], F32)
                ed_tm = sb0("ded", [64, 32, 4], F32)
                proj_tm(l, C_DNA, 8, lambda tt, pap, pkey: P.op(
                    "dve", lambda e: e.tensor_copy(out=abraw[:, tt, :], in_=pap), reads=[pkey], writes=["dab"]), ntok=64)
                fl = lambda t: t[:, :, :].rearrange("p a b -> p (a b)")
                P.dma(rep[:, :], dndt_d[l, 0:64, :], writes=["drep"], stream="c")
                P.op("dve", lambda e: e.tensor_tensor(out=t1[:, :, :], in0=abraw[:, :, 0:4],
                                                      in1=rep[:, :].rearrange("p (a b) -> p a b", b=4), op=ALU.add),
                     reads=["dab", "drep"], writes=["dt1"])
                P.op("act", lambda e: e.activation(out=t1[:, :, :], in_=t1[:, :, :], func=AF.Exp), reads=["dt1"], writes=["dt1"])
                P.op("act", lambda e: e.activation(out=t1[:, :, :], in_=t1[:, :, :], func=AF.Ln, bias=oneb[0:64, 0:1], scale=1.0),
                     reads=["dt1", "oneb"], writes=["dt1"])
                P.dma(rep[:, :], dnal_d[l, 0:64, :], writes=["drep"], stream="c")
                P.op("act", lambda e: e.activation(out=rep[:, :], in_=rep[:, :], func=AF.Exp), reads=["drep"], writes=["drep"])
                P.op("dve", lambda e: e.scalar_tensor_tensor(out=g_tm[:, :, :], in0=t1[:, :, :], scalar=-1.0,
                                                             in1=rep[:, :].rearrange("p (a b) -> p a b", b=4),
                                                             op0=ALU.mult, op1=ALU.mult),
                     reads=["dt1", "drep"], writes=["dg"])
                P.op("act", lambda e: e.activation(out=beta_tm[:, :, :], in_=abraw[:, :, 4:8], func=AF.Sigmoid),
                     reads=["dab"], writes=["dbeta"])
                P.op("pe", lambda e: e.matmul(ps[0][0:64, 0:128], lhsT=dnc[0:64, 2, 0:64], rhs=fl(g_tm), start=True, stop=True),
                     reads=["dnc", "dg"], writes=[PS[0]])
                P.op("dve", lambda e: e.tensor_copy(out=fl(gc_tm), in_=ps[0][0:64, 0:128]), reads=[PS[0]], writes=["dgc"])
                P.op("pe", lambda e: e.matmul(ps[0][:, 128:256], lhsT=dnc[0:64, 1, :], rhs=fl(g_tm), start=True, stop=True),
                     reads=["dnc", "dg"], writes=[PS[0]])
                P.op("dve", lambda e: e.tensor_tensor(out=fl(ed_tm), in0=ps[0][0:64, 128:256], in1=fl(gc_tm), op=ALU.subtract),
                     reads=[PS[0], "dgc"], writes=["ded"])
                P.op("act", lambda e: e.activation(out=fl(ed_tm), in_=fl(ed_tm), func=AF.Exp), reads=["ded"], writes=["ded"])
                P.op("act", lambda e: e.activation(out=fl(egl), in_=ps[0][:, 128:256], func=AF.Exp), reads=[PS[0]], writes=["degl"])
                P.op("act", lambda e: e.activation(out=fl(bg_tm), in_=fl(gc_tm), func=AF.Exp), reads=["dgc"], writes=["dbg"])
                P.op("dve", lambda e: e.tensor_tensor(out=fl(bg_tm), in0=fl(bg_tm), in1=fl(beta_tm), op=ALU.mult),
                     reads=["dbg", "dbeta"], writes=["dbg"])
                phase_barrier()
                for h in range(4):
                    with ExitStack() as ph1:
                        def sb1(name, shape, dt):
                            return ph1.enter_context(nc.sbuf_tensor(_u + ("s_" + name), list(shape), dt))
                        qT, kT, vT, zsT = mT[:, 0, :], mT[:, 1, :], mT[:, 2, :], mT[:, 3, :]
                        with ExitStack() as ph2:
                            xpads = [mT[:, 3:6, :].rearrange("p a b -> p (a b)").bitcast(F32)[:, 0:S + 3],
                                     ph2.enter_context(nc.sbuf_tensor(_u + "s_dxpad1", [128, S + 3], F32))[:, :],
                                     ph2.enter_context(nc.sbuf_tensor(_u + "s_dxpad2", [128, S + 3], F32))[:, :]]
                            accs = [ph2.enter_context(nc.sbuf_tensor(_u + "s_dacc0", [128, S], F32))[:, :],
                                    ph2.enter_context(nc.sbuf_tensor(_u + "s_dacc1", [128, S], F32))[:, :],
                                    mT[:, 6:8, :].rearrange("p a b -> p (a b)").bitcast(F32)]

                            def d1_stream(xi, c_base, dstT):
                                xpad, acc = xpads[xi], accs[xi]
                                kacc = ("dacc", xi)
                                pbank = [1 + 2 * xi, 2 + 2 * xi]
                                pn = 0 if xi == 0 else 7
                                P.op("dve", lambda e: e.memset(xpad[:, 0:3], 0.0), writes=[("dxpad0", xi)])
                                wb, wkey = load_w("w_in", l, c_base + h * 128)
                                yield
                                for tc in range(NT):
                                    pi = pbank[tc % 2]
                                    for k in range(8):
                                        P.op("pe", lambda e: e.matmul(ps[pi][:, :], lhsT=wb[:, k, :], rhs=hT[:, k, tc * TC:(tc + 1) * TC],
                                                                      start=(k == 0), stop=(k == 7)),
                                             reads=[wkey, ("hT", k)], writes=[PS[pi]], inc=(k == 7))
                                    yield
                                    P.op("act", lambda e: e.activation(out=xpad[:, 3 + tc * TC:3 + (tc + 1) * TC], in_=ps[pi][:, :], func=AF.Copy),
                                         reads=[PS[pi]], writes=[("dxpad", xi, tc)])
                                    yield
                                allx = [("dxpad", xi, tc) for tc in range(NT)] + [("dxpad0", xi)]
                                cw = dncw[:, l, xi * 4 + h, :]
                                P.op("dve", lambda e: e.tensor_scalar(out=acc, in0=xpad[:, 3:3 + S], scalar1=cw[:, 3:4],
                                                                      scalar2=None, op0=ALU.mult),
                                     reads=allx + ["dncw"], writes=[kacc])
                                yield
                                for j in range(3):
                                    P.op("dve", lambda e: e.scalar_tensor_tensor(out=acc, in0=xpad[:, j:j + S],
                                                                                 scalar=cw[:, j:j + 1], in1=acc,
                                                                                 op0=ALU.mult, op1=ALU.add),
                                         reads=allx + ["dncw", kacc], writes=[kacc])
                                    yield
                                P.op("act", lambda e: e.activation(out=acc, in_=acc, func=AF.Silu), reads=[kacc], writes=[kacc])
                                yield
                                if xi == 2:
                                    P.op("dve", lambda e: e.tensor_copy(out=vT, in_=acc), reads=[kacc], writes=["dvT"])
                                    yield
                                    return
                                sqi = sqt[:, xi, :]
                                ksq = ("sqt", xi)
                                for tc in range(NT):
                                    tsl = slice(tc * TC, (tc + 1) * TC)
                                    P.op("act", lambda e: e.activation(out=sqi, in_=acc[:, tsl], func=AF.Square),
                                         reads=[kacc], writes=[ksq])
                                    yield
                                    P.op("pe", lambda e: e.matmul(ps[pn][:, :], lhsT=cst["ones"][:, :], rhs=sqi, start=True, stop=True),
                                         reads=[ksq, "k_ones"], writes=[PS[pn]])
                                    yield
                                    P.op("act", lambda e: e.activation(out=rstd[:, :], in_=ps[pn][:, :], func=AF.Ln,
                                                                       bias=rtrp[:, 3:4], scale=1.0),
                                         reads=[PS[pn], "rtrp"], writes=["rstd"])
                                    P.op("act", lambda e: e.activation(out=rstd[:, :], in_=rstd[:, :], func=AF.Exp, scale=-0.5),
                                         reads=["rstd"], writes=["rstd"])
                                    sc = float(128.0 ** -0.5) if xi == 0 else 1.0
                                    P.op("dve", lambda e: e.scalar_tensor_tensor(out=dstT[:, tsl], in0=acc[:, tsl], scalar=sc,
                                                                                 in1=rstd[:, :], op0=ALU.mult, op1=ALU.mult),
                                         reads=[kacc, "rstd"], writes=["dqT" if xi == 0 else "dkT"])
                                    yield

                            run_streams([d1_stream(0, C_DNQ, qT), d1_stream(1, C_DNK, kT), d1_stream(2, C_DNV, vT)])
                            phase_barrier()
                        proj_fm(l, C_DNZ + h * 128, 128, lambda tc, pap, pkey: P.op(
                            "act", lambda e: e.activation(out=zsT[:, tc * TC:(tc + 1) * TC], in_=pap, func=AF.Silu),
                            reads=[pkey], writes=["dz"]))
                        NB = 8
                        B3 = [64, NB, 64]
                        gb = sb1("dgb", [64, NB, 64], F32)
                        dd = sb1("ddd", [64, NB, 64], F32)
                        LT = sb1("dLT", [64, NB, 64], F32)
                        egcb = sb1("degcb", [128, NB * 64], F32)
                        bs = sb1("dbs", [64, NB, 64], BF16)
                        Pm = sb1("dPm", [64, NB, 64], BF16)
                        PTm = sb1("dPTm", [64, NB, 64], BF16)
                        Tt = [sb1(f"dTt{i}", [64, NB, 64], BF16) for i in range(2)]
                        kbg = mT[0:64, 7, 0:1024].rearrange("p (a b) -> p a b", b=128)
                        vb = mT[0:64, 7, 1024:2048].rearrange("p (a b) -> p a b", b=128)
                        aT = sb1("daT", [64, NB, 64], BF16)
                        aTt = sb1("daTt", [64, NB, 64], BF16)
                        qg = sb1("dqg", [128, NB * 64], BF16)
                        u_sb = sb1("du", [64, NB, 128], BF16)
                        wT = sb1("dwT", [128, NB, 64], BF16)
                        kd = sb1("dkd", [64, NB, 128], BF16)
                        vnew = [sb1(f"dvnew{i}", [64, 128], BF16) for i in range(2)]
                        Sf = sb1("dSf", [128, 128], F32)
                        Sb = sb1("dSb", [128, 128], BF16)
                        nsq = sb1("dnsq", [128, TC], BF16)
                        ntmp = sb1("dntmp", [128, TC], F32)
                        identb = cst["ident"]
                        id64 = identb[0:64, 0:64]
                        fl3 = lambda t: t[:, :, :].rearrange("p a b -> p (a b)")
                        p3 = lambda i, w=64: ps[i][H, 0:NB * w].rearrange("p (a b) -> p a b", b=w)
                        bcast = lambda ap2: ap2.unsqueeze(1).broadcast_to(B3)

                        def stage_a(bt):
                            n0 = bt * NB
                            bsl = slice(n0 * 64, (n0 + NB) * 64)
                            nsl = slice(n0, n0 + NB)
                            sc_g = g_tm[:, nsl, h:h + 1].broadcast_to(B3)
                            sc_b = beta_tm[:, nsl, h:h + 1].broadcast_to(B3)
                            sc_gc = gc_tm[:, nsl, h:h + 1].broadcast_to(B3)
                            P.op("dve", lambda e: e.tensor_tensor(out=gb[:, :, :], in0=bcast(mincl), in1=sc_g, op=ALU.mult),
                                 reads=["dnc", "dg"], writes=["dgb"])
                            P.op("pe", lambda e: e.matmul(ps[0][:, :], lhsT=onesf[0:64, :], rhs=fl3(gb), start=True, stop=True),
                                 reads=["dnc", "dgb"], writes=[PS[0]])
                            yield
                            P.op("dve", lambda e: e.tensor_tensor(out=gb[:, :, :], in0=bcast(identf[0:64, 0:64]), in1=sc_b, op=ALU.mult),
                                 reads=["dnc", "dbeta"], writes=["dgb"])
                            P.op("pe", lambda e: e.matmul(ps[1][H, :], lhsT=onesf[0:64, 0:64], rhs=fl3(gb), start=True, stop=True),
                                 reads=["dnc", "dgb"], writes=[PS[1]])
                            yield
                            P.op("act", lambda e: e.activation(out=egcb[:, :], in_=ps[0][:, :], func=AF.Exp),
                                 reads=[PS[0]], writes=["degcb"])
                            P.op("dve", lambda e: e.tensor_tensor(out=dd[:, :, :], in0=p3(0), in1=sc_gc, op=ALU.subtract),
                                 reads=[PS[0], "dgc", "degcb"], writes=["ddd"])
                            yield
                            P.op("act", lambda e: e.activation(out=fl3(dd), in_=fl3(dd), func=AF.Exp), reads=["ddd"], writes=["ddd"])
                            P.op("dve", lambda e: e.tensor_tensor(out=bs[:, :, :], in0=p3(1), in1=bcast(mstrict), op=ALU.mult),
                                 reads=[PS[1], "dnc"], writes=["dbs"])
                            yield
                            P.op("dve", lambda e: e.scalar_tensor_tensor(out=dd[:, :, :], in0=dd[:, :, :], scalar=1.0, in1=bcast(mincl),
                                                                         op0=ALU.min, op1=ALU.mult),
                                 reads=["ddd", "dnc"], writes=["ddd"])
                            for i in range(NB):
                                csl = slice((n0 + i) * 64, (n0 + i + 1) * 64)
                                P.op("pe", lambda e: e.matmul(ps[2][H, i * 64:(i + 1) * 64], lhsT=kT[:, csl], rhs=kT[:, csl], start=True, stop=True),
                                     reads=["dkT"], writes=[PS[2]], inc=(i == NB - 1))
                            for i in range(NB):
                                csl = slice((n0 + i) * 64, (n0 + i + 1) * 64)
                                P.op("pe", lambda e: e.matmul(ps[3][H, i * 64:(i + 1) * 64], lhsT=kT[:, csl], rhs=qT[:, csl], start=True, stop=True),
                                     reads=["dkT", "dqT"], writes=[PS[3]], inc=(i == NB - 1))
                            yield
                            P.op("dve", lambda e: e.tensor_tensor(out=LT[:, :, :], in0=p3(2), in1=dd[:, :, :], op=ALU.mult),
                                 reads=[PS[2], "ddd"], writes=["dLT"])
                            P.op("dve", lambda e: e.tensor_tensor(out=aTt[:, :, :], in0=p3(3), in1=dd[:, :, :], op=ALU.mult),
                                 reads=[PS[3], "ddd"], writes=["daTt"])
                            yield
                            P.op("dve", lambda e: e.scalar_tensor_tensor(out=Pm[:, :, :], in0=LT[:, :, :], scalar=-1.0, in1=bs[:, :, :],
                                                                         op0=ALU.mult, op1=ALU.mult),
                                 reads=["dLT", "dbs"], writes=["dPm"])
                            yield
                            ptv = ps[4][H, 0:NB * 32].bitcast(BF16)
                            for i in range(NB):
                                P.op("pe", lambda e: e.transpose(ptv[:, i * 64:(i + 1) * 64], Pm[:, i, :], id64),
                                     reads=["dPm", "k_ident"], writes=[PS[4]], inc=(i == NB - 1))
                            P.op("dve", lambda e: e.tensor_tensor(out=Tt[0][:, :, :], in0=Pm[:, :, :], in1=bcast(id64), op=ALU.add),
                                 reads=["dPm", "k_ident"], writes=[("dTt", 0)])
                            yield
                            P.op("act", lambda e: e.activation(out=fl3(PTm), in_=ptv, func=AF.Copy), reads=[PS[4]], writes=["dPTm"])
                            yield
                            cur = 0
                            for lev in range(1, 6):
                                if lev < 5:
                                    for i in range(NB):
                                        P.op("pe", lambda e: e.matmul(ps[2][H, i * 64:(i + 1) * 64], lhsT=PTm[:, i, :], rhs=Pm[:, i, :],
                                                                      start=True, stop=True),
                                             reads=["dPm", "dPTm"], writes=[PS[2]], inc=(i == NB - 1))
                                for i in range(NB):
                                    P.op("pe", lambda e: e.matmul(ps[3][H, i * 64:(i + 1) * 64], lhsT=Pm[:, i, :], rhs=PTm[:, i, :],
                                                                  start=True, stop=True),
                                         reads=["dPm", "dPTm"], writes=[PS[3]], inc=(i == NB - 1))
                                yield
                                if lev < 5:
                                    P.op("act", lambda e: e.activation(out=fl3(Pm), in_=ps[2][H, :], func=AF.Copy), reads=[PS[2]], writes=["dPm"])
                                P.op("dve", lambda e: e.tensor_copy(out=fl3(PTm), in_=ps[3][H, :]), reads=[PS[3]], writes=["dPTm"])
                                yield
                                for i in range(NB):
                                    P.op("pe", lambda e: e.matmul(ps[4][H, i * 64:(i + 1) * 64], lhsT=PTm[:, i, :], rhs=Tt[cur][:, i, :],
                                                                  start=True, stop=False),
                                         reads=["dPTm", ("dTt", cur)], writes=[PS[4]], inc=False)
                                    P.op("pe", lambda e: e.matmul(ps[4][H, i * 64:(i + 1) * 64], lhsT=id64, rhs=Tt[cur][:, i, :],
                                                                  start=False, stop=True),
                                         reads=["k_ident", ("dTt", cur)], writes=[PS[4]], inc=(i == NB - 1))
                                yield
                                cur = 1 - cur
                                P.op("dve", lambda e: e.tensor_copy(out=fl3(Tt[cur]), in_=ps[4][H, :]), reads=[PS[4]], writes=[("dTt", cur)])
                                yield
                            kv = ps[0][H, :].bitcast(BF16)
                            vv = ps[1][H, :].bitcast(BF16)
                            for i in range(NB):
                                csl = slice((n0 + i) * 64, (n0 + i + 1) * 64)
                                P.op("pe", lambda e: e.transpose(kv[:, i * 128:(i + 1) * 128], kT[:, csl], identb[:, :]),
                                     reads=["dkT", "k_ident"], writes=[PS[0]], inc=(i == NB - 1))
                            for i in range(NB):
                                csl = slice((n0 + i) * 64, (n0 + i + 1) * 64)
                                P.op("pe", lambda e: e.transpose(vv[:, i * 128:(i + 1) * 128], vT[:, csl], identb[:, :]),
                                     reads=["dvT", "k_ident"], writes=[PS[1]], inc=(i == NB - 1))
                            yield
                            B3w = [64, NB, 128]
                            kv3 = kv.rearrange("p (a b) -> p a b", b=128)
                            vv3 = vv.rearrange("p (a b) -> p a b", b=128)
                            P.op("dve", lambda e: e.tensor_tensor(out=kbg[:, :, :], in0=kv3, in1=bg_tm[:, nsl, h:h + 1].broadcast_to(B3w), op=ALU.mult),
                                 reads=[PS[0], "dbg"], writes=["dkbg"])
                            P.op("dve", lambda e: e.tensor_tensor(out=vb[:, :, :], in0=vv3, in1=beta_tm[:, nsl, h:h + 1].broadcast_to(B3w), op=ALU.mult),
                                 reads=[PS[1], "dbeta"], writes=["dvb"])
                            yield
                            for i in range(NB):
                                pb = 2 + i // 4
                                P.op("pe", lambda e: e.matmul(ps[pb][H, (i % 4) * 128:(i % 4 + 1) * 128], lhsT=Tt[cur][:, i, :], rhs=vb[:, i, :],
                                                              start=True, stop=True),
                                     reads=[("dTt", cur), "dvb"], writes=[PS[pb]], inc=(i % 4 == 3))
                            for i in range(NB):
                                P.op("pe", lambda e: e.matmul(ps[4][:, i * 64:(i + 1) * 64], lhsT=kbg[:, i, :], rhs=Tt[cur][:, i, :],
                                                              start=True, stop=True),
                                     reads=[("dTt", cur), "dkbg"], writes=[PS[4]], inc=(i == NB - 1))
                            yield "OUT"
                            P.op("dve", lambda e: e.tensor_tensor(out=kd[:, :, :], in0=kv3, in1=ed_tm[:, nsl, h:h + 1].broadcast_to(B3w), op=ALU.mult),
                                 reads=[PS[0], "ded"], writes=["dkd"])
                            P.op("dve", lambda e: e.tensor_tensor(out=qg[:, :], in0=qT[:, bsl], in1=egcb[:, :], op=ALU.mult),
                                 reads=["dqT", "degcb"], writes=["dqg"])
                            P.op("pool", lambda e: e.tensor_copy(out=aT[:, :, :], in_=aTt[:, :, :]), reads=["daTt"], writes=["daT"])
                            for hb in range(2):
                                P.op("act", lambda e: e.activation(out=u_sb[:, hb * 4:(hb + 1) * 4, :].rearrange("p a b -> p (a b)"),
                                                                   in_=ps[2 + hb][H, :], func=AF.Copy),
                                     reads=[PS[2 + hb]], writes=["du"])
                            P.op("dve", lambda e: e.tensor_copy(out=wT[:, :, :].rearrange("p a b -> p (a b)"), in_=ps[4][:, :]),
                                 reads=[PS[4]], writes=["dwT"])
                            yield

                        def stage_b(bt):
                            n0 = bt * NB
                            bsl = slice(n0 * 64, (n0 + NB) * 64)
                            for i in range(NB):
                                n = n0 + i
                                vn = vnew[n % 2]
                                kvn = ("dvnew", n % 2)
                                if n == 0:
                                    P.op("dve", lambda e: e.tensor_copy(out=vn[:, :], in_=u_sb[:, i, :]), reads=["du"], writes=[kvn])
                                else:
                                    P.op("pe", lambda e: e.matmul(ps[5][H, 0:128], lhsT=wT[:, i, :], rhs=Sb[:, :], start=True, stop=True),
                                         reads=["dwT", "dSb"], writes=[PS[5]])
                                    yield
                                    P.op("dve", lambda e: e.tensor_tensor(out=vn[:, :], in0=u_sb[:, i, :], in1=ps[5][H, 0:128], op=ALU.subtract),
                                         reads=["du", PS[5]], writes=[kvn])
                                yield
                                osl = slice(i * 64, (i + 1) * 64)
                                if n < 31:
                                    P.op("pe", lambda e: e.matmul(ps[6][:, 0:128], lhsT=kd[:, i, :], rhs=vn[:, :], start=True, stop=True),
                                         reads=["dkd", kvn], writes=[PS[6]])
                                if n > 0:
                                    P.op("pe", lambda e: e.matmul(ps[7][:, osl], lhsT=Sb[:, :], rhs=qg[:, osl], start=True, stop=False),
                                         reads=["dSb", "dqg"], writes=[PS[7]], inc=False)
                                P.op("pe", lambda e: e.matmul(ps[7][:, osl], lhsT=vn[:, :], rhs=aT[:, i, :], start=(n == 0), stop=True),
                                     reads=[kvn, "daT"], writes=[PS[7]])
                                yield
                                if n < 31:
                                    if n == 0:
                                        P.op("dve", lambda e: e.tensor_copy(out=Sb[:, :], in_=ps[6][:, 0:128]), reads=[PS[6]], writes=["dSb"])
                                        P.op("dve", lambda e: e.tensor_copy(out=Sf[:, :], in_=ps[6][:, 0:128]), reads=[PS[6]], writes=["dSf"])
                                    else:
                                        P.op("dve", lambda e: e.scalar_tensor_tensor(out=Sb[:, :], in0=Sf[:, :], scalar=egl[:, n, h:h + 1],
                                                                                     in1=ps[6][:, 0:128], op0=ALU.mult, op1=ALU.add),
                                             reads=[PS[6], "dSf", "degl"], writes=["dSb"])
                                        P.op("dve", lambda e: e.scalar_tensor_tensor(out=Sf[:, :], in0=Sf[:, :], scalar=egl[:, n, h:h + 1],
                                                                                     in1=ps[6][:, 0:128], op0=ALU.mult, op1=ALU.add),
                                             reads=[PS[6], "dSf", "degl"], writes=["dSf"])
                                    yield
                            P.op("act", lambda e: e.activation(out=nsq[:, :], in_=ps[7][:, :], func=AF.Square),
                                 reads=[PS[7]], writes=["dnsq"])
                            yield
                            P.op("pe", lambda e: e.matmul(ps[5][:, :], lhsT=cst["ones"][:, :], rhs=nsq[:, :], start=True, stop=True),
                                 reads=["k_ones", "dnsq"], writes=[PS[5]])
                            yield
                            P.op("act", lambda e: e.activation(out=rstd[:, :], in_=ps[5][:, :], func=AF.Ln, bias=rtrp[:, 3:4],
                                                               scale=float(1.0 / 128.0)), reads=[PS[5], "rtrp"], writes=["rstd"])
                            P.op("act", lambda e: e.activation(out=rstd[:, :], in_=rstd[:, :], func=AF.Exp, scale=-0.5),
                                 reads=["rstd"], writes=["rstd"])
                            yield
                            P.op("dve", lambda e: e.scalar_tensor_tensor(out=ntmp[:, :], in0=ps[7][:, :], scalar=dng[:, l:l + 1],
                                                                         in1=rstd[:, :], op0=ALU.mult, op1=ALU.mult),
                                 reads=[PS[7], "dng", "rstd"], writes=["dntmp"])
                            P.op("dve", lambda e: e.tensor_tensor(out=obT[:, h, bsl], in0=ntmp[:, :], in1=zsT[:, bsl], op=ALU.mult),
                                 reads=["dntmp", "dz"], writes=[("obT", h)])
                            yield

                        import os as _os2
                        _stop = int(_os2.environ.get("DN_STOP", "-1"))
                        if _stop >= 0:
                            g_ = stage_a(0)
                            for _ in range(_stop):
                                next(g_)
                        else:
                            def drive(b_gen, a_gen):
                                a_wait, a_done, b_done = False, a_gen is None, b_gen is None
                                while not (b_done and (a_done or a_wait)):
                                    if not b_done:
                                        try:
                                            next(b_gen)
                                        except StopIteration:
                                            b_done = True
                                    if not a_done and not a_wait:
                                        try:
                                            if next(a_gen) == "OUT":
                                                a_wait = True
                                        except StopIteration:
                                            a_done = True
                                if not a_done:
                                    for _ in a_gen:
                                        pass

                            drive(None, stage_a(0))
                            for bt in range(4):
                                drive(stage_b(bt), stage_a(bt + 1) if bt + 1 < 4 else None)
                        phase_barrier()

        def dump(nm, tile, nchunks, keyname, nokey=False):
            if nokey:
                phase_barrier()
            with nc.sbuf_tensor(_u + "s_dbgf_" + nm, [128, S], F32) as dbgf:
                for k in range(nchunks):
                    P.op("dve", lambda e: e.tensor_copy(out=dbgf[:, :], in_=tile[:, k, :]),
                         reads=[(keyname, k)], writes=["dbgf"])
                    P.dma(dbg_d[nm][k * 128:(k + 1) * 128, :], dbgf[:, :], reads=["dbgf"], writes=["dbg_" + nm], stream="o")
                phase_barrier()

        for l in range(n_layers):
            norm_to(lambda k, tc: (hT[:, k, tc * TC:(tc + 1) * TC], ("hT", k)), normg[:, l, :], l)
            if debug and l == 0:
                with nc.sbuf_tensor(_u + "dbgf", [128, S], F32) as dbgf:
                    for k in range(8):
                        P.op("dve", lambda e, k=k: e.tensor_copy(out=dbgf[:, :], in_=hT[:, k, :]),
                             reads=[("hT", k)], writes=["dbgf"])
                        P.dma(dbg_d["hT"][k * 128:(k + 1) * 128, :], dbgf[:, :], reads=["dbgf"], writes=["dbg_hT"],
                              stream="o")
                    phase_barrier()

            def mem_attention(l):
                ph = ExitStack()
                def sbp(name, shape, dt):
                    return ph.enter_context(nc.sbuf_tensor(_u + "s_" + name, list(shape), dt))
                memT = sbp("memT", [128, 8, MEM_LEN], F32)
                memn = sbp("memn", [128, 8, MEM_LEN], BF16)
                kmT = sbp("kmT", [128, 2, MEM_LEN], BF16)
                vm = sbp("vm", [128, 2, 256], BF16)
                qmT = sbp("qmT", [128, 2, S], BF16)
                pT = [sbp(f"pT{i}", [128, TC], BF16) for i in range(2)]
                rden = sbp("rden", [128, TC], F32)
                for k in range(8):
                    P.dma(memT[:, k, :], memT_d[k * 128:(k + 1) * 128, :], writes=[("memT", k)], stream="x")
                P.op("dve", lambda e: e.tensor_scalar(out=g32[:, :], in0=memg[:, l, :], scalar1=float(math.sqrt(D)),
                                                      scalar2=None, op0=ALU.mult), reads=["memg"], writes=["g32"])
                rms_stats(memT, [("memT", k) for k in range(8)], 8, MEM_LEN, 0, 0, sqt, "sqt", rstd, "rstd", None)
                for k in range(8):
                    P.op("dve", lambda e, k=k: e.scalar_tensor_tensor(
                        out=memn[:, k, :], in0=memT[:, k, :], scalar=g32[:, k:k + 1], in1=rstd[:, 0:MEM_LEN],
                        op0=ALU.mult, op1=ALU.mult), reads=[("memT", k), "g32", "rstd"], writes=[("memn", k)])
                for ec in range(2):
                    wb, wkey = load_w("w_kv", l, ec * 128)
                    for k in range(8):
                        P.op("pe", lambda e, k=k, wb=wb: e.matmul(ps[1][:, 0:MEM_LEN], lhsT=wb[:, k, :], rhs=memn[:, k, :],
                                                                  start=(k == 0), stop=(k == 7)),
                             reads=[wkey, ("memn", k)], writes=[PS[1]], inc=(k == 7))
                    P.op("dve", lambda e, ec=ec: e.tensor_copy(out=kmT[:, ec, :], in_=ps[1][:, 0:MEM_LEN]),
                         reads=[PS[1]], writes=[("kmT", ec)])
                for vc in range(2):
                    wb, wkey = load_w("w_kv", l, 256 + vc * 128)
                    for mt in range(2):
                        for k in range(8):
                            P.op("pe", lambda e, k=k, wb=wb, mt=mt: e.matmul(
                                ps[2][:, 0:128], lhsT=memn[:, k, mt * 128:(mt + 1) * 128], rhs=wb[:, k, :],
                                start=(k == 0), stop=(k == 7)),
                                reads=[wkey, ("memn", k)], writes=[PS[2]], inc=(k == 7))
                        P.op("dve", lambda e, mt=mt, vc=vc: e.tensor_copy(out=vm[:, mt, vc * 128:(vc + 1) * 128],
                                                                          in_=ps[2][:, 0:128]),
                             reads=[PS[2]], writes=[("vm", mt)])
                for ec in range(2):
                    def ev(tc, pap, pkey, ec=ec):
                        P.op("act", lambda e: e.activation(out=qmT[:, ec, tc * TC:(tc + 1) * TC], in_=pap,
                                                           func=AF.Copy, scale=0.125),
                             reads=[pkey], writes=[("qmT", ec)])
                    proj_fm(l, C_MQ + ec * 128, 128, ev)
                for h in range(4):
                    ec, r0 = h // 2, (h % 2) * 64
                    for tc in range(NT):
                        tsl = slice(tc * TC, (tc + 1) * TC)
                        for mb in range(2):
                            pi = 3 + mb
                            P.op("pe", lambda e, mb=mb, pi=pi: e.matmul(
                                ps[pi][:, :], lhsT=kmT[r0:r0 + 64, ec, mb * 128:(mb + 1) * 128],
                                rhs=qmT[r0:r0 + 64, ec, tsl], start=True, stop=True),
                                reads=[("kmT", ec), ("qmT", ec)], writes=[PS[pi]])
                            P.op("act", lambda e, mb=mb, pi=pi: e.activation(out=pT[mb][:, :], in_=ps[pi][:, :],
                                                                             func=AF.Exp),
                                 reads=[PS[pi]], writes=[("pT", mb)])
                        for mb in range(2):
                            P.op("pe", lambda e, mb=mb: e.matmul(
                                ps[5][r0:r0 + 64, :], lhsT=vm[:, mb, h * 64:(h + 1) * 64], rhs=pT[mb][:, :],
                                start=(mb == 0), stop=(mb == 1)),
                                reads=[("vm", mb), ("pT", mb)], writes=[PS[5]], inc=(mb == 1))
                        for mb in range(2):
                            P.op("pe", lambda e, mb=mb: e.matmul(
                                ps[6][r0:r0 + 64, :], lhsT=cst["ones"][:, 0:64], rhs=pT[mb][:, :],
                                start=(mb == 0), stop=(mb == 1)),
                                reads=["k_ones", ("pT", mb)], writes=[PS[6]], inc=(mb == 1))
                        P.op("act", lambda e: e.activation(out=rden[r0:r0 + 64, :], in_=ps[6][r0:r0 + 64, :], func=AF.Ln),
                             reads=[PS[6]], writes=["rden"])
                        P.op("act", lambda e: e.activation(out=rden[r0:r0 + 64, :], in_=rden[r0:r0 + 64, :], func=AF.Exp, scale=-1.0),
                             reads=["rden"], writes=["rden"])
                        P.op("dve", lambda e, tsl=tsl: e.tensor_tensor(out=obT[r0:r0 + 64, ec, tsl], in0=ps[5][r0:r0 + 64, :],
                                                                      in1=rden[r0:r0 + 64, :], op=ALU.mult),
                             reads=[PS[5], "rden"], writes=[("obT", ec)])
                phase_barrier()
                ph.close()

            def merge(br, nwc, first):
                with ExitStack() as ph:
                    sg = [ph.enter_context(nc.sbuf_tensor(_u + f"sg{i}", [128, TC], F32)) for i in range(2)]
                    tmp = [ph.enter_context(nc.sbuf_tensor(_u + f"mtmp{i}", [128, TC], BF16)) for i in range(2)]
                    it = 0
                    for dc in range(8):
                        wg, wgk = load_w("w_in", l, C_G + br * D + dc * 128)
                        wr, wrk = load_w(f"w_br{br}", l, dc * 128, rows=nwc)
                        for tc in range(NT):
                            tsl = slice(tc * TC, (tc + 1) * TC)
                            b = it % 2
                            it += 1
                            pg, pp = 1 + b, 3 + b
                            for k in range(8):
                                P.op("pe", lambda e, k=k, pg=pg, tsl=tsl, wg=wg: e.matmul(
                                    ps[pg][:, :], lhsT=wg[:, k, :], rhs=hT[:, k, tsl], start=(k == 0), stop=(k == 7)),
                                    reads=[wgk, ("hT", k)], writes=[PS[pg]], inc=(k == 7))
                            for k in range(nwc):
                                P.op("pe", lambda e, k=k, pp=pp, tsl=tsl, wr=wr: e.matmul(
                                    ps[pp][:, :], lhsT=wr[:, k, :], rhs=obT[:, k, tsl], start=(k == 0), stop=(k == nwc - 1)),
                                    reads=[wrk, ("obT", k)], writes=[PS[pp]], inc=(k == nwc - 1))
                            P.op("act", lambda e, b=b, pg=pg, dc=dc: e.activation(
                                out=sg[b][:, :], in_=ps[pg][:, :], func=AF.Sigmoid,
                                bias=bgate[:, l, br * 8 + dc:br * 8 + dc + 1], scale=1.0),
                                reads=[PS[pg], "bgate"], writes=[("sg", b)])
                            if first:
                                P.op("dve", lambda e, b=b, pp=pp, dc=dc, tsl=tsl: e.tensor_tensor(
                                    out=mT[:, dc, tsl], in0=ps[pp][:, :], in1=sg[b][:, :], op=ALU.mult),
                                    reads=[PS[pp], ("sg", b)], writes=[("mT", dc, tc)])
                            else:
                                P.op("dve", lambda e, b=b, pp=pp: e.tensor_tensor(
                                    out=tmp[b][:, :], in0=ps[pp][:, :], in1=sg[b][:, :], op=ALU.mult),
                                    reads=[PS[pp], ("sg", b)], writes=[("mtmp", b)])
                                P.op("dve", lambda e, b=b, dc=dc, tsl=tsl: e.tensor_tensor(
                                    out=mT[:, dc, tsl], in0=mT[:, dc, tsl], in1=tmp[b][:, :], op=ALU.add),
                                    reads=[("mtmp", b), ("mT", dc, tc)], writes=[("mT", dc, tc)])
                    phase_barrier()

            if 1 in branches:
                deltanet(l)
                if debug and l == 0:
                    dump('odn', obT, 4, 'obT')
                merge(1, 4, True)
            mem_attention(l)
            if debug and l == 0:
                dump('omem', obT, 2, 'obT')
            merge(3, 2, 1 not in branches)
            if 0 in branches:
                sb_attention(l)
                if debug and l == 0:
                    dump('osb', obT, 4, 'obT')
                merge(0, 4, False)
            if 2 in branches:
                retention(l)
                if debug and l == 0:
                    dump('ort', obT, 4, 'obT')
                merge(2, 4, False)

            for ec in range(8):
                wo, wok = load_w("w_out", l, ec * 128)
                for tc in range(NT):
                    tsl = slice(tc * TC, (tc + 1) * TC)
                    pi = 1 + tc % 2
                    for dc in range(8):
                        P.op("pe", lambda e, dc=dc, pi=pi, tsl=tsl, wo=wo: e.matmul(
                            ps[pi][:, :], lhsT=wo[:, dc, :], rhs=mT[:, dc, tsl], start=(dc == 0), stop=(dc == 7)),
                            reads=[wok, ("mT", dc, tc)], writes=[PS[pi]], inc=(dc == 7))
                    P.op("dve", lambda e, ec=ec, pi=pi, tsl=tsl: e.tensor_tensor(
                        out=xT[:, ec, tsl], in0=xT[:, ec, tsl], in1=ps[pi][:, :], op=ALU.add),
                        reads=[PS[pi], ("xT", ec)], writes=[("xT", ec)])
            phase_barrier()

        with ExitStack() as ph:
            ot = [ph.enter_context(nc.sbuf_tensor(_u + f"ot{i}", [128, TC], F32)) for i in range(2)]
            cnt = [0]

            def dst(k, tc):
                b = cnt[0] % 2
                cnt[0] += 1
                return ot[b][:, :], ("ot", b)
            P.op("dve", lambda e: e.tensor_scalar(out=g32[:, :], in0=fing[:, :], scalar1=float(math.sqrt(D)), scalar2=None,
                                                  op0=ALU.mult), reads=["fing"], writes=["g32"])
            for tc in range(NT):
                rms_stats(xT, [("xT", k) for k in range(8)], 8, TC, tc * TC, 0, sqt, "sqt", rstd, "rstd", None)
                for k in range(8):
                    ap, key = dst(k, tc)
                    P.op("dve", lambda e, k=k, tc=tc, ap=ap: e.scalar_tensor_tensor(
                        out=ap, in0=xT[:, k, tc * TC:(tc + 1) * TC], scalar=g32[:, k:k + 1], in1=rstd[:, :],
                        op0=ALU.mult, op1=ALU.mult),
                        reads=[("xT", k), "g32", "rstd"], writes=[key])
                    P.dma(outT_d[k * 128:(k + 1) * 128, tc * TC:(tc + 1) * TC], ap, reads=[key], writes=["outT"],
                          stream="o")
            P.finish(["outT", "dbg_hT", "dbg_omem", "dbg_osb", "dbg_odn", "dbg_ort"])
            phase_barrier(full=True)
        P.emit()
        print("instructions recorded:", P.nins, {n: len(P.q[n]) for n in P.names})
    return nc


_NC_CACHE = {}


def _prep_inputs(inputs, b):
    f = np.float32
    m = {}
    m["xT"] = np.ascontiguousarray(inputs["x"][b].T.astype(f))
    m["memT"] = np.ascontiguousarray(inputs["mem"][b].T.astype(f))
    m["w_in"] = np.ascontiguousarray(inputs["w_in"], dtype=f)
    m["w_mem_kv"] = np.ascontiguousarray(inputs["w_mem_kv"], dtype=f)
    for n in ["w_br_sb", "w_br_dn", "w_br_ret", "w_br_mem", "w_out"]:
        m[n] = np.ascontiguousarray(inputs[n], dtype=f)
    m["norm_g"] = np.ascontiguousarray(inputs["norm_g"].reshape(DEPTH, 8, 128).transpose(0, 2, 1), dtype=f)
    m["mem_norm_g"] = np.ascontiguousarray(inputs["mem_norm_g"].reshape(DEPTH, 8, 128).transpose(0, 2, 1), dtype=f)
    m["b_gate"] = np.ascontiguousarray(inputs["b_gate"].reshape(DEPTH, 32, 128).transpose(0, 2, 1), dtype=f)
    m["final_norm_g"] = np.ascontiguousarray(inputs["final_norm_g"].reshape(8, 128).T, dtype=f)
    for n, v in _consts().items():
        m["c_" + n] = v
    for n, v in _rt_consts().items():
        m["c_" + n] = v
    m["c_dn"] = _dn_consts()
    m["dn_conv_w"] = np.ascontiguousarray(inputs["dn_conv_w"].reshape(DEPTH, 4, 12, 128).transpose(0, 3, 2, 1), dtype=f)
    m["dn_norm_g"] = np.ascontiguousarray(inputs["dn_norm_g"].T, dtype=f)
    m["dn_alog"] = np.ascontiguousarray(np.broadcast_to(np.tile(inputs["dn_a_log"], (1, 32))[:, None, :], (DEPTH, 128, 128)), dtype=f)
    m["dn_dtb"] = np.ascontiguousarray(np.broadcast_to(np.tile(inputs["dn_dt_bias"], (1, 32))[:, None, :], (DEPTH, 128, 128)), dtype=f)
    m["pos"] = np.ascontiguousarray(np.broadcast_to(inputs["positions"][b].astype(np.int32)[None, :], (128, S)))
    m["ret_norm_g"] = np.ascontiguousarray(inputs["ret_norm_g"].reshape(DEPTH, 4, 128).transpose(0, 2, 1), dtype=f)
    return m


def kernel(**inputs):
    inputs = {k: np.asarray(v) for k, v in inputs.items()}
    if "nc" not in _NC_CACHE:
        _NC_CACHE["nc"] = build()
    nc = _NC_CACHE["nc"]
    in_maps = [_prep_inputs(inputs, b) for b in range(8)]
    res = run_bass_kernel_spmd(nc, in_maps, core_ids=list(range(8)))
    out = np.stack([np.ascontiguousarray(res.results[b]["outT"].T) for b in range(8)], axis=0)
    return out.astype(np.float32)
```
